# Optimizing a Trainium2 kernel written in Bass

```python
import jax
import jax.numpy as jnp
from jax import lax
import numpy as np

D_MODEL = 1024
BATCH = 16
SEQ = 2048
DEPTH = 2

GRID_W = 64
CTX_LEN = 256
HEAD_DIM = 64
N_GROUPS = 4
GROUP_HEADS = D_MODEL // HEAD_DIM // N_GROUPS
GROUP_WIDTH = GROUP_HEADS * HEAD_DIM
MIX_WIDTH = N_GROUPS * GROUP_WIDTH
GQA_KV_HEADS = GROUP_HEADS // 2
KV_WIDTH = GQA_KV_HEADS * HEAD_DIM
MLP_HIDDEN = 4 * D_MODEL
CHUNK = 64
Q_BLOCK = 128
NA_KH = 8
NA_KW = 16
ROPE_THETA = 10000.0
EPS = 1e-6
NEG_BIG = -1e30
FORGET_FLOOR = 1e-20
IN_SPLITS = (
    GROUP_WIDTH, GROUP_WIDTH, GROUP_WIDTH, GROUP_WIDTH, GROUP_WIDTH,
    GROUP_WIDTH, GROUP_WIDTH, GROUP_WIDTH, GROUP_WIDTH, 4 * GROUP_HEADS,
    GROUP_WIDTH, KV_WIDTH, KV_WIDTH,
    GROUP_WIDTH, GROUP_WIDTH, GROUP_WIDTH,
)
IN_WIDTH = sum(IN_SPLITS)
F32 = jnp.float32

kernel_name = "hybrid_parallel_group_diffusion_block"


def _rmsnorm(t, g):
    tf = t.astype(F32)
    y = tf * lax.rsqrt(jnp.mean(tf * tf, axis=-1, keepdims=True) + EPS)
    return (y * g.astype(F32)).astype(t.dtype)


def _modulate(h, shift, scale):
    return h * (1 + scale) + shift


def _split_in(z):
    idx = [int(o) for o in np.cumsum(IN_SPLITS)[:-1]]
    return jnp.split(z, idx, axis=-1)


def _heads(t, nh):
    b, s, _ = t.shape
    return t.reshape(b, s, nh, -1).transpose(0, 2, 1, 3)


def _merge(t):
    b, h, s, e = t.shape
    return t.transpose(0, 2, 1, 3).reshape(b, s, h * e)


def _to_chunks(t):
    b, h, s, e = t.shape
    return t.reshape(b, h, s // CHUNK, CHUNK, e).transpose(2, 0, 1, 3, 4)


def _from_chunks(t):
    nc, b, h, l, e = t.shape
    return t.transpose(1, 2, 0, 3, 4).reshape(b, h, nc * l, e)


def _axial_rope(n_tokens, dtype):
    t = jnp.arange(n_tokens)
    row = (t // GRID_W).astype(F32)
    col = (t % GRID_W).astype(F32)
    axis_dims = HEAD_DIM // 2
    inv = jnp.power(ROPE_THETA, -2.0 * jnp.arange(axis_dims // 2, dtype=F32) / axis_dims)
    ang = jnp.concatenate([row[:, None] * inv, col[:, None] * inv], axis=-1)
    return jnp.cos(ang).astype(dtype), jnp.sin(ang).astype(dtype)


def _apply_rope(t, cos, sin):
    t2 = t.reshape(*t.shape[:-1], t.shape[-1] // 2, 2)
    t0, t1 = t2[..., 0], t2[..., 1]
    return jnp.stack([t0 * cos - t1 * sin, t0 * sin + t1 * cos], axis=-1).reshape(t.shape)


def _hgrn2_inputs(q, i, f, lb):
    q = _heads(jax.nn.silu(q), GROUP_HEADS).astype(F32) * HEAD_DIM ** -0.5
    v = _heads(i, GROUP_HEADS).astype(F32)
    f = _heads(f, GROUP_HEADS).astype(F32)
    lbh = lb.reshape(GROUP_HEADS, 1, HEAD_DIM).astype(F32)
    forget = lbh + (1 - lbh) * jax.nn.sigmoid(f)
    log_forget = jnp.log(jnp.maximum(forget, FORGET_FLOOR))
    k = (1 - lbh) * jax.nn.sigmoid(-f)
    return (q, k, v, log_forget)


def _hgrn2_scan(inputs, state):
    tri = jnp.tril(jnp.ones((CHUNK, CHUNK), bool))

    def step(s, blk):
        qc, kc, vc, gc = blk
        b = jnp.cumsum(gc, axis=2)
        rel = jnp.where(tri[:, :, None], b[:, :, :, None, :] - b[:, :, None, :, :], NEG_BIG)
        a = jnp.einsum('bhtd,bhsd,bhtsd->bhts', qc, kc, jnp.exp(rel))
        o = jnp.einsum('bhts,bhsv->bhtv', a, vc) + jnp.einsum('bhtd,bhdv->bhtv', qc * jnp.exp(b), s)
        b_end = b[:, :, -1, :]
        s_new = jnp.exp(b_end)[..., None] * s + jnp.einsum(
            'bhsd,bhsv->bhdv', kc * jnp.exp(b_end[:, :, None, :] - b), vc)
        return s_new, o

    s_fin, o = lax.scan(step, state, tuple(_to_chunks(t) for t in inputs))
    return _from_chunks(o), s_fin


def _mlstm_inputs(q, k, v, gates, gate_b):
    b, s, _ = gates.shape
    q = _heads(q, GROUP_HEADS).astype(F32)
    k = _heads(k, GROUP_HEADS).astype(F32) * HEAD_DIM ** -0.5
    v = _heads(v, GROUP_HEADS).astype(F32)
    g = (gates.astype(F32) + gate_b.astype(F32)).reshape(b, s, 4, GROUP_HEADS).transpose(2, 0, 3, 1)[..., None]
    ig_f, ig_b, fg_f, fg_b = g[0], g[1], g[2], g[3]
    return ((q, k, v, ig_f, jax.nn.log_sigmoid(fg_f)), (q, k, v, ig_b, jax.nn.log_sigmoid(fg_b)))


def _mlstm_scan(inputs, state):
    tri = jnp.tril(jnp.ones((CHUNK, CHUNK), bool))

    def step(carry, blk):
        cmat, nvec, m = carry
        qc, kc, vc, igc, lfc = blk
        igc, lfc = igc[..., 0], lfc[..., 0]
        b = jnp.cumsum(lfc, axis=-1)
        a = b + m[..., None]
        dlog = jnp.where(tri, b[..., :, None] - b[..., None, :] + igc[..., None, :], NEG_BIG)
        m_t = jnp.maximum(a, jnp.max(dlog, axis=-1))
        w_in = jnp.exp(a - m_t)
        p = jnp.exp(dlog - m_t[..., None]) * jnp.einsum('bhtd,bhsd->bhts', qc, kc)
        num = w_in[..., None] * jnp.einsum('bhtd,bhdv->bhtv', qc, cmat) + jnp.einsum('bhts,bhsv->bhtv', p, vc)
        den = w_in * jnp.einsum('bhtd,bhd->bht', qc, nvec) + jnp.sum(p, axis=-1)
        h = num / jnp.maximum(jnp.abs(den), jnp.exp(-m_t))[..., None]
        b_end = b[..., -1]
        g_s = b_end[..., None] - b + igc
        m_new = jnp.maximum(b_end + m, jnp.max(g_s, axis=-1))
        w_old = jnp.exp(b_end + m - m_new)
        w_s = jnp.exp(g_s - m_new[..., None])
        c_new = w_old[..., None, None] * cmat + jnp.einsum('bhs,bhsd,bhsv->bhdv', w_s, kc, vc)
        n_new = w_old[..., None] * nvec + jnp.einsum('bhs,bhsd->bhd', w_s, kc)
        return (c_new, n_new, m_new), h

    s_fin, h = lax.scan(step, state, tuple(_to_chunks(t) for t in inputs))
    return _from_chunks(h), s_fin


def _bidir_prefix(scan_fn, init_state, ctx_f, ctx_b, lat_f, lat_b):
    flip = lambda ts: tuple(jnp.flip(t, axis=2) for t in ts)
    oc_f, sc_f = scan_fn(ctx_f, init_state)
    ol_f, _ = scan_fn(lat_f, sc_f)
    oc_b, sc_b = scan_fn(flip(ctx_b), init_state)
    ol_b, _ = scan_fn(flip(lat_b), sc_b)
    return oc_f + jnp.flip(oc_b, axis=2), ol_f + jnp.flip(ol_b, axis=2)


def _gated_head_norm(o, g, gate):
    return _merge(_rmsnorm(o, g)).astype(gate.dtype) * gate


def _gqa_heads(q, k, v, qn_g, kn_g):
    return (_rmsnorm(_heads(q, GROUP_HEADS), qn_g), _rmsnorm(_heads(k, GQA_KV_HEADS), kn_g),
            _heads(v, GQA_KV_HEADS))


def _gqa_latent(q, k, v, k_ctx, v_ctx):
    b, hq, s, d = q.shape
    hkv = k.shape[1]
    kk = jnp.concatenate([k_ctx, k], axis=2)
    vv = jnp.concatenate([v_ctx, v], axis=2)
    qb = q.reshape(b, hkv, hq // hkv, s // Q_BLOCK, Q_BLOCK, d).transpose(3, 0, 1, 2, 4, 5)

    def block(qi):
        sc = jnp.einsum('bkgqd,bksd->bkgqs', qi, kk).astype(F32) * d ** -0.5
        return jnp.einsum('bkgqs,bksd->bkgqd', jax.nn.softmax(sc, axis=-1).astype(vv.dtype), vv)

    o = lax.map(block, qb)
    return o.transpose(1, 2, 3, 0, 4, 5).reshape(b, hq, s, d)


def _dense_attn(q, k, v):
    b, hq, s, d = q.shape
    hkv = k.shape[1]
    qg = q.reshape(b, hkv, hq // hkv, s, d)
    sc = jnp.einsum('bkgqd,bksd->bkgqs', qg, k).astype(F32) * d ** -0.5
    o = jnp.einsum('bkgqs,bksd->bkgqd', jax.nn.softmax(sc, axis=-1).astype(v.dtype), v)
    return o.reshape(b, hq, s, d)


def _neighbourhood_attn(q, k, v, k_ctx, v_ctx, rpb):
    b, h, s, d = q.shape
    rows = s // GRID_W
    kh = min(NA_KH, rows)
    scale = d ** -0.5
    qg = q.reshape(b, h, rows, GRID_W, d)
    kg = k.reshape(b, h, rows, GRID_W, d)
    vg = v.reshape(b, h, rows, GRID_W, d)
    col = jnp.arange(GRID_W)
    cstart = jnp.clip(col - NA_KW // 2, 0, GRID_W - NA_KW)
    col_in = (col[None, :] >= cstart[:, None]) & (col[None, :] < cstart[:, None] + NA_KW)
    col_idx = jnp.clip(col[None, :] - col[:, None], 1 - NA_KW, NA_KW - 1) + NA_KW - 1
    mask = jnp.broadcast_to(col_in[:, None, :], (GRID_W, kh, GRID_W)).reshape(GRID_W, kh * GRID_W)
    rpb_cols = rpb[:, :, col_idx]

    def row_block(r):
        r0 = jnp.clip(r - kh // 2, 0, rows - kh)
        kr = lax.dynamic_slice_in_dim(kg, r0, kh, axis=2).reshape(b, h, kh * GRID_W, d)
        vr = lax.dynamic_slice_in_dim(vg, r0, kh, axis=2).reshape(b, h, kh * GRID_W, d)
        qr = lax.dynamic_index_in_dim(qg, r, axis=2, keepdims=False)
        row_idx = r0 + jnp.arange(kh) - r + NA_KH - 1
        bias = jnp.take(rpb_cols, row_idx, axis=1).transpose(0, 2, 1, 3).reshape(h, GRID_W, kh * GRID_W)
        s_loc = jnp.einsum('bhqd,bhkd->bhqk', qr, kr).astype(F32) * scale + bias.astype(F32)
        s_loc = jnp.where(mask, s_loc, NEG_BIG)
        s_ctx = jnp.einsum('bhqd,bhcd->bhqc', qr, k_ctx).astype(F32) * scale
        p = jax.nn.softmax(jnp.concatenate([s_ctx, s_loc], axis=-1), axis=-1).astype(v.dtype)
        return jnp.einsum('bhqk,bhkd->bhqd', p, jnp.concatenate([v_ctx, vr], axis=2))

    o = lax.map(row_block, jnp.arange(rows))
    return o.transpose(1, 2, 0, 3, 4).reshape(b, h, s, d)


def _sqrelu_mlp(h, w1, w2):
    return jnp.square(jax.nn.relu(h @ w1)) @ w2


def _layer(x, xc, c, c_ctx, w_mod, b_mod, g1, g2, w_in, lb, hgrn_g, mlstm_b, mlstm_g,
           qn_g, kn_g, rpb, w_out, w1, w2, rope, need_ctx):
    dt = x.dtype
    mod = jnp.split((jax.nn.silu(c) @ w_mod + b_mod)[:, None, :], 6, axis=-1)
    modc = jnp.split((jax.nn.silu(c_ctx) @ w_mod + b_mod)[None, None, :], 6, axis=-1)
    zl = _split_in(_modulate(_rmsnorm(x, g1), mod[0], mod[1]) @ w_in)
    zc = _split_in(_modulate(_rmsnorm(xc, g1), modc[0], modc[1]) @ w_in)
    bsz = x.shape[0]

    a_c, a_l = _bidir_prefix(
        _hgrn2_scan, jnp.zeros((bsz, GROUP_HEADS, HEAD_DIM, HEAD_DIM), F32),
        _hgrn2_inputs(zc[0], zc[1], zc[3], lb[0]), _hgrn2_inputs(zc[0], zc[1], zc[4], lb[1]),
        _hgrn2_inputs(zl[0], zl[1], zl[3], lb[0]), _hgrn2_inputs(zl[0], zl[1], zl[4], lb[1]))
    a_l = _gated_head_norm(a_l, hgrn_g, jax.nn.silu(zl[2]))

    m_cf, m_cb = _mlstm_inputs(zc[5], zc[6], zc[7], zc[9], mlstm_b)
    m_lf, m_lb = _mlstm_inputs(zl[5], zl[6], zl[7], zl[9], mlstm_b)
    st0 = (jnp.zeros((bsz, GROUP_HEADS, HEAD_DIM, HEAD_DIM), F32),
           jnp.zeros((bsz, GROUP_HEADS, HEAD_DIM), F32), jnp.zeros((bsz, GROUP_HEADS), F32))
    b_c, b_l = _bidir_prefix(_mlstm_scan, st0, m_cf, m_cb, m_lf, m_lb)
    b_l = _gated_head_norm(b_l, mlstm_g, jax.nn.sigmoid(zl[8]))

    ql, kl, vl = _gqa_heads(zl[10], zl[11], zl[12], qn_g, kn_g)
    qc_, kc_, vc_ = _gqa_heads(zc[10], zc[11], zc[12], qn_g, kn_g)
    c_l = _merge(_gqa_latent(_apply_rope(ql, *rope), _apply_rope(kl, *rope), vl, kc_, vc_))

    nq, nk, nv = (_heads(t, GROUP_HEADS) for t in zl[13:16])
    cq, ck, cv = (_heads(t, GROUP_HEADS) for t in zc[13:16])
    d_l = _merge(_neighbourhood_attn(nq, nk, nv, ck, cv, rpb))

    y = jnp.concatenate([a_l.astype(dt), b_l.astype(dt), c_l.astype(dt), d_l.astype(dt)], axis=-1) @ w_out
    x = x + mod[2] * y
    x = x + mod[5] * _sqrelu_mlp(_modulate(_rmsnorm(x, g2), mod[3], mod[4]), w1, w2)

    if need_ctx:
        a_cc = _gated_head_norm(a_c, hgrn_g, jax.nn.silu(zc[2]))
        b_cc = _gated_head_norm(b_c, mlstm_g, jax.nn.sigmoid(zc[8]))
        c_cc = _merge(_dense_attn(qc_, kc_, vc_))
        d_cc = _merge(_dense_attn(cq, ck, cv))
        yc = jnp.concatenate([a_cc.astype(dt), b_cc.astype(dt), c_cc.astype(dt), d_cc.astype(dt)], axis=-1) @ w_out
        xc = xc + modc[2] * yc
        xc = xc + modc[5] * _sqrelu_mlp(_modulate(_rmsnorm(xc, g2), modc[3], modc[4]), w1, w2)
    return x, xc


def setup_inputs(seed: int = 0) -> dict:
    key = jax.random.key(seed)
    ks = jax.random.split(key, 20)
    d = D_MODEL
    nrm = lambda k, shape, s: jax.random.normal(k, shape, F32) * s
    gate_base = jnp.concatenate([jnp.zeros((2 * GROUP_HEADS,), F32),
                                 jnp.tile(jnp.linspace(3.0, 6.0, GROUP_HEADS, dtype=F32), 2)])
    return {
        "x": nrm(ks[0], (BATCH, SEQ, d), 1.0),
        "c": nrm(ks[1], (BATCH, d), 1.0),
        "ctx": nrm(ks[2], (BATCH, CTX_LEN, d), 1.0),
        "c_ctx": nrm(ks[3], (d,), 1.0),
        "w_mod": nrm(ks[4], (DEPTH, d, 6 * d), d ** -0.5),
        "b_mod": nrm(ks[5], (DEPTH, 6 * d), 0.02),
        "norm1_g": 1.0 + nrm(ks[6], (DEPTH, d), 0.05),
        "norm2_g": 1.0 + nrm(ks[7], (DEPTH, d), 0.05),
        "w_in": nrm(ks[8], (DEPTH, d, IN_WIDTH), d ** -0.5),
        "hgrn_lb_logits": nrm(ks[9], (DEPTH, 2, GROUP_WIDTH), 0.5),
        "hgrn_norm_g": 1.0 + nrm(ks[10], (DEPTH, HEAD_DIM), 0.05),
        "mlstm_gate_b": gate_base[None, :] + nrm(ks[11], (DEPTH, 4 * GROUP_HEADS), 0.1),
        "mlstm_norm_g": 1.0 + nrm(ks[12], (DEPTH, HEAD_DIM), 0.05),
        "gqa_qnorm_g": 1.0 + nrm(ks[13], (DEPTH, HEAD_DIM), 0.05),
        "gqa_knorm_g": 1.0 + nrm(ks[14], (DEPTH, HEAD_DIM), 0.05),
        "na_rpb": nrm(ks[15], (DEPTH, GROUP_HEADS, 2 * NA_KH - 1, 2 * NA_KW - 1), 0.1),
        "w_out": nrm(ks[16], (DEPTH, MIX_WIDTH, d), MIX_WIDTH ** -0.5),
        "w_mlp1": nrm(ks[17], (DEPTH, d, MLP_HIDDEN), d ** -0.5),
        "w_mlp2": nrm(ks[18], (DEPTH, MLP_HIDDEN, d), MLP_HIDDEN ** -0.5),
        "final_norm_g": 1.0 + nrm(ks[19], (d,), 0.05),
    }


def reference(x, c, ctx, c_ctx, w_mod, b_mod, norm1_g, norm2_g, w_in, hgrn_lb_logits, hgrn_norm_g,
              mlstm_gate_b, mlstm_norm_g, gqa_qnorm_g, gqa_knorm_g, na_rpb, w_out, w_mlp1, w_mlp2,
              final_norm_g):
    rope = _axial_rope(x.shape[1], x.dtype)
    sm = jax.nn.softmax(hgrn_lb_logits.astype(F32), axis=0)
    lbs = jnp.cumsum(sm, axis=0) - sm[0:1]
    xc = ctx
    for l in range(DEPTH):
        x, xc = _layer(x, xc, c, c_ctx, w_mod[l], b_mod[l], norm1_g[l], norm2_g[l], w_in[l], lbs[l],
                       hgrn_norm_g[l], mlstm_gate_b[l], mlstm_norm_g[l], gqa_qnorm_g[l], gqa_knorm_g[l],
                       na_rpb[l], w_out[l], w_mlp1[l], w_mlp2[l], rope, l < DEPTH - 1)
    return _rmsnorm(x, final_norm_g)
```

```python
import numpy as np
import ml_dtypes
import concourse.bass as bass
import concourse.mybir as mybir
from concourse.bass_utils import run_bass_kernel_spmd

F32 = mybir.dt.float32
BF16 = mybir.dt.bfloat16
AF = mybir.ActivationFunctionType
ALU = mybir.AluOpType

NT = 2304
NCTX = 256
EPS = 1e-6
NEG = -1e30
BLOCKS = [(0, 256), (256, 512), (768, 512), (1280, 512), (1792, 512)]
NCH = 36


class Buf:
    __slots__ = ("t", "name", "lw", "aw", "rd")

    def __init__(self, t, name):
        self.t = t
        self.name = name
        self.lw = {}
        self.aw = {}
        self.rd = {}

    def __getitem__(self, idx):
        return self.t[idx]


class Sch:
    NDMA = 10

    def __init__(self, nc):
        self.nc = nc
        self.eng = {"pe": nc.tensor, "dve": nc.vector, "act": nc.scalar, "pool": nc.gpsimd, "sp": nc.sync}
        self.sems = {}
        self.cnt = {}
        self.key = {}
        self.nsem = 0
        for e in self.eng:
            self._newsem(e)
        for q in ("sp", "act", "pool"):
            for k in range(self.NDMA):
                key = "d%s%d" % (q, k)
                self.sems[key] = nc.alloc_semaphore("s_" + key)
                self.cnt[key] = 0
        self.seen = {e: {} for e in self.eng}
        self.dma_rr = {"sp": 0, "act": 0, "pool": 0}
        self.nwaits = 0
        self.nops = 0
        self._stack = []

    def _newsem(self, e):
        self.nsem += 1
        key = "%s@%d" % (e, self.nsem)
        self.sems[key] = self.nc.alloc_semaphore("s_%s_%d" % (e, self.nsem))
        self.cnt[key] = 0
        self.key[e] = key

    def sbuf(self, name, shape, dtype):
        self.nsem += 1
        name = "%s_%d" % (name, self.nsem)
        g = self.nc.sbuf_tensor(name, list(shape), dtype)
        t = g.__enter__()
        self._stack.append(g)
        return Buf(t, name)

    def psum(self, name, shape, dtype=F32):
        self.nsem += 1
        name = "%s_%d" % (name, self.nsem)
        g = self.nc.psum_tensor(name, list(shape), dtype)
        t = g.__enter__()
        self._stack.append(g)
        return Buf(t, name)

    def dram(self, name, shape, dtype, kind="Internal"):
        t = self.nc.dram_tensor(name, list(shape), dtype, kind=kind)
        return Buf(t, name)

    @staticmethod
    def views(buf, n):
        return [Buf(buf.t, "%s.%d" % (buf.name, i)) for i in range(n)]

    def mark(self):
        return len(self._stack)

    def release(self, mark):
        self.barrier()
        while len(self._stack) > mark:
            g = self._stack.pop()
            g.__exit__(None, None, None)

    def _need(self, e, toks, selfdep=True, wtoks=()):
        best = {}
        for k, v in toks:
            if k.startswith("pe@") and e == "pe":
                continue
            if best.get(k, 0) < v:
                best[k] = v
        for k, v in wtoks:
            if k.startswith("pe@") and e == "pe":
                continue
            if (not selfdep) and k == self.key[e]:
                continue
            if best.get(k, 0) < v:
                best[k] = v
        for k, v in best.items():
            if self.seen[e].get(k, 0) >= v:
                continue
            self.eng[e].wait_ge(self.sems[k], v)
            self.seen[e][k] = v
            self.nwaits += 1

    @staticmethod
    def _deps(reads, writes, acc):
        rt, wt = [], []
        for b in reads:
            rt.extend(b.lw.items())
            rt.extend(b.aw.items())
        for b in writes:
            wt.extend(b.lw.items())
            wt.extend(b.aw.items())
            wt.extend(b.rd.items())
        for b in acc:
            wt.extend(b.lw.items())
            wt.extend(b.rd.items())
        return rt, wt

    @staticmethod
    def _commit(tok, reads, writes, acc):
        k, v = tok
        for b in reads:
            if b.rd.get(k, 0) < v:
                b.rd[k] = v
        for b in writes:
            b.lw = {k: v}
            b.aw = {}
            b.rd = {}
        for b in acc:
            if b.aw.get(k, 0) < v:
                b.aw[k] = v

    def op(self, e, fn, reads=(), writes=(), acc=(), selfdep=True):
        rt, wt = self._deps(reads, writes, acc)
        self._need(e, rt, selfdep, wt)
        ins = fn(self.eng[e])
        self.nops += 1
        key = self.key[e]
        self.cnt[key] += 1
        ins.then_inc(self.sems[key], 1)
        self._commit((key, self.cnt[key]), reads, writes, acc)
        return ins

    def mm(self, out_ap, lhsT, rhs, start, stop, reads=(), writes=(), acc=()):
        return self.op("pe", lambda e: e.matmul(out_ap, lhsT, rhs, start=start, stop=stop,
                                                skip_group_check=True), reads, writes, acc)

    def dma(self, q, out_ap, in_ap, reads=(), writes=(), acc=(), **kw):
        key = "d%s%d" % (q, self.dma_rr[q])
        self.dma_rr[q] = (self.dma_rr[q] + 1) % self.NDMA
        rt, wt = self._deps(reads, writes, acc)
        toks = rt + wt
        if self.cnt[key] > 0:
            toks.append((key, self.cnt[key]))
        self._need(q, toks)
        self.cnt[key] += 16
        ins = self.eng[q].dma_start(out=out_ap, in_=in_ap, **kw)
        ins.then_inc(self.sems[key], 16)
        self.nops += 1
        self._commit((key, self.cnt[key]), reads, writes, acc)
        return ins

    def barrier(self):
        toks = [(k, v) for k, v in self.cnt.items() if v > 0]
        for e in self.eng:
            self._need(e, toks)
        for e in list(self.eng):
            if self.cnt[self.key[e]] > 24000:
                self._newsem(e)

    def finish(self):
        toks = [(k, v) for k, v in self.cnt.items() if v > 0]
        self._need("sp", toks)


def _na_tiles():
    uniq = {}
    tmap = {}
    for t in range(16):
        lo, hi = 10 ** 9, -1
        for b in range(2):
            r0 = min(max(2 * t + b - 4, 0), 24)
            lo = min(lo, r0 // 2)
            hi = max(hi, (r0 + 7) // 2)
        for j in range(lo, hi + 1):
            pat = []
            for a in range(2):
                for b in range(2):
                    qr = 2 * t + b
                    kr = 2 * j + a
                    r0 = min(max(qr - 4, 0), 24)
                    pat.append((kr - qr) if (r0 <= kr < r0 + 8) else None)
            pat = tuple(pat)
            if pat not in uniq:
                uniq[pat] = len(uniq)
            tmap[(t, j)] = uniq[pat]
    pats = [None] * len(uniq)
    for p, i in uniq.items():
        pats[i] = p
    return pats, tmap


_NA_PATS, _NA_TMAP = _na_tiles()
NU = len(_NA_PATS)


def _na_gather_index():
    idx = np.full((NU, 128, 128), 465, np.int64)
    qc = np.arange(64)
    cstart = np.clip(qc - 8, 0, 48)
    kc = np.arange(64)
    col_in = (kc[:, None] >= cstart[None, :]) & (kc[:, None] < cstart[None, :] + 16)
    cidx = np.clip(kc[:, None] - qc[None, :], -15, 15) + 15
    for u, pat in enumerate(_NA_PATS):
        for a in range(2):
            for b in range(2):
                dr = pat[a * 2 + b]
                if dr is None:
                    continue
                blk = np.where(col_in, (dr + 7) * 31 + cidx, 465)
                idx[u, a * 64:(a + 1) * 64, b * 64:(b + 1) * 64] = blk
    return idx


def _consts():
    c = np.zeros((128, 576), np.float32)
    c[:, 0:128] = np.eye(128, dtype=np.float32)
    blk = np.zeros((128, 128), np.float32)
    blk[0:64, 0:64] = 1.0
    blk[64:128, 64:128] = 1.0
    c[:, 128:256] = blk
    rt = np.zeros((128, 128), np.float32)
    for i in range(64):
        rt[2 * i + 1, 2 * i] = -1.0
        rt[2 * i, 2 * i + 1] = 1.0
    c[:, 256:384] = rt
    sidx = np.arange(64)[:, None]
    tidx = np.arange(64)[None, :]
    mf = (sidx <= tidx).astype(np.float32)
    mb = (sidx >= tidx).astype(np.float32)
    c[:, 384:448] = np.concatenate([mf, mf], 0)
    c[:, 448:512] = np.concatenate([mb, mb], 0)
    p = np.arange(128)[:, None] % 32
    t16 = np.arange(16)[None, :]
    c[:, 512:528] = ((p <= t16) & (p < 16)).astype(np.float32)
    c[:, 528:544] = ((p >= t16) & (p < 16)).astype(np.float32)
    t = np.arange(2048)
    row = (t // 64).astype(np.float32)
    col = (t % 64).astype(np.float32)
    inv = np.power(np.float32(10000.0), (-2.0 * np.arange(16, dtype=np.float32) / np.float32(32.0))).astype(np.float32)
    ang = np.concatenate([row[:, None] * inv[None, :], col[:, None] * inv[None, :]], -1).astype(np.float32)
    cos = np.cos(ang).astype(np.float32)
    sin = np.sin(ang).astype(np.float32)
    cosf = np.repeat(cos, 2, axis=1).T
    sinf = np.repeat(sin, 2, axis=1).T
    rope = np.concatenate([np.concatenate([cosf, cosf], 0), np.concatenate([sinf, sinf], 0)], 1).astype(np.float32)
    return c, np.ascontiguousarray(rope)


def build(nlayers=2, nseq=2, stop_after=None, dbg=False):
    nc = bass.Bass("TRN2", target_bir_lowering=False)
    s = Sch(nc)

    def inp(name, shape, dt=F32):
        return s.dram(name, shape, dt, kind="ExternalInput")

    x_in = inp("x", [2, 2048, 1024])
    ctx_in = inp("ctx", [2, 256, 1024])
    cvec = inp("cvec", [3, 1024])
    w_mod = inp("w_mod", [2, 1024, 6144])
    b_mod = inp("b_mod", [2, 6144])
    n1g = inp("norm1_g", [2, 1024])
    n2g = inp("norm2_g", [2, 1024])
    w_in = inp("w_in", [2, 1024, 3600])
    lbl = inp("hgrn_lb_logits", [4, 256])
    hgg = inp("hgrn_norm_g", [2, 64])
    mgb = inp("mlstm_gate_b", [2, 16])
    mgg = inp("mlstm_norm_g", [2, 64])
    qng = inp("gqa_qnorm_g", [2, 64])
    kng = inp("gqa_knorm_g", [2, 64])
    nab = inp("na_bias", [2, 4 * NU, 128, 128])
    w_out = inp("w_out", [2, 1024, 1024])
    w1 = inp("w_mlp1", [2, 1024, 4096])
    w2 = inp("w_mlp2", [2, 4096, 1024])
    fng = inp("final_norm_g", [1, 1024])
    cst = inp("consts", [128, 576])
    ropec = inp("rope", [128, 4096])
    y = s.dram("y", [2, 2048, 1024], F32, kind="ExternalOutput")

    okind = "ExternalOutput" if dbg else "Internal"
    xs = s.dram("xs", [128, 8, NT], F32, kind=okind)
    zq_h = s.dram("zq_h", [256, NT], F32, kind=okind)
    zog_h = s.dram("zog_h", [256, NT], F32, kind=okind)
    zlf_h = s.dram("zlf_h", [2, 256, NT], F32, kind=okind)
    zkk_h = s.dram("zkk_h", [2, 256, NT], F32, kind=okind)
    zq_m = s.dram("zq_m", [256, NT], F32, kind=okind)
    zk_m = s.dram("zk_m", [256, NT], F32, kind=okind)
    zog_m = s.dram("zog_m", [256, NT], F32, kind=okind)
    zg_m = s.dram("zg_m", [16, NT], F32, kind=okind)
    zq_g = s.dram("zq_g", [256, NT], F32, kind=okind)
    zk_g = s.dram("zk_g", [2, 128, NT], F32, kind=okind)
    zq_n = s.dram("zq_n", [256, NT], BF16, kind=okind)
    zk_n = s.dram("zk_n", [256, NT], BF16, kind=okind)
    v_tok = s.dram("v_tok", [NT, 896], BF16, kind=okind)
    cat_dbg = s.dram("cat_dbg", [128, 8, NT], BF16, kind=okind) if dbg else None
    h_dbg = s.dram("h_dbg", [128, 8, NT], BF16, kind=okind) if dbg else None
    mod_dbg = s.dram("mod_dbg", [128, 2 * 48 * 3], F32, kind=okind) if dbg else None

    if dbg:
        dbgQb = s.dram("dbgQ", [2, 3, 128, NT], BF16, kind=okind)
        dbgPb = s.dram("dbgP", [128, NT + 1], F32, kind=okind)
        dbgQ = dbgQb
        dbgP = dbgPb
    NSL = True

    cs = s.sbuf("cs", [128, 576], F32)
    identb = s.sbuf("identb", [128, 128], BF16)
    onesb = s.sbuf("onesb", [128, 128], BF16)
    blkb = s.sbuf("blkb", [128, 128], BF16)
    bd1 = s.sbuf("bd1", [128, 128], BF16)
    hT = s.sbuf("hT", [128, 8, NT], BF16)
    hTb = Sch.views(hT, 5)
    modT = s.sbuf("modT", [128, 2, 48, 3], F32)
    AT = s.sbuf("AT", [128, 2, 2, 8, 3], F32)
    gT = s.sbuf("gT", [128, 5, 8], F32)
    gcol = s.sbuf("gcol", [128, 2, 4], F32)
    lb = s.sbuf("lb", [128, 2, 2, 2], F32)
    oml = s.sbuf("oml", [128, 2, 2, 2], F32)
    noml = s.sbuf("noml", [128, 2, 2, 2], F32)
    mgbT = s.sbuf("mgbT", [16, 2], F32)

    identf = cs[:, 0:128]
    blkf = cs[:, 128:256]
    ropeRT = cs[:, 256:384]
    maskFB = {64: [cs[:, 384:448], cs[:, 448:512]], 16: [cs[:, 512:528], cs[:, 528:544]]}

    s.dma("sp", cs[:, :], cst[:, :], writes=[cs])
    s.op("dve", lambda e: e.tensor_copy(out=identb[:, :], in_=identf), reads=[cs], writes=[identb])
    s.op("dve", lambda e: e.tensor_copy(out=blkb[:, :], in_=blkf), reads=[cs], writes=[blkb])
    s.op("dve", lambda e: e.tensor_copy(out=bd1[:, :], in_=blkf), reads=[cs], writes=[bd1])
    s.op("pool", lambda e: e.memset(onesb[:, :], 1.0), writes=[onesb])

    def prologue():
        m0 = s.mark()
        scT = s.sbuf("scT", [128, 8, 3], F32)
        bmT = s.sbuf("bmT", [128, 2, 48], F32)
        lg = s.sbuf("lg", [128, 4, 2], F32)
        wm = [s.sbuf("wm%d" % i, [128, 8, 512], F32) for i in range(2)]
        pm = s.psum("pm_mod", [128, 512], F32)
        for r in range(3):
            s.dma("sp", scT[:, :, r], cvec[r:r + 1, :].rearrange("o (k p) -> p (o k)", p=128), acc=[scT],
                  allow_slow_non_contiguous=NSL)
        for l in range(2):
            s.dma("sp", bmT[:, l, :], b_mod[l:l + 1, :].rearrange("o (j p) -> p (o j)", p=128), acc=[bmT],
                  allow_slow_non_contiguous=NSL)
        gsrc = [n1g[0:1, :], n1g[1:2, :], n2g[0:1, :], n2g[1:2, :], fng[0:1, :]]
        for i, g in enumerate(gsrc):
            s.dma("sp", gT[:, i, :], g.rearrange("o (k p) -> p (o k)", p=128), acc=[gT], allow_slow_non_contiguous=NSL)
        for r in range(4):
            s.dma("sp", lg[:, r, :], lbl[r:r + 1, :].rearrange("o (c p) -> p (o c)", p=128), acc=[lg],
                  allow_slow_non_contiguous=NSL)
        for l in range(2):
            for i, g in enumerate([hgg, mgg, qng, kng]):
                for hh in range(2):
                    s.dma("sp", gcol[64 * hh:64 * hh + 64, l, i:i + 1], g[l:l + 1, :].rearrange("o d -> d o"),
                          acc=[gcol], allow_slow_non_contiguous=NSL)
            s.dma("sp", mgbT[:, l:l + 1], mgb[l:l + 1, :].rearrange("o g -> g o"), acc=[mgbT],
                  allow_slow_non_contiguous=NSL)
        s.op("act", lambda e: e.activation(out=scT[:, :, :], in_=scT[:, :, :], func=AF.Silu), reads=[scT], writes=[scT])
        ex = s.sbuf("ex", [128, 4, 2], F32)
        den = s.sbuf("den", [128, 2, 2], F32)
        s.op("act", lambda e: e.activation(out=ex[:, :, :], in_=lg[:, :, :], func=AF.Exp), reads=[lg], writes=[ex])
        s.op("dve", lambda e: e.tensor_tensor(out=den[:, :, :], in0=ex[:, 0:2, :], in1=ex[:, 2:4, :], op=ALU.add),
             reads=[ex], writes=[den])
        s.op("dve", lambda e: e.reciprocal(out=den[:, :, :], in_=den[:, :, :]), reads=[den], writes=[den])
        s.op("pool", lambda e: e.memset(lb[:, 0, :, :], 0.0), acc=[lb])
        s.op("dve", lambda e: e.tensor_tensor(out=lb[:, 1, :, :], in0=ex[:, 2:4, :], in1=den[:, :, :], op=ALU.mult),
             reads=[ex, den], acc=[lb])
        s.op("dve", lambda e: e.tensor_scalar(out=oml[:, :, :, :], in0=lb[:, :, :, :], scalar1=-1.0, scalar2=1.0,
                                              op0=ALU.mult, op1=ALU.add), reads=[lb], writes=[oml])
        s.op("dve", lambda e: e.tensor_scalar(out=noml[:, :, :, :], in0=lb[:, :, :, :], scalar1=1.0, scalar2=-1.0,
                                              op0=ALU.mult, op1=ALU.add), reads=[lb], writes=[noml])
        it = 0
        for l in range(2):
            for grp in range(12):
                W = wm[it % 2]
                it += 1
                s.dma("sp", W[:, :, :], w_mod[l, :, grp * 512:(grp + 1) * 512].rearrange("(k p) n -> p k n", p=128),
                      writes=[W])
                for m in range(4):
                    idx = grp * 4 + m
                    for k in range(8):
                        s.mm(pm[:, idx * 3:idx * 3 + 3], W[:, k, m * 128:(m + 1) * 128], scT[:, k, :], k == 0, k == 7,
                             reads=[W, scT], acc=[pm])
            s.op("dve", lambda e: e.tensor_tensor(
                out=modT[:, l, :, :], in0=pm[:, 0:144].rearrange("p (j r) -> p j r", r=3),
                in1=bmT[:, l, :].unsqueeze(2).to_broadcast([128, 48, 3]), op=ALU.add),
                reads=[pm, bmT], acc=[modT])
        for l in range(2):
            for w in range(2):
                for k in range(8):
                    j = (1 if w == 0 else 4) * 8 + k
                    s.op("dve", lambda e: e.tensor_scalar(out=AT[:, l, w, k, :], in0=modT[:, l, j, :], scalar1=1.0,
                                                          scalar2=gT[:, w * 2 + l, k:k + 1], op0=ALU.add, op1=ALU.mult),
                         reads=[modT, gT], acc=[AT])
        if dbg:
            s.dma("sp", mod_dbg[:, :], modT[:, :, :, :].rearrange("p l j r -> p (l j r)"), reads=[modT], writes=[mod_dbg])
        s.release(m0)

    def norm_block(X, xoff, n, st, l, w, col, sq, pss, rs, tmps, j):
        s.op("act", lambda e: e.activation(out=sq[:, :, 0:n], in_=X[:, :, xoff:xoff + n], func=AF.Square),
             reads=[X], writes=[sq])
        for k in range(8):
            s.mm(pss[:, 0:n], onesb[:, :], sq[:, k, 0:n], k == 0, k == 7, reads=[sq, onesb],
                 writes=[pss] if k == 0 else [], acc=[] if k == 0 else [pss])
        s.op("act", lambda e: e.activation(out=rs[:, 0:n], in_=pss[:, 0:n], func=AF.Sqrt, scale=1.0 / 1024, bias=EPS),
             reads=[pss], writes=[rs])
        s.op("dve", lambda e: e.reciprocal(out=rs[:, 0:n], in_=rs[:, 0:n]), reads=[rs], writes=[rs])
        sh = 0 if w == 0 else 3
        for k in range(8):
            T = tmps[k % len(tmps)]
            s.op("dve", lambda e: e.scalar_tensor_tensor(out=T[:, 0:n], in0=X[:, k, xoff:xoff + n],
                                                         scalar=AT[:, l, w, k, col:col + 1], in1=rs[:, 0:n],
                                                         op0=ALU.mult, op1=ALU.mult), reads=[X, rs, AT], writes=[T])
            s.op("act", lambda e: e.activation(out=hT[:, k, st:st + n], in_=T[:, 0:n], func=AF.Identity,
                                               bias=modT[:, l, sh * 8 + k, col:col + 1], scale=1.0),
                 reads=[T, modT], acc=[hTb[j]], selfdep=False)

    def p1(b, l):
        m0 = s.mark()
        xb = [s.sbuf("xb%d" % i, [128, 8, 512], F32) for i in range(2)]
        sq = [s.sbuf("sq%d" % i, [128, 8, 512], BF16) for i in range(2)]
        rs = [s.sbuf("rs%d" % i, [128, 512], F32) for i in range(2)]
        tmps = [s.sbuf("tmp%d" % i, [128, 512], F32) for i in range(4)]
        pss = [s.psum("pss%d" % i, [128, 512], F32) for i in range(2)]
        if l == 0:
            xin = [s.sbuf("xin%d" % i, [128, 1024], F32) for i in range(3)]
            pst = [s.psum("pst%d" % i, [128, 512], F32) for i in range(4)]
        cnt = 0
        for j, (st, n) in enumerate(BLOCKS):
            X = xb[j % 2]
            if l == 0:
                for ti in range(n // 128):
                    tok0 = st + ti * 128
                    src = ctx_in[b, tok0:tok0 + 128, :] if tok0 < 256 else x_in[b, tok0 - 256:tok0 - 128, :]
                    xi = xin[cnt % 3]
                    s.dma("sp", xi[:, :], src, writes=[xi])
                    for half in range(2):
                        pt = pst[(cnt * 2 + half) % 4]
                        for kk in range(4):
                            k = half * 4 + kk
                            s.op("pe", lambda e: e.transpose(out=pt[:, kk * 128:(kk + 1) * 128],
                                                             in_=xi[:, k * 128:(k + 1) * 128], identity=identf),
                                 reads=[xi, cs], writes=[pt] if kk == 0 else [], acc=[] if kk == 0 else [pt])
                        s.op("act", lambda e: e.activation(
                            out=X[:, half * 4:(half + 1) * 4, ti * 128:(ti + 1) * 128],
                            in_=pt[:, :].rearrange("p (k t) -> p k t", t=128), func=AF.Copy),
                            reads=[pt], writes=[X] if (ti == 0 and half == 0) else [],
                            acc=[] if (ti == 0 and half == 0) else [X], selfdep=False)
                    cnt += 1
                s.dma("pool", xs[:, :, st:st + n], X[:, :, 0:n], reads=[X], acc=[xs])
            else:
                s.dma("sp", X[:, :, 0:n], xs[:, :, st:st + n], reads=[xs], writes=[X])
            col = 2 if st < 256 else b
            norm_block(X, 0, n, st, l, 0, col, sq[j % 2], pss[j % 2], rs[j % 2], tmps, j)
        if dbg:
            s.dma("sp", h_dbg[:, :, :], hT[:, :, :], reads=hTb, writes=[h_dbg])
        s.release(m0)

    GROUPS = [
        (0, 512, [("f", "hq", 0, 128, 0), ("f", "hq", 128, 128, 1), ("v", 256, 256, 0)]),
        (512, 512, [("f", "hog", 0, 128, 0), ("f", "hog", 128, 128, 1), ("f", "hf0", 256, 128, 0), ("f", "hf0", 384, 128, 1)]),
        (1024, 512, [("f", "hf1", 0, 128, 0), ("f", "hf1", 128, 128, 1), ("f", "mq", 256, 128, 0), ("f", "mq", 384, 128, 1)]),
        (1536, 512, [("f", "mk", 0, 128, 0), ("f", "mk", 128, 128, 1), ("v", 256, 256, 256)]),
        (2048, 272, [("f", "mog", 0, 128, 0), ("f", "mog", 128, 128, 1), ("f", "mg", 256, 16, 0)]),
        (2320, 512, [("f", "gq", 0, 128, 0), ("f", "gq", 128, 128, 1), ("f", "gk", 256, 64, 0), ("f", "gk", 320, 64, 1),
                     ("v", 384, 128, 512)]),
        (2832, 512, [("f", "nq", 0, 128, 0), ("f", "nq", 128, 128, 1), ("f", "nk", 256, 128, 0), ("f", "nk", 384, 128, 1)]),
        (3344, 256, [("v", 0, 256, 640)]),
    ]

    def p2(b, l):
        m0 = s.mark()
        wst = [s.sbuf("wst%d" % i, [128, 8, 512], F32) for i in range(2)]
        wbf = [s.sbuf("wbf%d" % i, [128, 8, 512], BF16) for i in range(2)]
        stg = [s.sbuf("stg%d" % i, [128, 512], F32) for i in range(8)]
        stb = [s.sbuf("stb%d" % i, [128, 512], BF16) for i in range(4)]
        ps = [s.psum("p2ps%d" % i, [128, 512], F32) for i in range(6)]
        st_i = [0]
        sb_i = [0]
        ps_i = [0]

        def nstg():
            st_i[0] += 1
            return stg[st_i[0] % 8]

        def nstb():
            sb_i[0] += 1
            return stb[sb_i[0] % 4]

        def load(gi):
            c0, w, _ = GROUPS[gi]
            s.dma("sp", wst[gi % 2][:, :, 0:w], w_in[l, :, c0:c0 + w].rearrange("(k p) n -> p k n", p=128),
                  writes=[wst[gi % 2]])

        def cast(gi):
            c0, w, _ = GROUPS[gi]
            s.op("pool", lambda e: e.tensor_copy(out=wbf[gi % 2][:, :, 0:w], in_=wst[gi % 2][:, :, 0:w]),
                 reads=[wst[gi % 2]], writes=[wbf[gi % 2]])

        load(0)
        cast(0)
        load(1)
        for gi, (c0, w, jobs) in enumerate(GROUPS):
            if gi + 1 < len(GROUPS):
                cast(gi + 1)
            if gi + 2 < len(GROUPS):
                load(gi + 2)
            W = wbf[gi % 2]
            for job in jobs:
                if job[0] == "v":
                    _, off, ncol, vdst = job
                    for ti in range(18):
                        P = ps[ps_i[0] % 6]
                        ps_i[0] += 1
                        jb = 0 if ti < 2 else 1 + (ti - 2) // 4
                        for k in range(8):
                            s.mm(P[:, 0:ncol], hT[:, k, ti * 128:(ti + 1) * 128], W[:, k, off:off + ncol], k == 0, k == 7,
                                 reads=[hTb[jb], W], writes=[P] if k == 0 else [], acc=[] if k == 0 else [P])
                        B_ = nstb()
                        if ti % 2 == 0:
                            s.op("act", lambda e: e.activation(out=B_[:, 0:ncol], in_=P[:, 0:ncol], func=AF.Copy),
                                 reads=[P], writes=[B_])
                        else:
                            s.op("dve", lambda e: e.tensor_copy(out=B_[:, 0:ncol], in_=P[:, 0:ncol]), reads=[P], writes=[B_])
                        s.dma("pool", v_tok[ti * 128:(ti + 1) * 128, vdst:vdst + ncol], B_[:, 0:ncol], reads=[B_], acc=[v_tok])
                    continue
                _, kind, off, m, pc = job
                for j, (st, n) in enumerate(BLOCKS):
                    P = ps[ps_i[0] % 6]
                    ps_i[0] += 1
                    if kind == "gk":
                        for half in range(2):
                            for k in range(8):
                                s.mm(P[64 * half:64 * half + 64, 0:n], W[:, k, off:off + 64], hT[:, k, st:st + n],
                                     k == 0, k == 7, reads=[hTb[j], W],
                                     writes=[P] if (k == 0 and half == 0) else [], acc=[] if (k == 0 and half == 0) else [P])
                        mm_ = 128
                    else:
                        for k in range(8):
                            s.mm(P[0:m, 0:n], W[:, k, off:off + m], hT[:, k, st:st + n], k == 0, k == 7,
                                 reads=[hTb[j], W], writes=[P] if k == 0 else [], acc=[] if k == 0 else [P])
                        mm_ = m
                    rows = slice(pc * 128, pc * 128 + 128)
                    if kind in ("hq", "hog", "mog", "mq", "mk", "gq", "gk"):
                        S_ = nstg()
                        if kind in ("hq", "hog"):
                            s.op("act", lambda e: e.activation(out=S_[:, 0:n], in_=P[:, 0:n], func=AF.Silu), reads=[P], writes=[S_])
                        elif kind == "mog":
                            s.op("act", lambda e: e.activation(out=S_[:, 0:n], in_=P[:, 0:n], func=AF.Sigmoid), reads=[P], writes=[S_])
                        elif kind == "mk":
                            s.op("dve", lambda e: e.tensor_scalar(out=S_[:, 0:n], in0=P[:, 0:n], scalar1=0.125, scalar2=None,
                                                                  op0=ALU.mult), reads=[P], writes=[S_])
                        else:
                            s.op("dve", lambda e: e.tensor_copy(out=S_[:, 0:n], in_=P[:, 0:n]), reads=[P], writes=[S_])
                        dst = {"hq": zq_h, "hog": zog_h, "mog": zog_m, "mq": zq_m, "mk": zk_m, "gq": zq_g}.get(kind)
                        if kind == "gk":
                            s.dma("pool", zk_g[pc, :, st:st + n], S_[:, 0:n], reads=[S_], acc=[zk_g])
                        else:
                            s.dma("pool", dst[rows, st:st + n], S_[:, 0:n], reads=[S_], acc=[dst])
                    elif kind in ("hf0", "hf1"):
                        d = 0 if kind == "hf0" else 1
                        SG = nstg()
                        FG = nstg()
                        KK = nstg()
                        s.op("act", lambda e: e.activation(out=SG[:, 0:n], in_=P[:, 0:n], func=AF.Sigmoid), reads=[P], writes=[SG])
                        s.op("dve", lambda e: e.tensor_scalar(out=FG[:, 0:n], in0=SG[:, 0:n], scalar1=oml[:, l, d, pc:pc + 1],
                                                              scalar2=lb[:, l, d, pc:pc + 1], op0=ALU.mult, op1=ALU.add),
                             reads=[SG, oml, lb], writes=[FG])
                        s.op("act", lambda e: e.activation(out=FG[:, 0:n], in_=FG[:, 0:n], func=AF.Ln), reads=[FG], writes=[FG])
                        s.op("dve", lambda e: e.tensor_scalar(out=KK[:, 0:n], in0=SG[:, 0:n], scalar1=noml[:, l, d, pc:pc + 1],
                                                              scalar2=oml[:, l, d, pc:pc + 1], op0=ALU.mult, op1=ALU.add),
                             reads=[SG, oml, noml], writes=[KK])
                        s.dma("pool", zlf_h[d, rows, st:st + n], FG[:, 0:n], reads=[FG], acc=[zlf_h])
                        s.dma("pool", zkk_h[d, rows, st:st + n], KK[:, 0:n], reads=[KK], acc=[zkk_h])
                    elif kind == "mg":
                        S1 = nstg()
                        S2 = nstg()
                        s.op("act", lambda e: e.activation(out=S1[0:16, 0:n], in_=P[0:16, 0:n], func=AF.Identity,
                                                           bias=mgbT[:, l:l + 1], scale=1.0), reads=[P, mgbT], writes=[S1])
                        s.op("act", lambda e: e.activation(out=S2[0:16, 0:n], in_=P[0:16, 0:n], func=AF.Sigmoid,
                                                           bias=mgbT[:, l:l + 1], scale=1.0), reads=[P, mgbT], writes=[S2])
                        s.op("act", lambda e: e.activation(out=S2[0:16, 0:n], in_=S2[0:16, 0:n], func=AF.Ln), reads=[S2], writes=[S2])
                        s.dma("pool", zg_m[0:8, st:st + n], S1[0:8, 0:n], reads=[S1], acc=[zg_m])
                        s.dma("pool", zg_m[8:16, st:st + n], S2[8:16, 0:n], reads=[S2], acc=[zg_m])
                    elif kind in ("nq", "nk"):
                        B_ = nstb()
                        s.op("act", lambda e: e.activation(out=B_[:, 0:n], in_=P[:, 0:n], func=AF.Copy), reads=[P], writes=[B_])
                        dst = zq_n if kind == "nq" else zk_n
                        s.dma("pool", dst[rows, st:st + n], B_[:, 0:n], reads=[B_], acc=[dst])
                    else:
                        raise ValueError(kind)
        s.release(m0)

    def head_norm(o, gate, gidx, l, chunk, blocks, sqs, pn, rss, tms):
        for j in blocks:
            st, n = BLOCKS[j]
            SQ = sqs[j % 2]
            R = rss[j % 2]
            T = tms[j % 2]
            s.op("act", lambda e: e.activation(out=SQ[:, 0:n], in_=o[:, st:st + n], func=AF.Square), reads=[o], writes=[SQ])
            s.mm(pn[:, 0:n], blkb[:, :], SQ[:, 0:n], True, True, reads=[blkb, SQ], writes=[pn])
            s.op("act", lambda e: e.activation(out=R[:, 0:n], in_=pn[:, 0:n], func=AF.Sqrt, scale=1.0 / 64, bias=EPS),
                 reads=[pn], writes=[R])
            s.op("dve", lambda e: e.reciprocal(out=R[:, 0:n], in_=R[:, 0:n]), reads=[R], writes=[R])
            s.op("dve", lambda e: e.scalar_tensor_tensor(out=T[:, 0:n], in0=o[:, st:st + n], scalar=gcol[:, l, gidx:gidx + 1],
                                                         in1=R[:, 0:n], op0=ALU.mult, op1=ALU.mult),
                 reads=[o, R, gcol], writes=[T])
            s.op("dve", lambda e: e.tensor_tensor(out=hT[:, chunk, st:st + n], in0=T[:, 0:n], in1=gate[:, st:st + n],
                                                  op=ALU.mult), reads=[T, gate], acc=[hTb[j]], selfdep=False)

    def recur(b, l, kind):
        m0 = s.mark()
        NV = 1 if kind == "h" else 2
        L = 16 if kind == "h" else 64
        NCH = NT // L
        CTXN = NCTX // L
        HB = 64 if L == 64 else 32
        SR = 2 * HB
        last = (l == nlayers - 1)
        blocks = [1, 2, 3, 4] if last else [0, 1, 2, 3, 4]
        PTR = [s.psum("ptr%d" % d, [128, 1024], BF16) for d in range(2)]
        PSC = [s.psum("psc%d" % d, [128, 512], F32) for d in range(2)]
        PSO = [s.psum("pso%d" % d, [128, 512], F32) for d in range(2)]
        PST = [s.psum("pstt%d" % d, [128, 512], F32) for d in range(2)]
        pn = PSC[0]
        lf = s.sbuf("r_lf", [128, NT], F32)
        PP = s.sbuf("r_PP", [128, NT + 1], F32)
        arg = s.sbuf("r_arg", [128, NT], F32)
        Dm = s.sbuf("r_D", [128, NT], F32)
        kk = s.sbuf("r_kk", [128, NT], F32)
        qs = s.sbuf("r_qs", [128, NT], F32)
        igb = s.sbuf("r_ig", [128, NT], F32) if kind == "m" else None
        Q = [s.sbuf("r_Q%d" % d, [128, NT], BF16) for d in range(2)]
        K = [s.sbuf("r_K%d" % d, [128, NT], BF16) for d in range(2)]
        KN = [s.sbuf("r_KN%d" % d, [128, NT], BF16) for d in range(2)]
        Vbd = s.sbuf("r_Vbd", [SR, NCH, 128], BF16)
        Vp = s.sbuf("r_Vp", [L, NCH, 128], BF16)
        Rn = s.sbuf("r_Rn", [128, NCH], F32)
        G = [s.sbuf("r_G%d" % d, [128, NCH], F32) for d in range(2)]
        W32 = [[s.sbuf("r_W%d%d" % (d, v), [128, 64], F32) for v in range(NV)] for d in range(2)]
        Wbf = [[s.sbuf("r_Wb%d%d" % (d, v), [128, 64], BF16) for v in range(NV)] for d in range(2)]
        kts = [s.sbuf("r_kt%d" % i, [L, 128], BF16) for i in range(4)]
        pms = [s.sbuf("r_pm%d" % i, [SR, L], BF16) for i in range(4)]
        for pmb in pms:
            s.op("pool", lambda e: e.memset(pmb[:, :], 0.0), writes=[pmb])
        sqs = [s.sbuf("r_sq%d" % i, [128, 512], BF16) for i in range(2)]
        rss = [s.sbuf("r_rs%d" % i, [128, 512], F32) for i in range(2)]
        tms = [s.sbuf("r_tm%d" % i, [128, 512], F32) for i in range(2)]
        if kind == "h":
            accs = [[lf], [arg]]
        else:
            accs = [[lf, arg], [Dm, kk]]
        zq = zq_h if kind == "h" else zq_m
        zog = zog_h if kind == "h" else zog_m
        vbase = 0 if kind == "h" else 256
        gidx = 0 if kind == "h" else 1
        order = [list(range(NCH)), list(range(CTXN - 1, -1, -1)) + list(range(NCH - 1, CTXN - 1, -1))]
        slot = 0
        for pc in range(2):
            rows = slice(pc * 128, pc * 128 + 128)
            vcol = vbase + pc * 128
            s.op("pool", lambda e: e.memset(Vbd[:, :, :], 0.0), writes=[Vbd])
            for hh in range(2):
                s.dma("sp", Vbd[HB * hh:HB * hh + L, :, 64 * hh:64 * hh + 64],
                      v_tok[:, vcol + 64 * hh:vcol + 64 * hh + 64].rearrange("(c p) d -> p c d", p=L),
                      reads=[v_tok], acc=[Vbd])
            s.dma("sp", Vp[:, :, :], v_tok[:, vcol:vcol + 128].rearrange("(c p) d -> p c d", p=L), reads=[v_tok], writes=[Vp])
            s.dma("sp", qs[:, :], zq[rows, :], reads=[zq], writes=[qs])
            for d in range(2):
                sg = 1.0 if d == 0 else -1.0
                if kind == "h":
                    s.dma("sp", lf[:, :], zlf_h[d, rows, :], reads=[zlf_h], writes=[lf])
                    s.dma("sp", kk[:, :], zkk_h[d, rows, :], reads=[zkk_h], writes=[kk])
                else:
                    for hh in range(2):
                        h = 2 * pc + hh
                        s.dma("sp", lf[64 * hh:64 * hh + 64, :], zg_m[8 + 4 * d + h:9 + 4 * d + h, :].partition_broadcast(64),
                              reads=[zg_m], writes=[lf] if hh == 0 else [], acc=[] if hh == 0 else [lf])
                        s.dma("sp", igb[64 * hh:64 * hh + 64, :], zg_m[4 * d + h:4 * d + h + 1, :].partition_broadcast(64),
                              reads=[zg_m], writes=[igb] if hh == 0 else [], acc=[] if hh == 0 else [igb])
                    s.dma("sp", kk[:, :], zk_m[rows, :], reads=[zk_m], writes=[kk])
                s.op("pool", lambda e: e.memset(PP[:, 0:1], 0.0), writes=[PP])
                s.op("dve", lambda e: e.tensor_tensor_scan(out=PP[:, 1:NT + 1], data0=lf[:, :], data1=lf[:, :], initial=0.0,
                                                           op0=ALU.add, op1=ALU.bypass), reads=[lf], acc=[PP])
                Rm = PP[:, 0:NT].rearrange("p (c l) -> p c l", l=L)[:, :, L // 2]
                if d == 0:
                    s.op("dve", lambda e: e.tensor_copy(out=Rn[:, 0:NCH - 1], in_=Rm[:, 1:NCH]), reads=[PP], writes=[Rn])
                    s.op("dve", lambda e: e.tensor_copy(out=Rn[:, NCH - 1:NCH], in_=Rm[:, NCH - 1:NCH]), reads=[PP], acc=[Rn])
                    s.op("dve", lambda e: e.tensor_tensor(out=G[d][:, :], in0=Rn[:, :], in1=Rm, op=ALU.subtract),
                         reads=[Rn, PP], writes=[G[d]])
                else:
                    s.op("dve", lambda e: e.tensor_copy(out=Rn[:, 1:NCH], in_=Rm[:, 0:NCH - 1]), reads=[PP], writes=[Rn])
                    s.op("dve", lambda e: e.tensor_copy(out=Rn[:, CTXN:CTXN + 1], in_=Rm[:, CTXN:CTXN + 1]), reads=[PP], acc=[Rn])
                    s.op("dve", lambda e: e.tensor_tensor(out=Rn[:, 0:1], in0=Rm[:, NCH - 1:NCH], in1=PP[:, NT:NT + 1],
                                                          op=ALU.subtract), reads=[PP], acc=[Rn])
                    s.op("dve", lambda e: e.tensor_tensor(out=G[d][:, :], in0=Rm, in1=Rn[:, :], op=ALU.subtract),
                         reads=[Rn, PP], writes=[G[d]])
                s.op("act", lambda e: e.activation(out=G[d][:, :], in_=G[d][:, :], func=AF.Exp), reads=[G[d]], writes=[G[d]])
                PPs = (PP[:, 1:NT + 1] if d == 0 else PP[:, 0:NT]).rearrange("p (c l) -> p c l", l=L)
                a3 = arg[:, :].rearrange("p (c l) -> p c l", l=L)
                s.op("dve", lambda e: e.tensor_tensor(out=a3, in0=PPs, in1=Rm.unsqueeze(2).to_broadcast([128, NCH, L]),
                                                      op=ALU.subtract), reads=[PP], writes=[arg])
                s.op("act", lambda e: e.activation(out=Dm[:, :], in_=arg[:, :], func=AF.Exp, scale=sg), reads=[arg], writes=[Dm])
                s.op("dve", lambda e: e.scalar_tensor_tensor(out=Q[d][:, :], in0=qs[:, :], scalar=(0.125 if kind == "h" else 1.0),
                                                             in1=Dm[:, :], op0=ALU.mult, op1=ALU.mult),
                     reads=[qs, Dm], writes=[Q[d]])
                if kind == "m":
                    s.op("dve", lambda e: e.scalar_tensor_tensor(out=Dm[:, :], in0=arg[:, :], scalar=-sg, in1=igb[:, :],
                                                                 op0=ALU.mult, op1=ALU.add), reads=[arg, igb], writes=[Dm])
                    s.op("act", lambda e: e.activation(out=Dm[:, :], in_=Dm[:, :], func=AF.Exp), reads=[Dm], writes=[Dm])
                else:
                    s.op("act", lambda e: e.activation(out=Dm[:, :], in_=arg[:, :], func=AF.Exp, scale=-sg), reads=[arg], writes=[Dm])
                s.op("dve", lambda e: e.tensor_tensor(out=K[d][:, :], in0=kk[:, :], in1=Dm[:, :], op=ALU.mult),
                     reads=[kk, Dm], writes=[K[d]])
                s.op("dve", lambda e: e.tensor_tensor(out=a3, in0=PPs, in1=Rn[:, :].unsqueeze(2).to_broadcast([128, NCH, L]),
                                                      op=ALU.subtract), reads=[PP, Rn], writes=[arg])
                if kind == "m":
                    s.op("dve", lambda e: e.scalar_tensor_tensor(out=Dm[:, :], in0=arg[:, :], scalar=-sg, in1=igb[:, :],
                                                                 op0=ALU.mult, op1=ALU.add), reads=[arg, igb], writes=[Dm])
                    s.op("act", lambda e: e.activation(out=Dm[:, :], in_=Dm[:, :], func=AF.Exp), reads=[Dm], writes=[Dm])
                else:
                    s.op("act", lambda e: e.activation(out=Dm[:, :], in_=arg[:, :], func=AF.Exp, scale=-sg), reads=[arg], writes=[Dm])
                s.op("dve", lambda e: e.tensor_tensor(out=KN[d][:, :], in0=kk[:, :], in1=Dm[:, :], op=ALU.mult),
                     reads=[kk, Dm], writes=[KN[d]])
                for v in range(NV):
                    s.op("pool", lambda e: e.memset(W32[d][v][:, :], 0.0), writes=[W32[d][v]])
            if dbg and stop_after == "hbuild":
                for d in range(2):
                    s.dma("sp", dbgQ[d, 0], Q[d][:, :], reads=[Q[d]], acc=[dbgQb])
                    s.dma("sp", dbgQ[d, 1], K[d][:, :], reads=[K[d]], acc=[dbgQb])
                    s.dma("sp", dbgQ[d, 2], KN[d][:, :], reads=[KN[d]], acc=[dbgQb])
                s.dma("sp", dbgP[:, :], PP[:, :], reads=[PP], writes=[dbgPb])
                s.release(m0)
                return
            for step in range(NCH):
                for d in range(2):
                    c = order[d][step]
                    csl = slice(c * L, c * L + L)
                    sl = slot % 4
                    slot += 1
                    first = (step == 0)
                    lastst = (step == NCH - 1)
                    kt = kts[sl]
                    pm = pms[sl]
                    ptr, psc, pso, pst = PTR[d], PSC[d], PSO[d], PST[d]
                    if not lastst:
                        s.op("pe", lambda e: e.transpose(out=ptr[0:L, 0:128], in_=KN[d][:, csl], identity=identb[:, :]),
                             reads=[KN[d], identb], writes=[ptr])
                        s.op("act", lambda e: e.activation(out=kt[:, :], in_=ptr[0:L, 0:128], func=AF.Copy),
                             reads=[ptr], writes=[kt])
                    for hh in range(2):
                        hs = slice(64 * hh, 64 * hh + 64)
                        ps_ = slice(HB * hh, HB * hh + L)
                        s.mm(psc[ps_, 0:L], K[d][hs, csl], Q[d][hs, csl], True, True, reads=[K[d], Q[d]],
                             writes=[psc] if hh == 0 else [], acc=[] if hh == 0 else [psc])
                    if L == 64:
                        s.op("dve", lambda e: e.tensor_tensor(out=pm[:, :], in0=psc[:, 0:L], in1=maskFB[L][d], op=ALU.mult),
                             reads=[psc, cs], writes=[pm])
                    else:
                        for hh in range(2):
                            ps_ = slice(HB * hh, HB * hh + L)
                            s.op("dve", lambda e: e.tensor_tensor(out=pm[ps_, :], in0=psc[ps_, 0:L], in1=maskFB[L][d][ps_, :],
                                                                  op=ALU.mult), reads=[psc, cs],
                                 writes=[pm] if hh == 0 else [], acc=[] if hh == 0 else [pm], selfdep=(hh == 0))
                    for v in range(NV):
                        Vb = Vbd[:, c, :] if v == 0 else bd1[:, :]
                        vo = slice(v * 64, v * 64 + L)
                        s.mm(pso[:, vo], Vb, pm[:, :], True, first, reads=[Vbd, bd1, pm],
                             writes=[pso] if v == 0 else [], acc=[] if v == 0 else [pso])
                        if not first:
                            for hh in range(2):
                                hs = slice(64 * hh, 64 * hh + 64)
                                s.mm(pso[hs, vo], Wbf[d][v][hs, :], Q[d][hs, csl], False, hh == 1,
                                     reads=[Wbf[d][v], Q[d]], acc=[pso])
                    for v in range(NV):
                        vo = slice(v * 64, v * 64 + L)
                        A_ = accs[d][v]
                        s.op("act", lambda e: e.activation(out=A_[:, csl], in_=pso[:, vo], func=AF.Copy),
                             reads=[pso], acc=[A_], selfdep=False)
                    if not lastst:
                        for v in range(NV):
                            vt = slice(v * 64, v * 64 + 64)
                            for hh in range(2):
                                hs = slice(64 * hh, 64 * hh + 64)
                                rhs = Vp[:, c, hs] if v == 0 else onesb[0:L, 0:64]
                                wr = (v == 0 and hh == 0)
                                s.mm(pst[hs, vt], kt[:, hs], rhs, True, True, reads=[kt, Vp, onesb],
                                     writes=[pst] if wr else [], acc=[] if wr else [pst])
                        for v in range(NV):
                            vt = slice(v * 64, v * 64 + 64)
                            s.op("dve", lambda e: e.scalar_tensor_tensor(out=W32[d][v][:, :], in0=W32[d][v][:, :],
                                                                         scalar=G[d][:, c:c + 1], in1=pst[:, vt],
                                                                         op0=ALU.mult, op1=ALU.add),
                                 reads=[W32[d][v], G[d], pst], writes=[W32[d][v]])
                            s.op("pool", lambda e: e.tensor_copy(out=Wbf[d][v][:, :], in_=W32[d][v][:, :]),
                                 reads=[W32[d][v]], writes=[Wbf[d][v]])
            if kind == "m":
                for d in range(2):
                    num, den = accs[d]
                    s.op("act", lambda e: e.activation(out=den[:, :], in_=den[:, :], func=AF.Abs), reads=[den], writes=[den])
                    s.op("pool", lambda e: e.tensor_scalar_max(out=den[:, :], in0=den[:, :], scalar1=1.0), reads=[den], writes=[den])
                    s.op("dve", lambda e: e.reciprocal(out=den[:, :], in_=den[:, :]), reads=[den], writes=[den])
                    s.op("pool", lambda e: e.tensor_tensor(out=num[:, :], in0=num[:, :], in1=den[:, :], op=ALU.mult),
                         reads=[num, den], writes=[num])
            o = accs[0][0]
            s.op("pool", lambda e: e.tensor_tensor(out=o[:, :], in0=o[:, :], in1=accs[1][0][:, :], op=ALU.add),
                 reads=[o, accs[1][0]], writes=[o])
            s.dma("sp", qs[:, :], zog[rows, :], reads=[zog], writes=[qs])
            chunk = (0 if kind == "h" else 2) + pc
            head_norm(o, qs, gidx, l, chunk, blocks, sqs, pn, rss, tms)
        s.release(m0)

    def attn_block(qb, kb, vfn, vbuf, st, n, kcs, chunk, sc_ps, num_ps, den_ps, pTs, rd, cnt):
        j = [i for i, (a, _) in enumerate(BLOCKS) if a <= st < a + BLOCKS[i][1]][0]
        nk = len(kcs)
        for ki, kc in enumerate(kcs):
            for hh in range(2):
                hs = slice(64 * hh, 64 * hh + 64)
                sc = sc_ps[cnt[0] % len(sc_ps)]
                pT = pTs[cnt[0] % len(pTs)]
                cnt[0] += 1
                s.mm(sc[:, 0:n], kb[hs, kc * 128:(kc + 1) * 128], qb[hs, st:st + n], True, True, reads=[kb, qb], writes=[sc])
                s.op("act", lambda e: e.activation(out=pT[:, 0:n], in_=sc[:, 0:n], func=AF.Exp, scale=0.125),
                     reads=[sc], writes=[pT])
                first = (ki == 0)
                s.mm(num_ps[hs, 0:n], vfn(kc, hh), pT[:, 0:n], first, ki == nk - 1, reads=[pT, vbuf],
                     writes=[num_ps] if (first and hh == 0) else [], acc=[] if (first and hh == 0) else [num_ps])
                s.mm(den_ps[hs, 0:n], onesb[:, 0:64], pT[:, 0:n], first, ki == nk - 1, reads=[pT, onesb],
                     writes=[den_ps] if (first and hh == 0) else [], acc=[] if (first and hh == 0) else [den_ps])
        s.op("dve", lambda e: e.reciprocal(out=rd[:, 0:n], in_=den_ps[:, 0:n]), reads=[den_ps], writes=[rd])
        s.op("dve", lambda e: e.tensor_tensor(out=hT[:, chunk, st:st + n], in0=num_ps[:, 0:n], in1=rd[:, 0:n], op=ALU.mult),
             reads=[num_ps, rd], acc=[hTb[j]], selfdep=False)

    def gqa(b, l):
        m0 = s.mark()
        last = (l == nlayers - 1)
        rp = s.sbuf("g_rope", [128, 4096], F32)
        s.dma("sp", rp[:, :], ropec[:, :], writes=[rp])
        raw = [s.sbuf("g_raw%d" % i, [128, NT], F32) for i in range(2)]
        QK = [s.sbuf("g_qk%d" % i, [128, NT], BF16) for i in range(4)]
        sqs = [s.sbuf("g_sq%d" % i, [128, 512], BF16) for i in range(2)]
        rss = [s.sbuf("g_rs%d" % i, [128, 512], F32) for i in range(2)]
        t1s = [s.sbuf("g_t1%d" % i, [128, 512], F32) for i in range(2)]
        t2s = [s.sbuf("g_t2%d" % i, [128, 512], F32) for i in range(2)]
        t3s = [s.sbuf("g_t3%d" % i, [128, 512], F32) for i in range(2)]
        Vg = s.sbuf("g_V", [128, 18, 128], BF16)
        pTs = [s.sbuf("g_pT%d" % i, [128, 512], BF16) for i in range(4)]
        rds = [s.sbuf("g_rd%d" % i, [128, 512], F32) for i in range(2)]
        sc_ps = [s.psum("g_sc%d" % i, [128, 512], F32) for i in range(4)]
        num_ps = [s.psum("g_num%d" % i, [128, 512], F32) for i in range(2)]
        den_ps = [s.psum("g_den%d" % i, [128, 512], F32) for i in range(2)]
        s.dma("sp", Vg[:, :, :], v_tok[:, 512:640].rearrange("(c p) d -> p c d", p=128), reads=[v_tok], writes=[Vg])
        srcs = [(zq_g[0:128, :], 2), (zq_g[128:256, :], 2), (zk_g[0, :, :], 3), (zk_g[1, :, :], 3)]
        it = 0
        for idx, (src, gi) in enumerate(srcs):
            R_ = raw[idx % 2]
            s.dma("sp", R_[:, :], src, reads=[zq_g, zk_g], writes=[R_])
            for j, (st, n) in enumerate(BLOCKS):
                SQ = sqs[it % 2]
                RS = rss[it % 2]
                T1 = t1s[it % 2]
                T2 = t2s[it % 2]
                T3 = t3s[it % 2]
                P1 = sc_ps[(2 * it) % 4]
                P2 = sc_ps[(2 * it + 1) % 4]
                it += 1
                s.op("act", lambda e: e.activation(out=SQ[:, 0:n], in_=R_[:, st:st + n], func=AF.Square), reads=[R_], writes=[SQ])
                s.mm(P1[:, 0:n], blkb[:, :], SQ[:, 0:n], True, True, reads=[blkb, SQ], writes=[P1])
                s.op("act", lambda e: e.activation(out=RS[:, 0:n], in_=P1[:, 0:n], func=AF.Sqrt, scale=1.0 / 64, bias=EPS),
                     reads=[P1], writes=[RS])
                s.op("dve", lambda e: e.reciprocal(out=RS[:, 0:n], in_=RS[:, 0:n]), reads=[RS], writes=[RS])
                s.op("dve", lambda e: e.scalar_tensor_tensor(out=T1[:, 0:n], in0=R_[:, st:st + n], scalar=gcol[:, l, gi:gi + 1],
                                                             in1=RS[:, 0:n], op0=ALU.mult, op1=ALU.mult),
                     reads=[R_, RS, gcol], writes=[T1])
                if st >= 256:
                    s.mm(P2[:, 0:n], ropeRT, T1[:, 0:n], True, True, reads=[cs, T1], writes=[P2])
                    s.op("pool", lambda e: e.tensor_tensor(out=T2[:, 0:n], in0=T1[:, 0:n], in1=rp[:, st - 256:st - 256 + n],
                                                           op=ALU.mult), reads=[T1, rp], writes=[T2])
                    s.op("dve", lambda e: e.tensor_tensor(out=T3[:, 0:n], in0=P2[:, 0:n],
                                                          in1=rp[:, 2048 + st - 256:2048 + st - 256 + n], op=ALU.mult),
                         reads=[P2, rp], writes=[T3])
                    s.op("pool", lambda e: e.tensor_tensor(out=QK[idx][:, st:st + n], in0=T2[:, 0:n], in1=T3[:, 0:n], op=ALU.add),
                         reads=[T2, T3], acc=[QK[idx]], selfdep=False)
                else:
                    s.op("act", lambda e: e.activation(out=QK[idx][:, st:st + n], in_=T1[:, 0:n], func=AF.Copy),
                         reads=[T1], acc=[QK[idx]], selfdep=False)
        cnt = [0]
        qblocks = [1, 2, 3, 4] if last else [0, 1, 2, 3, 4]
        bi = 0
        for pc in range(2):
            for j in qblocks:
                st, n = BLOCKS[j]
                kcs = list(range(18)) if st >= 256 else [0, 1]
                attn_block(QK[pc], QK[2 + pc], lambda kc, hh: Vg[:, kc, 64 * pc:64 * pc + 64], Vg, st, n, kcs, 4 + pc,
                           sc_ps, num_ps[bi % 2], den_ps[bi % 2], pTs, rds[bi % 2], cnt)
                bi += 1
        s.release(m0)

    def na(b, l):
        m0 = s.mark()
        last = (l == nlayers - 1)
        bias8 = s.sbuf("n_bias", [128, 4 * NU, 128], BF16)
        bst = [s.sbuf("n_bst%d" % i, [128, NU, 128], F32) for i in range(2)]
        qT = [s.sbuf("n_q%d" % i, [128, NT], BF16) for i in range(2)]
        kT = [s.sbuf("n_k%d" % i, [128, NT], BF16) for i in range(2)]
        Vn = s.sbuf("n_V", [128, 18, 256], BF16)
        pTs = [s.sbuf("n_pT%d" % i, [128, 1024], BF16) for i in range(3)]
        pTd = [s.sbuf("n_pTd%d" % i, [128, 512], BF16) for i in range(2)]
        rds = [s.sbuf("n_rd%d" % i, [128, 512], F32) for i in range(2)]
        sc_ps = [s.psum("n_sc%d" % i, [128, 512], F32) for i in range(4)]
        num_ps = [s.psum("n_num%d" % i, [128, 512], F32) for i in range(2)]
        den_ps = [s.psum("n_den%d" % i, [128, 512], F32) for i in range(2)]
        for h in range(4):
            B_ = bst[h % 2]
            s.dma("sp", B_[:, :, :], nab[l, h * NU:(h + 1) * NU, :, :].rearrange("u p q -> p u q"), writes=[B_])
            s.op("pool", lambda e: e.tensor_scalar(out=bias8[:, h * NU:(h + 1) * NU, :], in0=B_[:, :, :], scalar1=8.0,
                                                   scalar2=None, op0=ALU.mult), reads=[B_], acc=[bias8])
        for pc in range(2):
            rows = slice(pc * 128, pc * 128 + 128)
            s.dma("sp", qT[pc][:, :], zq_n[rows, :], reads=[zq_n], writes=[qT[pc]])
            s.dma("sp", kT[pc][:, :], zk_n[rows, :], reads=[zk_n], writes=[kT[pc]])
        s.dma("sp", Vn[:, :, :], v_tok[:, 640:896].rearrange("(c p) d -> p c d", p=128), reads=[v_tok], writes=[Vn])
        it = 0
        for pc in range(2):
            for t in range(16):
                q0 = 256 + 128 * t
                jb = 1 + t // 4
                loc = sorted([j for (tt, j) in _NA_TMAP if tt == t])
                allk = [(0, None), (1, None)] + [(2 + j, _NA_TMAP[(t, j)]) for j in loc]
                nk = len(allk)
                NUM = num_ps[it % 2]
                DEN = den_ps[it % 2]
                RD = rds[it % 2]
                it += 1
                for hh in range(2):
                    h = 2 * pc + hh
                    hs = slice(64 * hh, 64 * hh + 64)
                    banks = [sc_ps[(2 * (it * 2 + hh)) % 4], sc_ps[(2 * (it * 2 + hh) + 1) % 4]]
                    pT = pTs[(it * 2 + hh) % 3]
                    for i, (gc, u) in enumerate(allk):
                        bk = banks[i // 4]
                        cc = slice((i % 4) * 128, (i % 4) * 128 + 128)
                        firstw = (i % 4 == 0)
                        s.mm(bk[:, cc], kT[pc][hs, gc * 128:(gc + 1) * 128], qT[pc][hs, q0:q0 + 128], True, u is None,
                             reads=[kT[pc], qT[pc]], writes=[bk] if firstw else [], acc=[] if firstw else [bk])
                        if u is not None:
                            s.mm(bk[:, cc], identb[:, :], bias8[:, h * NU + u, :], False, True, reads=[identb, bias8], acc=[bk])
                    n0 = min(nk, 4) * 128
                    s.op("act", lambda e: e.activation(out=pT[:, 0:n0], in_=banks[0][:, 0:n0], func=AF.Exp, scale=0.125),
                         reads=[banks[0]], writes=[pT])
                    if nk > 4:
                        n1 = (nk - 4) * 128
                        s.op("act", lambda e: e.activation(out=pT[:, 512:512 + n1], in_=banks[1][:, 0:n1], func=AF.Exp, scale=0.125),
                             reads=[banks[1]], acc=[pT])
                    for i, (gc, u) in enumerate(allk):
                        wr = (i == 0 and hh == 0)
                        s.mm(NUM[hs, 0:128], Vn[:, gc, h * 64:(h + 1) * 64], pT[:, i * 128:(i + 1) * 128], i == 0, i == nk - 1,
                             reads=[Vn, pT], writes=[NUM] if wr else [], acc=[] if wr else [NUM])
                        s.mm(DEN[hs, 0:128], onesb[:, 0:64], pT[:, i * 128:(i + 1) * 128], i == 0, i == nk - 1,
                             reads=[onesb, pT], writes=[DEN] if wr else [], acc=[] if wr else [DEN])
                s.op("dve", lambda e: e.reciprocal(out=RD[:, 0:128], in_=DEN[:, 0:128]), reads=[DEN], writes=[RD])
                s.op("dve", lambda e: e.tensor_tensor(out=hT[:, 6 + pc, q0:q0 + 128], in0=NUM[:, 0:128], in1=RD[:, 0:128],
                                                      op=ALU.mult), reads=[NUM, RD], acc=[hTb[jb]], selfdep=False)
            if not last:
                cnt = [0]
                attn_block(qT[pc], kT[pc], lambda kc, hh: Vn[:, kc, (2 * pc + hh) * 64:(2 * pc + hh) * 64 + 64], Vn, 0, 256, [0, 1],
                           6 + pc, sc_ps, num_ps[0], den_ps[0], pTd, rds[0], cnt)
        s.release(m0)

    def p4(b, l):
        last = (l == nlayers - 1)
        halves = [[0, 1, 2], [3, 4]]
        if last:
            halves[0] = [1, 2]
        for blks in halves:
            m0 = s.mark()
            base = BLOCKS[blks[0]][0]
            xh = s.sbuf("xh", [128, 8, 1280], F32)
            xhv = {j: Buf(xh.t, "xh.%d" % j) for j in blks}
            stg = [s.sbuf("p4stg%d" % i, [128, 8, 512], F32) for i in range(2)]
            wb = [s.sbuf("p4wb%d" % i, [128, 8, 512], BF16) for i in range(2)]
            w2b = [s.sbuf("p4w2b%d" % i, [128, 4, 1024], BF16) for i in range(2)]
            ub = [s.sbuf("p4u%d" % i, [128, 4, 512], BF16) for i in range(2)]
            rb = [s.sbuf("p4r%d" % i, [128, 512], F32) for i in range(3)]
            sq = s.sbuf("p4sq", [128, 8, 512], BF16)
            rs = s.sbuf("p4rs", [128, 512], F32)
            tmps = [s.sbuf("p4t%d" % i, [128, 512], F32) for i in range(3)]
            ps = [s.psum("p4ps%d" % i, [128, 512], F32) for i in range(7)]
            pss = s.psum("p4pss", [128, 512], F32)
            pi = [0]

            def nps():
                pi[0] += 1
                return ps[pi[0] % 7]

            for j in blks:
                st, n = BLOCKS[j]
                s.dma("sp", xh[:, :, st - base:st - base + n], xs[:, :, st:st + n], reads=[xs], writes=[xhv[j]])
            for cg in range(2):
                s.dma("sp", stg[cg][:, :, :], w_out[l, :, cg * 512:(cg + 1) * 512].rearrange("(k p) n -> p k n", p=128),
                      writes=[stg[cg]])
                s.op("pool", lambda e: e.tensor_copy(out=wb[cg][:, :, :], in_=stg[cg][:, :, :]), reads=[stg[cg]], writes=[wb[cg]])
            for cg in range(2):
                for j in blks:
                    st, n = BLOCKS[j]
                    col = 2 if st < 256 else b
                    for mi in range(4):
                        m = cg * 4 + mi
                        P = nps()
                        for k in range(8):
                            s.mm(P[:, 0:n], wb[cg][:, k, mi * 128:(mi + 1) * 128], hT[:, k, st:st + n], k == 0, k == 7,
                                 reads=[wb[cg], hTb[j]], writes=[P] if k == 0 else [], acc=[] if k == 0 else [P])
                        xa = xh[:, m, st - base:st - base + n]
                        s.op("dve", lambda e: e.scalar_tensor_tensor(out=xa, in0=P[:, 0:n], scalar=modT[:, l, 16 + m, col:col + 1],
                                                                     in1=xa, op0=ALU.mult, op1=ALU.add),
                             reads=[P, modT, xhv[j]], acc=[xhv[j]], selfdep=False)
            for j in blks:
                st, n = BLOCKS[j]
                col = 2 if st < 256 else b
                norm_block(xhv[j], st - base, n, st, l, 1, col, sq, pss, rs, tmps, j)
            def loadw(g):
                s.dma("sp", stg[0][:, :, :], w1[l, :, g * 512:(g + 1) * 512].rearrange("(k p) n -> p k n", p=128), writes=[stg[0]])
                s.op("pool", lambda e: e.tensor_copy(out=wb[g % 2][:, :, :], in_=stg[0][:, :, :]), reads=[stg[0]], writes=[wb[g % 2]])
                s.dma("sp", stg[1][:, :, :].rearrange("p k n -> p (k n)").rearrange("p (c n) -> p c n", c=4),
                      w2[l, g * 512:(g + 1) * 512, :].rearrange("(c p) n -> p c n", p=128), writes=[stg[1]])
                s.op("pool", lambda e: e.tensor_copy(out=w2b[g % 2][:, :, :],
                                                     in_=stg[1][:, :, :].rearrange("p k n -> p (k n)").rearrange("p (c n) -> p c n", c=4)),
                     reads=[stg[1]], writes=[w2b[g % 2]])

            loadw(0)
            ui = 0
            for g in range(8):
                if g + 1 < 8:
                    loadw(g + 1)
                W1 = wb[g % 2]
                W2 = w2b[g % 2]
                for j in blks:
                    st, n = BLOCKS[j]
                    col = 2 if st < 256 else b
                    U = ub[ui % 2]
                    ui += 1
                    for hc in range(4):
                        P = nps()
                        for k in range(8):
                            s.mm(P[:, 0:n], W1[:, k, hc * 128:(hc + 1) * 128], hT[:, k, st:st + n], k == 0, k == 7,
                                 reads=[W1, hTb[j]], writes=[P] if k == 0 else [], acc=[] if k == 0 else [P])
                        R_ = rb[hc % 3]
                        s.op("act", lambda e: e.activation(out=R_[:, 0:n], in_=P[:, 0:n], func=AF.Relu), reads=[P], writes=[R_])
                        s.op("pool", lambda e: e.tensor_tensor(out=U[:, hc, 0:n], in0=R_[:, 0:n], in1=R_[:, 0:n], op=ALU.mult),
                             reads=[R_], writes=[U] if hc == 0 else [], acc=[] if hc == 0 else [U], selfdep=(hc == 0))
                    for m in range(8):
                        P = nps()
                        for hc in range(4):
                            s.mm(P[:, 0:n], W2[:, hc, m * 128:(m + 1) * 128], U[:, hc, 0:n], hc == 0, hc == 3,
                                 reads=[W2, U], writes=[P] if hc == 0 else [], acc=[] if hc == 0 else [P])
                        xa = xh[:, m, st - base:st - base + n]
                        s.op("dve", lambda e: e.scalar_tensor_tensor(out=xa, in0=P[:, 0:n], scalar=modT[:, l, 40 + m, col:col + 1],
                                                                     in1=xa, op0=ALU.mult, op1=ALU.add),
                             reads=[P, modT, xhv[j]], acc=[xhv[j]], selfdep=False)
            if not last:
                for j in blks:
                    st, n = BLOCKS[j]
                    s.dma("pool", xs[:, :, st:st + n], xh[:, :, st - base:st - base + n], reads=[xhv[j]], acc=[xs])
            else:
                yb = stg[0]
                youts = [Buf(stg[1].t, "yout%d" % i) for i in range(2)]
                oc = 0
                for j in blks:
                    st, n = BLOCKS[j]
                    s.op("act", lambda e: e.activation(out=sq[:, :, 0:n], in_=xh[:, :, st - base:st - base + n], func=AF.Square),
                         reads=[xhv[j]], writes=[sq])
                    for k in range(8):
                        s.mm(pss[:, 0:n], onesb[:, :], sq[:, k, 0:n], k == 0, k == 7, reads=[sq, onesb],
                             writes=[pss] if k == 0 else [], acc=[] if k == 0 else [pss])
                    s.op("act", lambda e: e.activation(out=rs[:, 0:n], in_=pss[:, 0:n], func=AF.Sqrt, scale=1.0 / 1024, bias=EPS),
                         reads=[pss], writes=[rs])
                    s.op("dve", lambda e: e.reciprocal(out=rs[:, 0:n], in_=rs[:, 0:n]), reads=[rs], writes=[rs])
                    for k in range(8):
                        s.op("dve", lambda e: e.scalar_tensor_tensor(out=yb[:, k, 0:n], in0=xh[:, k, st - base:st - base + n],
                                                                     scalar=gT[:, 4, k:k + 1], in1=rs[:, 0:n],
                                                                     op0=ALU.mult, op1=ALU.mult),
                             reads=[xhv[j], gT, rs], writes=[yb] if k == 0 else [], acc=[] if k == 0 else [yb], selfdep=(k == 0))
                    for ti in range(n // 128):
                        YO = youts[oc % 2]
                        yo_ap = stg[1][:, (oc % 2) * 2:(oc % 2) * 2 + 2, :].rearrange("p a n -> p (a n)")
                        oc += 1
                        for half in range(2):
                            P = nps()
                            for kk in range(4):
                                k = half * 4 + kk
                                s.op("pe", lambda e: e.transpose(out=P[:, kk * 128:(kk + 1) * 128],
                                                                 in_=yb[:, k, ti * 128:(ti + 1) * 128], identity=identf),
                                     reads=[yb, cs], writes=[P] if kk == 0 else [], acc=[] if kk == 0 else [P])
                            s.op("act", lambda e: e.activation(out=yo_ap[:, half * 512:(half + 1) * 512], in_=P[:, :], func=AF.Copy),
                                 reads=[P], writes=[YO] if half == 0 else [], acc=[] if half == 0 else [YO], selfdep=(half == 0))
                        tok = st - 256 + ti * 128
                        s.dma("pool", y[b, tok:tok + 128, :], yo_ap, reads=[YO], acc=[y])
            s.release(m0)

    prologue()
    for b in range(nseq):
        for l in range(nlayers):
            p1(b, l)
            if stop_after == "p1":
                break
            p2(b, l)
            if stop_after == "p2":
                break
            recur(b, l, "h")
            if stop_after in ("h", "hbuild"):
                break
            recur(b, l, "m")
            gqa(b, l)
            na(b, l)
            if stop_after == "p3":
                break
            p4(b, l)
        if stop_after is not None:
            break
    if dbg:
        s.dma("sp", cat_dbg[:, :, :], hT[:, :, :], reads=hTb, writes=[cat_dbg])
    s.finish()
    build.stats = (s.nops, s.nwaits, dict(s.cnt))
    return nc


_CACHE = {}


def _host_inputs(inputs, core):
    f = lambda a: np.ascontiguousarray(np.asarray(a, dtype=np.float32))
    b0 = 2 * core
    cst, rope = _CACHE["consts"]
    m = {
        "x": f(inputs["x"][b0:b0 + 2]),
        "ctx": f(inputs["ctx"][b0:b0 + 2]),
        "cvec": f(np.concatenate([inputs["c"][b0:b0 + 2], np.asarray(inputs["c_ctx"])[None, :]], 0)),
        "w_mod": f(inputs["w_mod"]), "b_mod": f(inputs["b_mod"]),
        "norm1_g": f(inputs["norm1_g"]), "norm2_g": f(inputs["norm2_g"]),
        "w_in": f(inputs["w_in"]),
        "hgrn_lb_logits": f(np.asarray(inputs["hgrn_lb_logits"]).reshape(4, 256)),
        "hgrn_norm_g": f(inputs["hgrn_norm_g"]), "mlstm_gate_b": f(inputs["mlstm_gate_b"]),
        "mlstm_norm_g": f(inputs["mlstm_norm_g"]), "gqa_qnorm_g": f(inputs["gqa_qnorm_g"]),
        "gqa_knorm_g": f(inputs["gqa_knorm_g"]),
        "na_bias": _CACHE["na_bias"],
        "w_out": f(inputs["w_out"]), "w_mlp1": f(inputs["w_mlp1"]), "w_mlp2": f(inputs["w_mlp2"]),
        "final_norm_g": f(np.asarray(inputs["final_norm_g"]).reshape(1, 1024)),
        "consts": cst, "rope": rope,
    }
    return m


def _prep(inputs):
    _CACHE["consts"] = _consts()
    idx = _na_gather_index()
    rpb = np.asarray(inputs["na_rpb"], np.float32)
    flat = np.concatenate([rpb.reshape(2, 4, 465), np.full((2, 4, 1), NEG, np.float32)], -1)
    nb = flat[:, :, idx]
    _CACHE["na_bias"] = np.ascontiguousarray(nb.reshape(2, 4 * NU, 128, 128))


def kernel(**inputs):
    _prep(inputs)
    nc = build()
    in_maps = [_host_inputs(inputs, c) for c in range(8)]
    res = run_bass_kernel_spmd(nc, in_maps, core_ids=list(range(8)))
    out = np.concatenate([np.asarray(r["y"], np.float32) for r in res.results], axis=0)
    return out
```

```python
import numpy as np
import ml_dtypes
import concourse.bass as bass
import concourse.mybir as mybir
from concourse.bass_utils import run_bass_kernel_spmd

F32 = mybir.dt.float32
BF16 = mybir.dt.bfloat16
AF = mybir.ActivationFunctionType
ALU = mybir.AluOpType

NT = 2304
NCTX = 256
EPS = 1e-6
NEG = -1e30
BLOCKS = [(0, 256), (256, 512), (768, 512), (1280, 512), (1792, 512)]
NCH = 36


class Buf:
    __slots__ = ("t", "name", "lw", "aw", "rd")

    def __init__(self, t, name):
        self.t = t
        self.name = name
        self.lw = {}
        self.aw = {}
        self.rd = {}

    def __getitem__(self, idx):
        return self.t[idx]


class Sch:
    NDMA = 10

    def __init__(self, nc):
        self.nc = nc
        self.eng = {"pe": nc.tensor, "dve": nc.vector, "act": nc.scalar, "pool": nc.gpsimd, "sp": nc.sync}
        self.sems = {}
        self.cnt = {}
        self.key = {}
        self.nsem = 0
        for e in self.eng:
            self._newsem(e)
        for q in ("sp", "act", "pool"):
            for k in range(self.NDMA):
                key = "d%s%d" % (q, k)
                self.sems[key] = nc.alloc_semaphore("s_" + key)
                self.cnt[key] = 0
        self.seen = {e: {} for e in self.eng}
        self.dma_rr = {"sp": 0, "act": 0, "pool": 0}
        self.nwaits = 0
        self.nops = 0
        self._stack = []

    def _newsem(self, e):
        self.nsem += 1
        key = "%s@%d" % (e, self.nsem)
        self.sems[key] = self.nc.alloc_semaphore("s_%s_%d" % (e, self.nsem))
        self.cnt[key] = 0
        self.key[e] = key

    def sbuf(self, name, shape, dtype):
        self.nsem += 1
        name = "%s_%d" % (name, self.nsem)
        g = self.nc.sbuf_tensor(name, list(shape), dtype)
        t = g.__enter__()
        self._stack.append(g)
        return Buf(t, name)

    def psum(self, name, shape, dtype=F32):
        self.nsem += 1
        name = "%s_%d" % (name, self.nsem)
        g = self.nc.psum_tensor(name, list(shape), dtype)
        t = g.__enter__()
        self._stack.append(g)
        return Buf(t, name)

    def dram(self, name, shape, dtype, kind="Internal"):
        t = self.nc.dram_tensor(name, list(shape), dtype, kind=kind)
        return Buf(t, name)

    @staticmethod
    def views(buf, n):
        return [Buf(buf.t, "%s.%d" % (buf.name, i)) for i in range(n)]

    def mark(self):
        return len(self._stack)

    def release(self, mark):
        self.barrier()
        while len(self._stack) > mark:
            g = self._stack.pop()
            g.__exit__(None, None, None)

    def _need(self, e, toks, selfdep=True, wtoks=()):
        best = {}
        for k, v in toks:
            if k.startswith("pe@") and e == "pe":
                continue
            if best.get(k, 0) < v:
                best[k] = v
        for k, v in wtoks:
            if k.startswith("pe@") and e == "pe":
                continue
            if (not selfdep) and k == self.key[e]:
                continue
            if best.get(k, 0) < v:
                best[k] = v
        for k, v in best.items():
            if self.seen[e].get(k, 0) >= v:
                continue
            self.eng[e].wait_ge(self.sems[k], v)
            self.seen[e][k] = v
            self.nwaits += 1

    @staticmethod
    def _deps(reads, writes, acc):
        rt, wt = [], []
        for b in reads:
            rt.extend(b.lw.items())
            rt.extend(b.aw.items())
        for b in writes:
            wt.extend(b.lw.items())
            wt.extend(b.aw.items())
            wt.extend(b.rd.items())
        for b in acc:
            wt.extend(b.lw.items())
            wt.extend(b.rd.items())
        return rt, wt

    @staticmethod
    def _commit(tok, reads, writes, acc):
        k, v = tok
        for b in reads:
            if b.rd.get(k, 0) < v:
                b.rd[k] = v
        for b in writes:
            b.lw = {k: v}
            b.aw = {}
            b.rd = {}
        for b in acc:
            if b.aw.get(k, 0) < v:
                b.aw[k] = v

    def op(self, e, fn, reads=(), writes=(), acc=(), selfdep=True):
        rt, wt = self._deps(reads, writes, acc)
        self._need(e, rt, selfdep, wt)
        ins = fn(self.eng[e])
        self.nops += 1
        key = self.key[e]
        self.cnt[key] += 1
        ins.then_inc(self.sems[key], 1)
        self._commit((key, self.cnt[key]), reads, writes, acc)
        return ins

    def mm(self, out_ap, lhsT, rhs, start, stop, reads=(), writes=(), acc=()):
        return self.op("pe", lambda e: e.matmul(out_ap, lhsT, rhs, start=start, stop=stop,
                                                skip_group_check=True), reads, writes, acc)

    def dma(self, q, out_ap, in_ap, reads=(), writes=(), acc=(), **kw):
        key = "d%s%d" % (q, self.dma_rr[q])
        self.dma_rr[q] = (self.dma_rr[q] + 1) % self.NDMA
        rt, wt = self._deps(reads, writes, acc)
        toks = rt + wt
        if self.cnt[key] > 0:
            toks.append((key, self.cnt[key]))
        self._need(q, toks)
        self.cnt[key] += 16
        ins = self.eng[q].dma_start(out=out_ap, in_=in_ap, **kw)
        ins.then_inc(self.sems[key], 16)
        self.nops += 1
        self._commit((key, self.cnt[key]), reads, writes, acc)
        return ins

    def barrier(self):
        toks = [(k, v) for k, v in self.cnt.items() if v > 0]
        for e in self.eng:
            self._need(e, toks)
        for e in list(self.eng):
            if self.cnt[self.key[e]] > 24000:
                self._newsem(e)

    def finish(self):
        toks = [(k, v) for k, v in self.cnt.items() if v > 0]
        self._need("sp", toks)


def _na_tiles():
    uniq = {}
    tmap = {}
    for t in range(16):
        lo, hi = 10 ** 9, -1
        for b in range(2):
            r0 = min(max(2 * t + b - 4, 0), 24)
            lo = min(lo, r0 // 2)
            hi = max(hi, (r0 + 7) // 2)
        for j in range(lo, hi + 1):
            pat = []
            for a in range(2):
                for b in range(2):
                    qr = 2 * t + b
                    kr = 2 * j + a
                    r0 = min(max(qr - 4, 0), 24)
                    pat.append((kr - qr) if (r0 <= kr < r0 + 8) else None)
            pat = tuple(pat)
            if pat not in uniq:
                uniq[pat] = len(uniq)
            tmap[(t, j)] = uniq[pat]
    pats = [None] * len(uniq)
    for p, i in uniq.items():
        pats[i] = p
    return pats, tmap


_NA_PATS, _NA_TMAP = _na_tiles()
NU = len(_NA_PATS)


def _na_gather_index():
    idx = np.full((NU, 128, 128), 465, np.int64)
    qc = np.arange(64)
    cstart = np.clip(qc - 8, 0, 48)
    kc = np.arange(64)
    col_in = (kc[:, None] >= cstart[None, :]) & (kc[:, None] < cstart[None, :] + 16)
    cidx = np.clip(kc[:, None] - qc[None, :], -15, 15) + 15
    for u, pat in enumerate(_NA_PATS):
        for a in range(2):
            for b in range(2):
                dr = pat[a * 2 + b]
                if dr is None:
                    continue
                blk = np.where(col_in, (dr + 7) * 31 + cidx, 465)
                idx[u, a * 64:(a + 1) * 64, b * 64:(b + 1) * 64] = blk
    return idx


def _consts():
    c = np.zeros((128, 576), np.float32)
    c[:, 0:128] = np.eye(128, dtype=np.float32)
    blk = np.zeros((128, 128), np.float32)
    blk[0:64, 0:64] = 1.0
    blk[64:128, 64:128] = 1.0
    c[:, 128:256] = blk
    rt = np.zeros((128, 128), np.float32)
    for i in range(64):
        rt[2 * i + 1, 2 * i] = -1.0
        rt[2 * i, 2 * i + 1] = 1.0
    c[:, 256:384] = rt
    sidx = np.arange(64)[:, None]
    tidx = np.arange(64)[None, :]
    mf = (sidx <= tidx).astype(np.float32)
    mb = (sidx >= tidx).astype(np.float32)
    c[:, 384:448] = np.concatenate([mf, mf], 0)
    c[:, 448:512] = np.concatenate([mb, mb], 0)
    p = np.arange(128)[:, None] % 32
    t16 = np.arange(16)[None, :]
    c[:, 512:528] = ((p <= t16) & (p < 16)).astype(np.float32)
    c[:, 528:544] = ((p >= t16) & (p < 16)).astype(np.float32)
    t = np.arange(2048)
    row = (t // 64).astype(np.float32)
    col = (t % 64).astype(np.float32)
    inv = np.power(np.float32(10000.0), (-2.0 * np.arange(16, dtype=np.float32) / np.float32(32.0))).astype(np.float32)
    ang = np.concatenate([row[:, None] * inv[None, :], col[:, None] * inv[None, :]], -1).astype(np.float32)
    cos = np.cos(ang).astype(np.float32)
    sin = np.sin(ang).astype(np.float32)
    cosf = np.repeat(cos, 2, axis=1).T
    sinf = np.repeat(sin, 2, axis=1).T
    rope = np.concatenate([np.concatenate([cosf, cosf], 0), np.concatenate([sinf, sinf], 0)], 1).astype(np.float32)
    return c, np.ascontiguousarray(rope)


def build(nlayers=2, nseq=2, stop_after=None, dbg=False):
    nc = bass.Bass("TRN2", target_bir_lowering=False)
    s = Sch(nc)

    def inp(name, shape, dt=F32):
        return s.dram(name, shape, dt, kind="ExternalInput")

    x_in = inp("x", [2, 2048, 1024])
    ctx_in = inp("ctx", [2, 256, 1024])
    cvec = inp("cvec", [3, 1024])
    w_mod = inp("w_mod", [2, 1024, 6144])
    b_mod = inp("b_mod", [2, 6144])
    n1g = inp("norm1_g", [2, 1024])
    n2g = inp("norm2_g", [2, 1024])
    w_in = inp("w_in", [2, 1024, 3600])
    lbl = inp("hgrn_lb_logits", [4, 256])
    hgg = inp("hgrn_norm_g", [2, 64])
    mgb = inp("mlstm_gate_b", [2, 16])
    mgg = inp("mlstm_norm_g", [2, 64])
    qng = inp("gqa_qnorm_g", [2, 64])
    kng = inp("gqa_knorm_g", [2, 64])
    nab = inp("na_bias", [2, 4 * NU, 128, 128])
    w_out = inp("w_out", [2, 1024, 1024])
    w1 = inp("w_mlp1", [2, 1024, 4096])
    w2 = inp("w_mlp2", [2, 4096, 1024])
    fng = inp("final_norm_g", [1, 1024])
    cst = inp("consts", [128, 576])
    ropec = inp("rope", [128, 4096])
    y = s.dram("y", [2, 2048, 1024], F32, kind="ExternalOutput")

    okind = "ExternalOutput" if dbg else "Internal"
    xs = s.dram("xs", [128, 8, NT], F32, kind=okind)
    zq_h = s.dram("zq_h", [256, NT], F32, kind=okind)
    zog_h = s.dram("zog_h", [256, NT], F32, kind=okind)
    zlf_h = s.dram("zlf_h", [2, 256, NT], F32, kind=okind)
    zkk_h = s.dram("zkk_h", [2, 256, NT], F32, kind=okind)
    zq_m = s.dram("zq_m", [256, NT], F32, kind=okind)
    zk_m = s.dram("zk_m", [256, NT], F32, kind=okind)
    zog_m = s.dram("zog_m", [256, NT], F32, kind=okind)
    zg_m = s.dram("zg_m", [16, NT], F32, kind=okind)
    zq_g = s.dram("zq_g", [256, NT], F32, kind=okind)
    zk_g = s.dram("zk_g", [2, 128, NT], F32, kind=okind)
    zq_n = s.dram("zq_n", [256, NT], BF16, kind=okind)
    zk_n = s.dram("zk_n", [256, NT], BF16, kind=okind)
    v_tok = s.dram("v_tok", [NT, 896], BF16, kind=okind)
    cat_dbg = s.dram("cat_dbg", [128, 8, NT], BF16, kind=okind) if dbg else None
    h_dbg = s.dram("h_dbg", [128, 8, NT], BF16, kind=okind) if dbg else None
    mod_dbg = s.dram("mod_dbg", [128, 2 * 48 * 3], F32, kind=okind) if dbg else None

    if dbg:
        dbgQb = s.dram("dbgQ", [2, 3, 128, NT], BF16, kind=okind)
        dbgPb = s.dram("dbgP", [128, NT + 1], F32, kind=okind)
        dbgQ = dbgQb
        dbgP = dbgPb
    NSL = True

    cs = s.sbuf("cs", [128, 576], F32)
    identb = s.sbuf("identb", [128, 128], BF16)
    onesb = s.sbuf("onesb", [128, 128], BF16)
    blkb = s.sbuf("blkb", [128, 128], BF16)
    bd1 = s.sbuf("bd1", [128, 128], BF16)
    hT = s.sbuf("hT", [128, 8, NT], BF16)
    hTb = Sch.views(hT, 5)
    modT = s.sbuf("modT", [128, 2, 48, 3], F32)
    AT = s.sbuf("AT", [128, 2, 2, 8, 3], F32)
    gT = s.sbuf("gT", [128, 5, 8], F32)
    gcol = s.sbuf("gcol", [128, 2, 4], F32)
    lb = s.sbuf("lb", [128, 2, 2, 2], F32)
    oml = s.sbuf("oml", [128, 2, 2, 2], F32)
    noml = s.sbuf("noml", [128, 2, 2, 2], F32)
    mgbT = s.sbuf("mgbT", [16, 2], F32)

    identf = cs[:, 0:128]
    blkf = cs[:, 128:256]
    ropeRT = cs[:, 256:384]
    maskFB = {64: [cs[:, 384:448], cs[:, 448:512]], 16: [cs[:, 512:528], cs[:, 528:544]]}

    s.dma("sp", cs[:, :], cst[:, :], writes=[cs])
    s.op("dve", lambda e: e.tensor_copy(out=identb[:, :], in_=identf), reads=[cs], writes=[identb])
    s.op("dve", lambda e: e.tensor_copy(out=blkb[:, :], in_=blkf), reads=[cs], writes=[blkb])
    s.op("dve", lambda e: e.tensor_copy(out=bd1[:, :], in_=blkf), reads=[cs], writes=[bd1])
    s.op("pool", lambda e: e.memset(onesb[:, :], 1.0), writes=[onesb])

    def prologue():
        m0 = s.mark()
        scT = s.sbuf("scT", [128, 8, 3], F32)
        bmT = s.sbuf("bmT", [128, 2, 48], F32)
        lg = s.sbuf("lg", [128, 4, 2], F32)
        wm = [s.sbuf("wm%d" % i, [128, 8, 512], F32) for i in range(2)]
        pm = s.psum("pm_mod", [128, 512], F32)
        for r in range(3):
            s.dma("sp", scT[:, :, r], cvec[r:r + 1, :].rearrange("o (k p) -> p (o k)", p=128), acc=[scT],
                  allow_slow_non_contiguous=NSL)
        for l in range(2):
            s.dma("sp", bmT[:, l, :], b_mod[l:l + 1, :].rearrange("o (j p) -> p (o j)", p=128), acc=[bmT],
                  allow_slow_non_contiguous=NSL)
        gsrc = [n1g[0:1, :], n1g[1:2, :], n2g[0:1, :], n2g[1:2, :], fng[0:1, :]]
        for i, g in enumerate(gsrc):
            s.dma("sp", gT[:, i, :], g.rearrange("o (k p) -> p (o k)", p=128), acc=[gT], allow_slow_non_contiguous=NSL)
        for r in range(4):
            s.dma("sp", lg[:, r, :], lbl[r:r + 1, :].rearrange("o (c p) -> p (o c)", p=128), acc=[lg],
                  allow_slow_non_contiguous=NSL)
        for l in range(2):
            for i, g in enumerate([hgg, mgg, qng, kng]):
                for hh in range(2):
                    s.dma("sp", gcol[64 * hh:64 * hh + 64, l, i:i + 1], g[l:l + 1, :].rearrange("o d -> d o"),
                          acc=[gcol], allow_slow_non_contiguous=NSL)
            s.dma("sp", mgbT[:, l:l + 1], mgb[l:l + 1, :].rearrange("o g -> g o"), acc=[mgbT],
                  allow_slow_non_contiguous=NSL)
        s.op("act", lambda e: e.activation(out=scT[:, :, :], in_=scT[:, :, :], func=AF.Silu), reads=[scT], writes=[scT])
        ex = s.sbuf("ex", [128, 4, 2], F32)
        den = s.sbuf("den", [128, 2, 2], F32)
        s.op("act", lambda e: e.activation(out=ex[:, :, :], in_=lg[:, :, :], func=AF.Exp), reads=[lg], writes=[ex])
        s.op("dve", lambda e: e.tensor_tensor(out=den[:, :, :], in0=ex[:, 0:2, :], in1=ex[:, 2:4, :], op=ALU.add),
             reads=[ex], writes=[den])
        s.op("dve", lambda e: e.reciprocal(out=den[:, :, :], in_=den[:, :, :]), reads=[den], writes=[den])
        s.op("pool", lambda e: e.memset(lb[:, 0, :, :], 0.0), acc=[lb])
        s.op("dve", lambda e: e.tensor_tensor(out=lb[:, 1, :, :], in0=ex[:, 2:4, :], in1=den[:, :, :], op=ALU.mult),
             reads=[ex, den], acc=[lb])
        s.op("dve", lambda e: e.tensor_scalar(out=oml[:, :, :, :], in0=lb[:, :, :, :], scalar1=-1.0, scalar2=1.0,
                                              op0=ALU.mult, op1=ALU.add), reads=[lb], writes=[oml])
        s.op("dve", lambda e: e.tensor_scalar(out=noml[:, :, :, :], in0=lb[:, :, :, :], scalar1=1.0, scalar2=-1.0,
                                              op0=ALU.mult, op1=ALU.add), reads=[lb], writes=[noml])
        it = 0
        for l in range(2):
            for grp in range(12):
                W = wm[it % 2]
                it += 1
                s.dma("sp", W[:, :, :], w_mod[l, :, grp * 512:(grp + 1) * 512].rearrange("(k p) n -> p k n", p=128),
                      writes=[W])
                for m in range(4):
                    idx = grp * 4 + m
                    for k in range(8):
                        s.mm(pm[:, idx * 3:idx * 3 + 3], W[:, k, m * 128:(m + 1) * 128], scT[:, k, :], k == 0, k == 7,
                             reads=[W, scT], acc=[pm])
            s.op("dve", lambda e: e.tensor_tensor(
                out=modT[:, l, :, :], in0=pm[:, 0:144].rearrange("p (j r) -> p j r", r=3),
                in1=bmT[:, l, :].unsqueeze(2).to_broadcast([128, 48, 3]), op=ALU.add),
                reads=[pm, bmT], acc=[modT])
        for l in range(2):
            for w in range(2):
                for k in range(8):
                    j = (1 if w == 0 else 4) * 8 + k
                    s.op("dve", lambda e: e.tensor_scalar(out=AT[:, l, w, k, :], in0=modT[:, l, j, :], scalar1=1.0,
                                                          scalar2=gT[:, w * 2 + l, k:k + 1], op0=ALU.add, op1=ALU.mult),
                         reads=[modT, gT], acc=[AT])
        if dbg:
            s.dma("sp", mod_dbg[:, :], modT[:, :, :, :].rearrange("p l j r -> p (l j r)"), reads=[modT], writes=[mod_dbg])
        s.release(m0)

    def norm_block(X, xoff, n, st, l, w, col, sq, pss, rs, tmps, j):
        s.op("act", lambda e: e.activation(out=sq[:, :, 0:n], in_=X[:, :, xoff:xoff + n], func=AF.Square),
             reads=[X], writes=[sq])
        for k in range(8):
            s.mm(pss[:, 0:n], onesb[:, :], sq[:, k, 0:n], k == 0, k == 7, reads=[sq, onesb],
                 writes=[pss] if k == 0 else [], acc=[] if k == 0 else [pss])
        s.op("act", lambda e: e.activation(out=rs[:, 0:n], in_=pss[:, 0:n], func=AF.Sqrt, scale=1.0 / 1024, bias=EPS),
             reads=[pss], writes=[rs])
        s.op("dve", lambda e: e.reciprocal(out=rs[:, 0:n], in_=rs[:, 0:n]), reads=[rs], writes=[rs])
        sh = 0 if w == 0 else 3
        for k in range(8):
            T = tmps[k % len(tmps)]
            s.op("dve", lambda e: e.scalar_tensor_tensor(out=T[:, 0:n], in0=X[:, k, xoff:xoff + n],
                                                         scalar=AT[:, l, w, k, col:col + 1], in1=rs[:, 0:n],
                                                         op0=ALU.mult, op1=ALU.mult), reads=[X, rs, AT], writes=[T])
            s.op("act", lambda e: e.activation(out=hT[:, k, st:st + n], in_=T[:, 0:n], func=AF.Identity,
                                               bias=modT[:, l, sh * 8 + k, col:col + 1], scale=1.0),
                 reads=[T, modT], acc=[hTb[j]], selfdep=False)

    def p1(b, l):
        m0 = s.mark()
        xb = [s.sbuf("xb%d" % i, [128, 8, 512], F32) for i in range(2)]
        sq = [s.sbuf("sq%d" % i, [128, 8, 512], BF16) for i in range(2)]
        rs = [s.sbuf("rs%d" % i, [128, 512], F32) for i in range(2)]
        tmps = [s.sbuf("tmp%d" % i, [128, 512], F32) for i in range(4)]
        pss = [s.psum("pss%d" % i, [128, 512], F32) for i in range(2)]
        if l == 0:
            xin = [s.sbuf("xin%d" % i, [128, 1024], F32) for i in range(3)]
            pst = [s.psum("pst%d" % i, [128, 512], F32) for i in range(4)]
        cnt = 0
        for j, (st, n) in enumerate(BLOCKS):
            X = xb[j % 2]
            if l == 0:
                for ti in range(n // 128):
                    tok0 = st + ti * 128
                    src = ctx_in[b, tok0:tok0 + 128, :] if tok0 < 256 else x_in[b, tok0 - 256:tok0 - 128, :]
                    xi = xin[cnt % 3]
                    s.dma("sp", xi[:, :], src, writes=[xi])
                    for half in range(2):
                        pt = pst[(cnt * 2 + half) % 4]
                        for kk in range(4):
                            k = half * 4 + kk
                            s.op("pe", lambda e: e.transpose(out=pt[:, kk * 128:(kk + 1) * 128],
                                                             in_=xi[:, k * 128:(k + 1) * 128], identity=identf),
                                 reads=[xi, cs], writes=[pt] if kk == 0 else [], acc=[] if kk == 0 else [pt])
                        s.op("act", lambda e: e.activation(
                            out=X[:, half * 4:(half + 1) * 4, ti * 128:(ti + 1) * 128],
                            in_=pt[:, :].rearrange("p (k t) -> p k t", t=128), func=AF.Copy),
                            reads=[pt], writes=[X] if (ti == 0 and half == 0) else [],
                            acc=[] if (ti == 0 and half == 0) else [X], selfdep=False)
                    cnt += 1
                s.dma("sp", xs[:, :, st:st + n], X[:, :, 0:n], reads=[X], acc=[xs])
            else:
                s.dma("sp", X[:, :, 0:n], xs[:, :, st:st + n], reads=[xs], writes=[X])
            col = 2 if st < 256 else b
            norm_block(X, 0, n, st, l, 0, col, sq[j % 2], pss[j % 2], rs[j % 2], tmps, j)
        if dbg:
            s.dma("sp", h_dbg[:, :, :], hT[:, :, :], reads=hTb, writes=[h_dbg])
        s.release(m0)

    GROUPS = [
        (0, 512, [("f", "hq", 0, 128, 0), ("f", "hq", 128, 128, 1), ("v", 256, 256, 0)]),
        (512, 512, [("f", "hog", 0, 128, 0), ("f", "hog", 128, 128, 1), ("f", "hf0", 256, 128, 0), ("f", "hf0", 384, 128, 1)]),
        (1024, 512, [("f", "hf1", 0, 128, 0), ("f", "hf1", 128, 128, 1), ("f", "mq", 256, 128, 0), ("f", "mq", 384, 128, 1)]),
        (1536, 512, [("f", "mk", 0, 128, 0), ("f", "mk", 128, 128, 1), ("v", 256, 256, 256)]),
        (2048, 272, [("f", "mog", 0, 128, 0), ("f", "mog", 128, 128, 1), ("f", "mg", 256, 16, 0)]),
        (2320, 512, [("f", "gq", 0, 128, 0), ("f", "gq", 128, 128, 1), ("f", "gk", 256, 64, 0), ("f", "gk", 320, 64, 1),
                     ("v", 384, 128, 512)]),
        (2832, 512, [("f", "nq", 0, 128, 0), ("f", "nq", 128, 128, 1), ("f", "nk", 256, 128, 0), ("f", "nk", 384, 128, 1)]),
        (3344, 256, [("v", 0, 256, 640)]),
    ]

    def p2(b, l):
        m0 = s.mark()
        wst = [s.sbuf("wst%d" % i, [128, 8, 512], F32) for i in range(2)]
        wbf = [s.sbuf("wbf%d" % i, [128, 8, 512], BF16) for i in range(2)]
        stg = [s.sbuf("stg%d" % i, [128, 512], F32) for i in range(8)]
        stb = [s.sbuf("stb%d" % i, [128, 512], BF16) for i in range(4)]
        ps = [s.psum("p2ps%d" % i, [128, 512], F32) for i in range(6)]
        st_i = [0]
        sb_i = [0]
        ps_i = [0]

        def nstg():
            st_i[0] += 1
            return stg[st_i[0] % 8]

        def nstb():
            sb_i[0] += 1
            return stb[sb_i[0] % 4]

        def load(gi):
            c0, w, _ = GROUPS[gi]
            s.dma("sp", wst[gi % 2][:, :, 0:w], w_in[l, :, c0:c0 + w].rearrange("(k p) n -> p k n", p=128),
                  writes=[wst[gi % 2]])

        def cast(gi):
            c0, w, _ = GROUPS[gi]
            if gi % 2 == 0:
                s.op("act", lambda e: e.activation(out=wbf[gi % 2][:, :, 0:w], in_=wst[gi % 2][:, :, 0:w], func=AF.Copy),
                     reads=[wst[gi % 2]], writes=[wbf[gi % 2]])
            else:
                s.op("dve", lambda e: e.tensor_copy(out=wbf[gi % 2][:, :, 0:w], in_=wst[gi % 2][:, :, 0:w]),
                     reads=[wst[gi % 2]], writes=[wbf[gi % 2]])

        load(0)
        cast(0)
        load(1)
        for gi, (c0, w, jobs) in enumerate(GROUPS):
            if gi + 1 < len(GROUPS):
                cast(gi + 1)
            if gi + 2 < len(GROUPS):
                load(gi + 2)
            W = wbf[gi % 2]
            for job in jobs:
                if job[0] == "v":
                    _, off, ncol, vdst = job
                    for ti in range(18):
                        P = ps[ps_i[0] % 6]
                        ps_i[0] += 1
                        jb = 0 if ti < 2 else 1 + (ti - 2) // 4
                        for k in range(8):
                            s.mm(P[:, 0:ncol], hT[:, k, ti * 128:(ti + 1) * 128], W[:, k, off:off + ncol], k == 0, k == 7,
                                 reads=[hTb[jb], W], writes=[P] if k == 0 else [], acc=[] if k == 0 else [P])
                        B_ = nstb()
                        if ti % 2 == 0:
                            s.op("act", lambda e: e.activation(out=B_[:, 0:ncol], in_=P[:, 0:ncol], func=AF.Copy),
                                 reads=[P], writes=[B_])
                        else:
                            s.op("dve", lambda e: e.tensor_copy(out=B_[:, 0:ncol], in_=P[:, 0:ncol]), reads=[P], writes=[B_])
                        s.dma("sp", v_tok[ti * 128:(ti + 1) * 128, vdst:vdst + ncol], B_[:, 0:ncol], reads=[B_], acc=[v_tok])
                    continue
                _, kind, off, m, pc = job
                for j, (st, n) in enumerate(BLOCKS):
                    P = ps[ps_i[0] % 6]
                    ps_i[0] += 1
                    if kind == "gk":
                        for half in range(2):
                            for k in range(8):
                                s.mm(P[64 * half:64 * half + 64, 0:n], W[:, k, off:off + 64], hT[:, k, st:st + n],
                                     k == 0, k == 7, reads=[hTb[j], W],
                                     writes=[P] if (k == 0 and half == 0) else [], acc=[] if (k == 0 and half == 0) else [P])
                        mm_ = 128
                    else:
                        for k in range(8):
                            s.mm(P[0:m, 0:n], W[:, k, off:off + m], hT[:, k, st:st + n], k == 0, k == 7,
                                 reads=[hTb[j], W], writes=[P] if k == 0 else [], acc=[] if k == 0 else [P])
                        mm_ = m
                    rows = slice(pc * 128, pc * 128 + 128)
                    if kind in ("hq", "hog", "mog", "mq", "mk", "gq", "gk"):
                        S_ = nstg()
                        if kind in ("hq", "hog"):
                            s.op("act", lambda e: e.activation(out=S_[:, 0:n], in_=P[:, 0:n], func=AF.Silu), reads=[P], writes=[S_])
                        elif kind == "mog":
                            s.op("act", lambda e: e.activation(out=S_[:, 0:n], in_=P[:, 0:n], func=AF.Sigmoid), reads=[P], writes=[S_])
                        elif kind == "mk":
                            s.op("dve", lambda e: e.tensor_scalar(out=S_[:, 0:n], in0=P[:, 0:n], scalar1=0.125, scalar2=None,
                                                                  op0=ALU.mult), reads=[P], writes=[S_])
                        else:
                            s.op("dve", lambda e: e.tensor_copy(out=S_[:, 0:n], in_=P[:, 0:n]), reads=[P], writes=[S_])
                        dst = {"hq": zq_h, "hog": zog_h, "mog": zog_m, "mq": zq_m, "mk": zk_m, "gq": zq_g}.get(kind)
                        if kind == "gk":
                            s.dma("sp", zk_g[pc, :, st:st + n], S_[:, 0:n], reads=[S_], acc=[zk_g])
                        else:
                            s.dma("sp", dst[rows, st:st + n], S_[:, 0:n], reads=[S_], acc=[dst])
                    elif kind in ("hf0", "hf1"):
                        d = 0 if kind == "hf0" else 1
                        SG = nstg()
                        FG = nstg()
                        KK = nstg()
                        s.op("act", lambda e: e.activation(out=SG[:, 0:n], in_=P[:, 0:n], func=AF.Sigmoid), reads=[P], writes=[SG])
                        s.op("dve", lambda e: e.tensor_scalar(out=FG[:, 0:n], in0=SG[:, 0:n], scalar1=oml[:, l, d, pc:pc + 1],
                                                              scalar2=lb[:, l, d, pc:pc + 1], op0=ALU.mult, op1=ALU.add),
                             reads=[SG, oml, lb], writes=[FG])
                        s.op("act", lambda e: e.activation(out=FG[:, 0:n], in_=FG[:, 0:n], func=AF.Ln), reads=[FG], writes=[FG])
                        s.op("dve", lambda e: e.tensor_scalar(out=KK[:, 0:n], in0=SG[:, 0:n], scalar1=noml[:, l, d, pc:pc + 1],
                                                              scalar2=oml[:, l, d, pc:pc + 1], op0=ALU.mult, op1=ALU.add),
                             reads=[SG, oml, noml], writes=[KK])
                        s.dma("sp", zlf_h[d, rows, st:st + n], FG[:, 0:n], reads=[FG], acc=[zlf_h])
                        s.dma("sp", zkk_h[d, rows, st:st + n], KK[:, 0:n], reads=[KK], acc=[zkk_h])
                    elif kind == "mg":
                        S1 = nstg()
                        S2 = nstg()
                        s.op("act", lambda e: e.activation(out=S1[0:16, 0:n], in_=P[0:16, 0:n], func=AF.Identity,
                                                           bias=mgbT[:, l:l + 1], scale=1.0), reads=[P, mgbT], writes=[S1])
                        s.op("act", lambda e: e.activation(out=S2[0:16, 0:n], in_=P[0:16, 0:n], func=AF.Sigmoid,
                                                           bias=mgbT[:, l:l + 1], scale=1.0), reads=[P, mgbT], writes=[S2])
                        s.op("act", lambda e: e.activation(out=S2[0:16, 0:n], in_=S2[0:16, 0:n], func=AF.Ln), reads=[S2], writes=[S2])
                        s.dma("sp", zg_m[0:8, st:st + n], S1[0:8, 0:n], reads=[S1], acc=[zg_m])
                        s.dma("sp", zg_m[8:16, st:st + n], S2[8:16, 0:n], reads=[S2], acc=[zg_m])
                    elif kind in ("nq", "nk"):
                        B_ = nstb()
                        s.op("act", lambda e: e.activation(out=B_[:, 0:n], in_=P[:, 0:n], func=AF.Copy), reads=[P], writes=[B_])
                        dst = zq_n if kind == "nq" else zk_n
                        s.dma("sp", dst[rows, st:st + n], B_[:, 0:n], reads=[B_], acc=[dst])
                    else:
                        raise ValueError(kind)
        s.release(m0)

    def head_norm(o, gate, gidx, l, chunk, blocks, sqs, pn, rss, tms):
        for j in blocks:
            st, n = BLOCKS[j]
            SQ = sqs[j % 2]
            R = rss[j % 2]
            T = tms[j % 2]
            s.op("act", lambda e: e.activation(out=SQ[:, 0:n], in_=o[:, st:st + n], func=AF.Square), reads=[o], writes=[SQ])
            s.mm(pn[:, 0:n], blkb[:, :], SQ[:, 0:n], True, True, reads=[blkb, SQ], writes=[pn])
            s.op("act", lambda e: e.activation(out=R[:, 0:n], in_=pn[:, 0:n], func=AF.Sqrt, scale=1.0 / 64, bias=EPS),
                 reads=[pn], writes=[R])
            s.op("dve", lambda e: e.reciprocal(out=R[:, 0:n], in_=R[:, 0:n]), reads=[R], writes=[R])
            s.op("dve", lambda e: e.scalar_tensor_tensor(out=T[:, 0:n], in0=o[:, st:st + n], scalar=gcol[:, l, gidx:gidx + 1],
                                                         in1=R[:, 0:n], op0=ALU.mult, op1=ALU.mult),
                 reads=[o, R, gcol], writes=[T])
            s.op("dve", lambda e: e.tensor_tensor(out=hT[:, chunk, st:st + n], in0=T[:, 0:n], in1=gate[:, st:st + n],
                                                  op=ALU.mult), reads=[T, gate], acc=[hTb[j]], selfdep=False)

    def recur(b, l, kind):
        m0 = s.mark()
        NV = 1 if kind == "h" else 2
        L = 16 if kind == "h" else 64
        NCH = NT // L
        CTXN = NCTX // L
        HB = 64 if L == 64 else 32
        SR = 2 * HB
        last = (l == nlayers - 1)
        blocks = [1, 2, 3, 4] if last else [0, 1, 2, 3, 4]
        PTR = [s.psum("ptr%d" % d, [128, 1024], BF16) for d in range(2)]
        PSC = [s.psum("psc%d" % d, [128, 512], F32) for d in range(2)]
        PSO = [s.psum("pso%d" % d, [128, 512], F32) for d in range(2)]
        PST = [s.psum("pstt%d" % d, [128, 512], F32) for d in range(2)]
        pn = PSC[0]
        lf = s.sbuf("r_lf", [128, NT], F32)
        PP = s.sbuf("r_PP", [128, NT + 1], F32)
        arg = s.sbuf("r_arg", [128, NT], F32)
        Dm = s.sbuf("r_D", [128, NT], F32)
        kk = s.sbuf("r_kk", [128, NT], F32)
        qs = s.sbuf("r_qs", [128, NT], F32)
        igb = s.sbuf("r_ig", [128, NT], F32) if kind == "m" else None
        Q = [s.sbuf("r_Q%d" % d, [128, NT], BF16) for d in range(2)]
        K = [s.sbuf("r_K%d" % d, [128, NT], BF16) for d in range(2)]
        KN = [s.sbuf("r_KN%d" % d, [128, NT], BF16) for d in range(2)]
        Vbd = s.sbuf("r_Vbd", [SR, NCH, 128], BF16)
        Vp = s.sbuf("r_Vp", [L, NCH, 128], BF16)
        Rn = s.sbuf("r_Rn", [128, NCH], F32)
        G = [s.sbuf("r_G%d" % d, [128, NCH], F32) for d in range(2)]
        W32 = [[s.sbuf("r_W%d%d" % (d, v), [128, 64], F32) for v in range(NV)] for d in range(2)]
        Wbf = [[s.sbuf("r_Wb%d%d" % (d, v), [128, 64], BF16) for v in range(NV)] for d in range(2)]
        kts = [s.sbuf("r_kt%d" % i, [L, 128], BF16) for i in range(4)]
        pms = [s.sbuf("r_pm%d" % i, [SR, L], BF16) for i in range(4)]
        for pmb in pms:
            s.op("pool", lambda e: e.memset(pmb[:, :], 0.0), writes=[pmb])
        sqs = [s.sbuf("r_sq%d" % i, [128, 512], BF16) for i in range(2)]
        rss = [s.sbuf("r_rs%d" % i, [128, 512], F32) for i in range(2)]
        tms = [s.sbuf("r_tm%d" % i, [128, 512], F32) for i in range(2)]
        if kind == "h":
            accs = [[lf], [arg]]
        else:
            accs = [[lf, arg], [Dm, kk]]
        zq = zq_h if kind == "h" else zq_m
        zog = zog_h if kind == "h" else zog_m
        vbase = 0 if kind == "h" else 256
        gidx = 0 if kind == "h" else 1
        order = [list(range(NCH)), list(range(CTXN - 1, -1, -1)) + list(range(NCH - 1, CTXN - 1, -1))]
        slot = 0
        for pc in range(2):
            rows = slice(pc * 128, pc * 128 + 128)
            vcol = vbase + pc * 128
            s.op("pool", lambda e: e.memset(Vbd[:, :, :], 0.0), writes=[Vbd])
            for hh in range(2):
                s.dma("sp", Vbd[HB * hh:HB * hh + L, :, 64 * hh:64 * hh + 64],
                      v_tok[:, vcol + 64 * hh:vcol + 64 * hh + 64].rearrange("(c p) d -> p c d", p=L),
                      reads=[v_tok], acc=[Vbd])
            s.dma("sp", Vp[:, :, :], v_tok[:, vcol:vcol + 128].rearrange("(c p) d -> p c d", p=L), reads=[v_tok], writes=[Vp])
            s.dma("sp", qs[:, :], zq[rows, :], reads=[zq], writes=[qs])
            for d in range(2):
                sg = 1.0 if d == 0 else -1.0
                if kind == "h":
                    s.dma("sp", lf[:, :], zlf_h[d, rows, :], reads=[zlf_h], writes=[lf])
                    s.dma("sp", kk[:, :], zkk_h[d, rows, :], reads=[zkk_h], writes=[kk])
                else:
                    for hh in range(2):
                        h = 2 * pc + hh
                        s.dma("sp", lf[64 * hh:64 * hh + 64, :], zg_m[8 + 4 * d + h:9 + 4 * d + h, :].partition_broadcast(64),
                              reads=[zg_m], writes=[lf] if hh == 0 else [], acc=[] if hh == 0 else [lf])
                        s.dma("sp", igb[64 * hh:64 * hh + 64, :], zg_m[4 * d + h:4 * d + h + 1, :].partition_broadcast(64),
                              reads=[zg_m], writes=[igb] if hh == 0 else [], acc=[] if hh == 0 else [igb])
                    s.dma("sp", kk[:, :], zk_m[rows, :], reads=[zk_m], writes=[kk])
                s.op("pool", lambda e: e.memset(PP[:, 0:1], 0.0), writes=[PP])
                s.op("dve", lambda e: e.tensor_tensor_scan(out=PP[:, 1:NT + 1], data0=lf[:, :], data1=lf[:, :], initial=0.0,
                                                           op0=ALU.add, op1=ALU.bypass), reads=[lf], acc=[PP])
                Rm = PP[:, 0:NT].rearrange("p (c l) -> p c l", l=L)[:, :, L // 2]
                if d == 0:
                    s.op("dve", lambda e: e.tensor_copy(out=Rn[:, 0:NCH - 1], in_=Rm[:, 1:NCH]), reads=[PP], writes=[Rn])
                    s.op("dve", lambda e: e.tensor_copy(out=Rn[:, NCH - 1:NCH], in_=Rm[:, NCH - 1:NCH]), reads=[PP], acc=[Rn])
                    s.op("dve", lambda e: e.tensor_tensor(out=G[d][:, :], in0=Rn[:, :], in1=Rm, op=ALU.subtract),
                         reads=[Rn, PP], writes=[G[d]])
                else:
                    s.op("dve", lambda e: e.tensor_copy(out=Rn[:, 1:NCH], in_=Rm[:, 0:NCH - 1]), reads=[PP], writes=[Rn])
                    s.op("dve", lambda e: e.tensor_copy(out=Rn[:, CTXN:CTXN + 1], in_=Rm[:, CTXN:CTXN + 1]), reads=[PP], acc=[Rn])
                    s.op("dve", lambda e: e.tensor_tensor(out=Rn[:, 0:1], in0=Rm[:, NCH - 1:NCH], in1=PP[:, NT:NT + 1],
                                                          op=ALU.subtract), reads=[PP], acc=[Rn])
                    s.op("dve", lambda e: e.tensor_tensor(out=G[d][:, :], in0=Rm, in1=Rn[:, :], op=ALU.subtract),
                         reads=[Rn, PP], writes=[G[d]])
                s.op("act", lambda e: e.activation(out=G[d][:, :], in_=G[d][:, :], func=AF.Exp), reads=[G[d]], writes=[G[d]])
                PPs = (PP[:, 1:NT + 1] if d == 0 else PP[:, 0:NT]).rearrange("p (c l) -> p c l", l=L)
                a3 = arg[:, :].rearrange("p (c l) -> p c l", l=L)
                s.op("dve", lambda e: e.tensor_tensor(out=a3, in0=PPs, in1=Rm.unsqueeze(2).to_broadcast([128, NCH, L]),
                                                      op=ALU.subtract), reads=[PP], writes=[arg])
                s.op("act", lambda e: e.activation(out=Dm[:, :], in_=arg[:, :], func=AF.Exp, scale=sg), reads=[arg], writes=[Dm])
                s.op("dve", lambda e: e.scalar_tensor_tensor(out=Q[d][:, :], in0=qs[:, :], scalar=(0.125 if kind == "h" else 1.0),
                                                             in1=Dm[:, :], op0=ALU.mult, op1=ALU.mult),
                     reads=[qs, Dm], writes=[Q[d]])
                if kind == "m":
                    s.op("dve", lambda e: e.scalar_tensor_tensor(out=Dm[:, :], in0=arg[:, :], scalar=-sg, in1=igb[:, :],
                                                                 op0=ALU.mult, op1=ALU.add), reads=[arg, igb], writes=[Dm])
                    s.op("act", lambda e: e.activation(out=Dm[:, :], in_=Dm[:, :], func=AF.Exp), reads=[Dm], writes=[Dm])
                else:
                    s.op("act", lambda e: e.activation(out=Dm[:, :], in_=arg[:, :], func=AF.Exp, scale=-sg), reads=[arg], writes=[Dm])
                s.op("dve", lambda e: e.tensor_tensor(out=K[d][:, :], in0=kk[:, :], in1=Dm[:, :], op=ALU.mult),
                     reads=[kk, Dm], writes=[K[d]])
                s.op("dve", lambda e: e.tensor_tensor(out=a3, in0=PPs, in1=Rn[:, :].unsqueeze(2).to_broadcast([128, NCH, L]),
                                                      op=ALU.subtract), reads=[PP, Rn], writes=[arg])
                if kind == "m":
                    s.op("dve", lambda e: e.scalar_tensor_tensor(out=Dm[:, :], in0=arg[:, :], scalar=-sg, in1=igb[:, :],
                                                                 op0=ALU.mult, op1=ALU.add), reads=[arg, igb], writes=[Dm])
                    s.op("act", lambda e: e.activation(out=Dm[:, :], in_=Dm[:, :], func=AF.Exp), reads=[Dm], writes=[Dm])
                else:
                    s.op("act", lambda e: e.activation(out=Dm[:, :], in_=arg[:, :], func=AF.Exp, scale=-sg), reads=[arg], writes=[Dm])
                s.op("dve", lambda e: e.tensor_tensor(out=KN[d][:, :], in0=kk[:, :], in1=Dm[:, :], op=ALU.mult),
                     reads=[kk, Dm], writes=[KN[d]])
                for v in range(NV):
                    s.op("pool", lambda e: e.memset(W32[d][v][:, :], 0.0), writes=[W32[d][v]])
            if dbg and stop_after == "hbuild":
                for d in range(2):
                    s.dma("sp", dbgQ[d, 0], Q[d][:, :], reads=[Q[d]], acc=[dbgQb])
                    s.dma("sp", dbgQ[d, 1], K[d][:, :], reads=[K[d]], acc=[dbgQb])
                    s.dma("sp", dbgQ[d, 2], KN[d][:, :], reads=[KN[d]], acc=[dbgQb])
                s.dma("sp", dbgP[:, :], PP[:, :], reads=[PP], writes=[dbgPb])
                s.release(m0)
                return
            seq = [(step, d) for step in range(NCH) for d in range(2)]
            slot0 = slot

            def phaseA(i):
                step, d = seq[i]
                c = order[d][step]
                csl = slice(c * L, c * L + L)
                sl = (slot0 + i) % 4
                lastst = (step == NCH - 1)
                kt = kts[sl]
                pm = pms[sl]
                ptr, psc = PTR[d], PSC[d]
                if not lastst:
                    s.op("pe", lambda e: e.transpose(out=ptr[0:L, 0:128], in_=KN[d][:, csl], identity=identb[:, :]),
                         reads=[KN[d], identb], writes=[ptr])
                    s.op("act", lambda e: e.activation(out=kt[:, :], in_=ptr[0:L, 0:128], func=AF.Copy),
                         reads=[ptr], writes=[kt])
                for hh in range(2):
                    hs = slice(64 * hh, 64 * hh + 64)
                    ps_ = slice(HB * hh, HB * hh + L)
                    s.mm(psc[ps_, 0:L], K[d][hs, csl], Q[d][hs, csl], True, True, reads=[K[d], Q[d]],
                         writes=[psc] if hh == 0 else [], acc=[] if hh == 0 else [psc])
                if L == 64:
                    s.op("dve", lambda e: e.tensor_tensor(out=pm[:, :], in0=psc[:, 0:L], in1=maskFB[L][d], op=ALU.mult),
                         reads=[psc, cs], writes=[pm])
                else:
                    for hh in range(2):
                        ps_ = slice(HB * hh, HB * hh + L)
                        s.op("dve", lambda e: e.tensor_tensor(out=pm[ps_, :], in0=psc[ps_, 0:L], in1=maskFB[L][d][ps_, :],
                                                              op=ALU.mult), reads=[psc, cs],
                             writes=[pm] if hh == 0 else [], acc=[] if hh == 0 else [pm], selfdep=(hh == 0))

            def phaseB(i):
                step, d = seq[i]
                c = order[d][step]
                csl = slice(c * L, c * L + L)
                sl = (slot0 + i) % 4
                first = (step == 0)
                lastst = (step == NCH - 1)
                kt = kts[sl]
                pm = pms[sl]
                pso, pst = PSO[d], PST[d]
                for v in range(NV):
                    Vb = Vbd[:, c, :] if v == 0 else bd1[:, :]
                    vo = slice(v * 64, v * 64 + L)
                    s.mm(pso[:, vo], Vb, pm[:, :], True, first, reads=[Vbd, bd1, pm],
                         writes=[pso] if v == 0 else [], acc=[] if v == 0 else [pso])
                    if not first:
                        for hh in range(2):
                            hs = slice(64 * hh, 64 * hh + 64)
                            s.mm(pso[hs, vo], Wbf[d][v][hs, :], Q[d][hs, csl], False, hh == 1,
                                 reads=[Wbf[d][v], Q[d]], acc=[pso])
                for v in range(NV):
                    vo = slice(v * 64, v * 64 + L)
                    A_ = accs[d][v]
                    s.op("act", lambda e: e.activation(out=A_[:, csl], in_=pso[:, vo], func=AF.Copy),
                         reads=[pso], acc=[A_], selfdep=False)
                if not lastst:
                    for v in range(NV):
                        vt = slice(v * 64, v * 64 + 64)
                        for hh in range(2):
                            hs = slice(64 * hh, 64 * hh + 64)
                            rhs = Vp[:, c, hs] if v == 0 else onesb[0:L, 0:64]
                            wr = (v == 0 and hh == 0)
                            s.mm(pst[hs, vt], kt[:, hs], rhs, True, True, reads=[kt, Vp, onesb],
                                 writes=[pst] if wr else [], acc=[] if wr else [pst])
                    for v in range(NV):
                        vt = slice(v * 64, v * 64 + 64)
                        s.op("dve", lambda e: e.scalar_tensor_tensor(out=W32[d][v][:, :], in0=W32[d][v][:, :],
                                                                     scalar=G[d][:, c:c + 1], in1=pst[:, vt],
                                                                     op0=ALU.mult, op1=ALU.add),
                             reads=[W32[d][v], G[d], pst], writes=[W32[d][v]])
                        s.op("pool", lambda e: e.tensor_copy(out=Wbf[d][v][:, :], in_=W32[d][v][:, :]),
                             reads=[W32[d][v]], writes=[Wbf[d][v]])

            phaseA(0)
            for i in range(len(seq)):
                if i + 1 < len(seq):
                    phaseA(i + 1)
                phaseB(i)
            slot += len(seq)
            if kind == "m":
                for d in range(2):
                    num, den = accs[d]
                    s.op("act", lambda e: e.activation(out=den[:, :], in_=den[:, :], func=AF.Abs), reads=[den], writes=[den])
                    s.op("pool", lambda e: e.tensor_scalar_max(out=den[:, :], in0=den[:, :], scalar1=1.0), reads=[den], writes=[den])
                    s.op("dve", lambda e: e.reciprocal(out=den[:, :], in_=den[:, :]), reads=[den], writes=[den])
                    s.op("pool", lambda e: e.tensor_tensor(out=num[:, :], in0=num[:, :], in1=den[:, :], op=ALU.mult),
                         reads=[num, den], writes=[num])
            o = accs[0][0]
            s.op("pool", lambda e: e.tensor_tensor(out=o[:, :], in0=o[:, :], in1=accs[1][0][:, :], op=ALU.add),
                 reads=[o, accs[1][0]], writes=[o])
            s.dma("sp", qs[:, :], zog[rows, :], reads=[zog], writes=[qs])
            chunk = (0 if kind == "h" else 2) + pc
            head_norm(o, qs, gidx, l, chunk, blocks, sqs, pn, rss, tms)
        s.release(m0)

    def attn_block(qb, kb, vfn, vbuf, st, n, kcs, chunk, sc_ps, num_ps, den_ps, pTs, rd, cnt):
        j = [i for i, (a, _) in enumerate(BLOCKS) if a <= st < a + BLOCKS[i][1]][0]
        nk = len(kcs)
        its = [(ki, kc, hh) for ki, kc in enumerate(kcs) for hh in range(2)]
        scs = {}

        def issue_sc(i):
            ki, kc, hh = its[i]
            hs = slice(64 * hh, 64 * hh + 64)
            sc = sc_ps[cnt[0] % len(sc_ps)]
            pT = pTs[cnt[0] % len(pTs)]
            cnt[0] += 1
            s.mm(sc[:, 0:n], kb[hs, kc * 128:(kc + 1) * 128], qb[hs, st:st + n], True, True, reads=[kb, qb], writes=[sc])
            scs[i] = (sc, pT)

        issue_sc(0)
        if len(its) > 1:
            issue_sc(1)
        for i, (ki, kc, hh) in enumerate(its):
            hs = slice(64 * hh, 64 * hh + 64)
            if i + 2 < len(its):
                issue_sc(i + 2)
            sc, pT = scs.pop(i)
            s.op("act", lambda e: e.activation(out=pT[:, 0:n], in_=sc[:, 0:n], func=AF.Exp, scale=0.125),
                 reads=[sc], writes=[pT])
            first = (ki == 0)
            s.mm(num_ps[hs, 0:n], vfn(kc, hh), pT[:, 0:n], first, ki == nk - 1, reads=[pT, vbuf],
                 writes=[num_ps] if (first and hh == 0) else [], acc=[] if (first and hh == 0) else [num_ps])
            s.mm(den_ps[hs, 0:n], onesb[:, 0:64], pT[:, 0:n], first, ki == nk - 1, reads=[pT, onesb],
                 writes=[den_ps] if (first and hh == 0) else [], acc=[] if (first and hh == 0) else [den_ps])
        s.op("dve", lambda e: e.reciprocal(out=rd[:, 0:n], in_=den_ps[:, 0:n]), reads=[den_ps], writes=[rd])
        s.op("dve", lambda e: e.tensor_tensor(out=hT[:, chunk, st:st + n], in0=num_ps[:, 0:n], in1=rd[:, 0:n], op=ALU.mult),
             reads=[num_ps, rd], acc=[hTb[j]], selfdep=False)

    def gqa(b, l):
        m0 = s.mark()
        last = (l == nlayers - 1)
        rp = s.sbuf("g_rope", [128, 4096], F32)
        s.dma("sp", rp[:, :], ropec[:, :], writes=[rp])
        raw = [s.sbuf("g_raw%d" % i, [128, NT], F32) for i in range(2)]
        QK = [s.sbuf("g_qk%d" % i, [128, NT], BF16) for i in range(4)]
        sqs = [s.sbuf("g_sq%d" % i, [128, 512], BF16) for i in range(2)]
        rss = [s.sbuf("g_rs%d" % i, [128, 512], F32) for i in range(2)]
        t1s = [s.sbuf("g_t1%d" % i, [128, 512], F32) for i in range(2)]
        t2s = [s.sbuf("g_t2%d" % i, [128, 512], F32) for i in range(2)]
        t3s = [s.sbuf("g_t3%d" % i, [128, 512], F32) for i in range(2)]
        Vg = s.sbuf("g_V", [128, 18, 128], BF16)
        pTs = [s.sbuf("g_pT%d" % i, [128, 512], BF16) for i in range(4)]
        rds = [s.sbuf("g_rd%d" % i, [128, 512], F32) for i in range(2)]
        sc_ps = [s.psum("g_sc%d" % i, [128, 512], F32) for i in range(4)]
        num_ps = [s.psum("g_num%d" % i, [128, 512], F32) for i in range(2)]
        den_ps = [s.psum("g_den%d" % i, [128, 512], F32) for i in range(2)]
        s.dma("sp", Vg[:, :, :], v_tok[:, 512:640].rearrange("(c p) d -> p c d", p=128), reads=[v_tok], writes=[Vg])
        srcs = [(zq_g[0:128, :], 2), (zq_g[128:256, :], 2), (zk_g[0, :, :], 3), (zk_g[1, :, :], 3)]
        it = 0
        for idx, (src, gi) in enumerate(srcs):
            R_ = raw[idx % 2]
            s.dma("sp", R_[:, :], src, reads=[zq_g, zk_g], writes=[R_])
            for j, (st, n) in enumerate(BLOCKS):
                SQ = sqs[it % 2]
                RS = rss[it % 2]
                T1 = t1s[it % 2]
                T2 = t2s[it % 2]
                T3 = t3s[it % 2]
                P1 = sc_ps[(2 * it) % 4]
                P2 = sc_ps[(2 * it + 1) % 4]
                it += 1
                s.op("act", lambda e: e.activation(out=SQ[:, 0:n], in_=R_[:, st:st + n], func=AF.Square), reads=[R_], writes=[SQ])
                s.mm(P1[:, 0:n], blkb[:, :], SQ[:, 0:n], True, True, reads=[blkb, SQ], writes=[P1])
                s.op("act", lambda e: e.activation(out=RS[:, 0:n], in_=P1[:, 0:n], func=AF.Sqrt, scale=1.0 / 64, bias=EPS),
                     reads=[P1], writes=[RS])
                s.op("dve", lambda e: e.reciprocal(out=RS[:, 0:n], in_=RS[:, 0:n]), reads=[RS], writes=[RS])
                s.op("dve", lambda e: e.scalar_tensor_tensor(out=T1[:, 0:n], in0=R_[:, st:st + n], scalar=gcol[:, l, gi:gi + 1],
                                                             in1=RS[:, 0:n], op0=ALU.mult, op1=ALU.mult),
                     reads=[R_, RS, gcol], writes=[T1])
                if st >= 256:
                    s.mm(P2[:, 0:n], ropeRT, T1[:, 0:n], True, True, reads=[cs, T1], writes=[P2])
                    s.op("pool", lambda e: e.tensor_tensor(out=T2[:, 0:n], in0=T1[:, 0:n], in1=rp[:, st - 256:st - 256 + n],
                                                           op=ALU.mult), reads=[T1, rp], writes=[T2])
                    s.op("dve", lambda e: e.tensor_tensor(out=T3[:, 0:n], in0=P2[:, 0:n],
                                                          in1=rp[:, 2048 + st - 256:2048 + st - 256 + n], op=ALU.mult),
                         reads=[P2, rp], writes=[T3])
                    s.op("pool", lambda e: e.tensor_tensor(out=QK[idx][:, st:st + n], in0=T2[:, 0:n], in1=T3[:, 0:n], op=ALU.add),
                         reads=[T2, T3], acc=[QK[idx]], selfdep=False)
                else:
                    s.op("act", lambda e: e.activation(out=QK[idx][:, st:st + n], in_=T1[:, 0:n], func=AF.Copy),
                         reads=[T1], acc=[QK[idx]], selfdep=False)
        cnt = [0]
        qblocks = [1, 2, 3, 4] if last else [0, 1, 2, 3, 4]
        bi = 0
        for pc in range(2):
            for j in qblocks:
                st, n = BLOCKS[j]
                kcs = list(range(18)) if st >= 256 else [0, 1]
                attn_block(QK[pc], QK[2 + pc], lambda kc, hh: Vg[:, kc, 64 * pc:64 * pc + 64], Vg, st, n, kcs, 4 + pc,
                           sc_ps, num_ps[bi % 2], den_ps[bi % 2], pTs, rds[bi % 2], cnt)
                bi += 1
        s.release(m0)

    def na(b, l):
        m0 = s.mark()
        last = (l == nlayers - 1)
        bias8 = s.sbuf("n_bias", [128, 4 * NU, 128], BF16)
        bst = [s.sbuf("n_bst%d" % i, [128, NU, 128], F32) for i in range(2)]
        qT = [s.sbuf("n_q%d" % i, [128, NT], BF16) for i in range(2)]
        kT = [s.sbuf("n_k%d" % i, [128, NT], BF16) for i in range(2)]
        Vn = s.sbuf("n_V", [128, 18, 256], BF16)
        pTs = [s.sbuf("n_pT%d" % i, [128, 1024], BF16) for i in range(3)]
        pTd = [s.sbuf("n_pTd%d" % i, [128, 512], BF16) for i in range(2)]
        rds = [s.sbuf("n_rd%d" % i, [128, 512], F32) for i in range(2)]
        sc_ps = [s.psum("n_sc%d" % i, [128, 512], F32) for i in range(4)]
        num_ps = [s.psum("n_num%d" % i, [128, 512], F32) for i in range(2)]
        den_ps = [s.psum("n_den%d" % i, [128, 512], F32) for i in range(2)]
        for h in range(4):
            B_ = bst[h % 2]
            s.dma("sp", B_[:, :, :], nab[l, h * NU:(h + 1) * NU, :, :].rearrange("u p q -> p u q"), writes=[B_])
            s.op("pool", lambda e: e.tensor_scalar(out=bias8[:, h * NU:(h + 1) * NU, :], in0=B_[:, :, :], scalar1=8.0,
                                                   scalar2=None, op0=ALU.mult), reads=[B_], acc=[bias8])
        for pc in range(2):
            rows = slice(pc * 128, pc * 128 + 128)
            s.dma("sp", qT[pc][:, :], zq_n[rows, :], reads=[zq_n], writes=[qT[pc]])
            s.dma("sp", kT[pc][:, :], zk_n[rows, :], reads=[zk_n], writes=[kT[pc]])
        s.dma("sp", Vn[:, :, :], v_tok[:, 640:896].rearrange("(c p) d -> p c d", p=128), reads=[v_tok], writes=[Vn])
        it = 0
        for pc in range(2):
            units = [(t, hh) for t in range(16) for hh in range(2)]
            info = {}

            def na_S(ui):
                t, hh = units[ui]
                q0 = 256 + 128 * t
                loc = sorted([j for (tt, j) in _NA_TMAP if tt == t])
                allk = [(0, None), (1, None)] + [(2 + j, _NA_TMAP[(t, j)]) for j in loc]
                h = 2 * pc + hh
                hs = slice(64 * hh, 64 * hh + 64)
                g = it + ui
                banks = [sc_ps[(2 * g) % 4], sc_ps[(2 * g + 1) % 4]]
                for i, (gc, u) in enumerate(allk):
                    bk = banks[i // 4]
                    cc = slice((i % 4) * 128, (i % 4) * 128 + 128)
                    firstw = (i % 4 == 0)
                    s.mm(bk[:, cc], kT[pc][hs, gc * 128:(gc + 1) * 128], qT[pc][hs, q0:q0 + 128], True, u is None,
                         reads=[kT[pc], qT[pc]], writes=[bk] if firstw else [], acc=[] if firstw else [bk])
                    if u is not None:
                        s.mm(bk[:, cc], identb[:, :], bias8[:, h * NU + u, :], False, True, reads=[identb, bias8], acc=[bk])
                info[ui] = (allk, banks, pTs[g % 3])

            def na_EV(ui):
                t, hh = units[ui]
                q0 = 256 + 128 * t
                jb = 1 + t // 4
                h = 2 * pc + hh
                hs = slice(64 * hh, 64 * hh + 64)
                allk, banks, pT = info.pop(ui)
                nk = len(allk)
                NUM = num_ps[t % 2]
                DEN = den_ps[t % 2]
                RD = rds[t % 2]
                n0 = min(nk, 4) * 128
                s.op("act", lambda e: e.activation(out=pT[:, 0:n0], in_=banks[0][:, 0:n0], func=AF.Exp, scale=0.125),
                     reads=[banks[0]], writes=[pT])
                if nk > 4:
                    n1 = (nk - 4) * 128
                    s.op("act", lambda e: e.activation(out=pT[:, 512:512 + n1], in_=banks[1][:, 0:n1], func=AF.Exp, scale=0.125),
                         reads=[banks[1]], acc=[pT])
                for i, (gc, u) in enumerate(allk):
                    wr = (i == 0 and hh == 0)
                    s.mm(NUM[hs, 0:128], Vn[:, gc, h * 64:(h + 1) * 64], pT[:, i * 128:(i + 1) * 128], i == 0, i == nk - 1,
                         reads=[Vn, pT], writes=[NUM] if wr else [], acc=[] if wr else [NUM])
                    s.mm(DEN[hs, 0:128], onesb[:, 0:64], pT[:, i * 128:(i + 1) * 128], i == 0, i == nk - 1,
                         reads=[onesb, pT], writes=[DEN] if wr else [], acc=[] if wr else [DEN])
                if hh == 1:
                    s.op("dve", lambda e: e.reciprocal(out=RD[:, 0:128], in_=DEN[:, 0:128]), reads=[DEN], writes=[RD])
                    s.op("dve", lambda e: e.tensor_tensor(out=hT[:, 6 + pc, q0:q0 + 128], in0=NUM[:, 0:128], in1=RD[:, 0:128],
                                                          op=ALU.mult), reads=[NUM, RD], acc=[hTb[jb]], selfdep=False)

            na_S(0)
            for ui in range(len(units)):
                if ui + 1 < len(units):
                    na_S(ui + 1)
                na_EV(ui)
            it += len(units)
            if not last:
                cnt = [0]
                attn_block(qT[pc], kT[pc], lambda kc, hh: Vn[:, kc, (2 * pc + hh) * 64:(2 * pc + hh) * 64 + 64], Vn, 0, 256, [0, 1],
                           6 + pc, sc_ps, num_ps[0], den_ps[0], pTd, rds[0], cnt)
        s.release(m0)

    def p4(b, l):
        last = (l == nlayers - 1)
        halves = [[0, 1, 2], [3, 4]]
        if last:
            halves[0] = [1, 2]
        for blks in halves:
            m0 = s.mark()
            base = BLOCKS[blks[0]][0]
            xh = s.sbuf("xh", [128, 8, 1280], F32)
            xhv = {j: Buf(xh.t, "xh.%d" % j) for j in blks}
            stg = [s.sbuf("p4stg%d" % i, [128, 8, 512], F32) for i in range(2)]
            wb = [s.sbuf("p4wb%d" % i, [128, 8, 512], BF16) for i in range(2)]
            w2b = [s.sbuf("p4w2b%d" % i, [128, 4, 1024], BF16) for i in range(2)]
            ub = [s.sbuf("p4u%d" % i, [128, 4, 512], BF16) for i in range(2)]
            rb = [s.sbuf("p4r%d" % i, [128, 512], F32) for i in range(3)]
            sq = s.sbuf("p4sq", [128, 8, 512], BF16)
            rs = s.sbuf("p4rs", [128, 512], F32)
            tmps = [s.sbuf("p4t%d" % i, [128, 512], F32) for i in range(3)]
            ps = [s.psum("p4ps%d" % i, [128, 512], F32) for i in range(7)]
            pss = s.psum("p4pss", [128, 512], F32)
            pi = [0]

            def nps():
                pi[0] += 1
                return ps[pi[0] % 7]

            for j in blks:
                st, n = BLOCKS[j]
                s.dma("sp", xh[:, :, st - base:st - base + n], xs[:, :, st:st + n], reads=[xs], writes=[xhv[j]])
            for cg in range(2):
                s.dma("sp", stg[cg][:, :, :], w_out[l, :, cg * 512:(cg + 1) * 512].rearrange("(k p) n -> p k n", p=128),
                      writes=[stg[cg]])
                if cg == 0:
                    s.op("act", lambda e: e.activation(out=wb[cg][:, :, :], in_=stg[cg][:, :, :], func=AF.Copy),
                         reads=[stg[cg]], writes=[wb[cg]])
                else:
                    s.op("dve", lambda e: e.tensor_copy(out=wb[cg][:, :, :], in_=stg[cg][:, :, :]), reads=[stg[cg]], writes=[wb[cg]])
            for cg in range(2):
                for j in blks:
                    st, n = BLOCKS[j]
                    col = 2 if st < 256 else b
                    for mi in range(4):
                        m = cg * 4 + mi
                        P = nps()
                        for k in range(8):
                            s.mm(P[:, 0:n], wb[cg][:, k, mi * 128:(mi + 1) * 128], hT[:, k, st:st + n], k == 0, k == 7,
                                 reads=[wb[cg], hTb[j]], writes=[P] if k == 0 else [], acc=[] if k == 0 else [P])
                        xa = xh[:, m, st - base:st - base + n]
                        s.op("dve", lambda e: e.scalar_tensor_tensor(out=xa, in0=P[:, 0:n], scalar=modT[:, l, 16 + m, col:col + 1],
                                                                     in1=xa, op0=ALU.mult, op1=ALU.add),
                             reads=[P, modT, xhv[j]], acc=[xhv[j]], selfdep=False)
            for j in blks:
                st, n = BLOCKS[j]
                col = 2 if st < 256 else b
                norm_block(xhv[j], st - base, n, st, l, 1, col, sq, pss, rs, tmps, j)
            def loadw_dma(g):
                s.dma("sp", stg[0][:, :, :], w1[l, :, g * 512:(g + 1) * 512].rearrange("(k p) n -> p k n", p=128), writes=[stg[0]])
                s.dma("sp", stg[1][:, :, :].rearrange("p k n -> p (k n)").rearrange("p (c n) -> p c n", c=4),
                      w2[l, g * 512:(g + 1) * 512, :].rearrange("(c p) n -> p c n", p=128), writes=[stg[1]])

            def loadw_cast(g):
                s.op("act", lambda e: e.activation(out=wb[g % 2][:, :, :], in_=stg[0][:, :, :], func=AF.Copy),
                     reads=[stg[0]], writes=[wb[g % 2]])
                s.op("dve", lambda e: e.tensor_copy(out=w2b[g % 2][:, :, :],
                                                    in_=stg[1][:, :, :].rearrange("p k n -> p (k n)").rearrange("p (c n) -> p c n", c=4)),
                     reads=[stg[1]], writes=[w2b[g % 2]])

            loadw_dma(0)
            loadw_cast(0)
            ui = 0
            for g in range(8):
                if g + 1 < 8:
                    loadw_dma(g + 1)
                W1 = wb[g % 2]
                W2 = w2b[g % 2]
                for bi_, j in enumerate(blks):
                    st, n = BLOCKS[j]
                    col = 2 if st < 256 else b
                    U = ub[ui % 2]
                    ui += 1
                    for hc in range(4):
                        P = nps()
                        for k in range(8):
                            s.mm(P[:, 0:n], W1[:, k, hc * 128:(hc + 1) * 128], hT[:, k, st:st + n], k == 0, k == 7,
                                 reads=[W1, hTb[j]], writes=[P] if k == 0 else [], acc=[] if k == 0 else [P])
                        R_ = rb[hc % 3]
                        s.op("act", lambda e: e.activation(out=R_[:, 0:n], in_=P[:, 0:n], func=AF.Relu), reads=[P], writes=[R_])
                        s.op("dve", lambda e: e.tensor_tensor(out=U[:, hc, 0:n], in0=R_[:, 0:n], in1=R_[:, 0:n], op=ALU.mult),
                             reads=[R_], writes=[U] if hc == 0 else [], acc=[] if hc == 0 else [U], selfdep=(hc == 0))
                    if g + 1 < 8 and bi_ == len(blks) - 1:
                        loadw_cast(g + 1)
                    for m in range(8):
                        P = nps()
                        for hc in range(4):
                            s.mm(P[:, 0:n], W2[:, hc, m * 128:(m + 1) * 128], U[:, hc, 0:n], hc == 0, hc == 3,
                                 reads=[W2, U], writes=[P] if hc == 0 else [], acc=[] if hc == 0 else [P])
                        xa = xh[:, m, st - base:st - base + n]
                        s.op("dve", lambda e: e.scalar_tensor_tensor(out=xa, in0=P[:, 0:n], scalar=modT[:, l, 40 + m, col:col + 1],
                                                                     in1=xa, op0=ALU.mult, op1=ALU.add),
                             reads=[P, modT, xhv[j]], acc=[xhv[j]], selfdep=False)
            if not last:
                for j in blks:
                    st, n = BLOCKS[j]
                    s.dma("sp", xs[:, :, st:st + n], xh[:, :, st - base:st - base + n], reads=[xhv[j]], acc=[xs])
            else:
                yb = stg[0]
                youts = [Buf(stg[1].t, "yout%d" % i) for i in range(2)]
                oc = 0
                for j in blks:
                    st, n = BLOCKS[j]
                    s.op("act", lambda e: e.activation(out=sq[:, :, 0:n], in_=xh[:, :, st - base:st - base + n], func=AF.Square),
                         reads=[xhv[j]], writes=[sq])
                    for k in range(8):
                        s.mm(pss[:, 0:n], onesb[:, :], sq[:, k, 0:n], k == 0, k == 7, reads=[sq, onesb],
                             writes=[pss] if k == 0 else [], acc=[] if k == 0 else [pss])
                    s.op("act", lambda e: e.activation(out=rs[:, 0:n], in_=pss[:, 0:n], func=AF.Sqrt, scale=1.0 / 1024, bias=EPS),
                         reads=[pss], writes=[rs])
                    s.op("dve", lambda e: e.reciprocal(out=rs[:, 0:n], in_=rs[:, 0:n]), reads=[rs], writes=[rs])
                    for k in range(8):
                        s.op("dve", lambda e: e.scalar_tensor_tensor(out=yb[:, k, 0:n], in0=xh[:, k, st - base:st - base + n],
                                                                     scalar=gT[:, 4, k:k + 1], in1=rs[:, 0:n],
                                                                     op0=ALU.mult, op1=ALU.mult),
                             reads=[xhv[j], gT, rs], writes=[yb] if k == 0 else [], acc=[] if k == 0 else [yb], selfdep=(k == 0))
                    for ti in range(n // 128):
                        YO = youts[oc % 2]
                        yo_ap = stg[1][:, (oc % 2) * 2:(oc % 2) * 2 + 2, :].rearrange("p a n -> p (a n)")
                        oc += 1
                        for half in range(2):
                            P = nps()
                            for kk in range(4):
                                k = half * 4 + kk
                                s.op("pe", lambda e: e.transpose(out=P[:, kk * 128:(kk + 1) * 128],
                                                                 in_=yb[:, k, ti * 128:(ti + 1) * 128], identity=identf),
                                     reads=[yb, cs], writes=[P] if kk == 0 else [], acc=[] if kk == 0 else [P])
                            s.op("act", lambda e: e.activation(out=yo_ap[:, half * 512:(half + 1) * 512], in_=P[:, :], func=AF.Copy),
                                 reads=[P], writes=[YO] if half == 0 else [], acc=[] if half == 0 else [YO], selfdep=(half == 0))
                        tok = st - 256 + ti * 128
                        s.dma("sp", y[b, tok:tok + 128, :], yo_ap, reads=[YO], acc=[y])
            s.release(m0)

    prologue()
    for b in range(nseq if stop_after != "pro" else 0):
        for l in range(nlayers):
            p1(b, l)
            if stop_after == "p1":
                break
            p2(b, l)
            if stop_after == "p2":
                break
            recur(b, l, "h")
            if stop_after in ("h", "hbuild"):
                break
            recur(b, l, "m")
            if stop_after == "m":
                break
            gqa(b, l)
            if stop_after == "g":
                break
            na(b, l)
            if stop_after == "p3":
                break
            p4(b, l)
        if stop_after is not None:
            break
    if dbg:
        s.dma("sp", cat_dbg[:, :, :], hT[:, :, :], reads=hTb, writes=[cat_dbg])
    s.finish()
    build.stats = (s.nops, s.nwaits, dict(s.cnt))
    return nc


_CACHE = {}


def _host_inputs(inputs, core):
    f = lambda a: np.ascontiguousarray(np.asarray(a, dtype=np.float32))
    b0 = 2 * core
    cst, rope = _CACHE["consts"]
    m = {
        "x": f(inputs["x"][b0:b0 + 2]),
        "ctx": f(inputs["ctx"][b0:b0 + 2]),
        "cvec": f(np.concatenate([inputs["c"][b0:b0 + 2], np.asarray(inputs["c_ctx"])[None, :]], 0)),
        "w_mod": f(inputs["w_mod"]), "b_mod": f(inputs["b_mod"]),
        "norm1_g": f(inputs["norm1_g"]), "norm2_g": f(inputs["norm2_g"]),
        "w_in": f(inputs["w_in"]),
        "hgrn_lb_logits": f(np.asarray(inputs["hgrn_lb_logits"]).reshape(4, 256)),
        "hgrn_norm_g": f(inputs["hgrn_norm_g"]), "mlstm_gate_b": f(inputs["mlstm_gate_b"]),
        "mlstm_norm_g": f(inputs["mlstm_norm_g"]), "gqa_qnorm_g": f(inputs["gqa_qnorm_g"]),
        "gqa_knorm_g": f(inputs["gqa_knorm_g"]),
        "na_bias": _CACHE["na_bias"],
        "w_out": f(inputs["w_out"]), "w_mlp1": f(inputs["w_mlp1"]), "w_mlp2": f(inputs["w_mlp2"]),
        "final_norm_g": f(np.asarray(inputs["final_norm_g"]).reshape(1, 1024)),
        "consts": cst, "rope": rope,
    }
    return m


def _prep(inputs):
    _CACHE["consts"] = _consts()
    idx = _na_gather_index()
    rpb = np.asarray(inputs["na_rpb"], np.float32)
    flat = np.concatenate([rpb.reshape(2, 4, 465), np.full((2, 4, 1), NEG, np.float32)], -1)
    nb = flat[:, :, idx]
    _CACHE["na_bias"] = np.ascontiguousarray(nb.reshape(2, 4 * NU, 128, 128))


def kernel(**inputs):
    _prep(inputs)
    nc = build()
    in_maps = [_host_inputs(inputs, c) for c in range(8)]
    res = run_bass_kernel_spmd(nc, in_maps, core_ids=list(range(8)))
    out = np.concatenate([np.asarray(r["y"], np.float32) for r in res.results], axis=0)
    return out
```

```python
import numpy as np
import ml_dtypes
import concourse.bass as bass
import concourse.mybir as mybir
from concourse.bass_utils import run_bass_kernel_spmd

F32 = mybir.dt.float32
BF16 = mybir.dt.bfloat16
AF = mybir.ActivationFunctionType
ALU = mybir.AluOpType

NT = 2304
NCTX = 256
EPS = 1e-6
NEG = -1e30
BLOCKS = [(0, 256), (256, 512), (768, 512), (1280, 512), (1792, 512)]
NCH = 36


class Buf:
    __slots__ = ("t", "name", "lw", "aw", "rd")

    def __init__(self, t, name):
        self.t = t
        self.name = name
        self.lw = {}
        self.aw = {}
        self.rd = {}

    def __getitem__(self, idx):
        return self.t[idx]


class Sch:
    NDMA = 10

    def __init__(self, nc):
        self.nc = nc
        self.eng = {"pe": nc.tensor, "dve": nc.vector, "act": nc.scalar, "pool": nc.gpsimd, "sp": nc.sync}
        self.sems = {}
        self.cnt = {}
        self.key = {}
        self.nsem = 0
        for e in self.eng:
            self._newsem(e)
        for q in ("sp", "act", "pool"):
            for k in range(self.NDMA):
                key = "d%s%d" % (q, k)
                self.sems[key] = nc.alloc_semaphore("s_" + key)
                self.cnt[key] = 0
        self.seen = {e: {} for e in self.eng}
        self.dma_rr = {"sp": 0, "act": 0, "pool": 0}
        self.nwaits = 0
        self.nops = 0
        self._stack = []

    def _newsem(self, e):
        self.nsem += 1
        key = "%s@%d" % (e, self.nsem)
        self.sems[key] = self.nc.alloc_semaphore("s_%s_%d" % (e, self.nsem))
        self.cnt[key] = 0
        self.key[e] = key

    def sbuf(self, name, shape, dtype):
        self.nsem += 1
        name = "%s_%d" % (name, self.nsem)
        g = self.nc.sbuf_tensor(name, list(shape), dtype)
        t = g.__enter__()
        self._stack.append(g)
        return Buf(t, name)

    def psum(self, name, shape, dtype=F32):
        self.nsem += 1
        name = "%s_%d" % (name, self.nsem)
        g = self.nc.psum_tensor(name, list(shape), dtype)
        t = g.__enter__()
        self._stack.append(g)
        return Buf(t, name)

    def dram(self, name, shape, dtype, kind="Internal"):
        t = self.nc.dram_tensor(name, list(shape), dtype, kind=kind)
        return Buf(t, name)

    @staticmethod
    def views(buf, n):
        return [Buf(buf.t, "%s.%d" % (buf.name, i)) for i in range(n)]

    def mark(self):
        return len(self._stack)

    def release(self, mark):
        self.barrier()
        while len(self._stack) > mark:
            g = self._stack.pop()
            g.__exit__(None, None, None)

    def _need(self, e, toks, selfdep=True, wtoks=()):
        best = {}
        for k, v in toks:
            if k.startswith("pe@") and e == "pe":
                continue
            if best.get(k, 0) < v:
                best[k] = v
        for k, v in wtoks:
            if k.startswith("pe@") and e == "pe":
                continue
            if (not selfdep) and k == self.key[e]:
                continue
            if best.get(k, 0) < v:
                best[k] = v
        for k, v in best.items():
            if self.seen[e].get(k, 0) >= v:
                continue
            self.eng[e].wait_ge(self.sems[k], v)
            self.seen[e][k] = v
            self.nwaits += 1

    @staticmethod
    def _deps(reads, writes, acc):
        rt, wt = [], []
        for b in reads:
            rt.extend(b.lw.items())
            rt.extend(b.aw.items())
        for b in writes:
            wt.extend(b.lw.items())
            wt.extend(b.aw.items())
            wt.extend(b.rd.items())
        for b in acc:
            wt.extend(b.lw.items())
            wt.extend(b.rd.items())
        return rt, wt

    @staticmethod
    def _commit(tok, reads, writes, acc):
        k, v = tok
        for b in reads:
            if b.rd.get(k, 0) < v:
                b.rd[k] = v
        for b in writes:
            b.lw = {k: v}
            b.aw = {}
            b.rd = {}
        for b in acc:
            if b.aw.get(k, 0) < v:
                b.aw[k] = v

    def op(self, e, fn, reads=(), writes=(), acc=(), selfdep=True):
        rt, wt = self._deps(reads, writes, acc)
        self._need(e, rt, selfdep, wt)
        ins = fn(self.eng[e])
        self.nops += 1
        key = self.key[e]
        self.cnt[key] += 1
        ins.then_inc(self.sems[key], 1)
        self._commit((key, self.cnt[key]), reads, writes, acc)
        return ins

    def mm(self, out_ap, lhsT, rhs, start, stop, reads=(), writes=(), acc=()):
        return self.op("pe", lambda e: e.matmul(out_ap, lhsT, rhs, start=start, stop=stop,
                                                skip_group_check=True), reads, writes, acc)

    def dma(self, q, out_ap, in_ap, reads=(), writes=(), acc=(), **kw):
        key = "d%s%d" % (q, self.dma_rr[q])
        self.dma_rr[q] = (self.dma_rr[q] + 1) % self.NDMA
        rt, wt = self._deps(reads, writes, acc)
        toks = rt + wt
        if self.cnt[key] > 0:
            toks.append((key, self.cnt[key]))
        self._need(q, toks)
        self.cnt[key] += 16
        ins = self.eng[q].dma_start(out=out_ap, in_=in_ap, **kw)
        ins.then_inc(self.sems[key], 16)
        self.nops += 1
        self._commit((key, self.cnt[key]), reads, writes, acc)
        return ins

    def barrier(self):
        toks = [(k, v) for k, v in self.cnt.items() if v > 0]
        for e in self.eng:
            self._need(e, toks)
        for e in list(self.eng):
            if self.cnt[self.key[e]] > 24000:
                self._newsem(e)

    def finish(self):
        toks = [(k, v) for k, v in self.cnt.items() if v > 0]
        self._need("sp", toks)


def _na_tiles():
    uniq = {}
    tmap = {}
    for t in range(16):
        lo, hi = 10 ** 9, -1
        for b in range(2):
            r0 = min(max(2 * t + b - 4, 0), 24)
            lo = min(lo, r0 // 2)
            hi = max(hi, (r0 + 7) // 2)
        for j in range(lo, hi + 1):
            pat = []
            for a in range(2):
                for b in range(2):
                    qr = 2 * t + b
                    kr = 2 * j + a
                    r0 = min(max(qr - 4, 0), 24)
                    pat.append((kr - qr) if (r0 <= kr < r0 + 8) else None)
            pat = tuple(pat)
            if pat not in uniq:
                uniq[pat] = len(uniq)
            tmap[(t, j)] = uniq[pat]
    pats = [None] * len(uniq)
    for p, i in uniq.items():
        pats[i] = p
    return pats, tmap


_NA_PATS, _NA_TMAP = _na_tiles()
NU = len(_NA_PATS)


def _na_gather_index():
    idx = np.full((NU, 128, 128), 465, np.int64)
    qc = np.arange(64)
    cstart = np.clip(qc - 8, 0, 48)
    kc = np.arange(64)
    col_in = (kc[:, None] >= cstart[None, :]) & (kc[:, None] < cstart[None, :] + 16)
    cidx = np.clip(kc[:, None] - qc[None, :], -15, 15) + 15
    for u, pat in enumerate(_NA_PATS):
        for a in range(2):
            for b in range(2):
                dr = pat[a * 2 + b]
                if dr is None:
                    continue
                blk = np.where(col_in, (dr + 7) * 31 + cidx, 465)
                idx[u, a * 64:(a + 1) * 64, b * 64:(b + 1) * 64] = blk
    return idx


def _consts():
    c = np.zeros((128, 576), np.float32)
    c[:, 0:128] = np.eye(128, dtype=np.float32)
    blk = np.zeros((128, 128), np.float32)
    blk[0:64, 0:64] = 1.0
    blk[64:128, 64:128] = 1.0
    c[:, 128:256] = blk
    rt = np.zeros((128, 128), np.float32)
    for i in range(64):
        rt[2 * i + 1, 2 * i] = -1.0
        rt[2 * i, 2 * i + 1] = 1.0
    c[:, 256:384] = rt
    sidx = np.arange(64)[:, None]
    tidx = np.arange(64)[None, :]
    mf = (sidx <= tidx).astype(np.float32)
    mb = (sidx >= tidx).astype(np.float32)
    c[:, 384:448] = np.concatenate([mf, mf], 0)
    c[:, 448:512] = np.concatenate([mb, mb], 0)
    p = np.arange(128)[:, None] % 16
    t16 = np.arange(16)[None, :]
    c[:, 512:528] = (p <= t16).astype(np.float32)
    c[:, 528:544] = (p >= t16).astype(np.float32)
    t = np.arange(2048)
    row = (t // 64).astype(np.float32)
    col = (t % 64).astype(np.float32)
    inv = np.power(np.float32(10000.0), (-2.0 * np.arange(16, dtype=np.float32) / np.float32(32.0))).astype(np.float32)
    ang = np.concatenate([row[:, None] * inv[None, :], col[:, None] * inv[None, :]], -1).astype(np.float32)
    cos = np.cos(ang).astype(np.float32)
    sin = np.sin(ang).astype(np.float32)
    cosf = np.repeat(cos, 2, axis=1).T
    sinf = np.repeat(sin, 2, axis=1).T
    rope = np.concatenate([np.concatenate([cosf, cosf], 0), np.concatenate([sinf, sinf], 0)], 1).astype(np.float32)
    return c, np.ascontiguousarray(rope)


def build(nlayers=2, nseq=2, stop_after=None, dbg=False):
    nc = bass.Bass("TRN2", target_bir_lowering=False)
    s = Sch(nc)

    def inp(name, shape, dt=F32):
        return s.dram(name, shape, dt, kind="ExternalInput")

    x_in = inp("x", [2, 2048, 1024])
    ctx_in = inp("ctx", [2, 256, 1024])
    cvec = inp("cvec", [3, 1024])
    w_mod = inp("w_mod", [2, 1024, 6144])
    b_mod = inp("b_mod", [2, 6144])
    n1g = inp("norm1_g", [2, 1024])
    n2g = inp("norm2_g", [2, 1024])
    w_in = inp("w_in", [2, 1024, 3600])
    lbl = inp("hgrn_lb_logits", [4, 256])
    hgg = inp("hgrn_norm_g", [2, 64])
    mgb = inp("mlstm_gate_b", [2, 16])
    mgg = inp("mlstm_norm_g", [2, 64])
    qng = inp("gqa_qnorm_g", [2, 64])
    kng = inp("gqa_knorm_g", [2, 64])
    nab = inp("na_bias", [2, 4 * NU, 128, 128])
    w_out = inp("w_out", [2, 1024, 1024])
    w1 = inp("w_mlp1", [2, 1024, 4096])
    w2 = inp("w_mlp2", [2, 4096, 1024])
    fng = inp("final_norm_g", [1, 1024])
    cst = inp("consts", [128, 576])
    ropec = inp("rope", [128, 4096])
    y = s.dram("y", [2, 2048, 1024], F32, kind="ExternalOutput")

    okind = "ExternalOutput" if dbg else "Internal"
    xs = s.dram("xs", [128, 8, NT], F32, kind=okind)
    zq_h = s.dram("zq_h", [256, NT], F32, kind=okind)
    zog_h = s.dram("zog_h", [256, NT], F32, kind=okind)
    zlf_h = s.dram("zlf_h", [2, 256, NT], F32, kind=okind)
    zkk_h = s.dram("zkk_h", [2, 256, NT], F32, kind=okind)
    zq_m = s.dram("zq_m", [256, NT], F32, kind=okind)
    zk_m = s.dram("zk_m", [256, NT], F32, kind=okind)
    zog_m = s.dram("zog_m", [256, NT], F32, kind=okind)
    zg_m = s.dram("zg_m", [16, NT], F32, kind=okind)
    zq_g = s.dram("zq_g", [256, NT], F32, kind=okind)
    zk_g = s.dram("zk_g", [2, 128, NT], F32, kind=okind)
    zq_n = s.dram("zq_n", [256, NT], BF16, kind=okind)
    zk_n = s.dram("zk_n", [256, NT], BF16, kind=okind)
    v_tok = s.dram("v_tok", [NT, 896], BF16, kind=okind)
    cat_dbg = s.dram("cat_dbg", [128, 8, NT], BF16, kind=okind) if dbg else None
    h_dbg = s.dram("h_dbg", [128, 8, NT], BF16, kind=okind) if dbg else None
    mod_dbg = s.dram("mod_dbg", [128, 2 * 48 * 3], F32, kind=okind) if dbg else None

    if dbg:
        dbgQb = s.dram("dbgQ", [2, 3, 128, NT], BF16, kind=okind)
        dbgPb = s.dram("dbgP", [128, NT + 1], F32, kind=okind)
        dbgQ = dbgQb
        dbgP = dbgPb
    NSL = True

    cs = s.sbuf("cs", [128, 576], F32)
    identb = s.sbuf("identb", [128, 128], BF16)
    onesb = s.sbuf("onesb", [128, 128], BF16)
    blkb = s.sbuf("blkb", [128, 128], BF16)
    bd1 = s.sbuf("bd1", [128, 128], BF16)
    hT = s.sbuf("hT", [128, 8, NT], BF16)
    hTb = Sch.views(hT, 5)
    modT = s.sbuf("modT", [128, 2, 48, 3], F32)
    AT = s.sbuf("AT", [128, 2, 2, 8, 3], F32)
    gT = s.sbuf("gT", [128, 5, 8], F32)
    gcol = s.sbuf("gcol", [128, 2, 4], F32)
    lb = s.sbuf("lb", [128, 2, 2, 2], F32)
    oml = s.sbuf("oml", [128, 2, 2, 2], F32)
    noml = s.sbuf("noml", [128, 2, 2, 2], F32)
    mgbT = s.sbuf("mgbT", [16, 2], F32)

    identf = cs[:, 0:128]
    blkf = cs[:, 128:256]
    ropeRT = cs[:, 256:384]
    maskFB = {64: [cs[:, 384:448], cs[:, 448:512]], 16: [cs[:, 512:528], cs[:, 528:544]]}

    s.dma("sp", cs[:, :], cst[:, :], writes=[cs])
    s.op("dve", lambda e: e.tensor_copy(out=identb[:, :], in_=identf), reads=[cs], writes=[identb])
    s.op("dve", lambda e: e.tensor_copy(out=blkb[:, :], in_=blkf), reads=[cs], writes=[blkb])
    s.op("dve", lambda e: e.tensor_copy(out=bd1[:, :], in_=blkf), reads=[cs], writes=[bd1])
    s.op("pool", lambda e: e.memset(onesb[:, :], 1.0), writes=[onesb])

    def prologue():
        m0 = s.mark()
        scT = s.sbuf("scT", [128, 8, 3], F32)
        bmT = s.sbuf("bmT", [128, 2, 48], F32)
        lg = s.sbuf("lg", [128, 4, 2], F32)
        wm = [s.sbuf("wm%d" % i, [128, 8, 512], F32) for i in range(2)]
        pm = s.psum("pm_mod", [128, 512], F32)
        for r in range(3):
            s.dma("sp", scT[:, :, r], cvec[r:r + 1, :].rearrange("o (k p) -> p (o k)", p=128), acc=[scT],
                  allow_slow_non_contiguous=NSL)
        for l in range(2):
            s.dma("sp", bmT[:, l, :], b_mod[l:l + 1, :].rearrange("o (j p) -> p (o j)", p=128), acc=[bmT],
                  allow_slow_non_contiguous=NSL)
        gsrc = [n1g[0:1, :], n1g[1:2, :], n2g[0:1, :], n2g[1:2, :], fng[0:1, :]]
        for i, g in enumerate(gsrc):
            s.dma("sp", gT[:, i, :], g.rearrange("o (k p) -> p (o k)", p=128), acc=[gT], allow_slow_non_contiguous=NSL)
        for r in range(4):
            s.dma("sp", lg[:, r, :], lbl[r:r + 1, :].rearrange("o (c p) -> p (o c)", p=128), acc=[lg],
                  allow_slow_non_contiguous=NSL)
        for l in range(2):
            for i, g in enumerate([hgg, mgg, qng, kng]):
                for hh in range(2):
                    s.dma("sp", gcol[64 * hh:64 * hh + 64, l, i:i + 1], g[l:l + 1, :].rearrange("o d -> d o"),
                          acc=[gcol], allow_slow_non_contiguous=NSL)
            s.dma("sp", mgbT[:, l:l + 1], mgb[l:l + 1, :].rearrange("o g -> g o"), acc=[mgbT],
                  allow_slow_non_contiguous=NSL)
        s.op("act", lambda e: e.activation(out=scT[:, :, :], in_=scT[:, :, :], func=AF.Silu), reads=[scT], writes=[scT])
        ex = s.sbuf("ex", [128, 4, 2], F32)
        den = s.sbuf("den", [128, 2, 2], F32)
        s.op("act", lambda e: e.activation(out=ex[:, :, :], in_=lg[:, :, :], func=AF.Exp), reads=[lg], writes=[ex])
        s.op("dve", lambda e: e.tensor_tensor(out=den[:, :, :], in0=ex[:, 0:2, :], in1=ex[:, 2:4, :], op=ALU.add),
             reads=[ex], writes=[den])
        s.op("dve", lambda e: e.reciprocal(out=den[:, :, :], in_=den[:, :, :]), reads=[den], writes=[den])
        s.op("pool", lambda e: e.memset(lb[:, 0, :, :], 0.0), acc=[lb])
        s.op("dve", lambda e: e.tensor_tensor(out=lb[:, 1, :, :], in0=ex[:, 2:4, :], in1=den[:, :, :], op=ALU.mult),
             reads=[ex, den], acc=[lb])
        s.op("dve", lambda e: e.tensor_scalar(out=oml[:, :, :, :], in0=lb[:, :, :, :], scalar1=-1.0, scalar2=1.0,
                                              op0=ALU.mult, op1=ALU.add), reads=[lb], writes=[oml])
        s.op("dve", lambda e: e.tensor_scalar(out=noml[:, :, :, :], in0=lb[:, :, :, :], scalar1=1.0, scalar2=-1.0,
                                              op0=ALU.mult, op1=ALU.add), reads=[lb], writes=[noml])
        it = 0
        for l in range(2):
            for grp in range(12):
                W = wm[it % 2]
                it += 1
                s.dma("sp", W[:, :, :], w_mod[l, :, grp * 512:(grp + 1) * 512].rearrange("(k p) n -> p k n", p=128),
                      writes=[W])
                for m in range(4):
                    idx = grp * 4 + m
                    for k in range(8):
                        s.mm(pm[:, idx * 3:idx * 3 + 3], W[:, k, m * 128:(m + 1) * 128], scT[:, k, :], k == 0, k == 7,
                             reads=[W, scT], acc=[pm])
            s.op("dve", lambda e: e.tensor_tensor(
                out=modT[:, l, :, :], in0=pm[:, 0:144].rearrange("p (j r) -> p j r", r=3),
                in1=bmT[:, l, :].unsqueeze(2).to_broadcast([128, 48, 3]), op=ALU.add),
                reads=[pm, bmT], acc=[modT])
        for l in range(2):
            for w in range(2):
                for k in range(8):
                    j = (1 if w == 0 else 4) * 8 + k
                    s.op("dve", lambda e: e.tensor_scalar(out=AT[:, l, w, k, :], in0=modT[:, l, j, :], scalar1=1.0,
                                                          scalar2=gT[:, w * 2 + l, k:k + 1], op0=ALU.add, op1=ALU.mult),
                         reads=[modT, gT], acc=[AT])
        if dbg:
            s.dma("sp", mod_dbg[:, :], modT[:, :, :, :].rearrange("p l j r -> p (l j r)"), reads=[modT], writes=[mod_dbg])
        s.release(m0)

    def norm_block(X, xoff, n, st, l, w, col, sq, pss, rs, tmps, j):
        s.op("act", lambda e: e.activation(out=sq[:, :, 0:n], in_=X[:, :, xoff:xoff + n], func=AF.Square),
             reads=[X], writes=[sq])
        for k in range(8):
            s.mm(pss[:, 0:n], onesb[:, :], sq[:, k, 0:n], k == 0, k == 7, reads=[sq, onesb],
                 writes=[pss] if k == 0 else [], acc=[] if k == 0 else [pss])
        s.op("act", lambda e: e.activation(out=rs[:, 0:n], in_=pss[:, 0:n], func=AF.Sqrt, scale=1.0 / 1024, bias=EPS),
             reads=[pss], writes=[rs])
        s.op("dve", lambda e: e.reciprocal(out=rs[:, 0:n], in_=rs[:, 0:n]), reads=[rs], writes=[rs])
        sh = 0 if w == 0 else 3
        for k in range(8):
            T = tmps[k % len(tmps)]
            s.op("dve", lambda e: e.scalar_tensor_tensor(out=T[:, 0:n], in0=X[:, k, xoff:xoff + n],
                                                         scalar=AT[:, l, w, k, col:col + 1], in1=rs[:, 0:n],
                                                         op0=ALU.mult, op1=ALU.mult), reads=[X, rs, AT], writes=[T])
            s.op("act", lambda e: e.activation(out=hT[:, k, st:st + n], in_=T[:, 0:n], func=AF.Identity,
                                               bias=modT[:, l, sh * 8 + k, col:col + 1], scale=1.0),
                 reads=[T, modT], acc=[hTb[j]], selfdep=False)

    def p1(b, l):
        m0 = s.mark()
        xb = [s.sbuf("xb%d" % i, [128, 8, 512], F32) for i in range(2)]
        sq = [s.sbuf("sq%d" % i, [128, 8, 512], BF16) for i in range(2)]
        rs = [s.sbuf("rs%d" % i, [128, 512], F32) for i in range(2)]
        tmps = [s.sbuf("tmp%d" % i, [128, 512], F32) for i in range(4)]
        pss = [s.psum("pss%d" % i, [128, 512], F32) for i in range(2)]
        if l == 0:
            xin = [s.sbuf("xin%d" % i, [128, 1024], F32) for i in range(3)]
            pst = [s.psum("pst%d" % i, [128, 512], F32) for i in range(4)]
        cnt = 0
        for j, (st, n) in enumerate(BLOCKS):
            X = xb[j % 2]
            if l == 0:
                for ti in range(n // 128):
                    tok0 = st + ti * 128
                    src = ctx_in[b, tok0:tok0 + 128, :] if tok0 < 256 else x_in[b, tok0 - 256:tok0 - 128, :]
                    xi = xin[cnt % 3]
                    s.dma("sp", xi[:, :], src, writes=[xi])
                    for half in range(2):
                        pt = pst[(cnt * 2 + half) % 4]
                        for kk in range(4):
                            k = half * 4 + kk
                            s.op("pe", lambda e: e.transpose(out=pt[:, kk * 128:(kk + 1) * 128],
                                                             in_=xi[:, k * 128:(k + 1) * 128], identity=identf),
                                 reads=[xi, cs], writes=[pt] if kk == 0 else [], acc=[] if kk == 0 else [pt])
                        s.op("act", lambda e: e.activation(
                            out=X[:, half * 4:(half + 1) * 4, ti * 128:(ti + 1) * 128],
                            in_=pt[:, :].rearrange("p (k t) -> p k t", t=128), func=AF.Copy),
                            reads=[pt], writes=[X] if (ti == 0 and half == 0) else [],
                            acc=[] if (ti == 0 and half == 0) else [X], selfdep=False)
                    cnt += 1
                s.dma("sp", xs[:, :, st:st + n], X[:, :, 0:n], reads=[X], acc=[xs])
            else:
                s.dma("sp", X[:, :, 0:n], xs[:, :, st:st + n], reads=[xs], writes=[X])
            col = 2 if st < 256 else b
            norm_block(X, 0, n, st, l, 0, col, sq[j % 2], pss[j % 2], rs[j % 2], tmps, j)
        if dbg:
            s.dma("sp", h_dbg[:, :, :], hT[:, :, :], reads=hTb, writes=[h_dbg])
        s.release(m0)

    GROUPS = [
        (0, 512, [("f", "hq", 0, 128, 0), ("f", "hq", 128, 128, 1), ("v", 256, 256, 0)]),
        (512, 512, [("f", "hog", 0, 128, 0), ("f", "hog", 128, 128, 1), ("f", "hf0", 256, 128, 0), ("f", "hf0", 384, 128, 1)]),
        (1024, 512, [("f", "hf1", 0, 128, 0), ("f", "hf1", 128, 128, 1), ("f", "mq", 256, 128, 0), ("f", "mq", 384, 128, 1)]),
        (1536, 512, [("f", "mk", 0, 128, 0), ("f", "mk", 128, 128, 1), ("v", 256, 256, 256)]),
        (2048, 272, [("f", "mog", 0, 128, 0), ("f", "mog", 128, 128, 1), ("f", "mg", 256, 16, 0)]),
        (2320, 512, [("f", "gq", 0, 128, 0), ("f", "gq", 128, 128, 1), ("f", "gk", 256, 64, 0), ("f", "gk", 320, 64, 1),
                     ("v", 384, 128, 512)]),
        (2832, 512, [("f", "nq", 0, 128, 0), ("f", "nq", 128, 128, 1), ("f", "nk", 256, 128, 0), ("f", "nk", 384, 128, 1)]),
        (3344, 256, [("v", 0, 256, 640)]),
    ]

    def p2(b, l):
        m0 = s.mark()
        wst = [s.sbuf("wst%d" % i, [128, 8, 512], F32) for i in range(2)]
        wbf = [s.sbuf("wbf%d" % i, [128, 8, 512], BF16) for i in range(2)]
        stg = [s.sbuf("stg%d" % i, [128, 512], F32) for i in range(8)]
        stb = [s.sbuf("stb%d" % i, [128, 512], BF16) for i in range(4)]
        ps = [s.psum("p2ps%d" % i, [128, 512], F32) for i in range(6)]
        st_i = [0]
        sb_i = [0]
        ps_i = [0]

        def nstg():
            st_i[0] += 1
            return stg[st_i[0] % 8]

        def nstb():
            sb_i[0] += 1
            return stb[sb_i[0] % 4]

        def load(gi):
            c0, w, _ = GROUPS[gi]
            s.dma("sp", wst[gi % 2][:, :, 0:w], w_in[l, :, c0:c0 + w].rearrange("(k p) n -> p k n", p=128),
                  writes=[wst[gi % 2]])

        def cast(gi):
            c0, w, _ = GROUPS[gi]
            if gi % 2 == 0:
                s.op("act", lambda e: e.activation(out=wbf[gi % 2][:, :, 0:w], in_=wst[gi % 2][:, :, 0:w], func=AF.Copy),
                     reads=[wst[gi % 2]], writes=[wbf[gi % 2]])
            else:
                s.op("dve", lambda e: e.tensor_copy(out=wbf[gi % 2][:, :, 0:w], in_=wst[gi % 2][:, :, 0:w]),
                     reads=[wst[gi % 2]], writes=[wbf[gi % 2]])

        load(0)
        cast(0)
        load(1)
        for gi, (c0, w, jobs) in enumerate(GROUPS):
            if gi + 1 < len(GROUPS):
                cast(gi + 1)
            if gi + 2 < len(GROUPS):
                load(gi + 2)
            W = wbf[gi % 2]
            for job in jobs:
                if job[0] == "v":
                    _, off, ncol, vdst = job
                    for ti in range(18):
                        P = ps[ps_i[0] % 6]
                        ps_i[0] += 1
                        jb = 0 if ti < 2 else 1 + (ti - 2) // 4
                        for k in range(8):
                            s.mm(P[:, 0:ncol], hT[:, k, ti * 128:(ti + 1) * 128], W[:, k, off:off + ncol], k == 0, k == 7,
                                 reads=[hTb[jb], W], writes=[P] if k == 0 else [], acc=[] if k == 0 else [P])
                        B_ = nstb()
                        if ti % 2 == 0:
                            s.op("act", lambda e: e.activation(out=B_[:, 0:ncol], in_=P[:, 0:ncol], func=AF.Copy),
                                 reads=[P], writes=[B_])
                        else:
                            s.op("dve", lambda e: e.tensor_copy(out=B_[:, 0:ncol], in_=P[:, 0:ncol]), reads=[P], writes=[B_])
                        s.dma("sp", v_tok[ti * 128:(ti + 1) * 128, vdst:vdst + ncol], B_[:, 0:ncol], reads=[B_], acc=[v_tok])
                    continue
                _, kind, off, m, pc = job
                for j, (st, n) in enumerate(BLOCKS):
                    P = ps[ps_i[0] % 6]
                    ps_i[0] += 1
                    if kind == "gk":
                        for half in range(2):
                            for k in range(8):
                                s.mm(P[64 * half:64 * half + 64, 0:n], W[:, k, off:off + 64], hT[:, k, st:st + n],
                                     k == 0, k == 7, reads=[hTb[j], W],
                                     writes=[P] if (k == 0 and half == 0) else [], acc=[] if (k == 0 and half == 0) else [P])
                        mm_ = 128
                    else:
                        for k in range(8):
                            s.mm(P[0:m, 0:n], W[:, k, off:off + m], hT[:, k, st:st + n], k == 0, k == 7,
                                 reads=[hTb[j], W], writes=[P] if k == 0 else [], acc=[] if k == 0 else [P])
                        mm_ = m
                    rows = slice(pc * 128, pc * 128 + 128)
                    if kind in ("hq", "hog", "mog", "mq", "mk", "gq", "gk"):
                        S_ = nstg()
                        if kind in ("hq", "hog"):
                            s.op("act", lambda e: e.activation(out=S_[:, 0:n], in_=P[:, 0:n], func=AF.Silu), reads=[P], writes=[S_])
                        elif kind == "mog":
                            s.op("act", lambda e: e.activation(out=S_[:, 0:n], in_=P[:, 0:n], func=AF.Sigmoid), reads=[P], writes=[S_])
                        elif kind == "mk":
                            s.op("dve", lambda e: e.tensor_scalar(out=S_[:, 0:n], in0=P[:, 0:n], scalar1=0.125, scalar2=None,
                                                                  op0=ALU.mult), reads=[P], writes=[S_])
                        else:
                            s.op("dve", lambda e: e.tensor_copy(out=S_[:, 0:n], in_=P[:, 0:n]), reads=[P], writes=[S_])
                        dst = {"hq": zq_h, "hog": zog_h, "mog": zog_m, "mq": zq_m, "mk": zk_m, "gq": zq_g}.get(kind)
                        if kind == "gk":
                            s.dma("sp", zk_g[pc, :, st:st + n], S_[:, 0:n], reads=[S_], acc=[zk_g])
                        else:
                            s.dma("sp", dst[rows, st:st + n], S_[:, 0:n], reads=[S_], acc=[dst])
                    elif kind in ("hf0", "hf1"):
                        d = 0 if kind == "hf0" else 1
                        SG = nstg()
                        FG = nstg()
                        KK = nstg()
                        s.op("act", lambda e: e.activation(out=SG[:, 0:n], in_=P[:, 0:n], func=AF.Sigmoid), reads=[P], writes=[SG])
                        s.op("dve", lambda e: e.tensor_scalar(out=FG[:, 0:n], in0=SG[:, 0:n], scalar1=oml[:, l, d, pc:pc + 1],
                                                              scalar2=lb[:, l, d, pc:pc + 1], op0=ALU.mult, op1=ALU.add),
                             reads=[SG, oml, lb], writes=[FG])
                        s.op("act", lambda e: e.activation(out=FG[:, 0:n], in_=FG[:, 0:n], func=AF.Ln), reads=[FG], writes=[FG])
                        s.op("dve", lambda e: e.tensor_scalar(out=KK[:, 0:n], in0=SG[:, 0:n], scalar1=noml[:, l, d, pc:pc + 1],
                                                              scalar2=oml[:, l, d, pc:pc + 1], op0=ALU.mult, op1=ALU.add),
                             reads=[SG, oml, noml], writes=[KK])
                        s.dma("sp", zlf_h[d, rows, st:st + n], FG[:, 0:n], reads=[FG], acc=[zlf_h])
                        s.dma("sp", zkk_h[d, rows, st:st + n], KK[:, 0:n], reads=[KK], acc=[zkk_h])
                    elif kind == "mg":
                        S1 = nstg()
                        S2 = nstg()
                        s.op("act", lambda e: e.activation(out=S1[0:16, 0:n], in_=P[0:16, 0:n], func=AF.Identity,
                                                           bias=mgbT[:, l:l + 1], scale=1.0), reads=[P, mgbT], writes=[S1])
                        s.op("act", lambda e: e.activation(out=S2[0:16, 0:n], in_=P[0:16, 0:n], func=AF.Sigmoid,
                                                           bias=mgbT[:, l:l + 1], scale=1.0), reads=[P, mgbT], writes=[S2])
                        s.op("act", lambda e: e.activation(out=S2[0:16, 0:n], in_=S2[0:16, 0:n], func=AF.Ln), reads=[S2], writes=[S2])
                        s.dma("sp", zg_m[0:8, st:st + n], S1[0:8, 0:n], reads=[S1], acc=[zg_m])
                        s.dma("sp", zg_m[8:16, st:st + n], S2[8:16, 0:n], reads=[S2], acc=[zg_m])
                    elif kind in ("nq", "nk"):
                        B_ = nstb()
                        s.op("act", lambda e: e.activation(out=B_[:, 0:n], in_=P[:, 0:n], func=AF.Copy), reads=[P], writes=[B_])
                        dst = zq_n if kind == "nq" else zk_n
                        s.dma("sp", dst[rows, st:st + n], B_[:, 0:n], reads=[B_], acc=[dst])
                    else:
                        raise ValueError(kind)
        s.release(m0)

    def head_norm(o, gate, gidx, l, chunk, blocks, sqs, pn, rss, tms):
        for j in blocks:
            st, n = BLOCKS[j]
            SQ = sqs[j % 2]
            R = rss[j % 2]
            T = tms[j % 2]
            s.op("act", lambda e: e.activation(out=SQ[:, 0:n], in_=o[:, st:st + n], func=AF.Square), reads=[o], writes=[SQ])
            s.mm(pn[:, 0:n], blkb[:, :], SQ[:, 0:n], True, True, reads=[blkb, SQ], writes=[pn])
            s.op("act", lambda e: e.activation(out=R[:, 0:n], in_=pn[:, 0:n], func=AF.Sqrt, scale=1.0 / 64, bias=EPS),
                 reads=[pn], writes=[R])
            s.op("dve", lambda e: e.reciprocal(out=R[:, 0:n], in_=R[:, 0:n]), reads=[R], writes=[R])
            s.op("dve", lambda e: e.scalar_tensor_tensor(out=T[:, 0:n], in0=o[:, st:st + n], scalar=gcol[:, l, gidx:gidx + 1],
                                                         in1=R[:, 0:n], op0=ALU.mult, op1=ALU.mult),
                 reads=[o, R, gcol], writes=[T])
            s.op("dve", lambda e: e.tensor_tensor(out=hT[:, chunk, st:st + n], in0=T[:, 0:n], in1=gate[:, st:st + n],
                                                  op=ALU.mult), reads=[T, gate], acc=[hTb[j]], selfdep=False)

    def recur(b, l, kind):
        m0 = s.mark()
        NV = 1 if kind == "h" else 2
        L = 16 if kind == "h" else 64
        NCH = NT // L
        CTXN = NCTX // L
        HB = L
        SR = 2 * L
        PG = min(128 // SR, 3)
        last = (l == nlayers - 1)
        blocks = [1, 2, 3, 4] if last else [0, 1, 2, 3, 4]
        PTR = [s.psum("ptr%d" % d, [128, 1024], BF16) for d in range(2)]
        PSC = [s.psum("psc%d" % d, [128, 512], F32) for d in range(2)]
        PSO = [s.psum("pso%d" % d, [128, 512], F32) for d in range(2)]
        PST = [s.psum("pstt%d" % d, [128, 512], F32) for d in range(2)]
        pn = PSC[0]
        lf = s.sbuf("r_lf", [128, NT], F32)
        PP = s.sbuf("r_PP", [128, NT + 1], F32)
        arg = s.sbuf("r_arg", [128, NT], F32)
        Dm = s.sbuf("r_D", [128, NT], F32)
        kk = s.sbuf("r_kk", [128, NT], F32)
        qs = s.sbuf("r_qs", [128, NT], F32)
        igb = s.sbuf("r_ig", [128, NT], F32) if kind == "m" else None
        Q = [s.sbuf("r_Q%d" % d, [128, NT], BF16) for d in range(2)]
        K = [s.sbuf("r_K%d" % d, [128, NCH, 2 * L], BF16) for d in range(2)]
        KZ = []
        KN = [s.sbuf("r_KN%d" % d, [128, NT], BF16) for d in range(2)]
        Vbd = s.sbuf("r_Vbd", [128, NCH // PG, 128], BF16)
        Vp = s.sbuf("r_Vp", [128, NCH // PG, 128], BF16)
        Rn = [s.sbuf("r_Rn%d" % d, [128, NCH], F32) for d in range(2)]
        btm = [s.sbuf("r_bt%d" % i, [128, 512], F32) for i in range(10)]
        bti = [0]

        def ntmp():
            bti[0] += 1
            return btm[bti[0] % 10]

        def blk_of(c):
            t = c * L
            return 0 if t < 256 else 1 + (t - 256) // 512
        G = [s.sbuf("r_G%d" % d, [128, NCH], F32) for d in range(2)]
        W32 = [[s.sbuf("r_W%d%d" % (d, v), [128, 64], F32) for v in range(NV)] for d in range(2)]
        Wbf = [[s.sbuf("r_Wb%d%d" % (d, v), [128, 64], BF16) for v in range(NV)] for d in range(2)]
        kts = [s.sbuf("r_kt%d" % i, [128, 128], BF16) for i in range(4)]
        pms = [s.sbuf("r_pm%d" % i, [128, L], BF16) for i in range(4)]
        for pmb in pms:
            s.op("pool", lambda e: e.memset(pmb[:, :], 0.0), writes=[pmb])
        sqs = [s.sbuf("r_sq%d" % i, [128, 512], BF16) for i in range(2)]
        rss = [s.sbuf("r_rs%d" % i, [128, 512], F32) for i in range(2)]
        tms = [s.sbuf("r_tm%d" % i, [128, 512], F32) for i in range(2)]
        Qv = [Sch.views(Q[d], 5) for d in range(2)]
        Kv = [Sch.views(K[d], 5) for d in range(2)]
        KNv = [Sch.views(KN[d], 5) for d in range(2)]
        for d in range(2):
            s.op("pool", lambda e: e.memset(K[d][:, :, :], 0.0), writes=Kv[d])
        if kind == "h":
            accs = [[lf], [arg]]
        else:
            accs = [[lf, arg], [Dm, kk]]
        zq = zq_h if kind == "h" else zq_m
        zog = zog_h if kind == "h" else zog_m
        vbase = 0 if kind == "h" else 256
        gidx = 0 if kind == "h" else 1
        order = [list(range(NCH)), list(range(CTXN - 1, -1, -1)) + list(range(NCH - 1, CTXN - 1, -1))]
        slot = 0
        for pc in range(2):
            rows = slice(pc * 128, pc * 128 + 128)
            vcol = vbase + pc * 128
            s.op("pool", lambda e: e.memset(Vbd[:, :, :], 0.0), writes=[Vbd])
            for g in range(PG):
                for hh in range(2):
                    s.dma("sp", Vbd[g * SR + HB * hh:g * SR + HB * hh + L, :, 64 * hh:64 * hh + 64],
                          v_tok[:, vcol + 64 * hh:vcol + 64 * hh + 64].rearrange("(cq g p) d -> g p cq d", g=PG, p=L)[g],
                          reads=[v_tok], acc=[Vbd])
                s.dma("sp", Vp[g * SR:g * SR + L, :, :],
                      v_tok[:, vcol:vcol + 128].rearrange("(cq g p) d -> g p cq d", g=PG, p=L)[g],
                      reads=[v_tok], writes=[Vp] if g == 0 else [], acc=[] if g == 0 else [Vp])
            s.dma("sp", qs[:, :], zq[rows, :], reads=[zq], writes=[qs])
            for d in range(2):
                sg = 1.0 if d == 0 else -1.0
                if kind == "h":
                    s.dma("sp", lf[:, :], zlf_h[d, rows, :], reads=[zlf_h], writes=[lf])
                    s.dma("sp", kk[:, :], zkk_h[d, rows, :], reads=[zkk_h], writes=[kk])
                else:
                    for hh in range(2):
                        h = 2 * pc + hh
                        s.dma("sp", lf[64 * hh:64 * hh + 64, :], zg_m[8 + 4 * d + h:9 + 4 * d + h, :].partition_broadcast(64),
                              reads=[zg_m], writes=[lf] if hh == 0 else [], acc=[] if hh == 0 else [lf])
                        s.dma("sp", igb[64 * hh:64 * hh + 64, :], zg_m[4 * d + h:4 * d + h + 1, :].partition_broadcast(64),
                              reads=[zg_m], writes=[igb] if hh == 0 else [], acc=[] if hh == 0 else [igb])
                    s.dma("sp", kk[:, :], zk_m[rows, :], reads=[zk_m], writes=[kk])
                s.op("pool", lambda e: e.memset(PP[:, 0:1], 0.0), writes=[PP])
                s.op("dve", lambda e: e.tensor_tensor_scan(out=PP[:, 1:NT + 1], data0=lf[:, :], data1=lf[:, :], initial=0.0,
                                                           op0=ALU.add, op1=ALU.bypass), reads=[lf], acc=[PP])
                Rm = PP[:, 0:NT].rearrange("p (c l) -> p c l", l=L)[:, :, L // 2]
                if d == 0:
                    s.op("dve", lambda e: e.tensor_copy(out=Rn[d][:, 0:NCH - 1], in_=Rm[:, 1:NCH]), reads=[PP], writes=[Rn[d]])
                    s.op("dve", lambda e: e.tensor_copy(out=Rn[d][:, NCH - 1:NCH], in_=Rm[:, NCH - 1:NCH]), reads=[PP], acc=[Rn[d]])
                    s.op("dve", lambda e: e.tensor_tensor(out=G[d][:, :], in0=Rn[d][:, :], in1=Rm, op=ALU.subtract),
                         reads=[Rn[d], PP], writes=[G[d]])
                else:
                    s.op("dve", lambda e: e.tensor_copy(out=Rn[d][:, 1:NCH], in_=Rm[:, 0:NCH - 1]), reads=[PP], writes=[Rn[d]])
                    s.op("dve", lambda e: e.tensor_copy(out=Rn[d][:, CTXN:CTXN + 1], in_=Rm[:, CTXN:CTXN + 1]), reads=[PP], acc=[Rn[d]])
                    s.op("dve", lambda e: e.tensor_tensor(out=Rn[d][:, 0:1], in0=Rm[:, NCH - 1:NCH], in1=PP[:, NT:NT + 1],
                                                          op=ALU.subtract), reads=[PP], acc=[Rn[d]])
                    s.op("dve", lambda e: e.tensor_tensor(out=G[d][:, :], in0=Rm, in1=Rn[d][:, :], op=ALU.subtract),
                         reads=[Rn[d], PP], writes=[G[d]])
                s.op("act", lambda e: e.activation(out=G[d][:, :], in_=G[d][:, :], func=AF.Exp), reads=[G[d]], writes=[G[d]])
                border = [0, 1, 2, 3, 4] if d == 0 else [0, 4, 3, 2, 1]
                for j in border:
                    st, n = BLOCKS[j]
                    c0, ncb = st // L, n // L
                    off = 1 if d == 0 else 0
                    PPs = PP[:, st + off:st + off + n].rearrange("p (c l) -> p c l", l=L)
                    T1, T2, T3, T4, T5 = [ntmp() for _ in range(5)]
                    v3 = lambda T: T[:, 0:n].rearrange("p (c l) -> p c l", l=L)
                    s.op("dve", lambda e: e.tensor_tensor(out=v3(T1), in0=PPs,
                                                          in1=Rm[:, c0:c0 + ncb].unsqueeze(2).to_broadcast([128, ncb, L]),
                                                          op=ALU.subtract), reads=[PP], writes=[T1])
                    s.op("act", lambda e: e.activation(out=T2[:, 0:n], in_=T1[:, 0:n], func=AF.Exp, scale=sg), reads=[T1], writes=[T2])
                    s.op("dve", lambda e: e.scalar_tensor_tensor(out=Q[d][:, st:st + n], in0=qs[:, st:st + n],
                                                                 scalar=(0.125 if kind == "h" else 1.0), in1=T2[:, 0:n],
                                                                 op0=ALU.mult, op1=ALU.mult),
                         reads=[qs, T2], acc=[Qv[d][j]], selfdep=False)
                    if kind == "m":
                        s.op("dve", lambda e: e.scalar_tensor_tensor(out=T3[:, 0:n], in0=T1[:, 0:n], scalar=-sg, in1=igb[:, st:st + n],
                                                                     op0=ALU.mult, op1=ALU.add), reads=[T1, igb], writes=[T3])
                        s.op("act", lambda e: e.activation(out=T3[:, 0:n], in_=T3[:, 0:n], func=AF.Exp), reads=[T3], writes=[T3])
                    else:
                        s.op("act", lambda e: e.activation(out=T3[:, 0:n], in_=T1[:, 0:n], func=AF.Exp, scale=-sg), reads=[T1], writes=[T3])
                    s.op("dve", lambda e: e.tensor_tensor(out=T4[:, 0:n], in0=kk[:, st:st + n], in1=T3[:, 0:n], op=ALU.mult),
                         reads=[kk, T3], writes=[T4])
                    for hh in range(2):
                        hs = slice(64 * hh, 64 * hh + 64)
                        s.op("act", lambda e: e.activation(out=K[d][hs, c0:c0 + ncb, L * hh:L * hh + L],
                                                           in_=T4[hs, 0:n].rearrange("p (c l) -> p c l", l=L), func=AF.Copy),
                             reads=[T4], acc=[Kv[d][j]], selfdep=False)
                    s.op("pool", lambda e: e.tensor_tensor(out=KN[d][:, st:st + n].rearrange("p (c l) -> p c l", l=L), in0=v3(T4),
                                                           in1=G[d][:, c0:c0 + ncb].unsqueeze(2).to_broadcast([128, ncb, L]),
                                                           op=ALU.mult), reads=[T4, G[d]], acc=[KNv[d][j]], selfdep=False)
                for v in range(NV):
                    s.op("pool", lambda e: e.memset(W32[d][v][:, :], 0.0), writes=[W32[d][v]])
            if dbg and stop_after == "hbuild":
                for d in range(2):
                    pass
                s.dma("sp", dbgP[:, :], PP[:, :], reads=[PP], writes=[dbgPb])
                s.release(m0)
                return
            seq = [(step, d) for step in range(NCH) for d in range(2)]
            slot0 = slot

            def phaseA(i):
                step, d = seq[i]
                c = order[d][step]
                csl = slice(c * L, c * L + L)
                sl = (slot0 + i) % 4
                lastst = (step == NCH - 1)
                kt = kts[sl]
                pm = pms[sl]
                ptr, psc = PTR[d], PSC[d]
                pb = (c % PG) * SR
                if not lastst:
                    s.op("pe", lambda e: e.transpose(out=ptr[pb:pb + L, 0:128], in_=KN[d][:, csl], identity=identb[:, :]),
                         reads=[KNv[d][blk_of(c)], identb], writes=[ptr])
                    s.op("act", lambda e: e.activation(out=kt[pb:pb + L, :], in_=ptr[pb:pb + L, 0:128], func=AF.Copy),
                         reads=[ptr], writes=[kt])
                s.mm(psc[pb:pb + SR, 0:L], K[d][:, c, :], Q[d][:, csl], True, True,
                     reads=[Kv[d][blk_of(c)], Qv[d][blk_of(c)]], writes=[psc])
                s.op("dve", lambda e: e.tensor_tensor(out=pm[pb:pb + SR, :], in0=psc[pb:pb + SR, 0:L],
                                                      in1=maskFB[L][d][pb:pb + SR, :], op=ALU.mult),
                     reads=[psc, cs], writes=[pm])

            def phaseB(i):
                step, d = seq[i]
                c = order[d][step]
                csl = slice(c * L, c * L + L)
                sl = (slot0 + i) % 4
                first = (step == 0)
                lastst = (step == NCH - 1)
                kt = kts[sl]
                pm = pms[sl]
                pso, pst = PSO[d], PST[d]
                pb = (c % PG) * SR
                cq = c // PG
                for v in range(NV):
                    Vb = Vbd[pb:pb + SR, cq, :] if v == 0 else bd1[:, :]
                    vo = slice(v * 64, v * 64 + L)
                    s.mm(pso[:, vo], Vb, pm[pb:pb + SR, :], True, first, reads=[Vbd, bd1, pm],
                         writes=[pso] if v == 0 else [], acc=[] if v == 0 else [pso])
                    if not first:
                        for hh in range(2):
                            hs = slice(64 * hh, 64 * hh + 64)
                            s.mm(pso[hs, vo], Wbf[d][v][hs, :], Q[d][hs, csl], False, hh == 1,
                                 reads=[Wbf[d][v], Qv[d][blk_of(c)]], acc=[pso])
                for v in range(NV):
                    vo = slice(v * 64, v * 64 + L)
                    A_ = accs[d][v]
                    s.op("act", lambda e: e.activation(out=A_[:, csl], in_=pso[:, vo], func=AF.Copy),
                         reads=[pso], acc=[A_], selfdep=False)
                if not lastst:
                    for v in range(NV):
                        vt = slice(v * 64, v * 64 + 64)
                        for hh in range(2):
                            hs = slice(64 * hh, 64 * hh + 64)
                            rhs = Vp[pb:pb + L, cq, hs] if v == 0 else onesb[pb:pb + L, 0:64]
                            wr = (v == 0 and hh == 0)
                            s.mm(pst[hs, vt], kt[pb:pb + L, hs], rhs, True, True, reads=[kt, Vp, onesb],
                                 writes=[pst] if wr else [], acc=[] if wr else [pst])
                    for v in range(NV):
                        vt = slice(v * 64, v * 64 + 64)
                        s.op("dve", lambda e: e.scalar_tensor_tensor(out=Wbf[d][v][:, :], in0=W32[d][v][:, :],
                                                                     scalar=G[d][:, c:c + 1], in1=pst[:, vt],
                                                                     op0=ALU.mult, op1=ALU.add),
                             reads=[W32[d][v], G[d], pst], writes=[Wbf[d][v]])
                    for v in range(NV):
                        vt = slice(v * 64, v * 64 + 64)
                        s.op("dve", lambda e: e.scalar_tensor_tensor(out=W32[d][v][:, :], in0=W32[d][v][:, :],
                                                                     scalar=G[d][:, c:c + 1], in1=pst[:, vt],
                                                                     op0=ALU.mult, op1=ALU.add),
                             reads=[W32[d][v], G[d], pst], writes=[W32[d][v]])

            phaseA(0)
            for i in range(len(seq)):
                if i + 1 < len(seq):
                    phaseA(i + 1)
                phaseB(i)
            slot += len(seq)
            s.dma("sp", qs[:, :], zog[rows, :], reads=[zog] + Qv[0] + Qv[1], writes=[qs])
            chunk = (0 if kind == "h" else 2) + pc
            for j in blocks:
                st, n = BLOCKS[j]
                O = ntmp()
                if kind == "m":
                    hd = []
                    for d in range(2):
                        num, den = accs[d]
                        Ta = ntmp()
                        Th = ntmp()
                        s.op("act", lambda e: e.activation(out=Ta[:, 0:n], in_=den[:, st:st + n], func=AF.Abs), reads=[den], writes=[Ta])
                        s.op("dve", lambda e: e.tensor_scalar_max(out=Ta[:, 0:n], in0=Ta[:, 0:n], scalar1=1.0), reads=[Ta], writes=[Ta])
                        s.op("dve", lambda e: e.reciprocal(out=Ta[:, 0:n], in_=Ta[:, 0:n]), reads=[Ta], writes=[Ta])
                        s.op("dve", lambda e: e.tensor_tensor(out=Th[:, 0:n], in0=num[:, st:st + n], in1=Ta[:, 0:n], op=ALU.mult),
                             reads=[num, Ta], writes=[Th])
                        hd.append(Th)
                    s.op("pool", lambda e: e.tensor_tensor(out=O[:, 0:n], in0=hd[0][:, 0:n], in1=hd[1][:, 0:n], op=ALU.add),
                         reads=hd, writes=[O])
                else:
                    s.op("pool", lambda e: e.tensor_tensor(out=O[:, 0:n], in0=accs[0][0][:, st:st + n], in1=accs[1][0][:, st:st + n],
                                                           op=ALU.add), reads=[accs[0][0], accs[1][0]], writes=[O])
                SQ = sqs[j % 2]
                R = ntmp()
                T = ntmp()
                s.op("act", lambda e: e.activation(out=SQ[:, 0:n], in_=O[:, 0:n], func=AF.Square), reads=[O], writes=[SQ])
                s.mm(pn[:, 0:n], blkb[:, :], SQ[:, 0:n], True, True, reads=[blkb, SQ], writes=[pn])
                s.op("act", lambda e: e.activation(out=R[:, 0:n], in_=pn[:, 0:n], func=AF.Sqrt, scale=1.0 / 64, bias=EPS),
                     reads=[pn], writes=[R])
                s.op("dve", lambda e: e.reciprocal(out=R[:, 0:n], in_=R[:, 0:n]), reads=[R], writes=[R])
                s.op("dve", lambda e: e.scalar_tensor_tensor(out=T[:, 0:n], in0=O[:, 0:n], scalar=gcol[:, l, gidx:gidx + 1],
                                                             in1=R[:, 0:n], op0=ALU.mult, op1=ALU.mult),
                     reads=[O, R, gcol], writes=[T])
                s.op("dve", lambda e: e.tensor_tensor(out=hT[:, chunk, st:st + n], in0=T[:, 0:n], in1=qs[:, st:st + n],
                                                      op=ALU.mult), reads=[T, qs], acc=[hTb[j]], selfdep=False)
        s.release(m0)

    def attn_block(qb, kb, vfn, vbuf, st, n, kcs, chunk, sc_ps, num_ps, den_ps, pTs, rd, cnt):
        j = [i for i, (a, _) in enumerate(BLOCKS) if a <= st < a + BLOCKS[i][1]][0]
        nk = len(kcs)
        its = [(ki, kc, hh) for ki, kc in enumerate(kcs) for hh in range(2)]
        scs = {}

        def issue_sc(i):
            ki, kc, hh = its[i]
            hs = slice(64 * hh, 64 * hh + 64)
            sc = sc_ps[cnt[0] % len(sc_ps)]
            pT = pTs[cnt[0] % len(pTs)]
            cnt[0] += 1
            s.mm(sc[:, 0:n], kb[hs, kc * 128:(kc + 1) * 128], qb[hs, st:st + n], True, True, reads=[kb, qb], writes=[sc])
            scs[i] = (sc, pT)

        issue_sc(0)
        if len(its) > 1:
            issue_sc(1)
        for i, (ki, kc, hh) in enumerate(its):
            hs = slice(64 * hh, 64 * hh + 64)
            if i + 2 < len(its):
                issue_sc(i + 2)
            sc, pT = scs.pop(i)
            s.op("act", lambda e: e.activation(out=pT[:, 0:n], in_=sc[:, 0:n], func=AF.Exp, scale=0.125),
                 reads=[sc], writes=[pT])
            first = (ki == 0)
            s.mm(num_ps[hs, 0:n], vfn(kc, hh), pT[:, 0:n], first, ki == nk - 1, reads=[pT, vbuf],
                 writes=[num_ps] if (first and hh == 0) else [], acc=[] if (first and hh == 0) else [num_ps])
            s.mm(den_ps[hs, 0:n], onesb[:, 0:64], pT[:, 0:n], first, ki == nk - 1, reads=[pT, onesb],
                 writes=[den_ps] if (first and hh == 0) else [], acc=[] if (first and hh == 0) else [den_ps])
        s.op("dve", lambda e: e.reciprocal(out=rd[:, 0:n], in_=den_ps[:, 0:n]), reads=[den_ps], writes=[rd])
        s.op("dve", lambda e: e.tensor_tensor(out=hT[:, chunk, st:st + n], in0=num_ps[:, 0:n], in1=rd[:, 0:n], op=ALU.mult),
             reads=[num_ps, rd], acc=[hTb[j]], selfdep=False)

    def gqa(b, l):
        m0 = s.mark()
        last = (l == nlayers - 1)
        rp = s.sbuf("g_rope", [128, 4096], F32)
        s.dma("sp", rp[:, :], ropec[:, :], writes=[rp])
        raw = [s.sbuf("g_raw%d" % i, [128, NT], F32) for i in range(2)]
        QK = [s.sbuf("g_qk%d" % i, [128, NT], BF16) for i in range(4)]
        sqs = [s.sbuf("g_sq%d" % i, [128, 512], BF16) for i in range(2)]
        rss = [s.sbuf("g_rs%d" % i, [128, 512], F32) for i in range(2)]
        t1s = [s.sbuf("g_t1%d" % i, [128, 512], F32) for i in range(2)]
        t2s = [s.sbuf("g_t2%d" % i, [128, 512], F32) for i in range(2)]
        t3s = [s.sbuf("g_t3%d" % i, [128, 512], F32) for i in range(2)]
        Vg = s.sbuf("g_V", [128, 18, 128], BF16)
        pTs = [s.sbuf("g_pT%d" % i, [128, 512], BF16) for i in range(4)]
        rds = [s.sbuf("g_rd%d" % i, [128, 512], F32) for i in range(2)]
        sc_ps = [s.psum("g_sc%d" % i, [128, 512], F32) for i in range(4)]
        num_ps = [s.psum("g_num%d" % i, [128, 512], F32) for i in range(2)]
        den_ps = [s.psum("g_den%d" % i, [128, 512], F32) for i in range(2)]
        s.dma("sp", Vg[:, :, :], v_tok[:, 512:640].rearrange("(c p) d -> p c d", p=128), reads=[v_tok], writes=[Vg])
        srcs = [(zq_g[0:128, :], 2), (zq_g[128:256, :], 2), (zk_g[0, :, :], 3), (zk_g[1, :, :], 3)]
        it = 0
        for idx, (src, gi) in enumerate(srcs):
            R_ = raw[idx % 2]
            s.dma("sp", R_[:, :], src, reads=[zq_g, zk_g], writes=[R_])
            for j, (st, n) in enumerate(BLOCKS):
                SQ = sqs[it % 2]
                RS = rss[it % 2]
                T1 = t1s[it % 2]
                T2 = t2s[it % 2]
                T3 = t3s[it % 2]
                P1 = sc_ps[(2 * it) % 4]
                P2 = sc_ps[(2 * it + 1) % 4]
                it += 1
                s.op("act", lambda e: e.activation(out=SQ[:, 0:n], in_=R_[:, st:st + n], func=AF.Square), reads=[R_], writes=[SQ])
                s.mm(P1[:, 0:n], blkb[:, :], SQ[:, 0:n], True, True, reads=[blkb, SQ], writes=[P1])
                s.op("act", lambda e: e.activation(out=RS[:, 0:n], in_=P1[:, 0:n], func=AF.Sqrt, scale=1.0 / 64, bias=EPS),
                     reads=[P1], writes=[RS])
                s.op("dve", lambda e: e.reciprocal(out=RS[:, 0:n], in_=RS[:, 0:n]), reads=[RS], writes=[RS])
                s.op("dve", lambda e: e.scalar_tensor_tensor(out=T1[:, 0:n], in0=R_[:, st:st + n], scalar=gcol[:, l, gi:gi + 1],
                                                             in1=RS[:, 0:n], op0=ALU.mult, op1=ALU.mult),
                     reads=[R_, RS, gcol], writes=[T1])
                if st >= 256:
                    s.mm(P2[:, 0:n], ropeRT, T1[:, 0:n], True, True, reads=[cs, T1], writes=[P2])
                    s.op("pool", lambda e: e.tensor_tensor(out=T2[:, 0:n], in0=T1[:, 0:n], in1=rp[:, st - 256:st - 256 + n],
                                                           op=ALU.mult), reads=[T1, rp], writes=[T2])
                    s.op("dve", lambda e: e.tensor_tensor(out=T3[:, 0:n], in0=P2[:, 0:n],
                                                          in1=rp[:, 2048 + st - 256:2048 + st - 256 + n], op=ALU.mult),
                         reads=[P2, rp], writes=[T3])
                    s.op("pool", lambda e: e.tensor_tensor(out=QK[idx][:, st:st + n], in0=T2[:, 0:n], in1=T3[:, 0:n], op=ALU.add),
                         reads=[T2, T3], acc=[QK[idx]], selfdep=False)
                else:
                    s.op("act", lambda e: e.activation(out=QK[idx][:, st:st + n], in_=T1[:, 0:n], func=AF.Copy),
                         reads=[T1], acc=[QK[idx]], selfdep=False)
        cnt = [0]
        qblocks = [1, 2, 3, 4] if last else [0, 1, 2, 3, 4]
        bi = 0
        for pc in range(2):
            for j in qblocks:
                st, n = BLOCKS[j]
                kcs = list(range(18)) if st >= 256 else [0, 1]
                attn_block(QK[pc], QK[2 + pc], lambda kc, hh: Vg[:, kc, 64 * pc:64 * pc + 64], Vg, st, n, kcs, 4 + pc,
                           sc_ps, num_ps[bi % 2], den_ps[bi % 2], pTs, rds[bi % 2], cnt)
                bi += 1
        s.release(m0)

    def na(b, l):
        m0 = s.mark()
        last = (l == nlayers - 1)
        bias8 = s.sbuf("n_bias", [128, 4 * NU, 128], BF16)
        bst = [s.sbuf("n_bst%d" % i, [128, NU, 128], F32) for i in range(2)]
        qT = [s.sbuf("n_q%d" % i, [128, NT], BF16) for i in range(2)]
        kT = [s.sbuf("n_k%d" % i, [128, NT], BF16) for i in range(2)]
        Vn = s.sbuf("n_V", [128, 18, 256], BF16)
        pTs = [s.sbuf("n_pT%d" % i, [128, 1024], BF16) for i in range(3)]
        pTd = [s.sbuf("n_pTd%d" % i, [128, 512], BF16) for i in range(2)]
        rds = [s.sbuf("n_rd%d" % i, [128, 512], F32) for i in range(2)]
        sc_ps = [s.psum("n_sc%d" % i, [128, 512], F32) for i in range(4)]
        num_ps = [s.psum("n_num%d" % i, [128, 512], F32) for i in range(2)]
        den_ps = [s.psum("n_den%d" % i, [128, 512], F32) for i in range(2)]
        for h in range(4):
            B_ = bst[h % 2]
            s.dma("sp", B_[:, :, :], nab[l, h * NU:(h + 1) * NU, :, :].rearrange("u p q -> p u q"), writes=[B_])
            s.op("pool", lambda e: e.tensor_scalar(out=bias8[:, h * NU:(h + 1) * NU, :], in0=B_[:, :, :], scalar1=8.0,
                                                   scalar2=None, op0=ALU.mult), reads=[B_], acc=[bias8])
        for pc in range(2):
            rows = slice(pc * 128, pc * 128 + 128)
            s.dma("sp", qT[pc][:, :], zq_n[rows, :], reads=[zq_n], writes=[qT[pc]])
            s.dma("sp", kT[pc][:, :], zk_n[rows, :], reads=[zk_n], writes=[kT[pc]])
        s.dma("sp", Vn[:, :, :], v_tok[:, 640:896].rearrange("(c p) d -> p c d", p=128), reads=[v_tok], writes=[Vn])
        it = 0
        for pc in range(2):
            units = [(t, hh) for t in range(16) for hh in range(2)]
            info = {}

            def na_S(ui):
                t, hh = units[ui]
                q0 = 256 + 128 * t
                loc = sorted([j for (tt, j) in _NA_TMAP if tt == t])
                allk = [(0, None), (1, None)] + [(2 + j, _NA_TMAP[(t, j)]) for j in loc]
                h = 2 * pc + hh
                hs = slice(64 * hh, 64 * hh + 64)
                g = it + ui
                banks = [sc_ps[(2 * g) % 4], sc_ps[(2 * g + 1) % 4]]
                for i, (gc, u) in enumerate(allk):
                    bk = banks[i // 4]
                    cc = slice((i % 4) * 128, (i % 4) * 128 + 128)
                    firstw = (i % 4 == 0)
                    s.mm(bk[:, cc], kT[pc][hs, gc * 128:(gc + 1) * 128], qT[pc][hs, q0:q0 + 128], True, u is None,
                         reads=[kT[pc], qT[pc]], writes=[bk] if firstw else [], acc=[] if firstw else [bk])
                    if u is not None:
                        s.mm(bk[:, cc], identb[:, :], bias8[:, h * NU + u, :], False, True, reads=[identb, bias8], acc=[bk])
                info[ui] = (allk, banks, pTs[g % 3])

            def na_EV(ui):
                t, hh = units[ui]
                q0 = 256 + 128 * t
                jb = 1 + t // 4
                h = 2 * pc + hh
                hs = slice(64 * hh, 64 * hh + 64)
                allk, banks, pT = info.pop(ui)
                nk = len(allk)
                NUM = num_ps[t % 2]
                DEN = den_ps[t % 2]
                RD = rds[t % 2]
                n0 = min(nk, 4) * 128
                s.op("act", lambda e: e.activation(out=pT[:, 0:n0], in_=banks[0][:, 0:n0], func=AF.Exp, scale=0.125),
                     reads=[banks[0]], writes=[pT])
                if nk > 4:
                    n1 = (nk - 4) * 128
                    s.op("act", lambda e: e.activation(out=pT[:, 512:512 + n1], in_=banks[1][:, 0:n1], func=AF.Exp, scale=0.125),
                         reads=[banks[1]], acc=[pT])
                for i, (gc, u) in enumerate(allk):
                    wr = (i == 0 and hh == 0)
                    s.mm(NUM[hs, 0:128], Vn[:, gc, h * 64:(h + 1) * 64], pT[:, i * 128:(i + 1) * 128], i == 0, i == nk - 1,
                         reads=[Vn, pT], writes=[NUM] if wr else [], acc=[] if wr else [NUM])
                    s.mm(DEN[hs, 0:128], onesb[:, 0:64], pT[:, i * 128:(i + 1) * 128], i == 0, i == nk - 1,
                         reads=[onesb, pT], writes=[DEN] if wr else [], acc=[] if wr else [DEN])
                if hh == 1:
                    s.op("dve", lambda e: e.reciprocal(out=RD[:, 0:128], in_=DEN[:, 0:128]), reads=[DEN], writes=[RD])
                    s.op("dve", lambda e: e.tensor_tensor(out=hT[:, 6 + pc, q0:q0 + 128], in0=NUM[:, 0:128], in1=RD[:, 0:128],
                                                          op=ALU.mult), reads=[NUM, RD], acc=[hTb[jb]], selfdep=False)

            na_S(0)
            for ui in range(len(units)):
                if ui + 1 < len(units):
                    na_S(ui + 1)
                na_EV(ui)
            it += len(units)
            if not last:
                cnt = [0]
                attn_block(qT[pc], kT[pc], lambda kc, hh: Vn[:, kc, (2 * pc + hh) * 64:(2 * pc + hh) * 64 + 64], Vn, 0, 256, [0, 1],
                           6 + pc, sc_ps, num_ps[0], den_ps[0], pTd, rds[0], cnt)
        s.release(m0)

    def p4(b, l):
        last = (l == nlayers - 1)
        halves = [[0, 1, 2], [3, 4]]
        if last:
            halves[0] = [1, 2]
        for blks in halves:
            m0 = s.mark()
            base = BLOCKS[blks[0]][0]
            xh = s.sbuf("xh", [128, 8, 1280], F32)
            xhv = {j: Buf(xh.t, "xh.%d" % j) for j in blks}
            stg = [s.sbuf("p4stg%d" % i, [128, 8, 512], F32) for i in range(2)]
            wb = [s.sbuf("p4wb%d" % i, [128, 8, 512], BF16) for i in range(2)]
            w2b = [s.sbuf("p4w2b%d" % i, [128, 4, 1024], BF16) for i in range(2)]
            ub = [s.sbuf("p4u%d" % i, [128, 4, 512], BF16) for i in range(2)]
            rb = [s.sbuf("p4r%d" % i, [128, 512], F32) for i in range(3)]
            sq = s.sbuf("p4sq", [128, 8, 512], BF16)
            rs = s.sbuf("p4rs", [128, 512], F32)
            tmps = [s.sbuf("p4t%d" % i, [128, 512], F32) for i in range(3)]
            ps = [s.psum("p4ps%d" % i, [128, 512], F32) for i in range(7)]
            pss = s.psum("p4pss", [128, 512], F32)
            pi = [0]

            def nps():
                pi[0] += 1
                return ps[pi[0] % 7]

            for j in blks:
                st, n = BLOCKS[j]
                s.dma("sp", xh[:, :, st - base:st - base + n], xs[:, :, st:st + n], reads=[xs], writes=[xhv[j]])
            for cg in range(2):
                s.dma("sp", stg[cg][:, :, :], w_out[l, :, cg * 512:(cg + 1) * 512].rearrange("(k p) n -> p k n", p=128),
                      writes=[stg[cg]])
                if cg == 0:
                    s.op("act", lambda e: e.activation(out=wb[cg][:, :, :], in_=stg[cg][:, :, :], func=AF.Copy),
                         reads=[stg[cg]], writes=[wb[cg]])
                else:
                    s.op("dve", lambda e: e.tensor_copy(out=wb[cg][:, :, :], in_=stg[cg][:, :, :]), reads=[stg[cg]], writes=[wb[cg]])
            for cg in range(2):
                for j in blks:
                    st, n = BLOCKS[j]
                    col = 2 if st < 256 else b
                    for mi in range(4):
                        m = cg * 4 + mi
                        P = nps()
                        for k in range(8):
                            s.mm(P[:, 0:n], wb[cg][:, k, mi * 128:(mi + 1) * 128], hT[:, k, st:st + n], k == 0, k == 7,
                                 reads=[wb[cg], hTb[j]], writes=[P] if k == 0 else [], acc=[] if k == 0 else [P])
                        xa = xh[:, m, st - base:st - base + n]
                        s.op("dve", lambda e: e.scalar_tensor_tensor(out=xa, in0=P[:, 0:n], scalar=modT[:, l, 16 + m, col:col + 1],
                                                                     in1=xa, op0=ALU.mult, op1=ALU.add),
                             reads=[P, modT, xhv[j]], acc=[xhv[j]], selfdep=False)
            for j in blks:
                st, n = BLOCKS[j]
                col = 2 if st < 256 else b
                norm_block(xhv[j], st - base, n, st, l, 1, col, sq, pss, rs, tmps, j)
            def loadw_dma(g):
                s.dma("sp", stg[0][:, :, :], w1[l, :, g * 512:(g + 1) * 512].rearrange("(k p) n -> p k n", p=128), writes=[stg[0]])
                s.dma("sp", stg[1][:, :, :].rearrange("p k n -> p (k n)").rearrange("p (c n) -> p c n", c=4),
                      w2[l, g * 512:(g + 1) * 512, :].rearrange("(c p) n -> p c n", p=128), writes=[stg[1]])

            def loadw_cast(g):
                s.op("act", lambda e: e.activation(out=wb[g % 2][:, :, :], in_=stg[0][:, :, :], func=AF.Copy),
                     reads=[stg[0]], writes=[wb[g % 2]])
                s.op("dve", lambda e: e.tensor_copy(out=w2b[g % 2][:, :, :],
                                                    in_=stg[1][:, :, :].rearrange("p k n -> p (k n)").rearrange("p (c n) -> p c n", c=4)),
                     reads=[stg[1]], writes=[w2b[g % 2]])

            loadw_dma(0)
            loadw_cast(0)
            items = [(g, bi_, j) for g in range(8) for bi_, j in enumerate(blks)]
            Us = {}

            def mlp_u(i):
                g, bi_, j = items[i]
                if bi_ == 0 and g + 1 < 8:
                    loadw_dma(g + 1)
                W1 = wb[g % 2]
                st, n = BLOCKS[j]
                U = ub[i % 2]
                Us[i] = U
                for hc in range(4):
                    P = nps()
                    for k in range(8):
                        s.mm(P[:, 0:n], W1[:, k, hc * 128:(hc + 1) * 128], hT[:, k, st:st + n], k == 0, k == 7,
                             reads=[W1, hTb[j]], writes=[P] if k == 0 else [], acc=[] if k == 0 else [P])
                    R_ = rb[hc % 3]
                    s.op("act", lambda e: e.activation(out=R_[:, 0:n], in_=P[:, 0:n], func=AF.Relu), reads=[P], writes=[R_])
                    s.op("dve", lambda e: e.tensor_tensor(out=U[:, hc, 0:n], in0=R_[:, 0:n], in1=R_[:, 0:n], op=ALU.mult),
                         reads=[R_], writes=[U] if hc == 0 else [], acc=[] if hc == 0 else [U], selfdep=(hc == 0))

            def mlp_y(i):
                g, bi_, j = items[i]
                W2 = w2b[g % 2]
                st, n = BLOCKS[j]
                col = 2 if st < 256 else b
                U = Us.pop(i)
                for m in range(8):
                    P = nps()
                    for hc in range(4):
                        s.mm(P[:, 0:n], W2[:, hc, m * 128:(m + 1) * 128], U[:, hc, 0:n], hc == 0, hc == 3,
                             reads=[W2, U], writes=[P] if hc == 0 else [], acc=[] if hc == 0 else [P])
                    xa = xh[:, m, st - base:st - base + n]
                    s.op("dve", lambda e: e.scalar_tensor_tensor(out=xa, in0=P[:, 0:n], scalar=modT[:, l, 40 + m, col:col + 1],
                                                                 in1=xa, op0=ALU.mult, op1=ALU.add),
                         reads=[P, modT, xhv[j]], acc=[xhv[j]], selfdep=False)

            mlp_u(0)
            for i in range(len(items)):
                g, bi_, j = items[i]
                if i + 1 < len(items):
                    g2, b2, _ = items[i + 1]
                    if b2 == 0:
                        loadw_cast(g2)
                    mlp_u(i + 1)
                mlp_y(i)
            if not last:
                for j in blks:
                    st, n = BLOCKS[j]
                    s.dma("sp", xs[:, :, st:st + n], xh[:, :, st - base:st - base + n], reads=[xhv[j]], acc=[xs])
            else:
                yb = stg[0]
                youts = [Buf(stg[1].t, "yout%d" % i) for i in range(2)]
                oc = 0
                for j in blks:
                    st, n = BLOCKS[j]
                    s.op("act", lambda e: e.activation(out=sq[:, :, 0:n], in_=xh[:, :, st - base:st - base + n], func=AF.Square),
                         reads=[xhv[j]], writes=[sq])
                    for k in range(8):
                        s.mm(pss[:, 0:n], onesb[:, :], sq[:, k, 0:n], k == 0, k == 7, reads=[sq, onesb],
                             writes=[pss] if k == 0 else [], acc=[] if k == 0 else [pss])
                    s.op("act", lambda e: e.activation(out=rs[:, 0:n], in_=pss[:, 0:n], func=AF.Sqrt, scale=1.0 / 1024, bias=EPS),
                         reads=[pss], writes=[rs])
                    s.op("dve", lambda e: e.reciprocal(out=rs[:, 0:n], in_=rs[:, 0:n]), reads=[rs], writes=[rs])
                    for k in range(8):
                        s.op("dve", lambda e: e.scalar_tensor_tensor(out=yb[:, k, 0:n], in0=xh[:, k, st - base:st - base + n],
                                                                     scalar=gT[:, 4, k:k + 1], in1=rs[:, 0:n],
                                                                     op0=ALU.mult, op1=ALU.mult),
                             reads=[xhv[j], gT, rs], writes=[yb] if k == 0 else [], acc=[] if k == 0 else [yb], selfdep=(k == 0))
                    for ti in range(n // 128):
                        YO = youts[oc % 2]
                        yo_ap = stg[1][:, (oc % 2) * 2:(oc % 2) * 2 + 2, :].rearrange("p a n -> p (a n)")
                        oc += 1
                        for half in range(2):
                            P = nps()
                            for kk in range(4):
                                k = half * 4 + kk
                                s.op("pe", lambda e: e.transpose(out=P[:, kk * 128:(kk + 1) * 128],
                                                                 in_=yb[:, k, ti * 128:(ti + 1) * 128], identity=identf),
                                     reads=[yb, cs], writes=[P] if kk == 0 else [], acc=[] if kk == 0 else [P])
                            s.op("act", lambda e: e.activation(out=yo_ap[:, half * 512:(half + 1) * 512], in_=P[:, :], func=AF.Copy),
                                 reads=[P], writes=[YO] if half == 0 else [], acc=[] if half == 0 else [YO], selfdep=(half == 0))
                        tok = st - 256 + ti * 128
                        s.dma("sp", y[b, tok:tok + 128, :], yo_ap, reads=[YO], acc=[y])
            s.release(m0)

    prologue()
    for b in range(nseq if stop_after != "pro" else 0):
        for l in range(nlayers):
            p1(b, l)
            if stop_after == "p1":
                break
            p2(b, l)
            if stop_after == "p2":
                break
            recur(b, l, "h")
            if stop_after in ("h", "hbuild"):
                break
            recur(b, l, "m")
            if stop_after == "m":
                break
            gqa(b, l)
            if stop_after == "g":
                break
            na(b, l)
            if stop_after == "p3":
                break
            p4(b, l)
        if stop_after is not None:
            break
    if dbg:
        s.dma("sp", cat_dbg[:, :, :], hT[:, :, :], reads=hTb, writes=[cat_dbg])
    s.finish()
    build.stats = (s.nops, s.nwaits, dict(s.cnt))
    return nc


_CACHE = {}


def _host_inputs(inputs, core):
    f = lambda a: np.ascontiguousarray(np.asarray(a, dtype=np.float32))
    b0 = 2 * core
    cst, rope = _CACHE["consts"]
    m = {
        "x": f(inputs["x"][b0:b0 + 2]),
        "ctx": f(inputs["ctx"][b0:b0 + 2]),
        "cvec": f(np.concatenate([inputs["c"][b0:b0 + 2], np.asarray(inputs["c_ctx"])[None, :]], 0)),
        "w_mod": f(inputs["w_mod"]), "b_mod": f(inputs["b_mod"]),
        "norm1_g": f(inputs["norm1_g"]), "norm2_g": f(inputs["norm2_g"]),
        "w_in": f(inputs["w_in"]),
        "hgrn_lb_logits": f(np.asarray(inputs["hgrn_lb_logits"]).reshape(4, 256)),
        "hgrn_norm_g": f(inputs["hgrn_norm_g"]), "mlstm_gate_b": f(inputs["mlstm_gate_b"]),
        "mlstm_norm_g": f(inputs["mlstm_norm_g"]), "gqa_qnorm_g": f(inputs["gqa_qnorm_g"]),
        "gqa_knorm_g": f(inputs["gqa_knorm_g"]),
        "na_bias": _CACHE["na_bias"],
        "w_out": f(inputs["w_out"]), "w_mlp1": f(inputs["w_mlp1"]), "w_mlp2": f(inputs["w_mlp2"]),
        "final_norm_g": f(np.asarray(inputs["final_norm_g"]).reshape(1, 1024)),
        "consts": cst, "rope": rope,
    }
    return m


def _prep(inputs):
    _CACHE["consts"] = _consts()
    idx = _na_gather_index()
    rpb = np.asarray(inputs["na_rpb"], np.float32)
    flat = np.concatenate([rpb.reshape(2, 4, 465), np.full((2, 4, 1), NEG, np.float32)], -1)
    nb = flat[:, :, idx]
    _CACHE["na_bias"] = np.ascontiguousarray(nb.reshape(2, 4 * NU, 128, 128))


def kernel(**inputs):
    _prep(inputs)
    nc = build()
    in_maps = [_host_inputs(inputs, c) for c in range(8)]
    res = run_bass_kernel_spmd(nc, in_maps, core_ids=list(range(8)))
    out = np.concatenate([np.asarray(r["y"], np.float32) for r in res.results], axis=0)
    return out
```

```python
import numpy as np
import ml_dtypes
import concourse.bass as bass
import concourse.mybir as mybir
from concourse.bass_utils import run_bass_kernel_spmd

F32 = mybir.dt.float32
BF16 = mybir.dt.bfloat16
AF = mybir.ActivationFunctionType
ALU = mybir.AluOpType

NT = 2304
NCTX = 256
EPS = 1e-6
NEG = -1e30
BLOCKS = [(0, 256), (256, 512), (768, 512), (1280, 512), (1792, 512)]
NCH = 36


class Buf:
    __slots__ = ("t", "name", "lw", "aw", "rd")

    def __init__(self, t, name):
        self.t = t
        self.name = name
        self.lw = {}
        self.aw = {}
        self.rd = {}

    def __getitem__(self, idx):
        return self.t[idx]


class Sch:
    NDMA = 10

    def __init__(self, nc):
        self.nc = nc
        self.eng = {"pe": nc.tensor, "dve": nc.vector, "act": nc.scalar, "pool": nc.gpsimd, "sp": nc.sync}
        self.sems = {}
        self.cnt = {}
        self.key = {}
        self.nsem = 0
        for e in self.eng:
            self._newsem(e)
        for q in ("sp", "act", "pool"):
            for k in range(self.NDMA):
                key = "d%s%d" % (q, k)
                self.sems[key] = nc.alloc_semaphore("s_" + key)
                self.cnt[key] = 0
        self.seen = {e: {} for e in self.eng}
        self.dma_rr = {"sp": 0, "act": 0, "pool": 0}
        self.nwaits = 0
        self.nops = 0
        self._stack = []

    def _newsem(self, e):
        self.nsem += 1
        key = "%s@%d" % (e, self.nsem)
        self.sems[key] = self.nc.alloc_semaphore("s_%s_%d" % (e, self.nsem))
        self.cnt[key] = 0
        self.key[e] = key

    def sbuf(self, name, shape, dtype):
        self.nsem += 1
        name = "%s_%d" % (name, self.nsem)
        g = self.nc.sbuf_tensor(name, list(shape), dtype)
        t = g.__enter__()
        self._stack.append(g)
        return Buf(t, name)

    def psum(self, name, shape, dtype=F32):
        self.nsem += 1
        name = "%s_%d" % (name, self.nsem)
        g = self.nc.psum_tensor(name, list(shape), dtype)
        t = g.__enter__()
        self._stack.append(g)
        return Buf(t, name)

    def dram(self, name, shape, dtype, kind="Internal"):
        t = self.nc.dram_tensor(name, list(shape), dtype, kind=kind)
        return Buf(t, name)

    @staticmethod
    def views(buf, n):
        return [Buf(buf.t, "%s.%d" % (buf.name, i)) for i in range(n)]

    def mark(self):
        return len(self._stack)

    def release(self, mark):
        self.barrier()
        while len(self._stack) > mark:
            g = self._stack.pop()
            g.__exit__(None, None, None)

    def _need(self, e, toks, selfdep=True, wtoks=()):
        best = {}
        for k, v in toks:
            if k.startswith("pe@") and e == "pe":
                continue
            if best.get(k, 0) < v:
                best[k] = v
        for k, v in wtoks:
            if k.startswith("pe@") and e == "pe":
                continue
            if (not selfdep) and k == self.key[e]:
                continue
            if best.get(k, 0) < v:
                best[k] = v
        for k, v in best.items():
            if self.seen[e].get(k, 0) >= v:
                continue
            self.eng[e].wait_ge(self.sems[k], v)
            self.seen[e][k] = v
            self.nwaits += 1

    @staticmethod
    def _deps(reads, writes, acc):
        rt, wt = [], []
        for b in reads:
            rt.extend(b.lw.items())
            rt.extend(b.aw.items())
        for b in writes:
            wt.extend(b.lw.items())
            wt.extend(b.aw.items())
            wt.extend(b.rd.items())
        for b in acc:
            wt.extend(b.lw.items())
            wt.extend(b.rd.items())
        return rt, wt

    @staticmethod
    def _commit(tok, reads, writes, acc):
        k, v = tok
        for b in reads:
            if b.rd.get(k, 0) < v:
                b.rd[k] = v
        for b in writes:
            b.lw = {k: v}
            b.aw = {}
            b.rd = {}
        for b in acc:
            if b.aw.get(k, 0) < v:
                b.aw[k] = v

    def op(self, e, fn, reads=(), writes=(), acc=(), selfdep=True):
        rt, wt = self._deps(reads, writes, acc)
        self._need(e, rt, selfdep, wt)
        ins = fn(self.eng[e])
        self.nops += 1
        key = self.key[e]
        self.cnt[key] += 1
        ins.then_inc(self.sems[key], 1)
        self._commit((key, self.cnt[key]), reads, writes, acc)
        return ins

    def mm(self, out_ap, lhsT, rhs, start, stop, reads=(), writes=(), acc=()):
        return self.op("pe", lambda e: e.matmul(out_ap, lhsT, rhs, start=start, stop=stop,
                                                skip_group_check=True), reads, writes, acc)

    def dma(self, q, out_ap, in_ap, reads=(), writes=(), acc=(), **kw):
        key = "d%s%d" % (q, self.dma_rr[q])
        self.dma_rr[q] = (self.dma_rr[q] + 1) % self.NDMA
        rt, wt = self._deps(reads, writes, acc)
        toks = rt + wt
        if self.cnt[key] > 0:
            toks.append((key, self.cnt[key]))
        self._need(q, toks)
        self.cnt[key] += 16
        ins = self.eng[q].dma_start(out=out_ap, in_=in_ap, **kw)
        ins.then_inc(self.sems[key], 16)
        self.nops += 1
        self._commit((key, self.cnt[key]), reads, writes, acc)
        return ins

    def barrier(self):
        toks = [(k, v) for k, v in self.cnt.items() if v > 0]
        for e in self.eng:
            self._need(e, toks)
        for e in list(self.eng):
            if self.cnt[self.key[e]] > 24000:
                self._newsem(e)

    def finish(self):
        toks = [(k, v) for k, v in self.cnt.items() if v > 0]
        self._need("sp", toks)


def _na_tiles():
    uniq = {}
    tmap = {}
    for t in range(16):
        lo, hi = 10 ** 9, -1
        for b in range(2):
            r0 = min(max(2 * t + b - 4, 0), 24)
            lo = min(lo, r0 // 2)
            hi = max(hi, (r0 + 7) // 2)
        for j in range(lo, hi + 1):
            pat = []
            for a in range(2):
                for b in range(2):
                    qr = 2 * t + b
                    kr = 2 * j + a
                    r0 = min(max(qr - 4, 0), 24)
                    pat.append((kr - qr) if (r0 <= kr < r0 + 8) else None)
            pat = tuple(pat)
            if pat not in uniq:
                uniq[pat] = len(uniq)
            tmap[(t, j)] = uniq[pat]
    pats = [None] * len(uniq)
    for p, i in uniq.items():
        pats[i] = p
    return pats, tmap


_NA_PATS, _NA_TMAP = _na_tiles()
NU = len(_NA_PATS)


def _na_gather_index():
    idx = np.full((NU, 128, 128), 465, np.int64)
    qc = np.arange(64)
    cstart = np.clip(qc - 8, 0, 48)
    kc = np.arange(64)
    col_in = (kc[:, None] >= cstart[None, :]) & (kc[:, None] < cstart[None, :] + 16)
    cidx = np.clip(kc[:, None] - qc[None, :], -15, 15) + 15
    for u, pat in enumerate(_NA_PATS):
        for a in range(2):
            for b in range(2):
                dr = pat[a * 2 + b]
                if dr is None:
                    continue
                blk = np.where(col_in, (dr + 7) * 31 + cidx, 465)
                idx[u, a * 64:(a + 1) * 64, b * 64:(b + 1) * 64] = blk
    return idx


def _consts():
    c = np.zeros((128, 576), np.float32)
    c[:, 0:128] = np.eye(128, dtype=np.float32)
    blk = np.zeros((128, 128), np.float32)
    blk[0:64, 0:64] = 1.0
    blk[64:128, 64:128] = 1.0
    c[:, 128:256] = blk
    rt = np.zeros((128, 128), np.float32)
    for i in range(64):
        rt[2 * i + 1, 2 * i] = -1.0
        rt[2 * i, 2 * i + 1] = 1.0
    c[:, 256:384] = rt
    sidx = np.arange(64)[:, None]
    tidx = np.arange(64)[None, :]
    mf = (sidx <= tidx).astype(np.float32)
    mb = (sidx >= tidx).astype(np.float32)
    c[:, 384:448] = np.concatenate([mf, mf], 0)
    c[:, 448:512] = np.concatenate([mb, mb], 0)
    p = np.arange(128)[:, None] % 16
    t16 = np.arange(16)[None, :]
    c[:, 512:528] = (p <= t16).astype(np.float32)
    c[:, 528:544] = (p >= t16).astype(np.float32)
    t = np.arange(2048)
    row = (t // 64).astype(np.float32)
    col = (t % 64).astype(np.float32)
    inv = np.power(np.float32(10000.0), (-2.0 * np.arange(16, dtype=np.float32) / np.float32(32.0))).astype(np.float32)
    ang = np.concatenate([row[:, None] * inv[None, :], col[:, None] * inv[None, :]], -1).astype(np.float32)
    cos = np.cos(ang).astype(np.float32)
    sin = np.sin(ang).astype(np.float32)
    cosf = np.repeat(cos, 2, axis=1).T
    sinf = np.repeat(sin, 2, axis=1).T
    rope = np.concatenate([np.concatenate([cosf, cosf], 0), np.concatenate([sinf, sinf], 0)], 1).astype(np.float32)
    return c, np.ascontiguousarray(rope)


def build(nlayers=2, nseq=2, stop_after=None, dbg=False):
    nc = bass.Bass("TRN2", target_bir_lowering=False)
    s = Sch(nc)

    def inp(name, shape, dt=F32):
        return s.dram(name, shape, dt, kind="ExternalInput")

    x_in = inp("x", [2, 2048, 1024])
    ctx_in = inp("ctx", [2, 256, 1024])
    cvec = inp("cvec", [3, 1024])
    w_mod = inp("w_mod", [2, 1024, 6144])
    b_mod = inp("b_mod", [2, 6144])
    n1g = inp("norm1_g", [2, 1024])
    n2g = inp("norm2_g", [2, 1024])
    w_in = inp("w_in", [2, 1024, 3600])
    lbl = inp("hgrn_lb_logits", [4, 256])
    hgg = inp("hgrn_norm_g", [2, 64])
    mgb = inp("mlstm_gate_b", [2, 16])
    mgg = inp("mlstm_norm_g", [2, 64])
    qng = inp("gqa_qnorm_g", [2, 64])
    kng = inp("gqa_knorm_g", [2, 64])
    nab = inp("na_bias", [2, 4 * NU, 128, 128])
    w_out = inp("w_out", [2, 1024, 1024])
    w1 = inp("w_mlp1", [2, 1024, 4096])
    w2 = inp("w_mlp2", [2, 4096, 1024])
    fng = inp("final_norm_g", [1, 1024])
    cst = inp("consts", [128, 576])
    ropec = inp("rope", [128, 4096])
    y = s.dram("y", [2, 2048, 1024], F32, kind="ExternalOutput")

    okind = "ExternalOutput" if dbg else "Internal"
    xs = s.dram("xs", [128, 8, NT], F32, kind=okind)
    zq_h = s.dram("zq_h", [256, NT], F32, kind=okind)
    zog_h = s.dram("zog_h", [256, NT], F32, kind=okind)
    zlf_h = s.dram("zlf_h", [2, 256, NT], F32, kind=okind)
    zkk_h = s.dram("zkk_h", [2, 256, NT], F32, kind=okind)
    zq_m = s.dram("zq_m", [256, NT], F32, kind=okind)
    zk_m = s.dram("zk_m", [256, NT], F32, kind=okind)
    zog_m = s.dram("zog_m", [256, NT], F32, kind=okind)
    zg_m = s.dram("zg_m", [16, NT], F32, kind=okind)
    zq_g = s.dram("zq_g", [256, NT], F32, kind=okind)
    zk_g = s.dram("zk_g", [2, 128, NT], F32, kind=okind)
    zq_n = s.dram("zq_n", [256, NT], BF16, kind=okind)
    zk_n = s.dram("zk_n", [256, NT], BF16, kind=okind)
    v_tok = s.dram("v_tok", [NT, 896], BF16, kind=okind)
    cat_dbg = s.dram("cat_dbg", [128, 8, NT], BF16, kind=okind) if dbg else None
    h_dbg = s.dram("h_dbg", [128, 8, NT], BF16, kind=okind) if dbg else None
    mod_dbg = s.dram("mod_dbg", [128, 2 * 48 * 3], F32, kind=okind) if dbg else None

    if dbg:
        dbgQb = s.dram("dbgQ", [2, 3, 128, NT], BF16, kind=okind)
        dbgPb = s.dram("dbgP", [128, NT + 1], F32, kind=okind)
        dbgQ = dbgQb
        dbgP = dbgPb
    NSL = True

    cs = s.sbuf("cs", [128, 576], F32)
    identb = s.sbuf("identb", [128, 128], BF16)
    onesb = s.sbuf("onesb", [128, 128], BF16)
    blkb = s.sbuf("blkb", [128, 128], BF16)
    bd1 = s.sbuf("bd1", [128, 128], BF16)
    hT = s.sbuf("hT", [128, 8, NT], BF16)
    hTb = Sch.views(hT, 5)
    modT = s.sbuf("modT", [128, 2, 48, 3], F32)
    AT = s.sbuf("AT", [128, 2, 2, 8, 3], F32)
    gT = s.sbuf("gT", [128, 5, 8], F32)
    gcol = s.sbuf("gcol", [128, 2, 4], F32)
    lb = s.sbuf("lb", [128, 2, 2, 2], F32)
    oml = s.sbuf("oml", [128, 2, 2, 2], F32)
    noml = s.sbuf("noml", [128, 2, 2, 2], F32)
    mgbT = s.sbuf("mgbT", [16, 2], F32)

    identf = cs[:, 0:128]
    blkf = cs[:, 128:256]
    ropeRT = cs[:, 256:384]
    maskFB = {64: [cs[:, 384:448], cs[:, 448:512]], 16: [cs[:, 512:528], cs[:, 528:544]]}

    s.dma("sp", cs[:, :], cst[:, :], writes=[cs])
    s.op("dve", lambda e: e.tensor_copy(out=identb[:, :], in_=identf), reads=[cs], writes=[identb])
    s.op("dve", lambda e: e.tensor_copy(out=blkb[:, :], in_=blkf), reads=[cs], writes=[blkb])
    s.op("dve", lambda e: e.tensor_copy(out=bd1[:, :], in_=blkf), reads=[cs], writes=[bd1])
    s.op("pool", lambda e: e.memset(onesb[:, :], 1.0), writes=[onesb])

    def prologue():
        m0 = s.mark()
        scT = s.sbuf("scT", [128, 8, 3], F32)
        bmT = s.sbuf("bmT", [128, 2, 48], F32)
        lg = s.sbuf("lg", [128, 4, 2], F32)
        wm = [s.sbuf("wm%d" % i, [128, 8, 512], F32) for i in range(2)]
        pm = s.psum("pm_mod", [128, 512], F32)
        for r in range(3):
            s.dma("sp", scT[:, :, r], cvec[r:r + 1, :].rearrange("o (k p) -> p (o k)", p=128), acc=[scT],
                  allow_slow_non_contiguous=NSL)
        for l in range(2):
            s.dma("sp", bmT[:, l, :], b_mod[l:l + 1, :].rearrange("o (j p) -> p (o j)", p=128), acc=[bmT],
                  allow_slow_non_contiguous=NSL)
        gsrc = [n1g[0:1, :], n1g[1:2, :], n2g[0:1, :], n2g[1:2, :], fng[0:1, :]]
        for i, g in enumerate(gsrc):
            s.dma("sp", gT[:, i, :], g.rearrange("o (k p) -> p (o k)", p=128), acc=[gT], allow_slow_non_contiguous=NSL)
        for r in range(4):
            s.dma("sp", lg[:, r, :], lbl[r:r + 1, :].rearrange("o (c p) -> p (o c)", p=128), acc=[lg],
                  allow_slow_non_contiguous=NSL)
        for l in range(2):
            for i, g in enumerate([hgg, mgg, qng, kng]):
                for hh in range(2):
                    s.dma("sp", gcol[64 * hh:64 * hh + 64, l, i:i + 1], g[l:l + 1, :].rearrange("o d -> d o"),
                          acc=[gcol], allow_slow_non_contiguous=NSL)
            s.dma("sp", mgbT[:, l:l + 1], mgb[l:l + 1, :].rearrange("o g -> g o"), acc=[mgbT],
                  allow_slow_non_contiguous=NSL)
        s.op("act", lambda e: e.activation(out=scT[:, :, :], in_=scT[:, :, :], func=AF.Silu), reads=[scT], writes=[scT])
        ex = s.sbuf("ex", [128, 4, 2], F32)
        den = s.sbuf("den", [128, 2, 2], F32)
        s.op("act", lambda e: e.activation(out=ex[:, :, :], in_=lg[:, :, :], func=AF.Exp), reads=[lg], writes=[ex])
        s.op("dve", lambda e: e.tensor_tensor(out=den[:, :, :], in0=ex[:, 0:2, :], in1=ex[:, 2:4, :], op=ALU.add),
             reads=[ex], writes=[den])
        s.op("dve", lambda e: e.reciprocal(out=den[:, :, :], in_=den[:, :, :]), reads=[den], writes=[den])
        s.op("pool", lambda e: e.memset(lb[:, 0, :, :], 0.0), acc=[lb])
        s.op("dve", lambda e: e.tensor_tensor(out=lb[:, 1, :, :], in0=ex[:, 2:4, :], in1=den[:, :, :], op=ALU.mult),
             reads=[ex, den], acc=[lb])
        s.op("dve", lambda e: e.tensor_scalar(out=oml[:, :, :, :], in0=lb[:, :, :, :], scalar1=-1.0, scalar2=1.0,
                                              op0=ALU.mult, op1=ALU.add), reads=[lb], writes=[oml])
        s.op("dve", lambda e: e.tensor_scalar(out=noml[:, :, :, :], in0=lb[:, :, :, :], scalar1=1.0, scalar2=-1.0,
                                              op0=ALU.mult, op1=ALU.add), reads=[lb], writes=[noml])
        it = 0
        for l in range(2):
            for grp in range(12):
                W = wm[it % 2]
                it += 1
                s.dma("sp", W[:, :, :], w_mod[l, :, grp * 512:(grp + 1) * 512].rearrange("(k p) n -> p k n", p=128),
                      writes=[W])
                for m in range(4):
                    idx = grp * 4 + m
                    for k in range(8):
                        s.mm(pm[:, idx * 3:idx * 3 + 3], W[:, k, m * 128:(m + 1) * 128], scT[:, k, :], k == 0, k == 7,
                             reads=[W, scT], acc=[pm])
            s.op("dve", lambda e: e.tensor_tensor(
                out=modT[:, l, :, :], in0=pm[:, 0:144].rearrange("p (j r) -> p j r", r=3),
                in1=bmT[:, l, :].unsqueeze(2).to_broadcast([128, 48, 3]), op=ALU.add),
                reads=[pm, bmT], acc=[modT])
        for l in range(2):
            for w in range(2):
                for k in range(8):
                    j = (1 if w == 0 else 4) * 8 + k
                    s.op("dve", lambda e: e.tensor_scalar(out=AT[:, l, w, k, :], in0=modT[:, l, j, :], scalar1=1.0,
                                                          scalar2=gT[:, w * 2 + l, k:k + 1], op0=ALU.add, op1=ALU.mult),
                         reads=[modT, gT], acc=[AT])
        if dbg:
            s.dma("sp", mod_dbg[:, :], modT[:, :, :, :].rearrange("p l j r -> p (l j r)"), reads=[modT], writes=[mod_dbg])
        s.release(m0)

    def norm_block(X, xoff, n, st, l, w, col, sq, pss, rs, tmps, j):
        s.op("act", lambda e: e.activation(out=sq[:, :, 0:n], in_=X[:, :, xoff:xoff + n], func=AF.Square),
             reads=[X], writes=[sq])
        for k in range(8):
            s.mm(pss[:, 0:n], onesb[:, :], sq[:, k, 0:n], k == 0, k == 7, reads=[sq, onesb],
                 writes=[pss] if k == 0 else [], acc=[] if k == 0 else [pss])
        s.op("act", lambda e: e.activation(out=rs[:, 0:n], in_=pss[:, 0:n], func=AF.Ln, scale=1.0 / 1024, bias=EPS),
             reads=[pss], writes=[rs])
        s.op("act", lambda e: e.activation(out=rs[:, 0:n], in_=rs[:, 0:n], func=AF.Exp, scale=-0.5), reads=[rs], writes=[rs])
        sh = 0 if w == 0 else 3
        for k in range(8):
            T = tmps[k % len(tmps)]
            s.op("dve", lambda e: e.scalar_tensor_tensor(out=T[:, 0:n], in0=X[:, k, xoff:xoff + n],
                                                         scalar=AT[:, l, w, k, col:col + 1], in1=rs[:, 0:n],
                                                         op0=ALU.mult, op1=ALU.mult), reads=[X, rs, AT], writes=[T])
            s.op("act", lambda e: e.activation(out=hT[:, k, st:st + n], in_=T[:, 0:n], func=AF.Identity,
                                               bias=modT[:, l, sh * 8 + k, col:col + 1], scale=1.0),
                 reads=[T, modT], acc=[hTb[j]], selfdep=False)

    def p1(b, l):
        m0 = s.mark()
        xb = [s.sbuf("xb%d" % i, [128, 8, 512], F32) for i in range(2)]
        sq = [s.sbuf("sq%d" % i, [128, 8, 512], BF16) for i in range(2)]
        rs = [s.sbuf("rs%d" % i, [128, 512], F32) for i in range(2)]
        tmps = [s.sbuf("tmp%d" % i, [128, 512], F32) for i in range(4)]
        pss = [s.psum("pss%d" % i, [128, 512], F32) for i in range(2)]
        if l == 0:
            xin = [s.sbuf("xin%d" % i, [128, 1024], F32) for i in range(3)]
            pst = [s.psum("pst%d" % i, [128, 512], F32) for i in range(4)]
        cnt = 0
        for j, (st, n) in enumerate(BLOCKS):
            X = xb[j % 2]
            if l == 0:
                for ti in range(n // 128):
                    tok0 = st + ti * 128
                    src = ctx_in[b, tok0:tok0 + 128, :] if tok0 < 256 else x_in[b, tok0 - 256:tok0 - 128, :]
                    xi = xin[cnt % 3]
                    s.dma("sp", xi[:, :], src, writes=[xi])
                    for half in range(2):
                        pt = pst[(cnt * 2 + half) % 4]
                        for kk in range(4):
                            k = half * 4 + kk
                            s.op("pe", lambda e: e.transpose(out=pt[:, kk * 128:(kk + 1) * 128],
                                                             in_=xi[:, k * 128:(k + 1) * 128], identity=identf),
                                 reads=[xi, cs], writes=[pt] if kk == 0 else [], acc=[] if kk == 0 else [pt])
                        s.op("act", lambda e: e.activation(
                            out=X[:, half * 4:(half + 1) * 4, ti * 128:(ti + 1) * 128],
                            in_=pt[:, :].rearrange("p (k t) -> p k t", t=128), func=AF.Copy),
                            reads=[pt], writes=[X] if (ti == 0 and half == 0) else [],
                            acc=[] if (ti == 0 and half == 0) else [X], selfdep=False)
                    cnt += 1
                s.dma("sp", xs[:, :, st:st + n], X[:, :, 0:n], reads=[X], acc=[xs])
            else:
                s.dma("sp", X[:, :, 0:n], xs[:, :, st:st + n], reads=[xs], writes=[X])
            col = 2 if st < 256 else b
            norm_block(X, 0, n, st, l, 0, col, sq[j % 2], pss[j % 2], rs[j % 2], tmps, j)
        if dbg:
            s.dma("sp", h_dbg[:, :, :], hT[:, :, :], reads=hTb, writes=[h_dbg])
        s.release(m0)

    GROUPS = [
        (0, 512, [("f", "hq", 0, 128, 0), ("f", "hq", 128, 128, 1), ("v", 256, 256, 0)]),
        (512, 512, [("f", "hog", 0, 128, 0), ("f", "hog", 128, 128, 1), ("f", "hf0", 256, 128, 0), ("f", "hf0", 384, 128, 1)]),
        (1024, 512, [("f", "hf1", 0, 128, 0), ("f", "hf1", 128, 128, 1), ("f", "mq", 256, 128, 0), ("f", "mq", 384, 128, 1)]),
        (1536, 512, [("f", "mk", 0, 128, 0), ("f", "mk", 128, 128, 1), ("v", 256, 256, 256)]),
        (2048, 272, [("f", "mog", 0, 128, 0), ("f", "mog", 128, 128, 1), ("f", "mg", 256, 16, 0)]),
        (2320, 512, [("f", "gq", 0, 128, 0), ("f", "gq", 128, 128, 1), ("f", "gk", 256, 64, 0), ("f", "gk", 320, 64, 1),
                     ("v", 384, 128, 512)]),
        (2832, 512, [("f", "nq", 0, 128, 0), ("f", "nq", 128, 128, 1), ("f", "nk", 256, 128, 0), ("f", "nk", 384, 128, 1)]),
        (3344, 256, [("v", 0, 256, 640)]),
    ]

    def p2(b, l):
        m0 = s.mark()
        wst = [s.sbuf("wst%d" % i, [128, 8, 512], F32) for i in range(2)]
        wbf = [s.sbuf("wbf%d" % i, [128, 8, 512], BF16) for i in range(2)]
        stg = [s.sbuf("stg%d" % i, [128, 512], F32) for i in range(8)]
        stb = [s.sbuf("stb%d" % i, [128, 512], BF16) for i in range(4)]
        ps = [s.psum("p2ps%d" % i, [128, 512], F32) for i in range(6)]
        st_i = [0]
        sb_i = [0]
        ps_i = [0]

        def nstg():
            st_i[0] += 1
            return stg[st_i[0] % 8]

        def nstb():
            sb_i[0] += 1
            return stb[sb_i[0] % 4]

        def load(gi):
            c0, w, _ = GROUPS[gi]
            s.dma("sp", wst[gi % 2][:, :, 0:w], w_in[l, :, c0:c0 + w].rearrange("(k p) n -> p k n", p=128),
                  writes=[wst[gi % 2]])

        def cast(gi):
            c0, w, _ = GROUPS[gi]
            if gi % 2 == 0:
                s.op("act", lambda e: e.activation(out=wbf[gi % 2][:, :, 0:w], in_=wst[gi % 2][:, :, 0:w], func=AF.Copy),
                     reads=[wst[gi % 2]], writes=[wbf[gi % 2]])
            else:
                s.op("dve", lambda e: e.tensor_copy(out=wbf[gi % 2][:, :, 0:w], in_=wst[gi % 2][:, :, 0:w]),
                     reads=[wst[gi % 2]], writes=[wbf[gi % 2]])

        load(0)
        cast(0)
        load(1)
        for gi, (c0, w, jobs) in enumerate(GROUPS):
            if gi + 1 < len(GROUPS):
                cast(gi + 1)
            if gi + 2 < len(GROUPS):
                load(gi + 2)
            W = wbf[gi % 2]
            for job in jobs:
                if job[0] == "v":
                    _, off, ncol, vdst = job
                    for ti in range(18):
                        P = ps[ps_i[0] % 6]
                        ps_i[0] += 1
                        jb = 0 if ti < 2 else 1 + (ti - 2) // 4
                        for k in range(8):
                            s.mm(P[:, 0:ncol], hT[:, k, ti * 128:(ti + 1) * 128], W[:, k, off:off + ncol], k == 0, k == 7,
                                 reads=[hTb[jb], W], writes=[P] if k == 0 else [], acc=[] if k == 0 else [P])
                        B_ = nstb()
                        if ti % 2 == 0:
                            s.op("act", lambda e: e.activation(out=B_[:, 0:ncol], in_=P[:, 0:ncol], func=AF.Copy),
                                 reads=[P], writes=[B_])
                        else:
                            s.op("dve", lambda e: e.tensor_copy(out=B_[:, 0:ncol], in_=P[:, 0:ncol]), reads=[P], writes=[B_])
                        s.dma("sp", v_tok[ti * 128:(ti + 1) * 128, vdst:vdst + ncol], B_[:, 0:ncol], reads=[B_], acc=[v_tok])
                    continue
                _, kind, off, m, pc = job
                for j, (st, n) in enumerate(BLOCKS):
                    P = ps[ps_i[0] % 6]
                    ps_i[0] += 1
                    if kind == "gk":
                        for half in range(2):
                            for k in range(8):
                                s.mm(P[64 * half:64 * half + 64, 0:n], W[:, k, off:off + 64], hT[:, k, st:st + n],
                                     k == 0, k == 7, reads=[hTb[j], W],
                                     writes=[P] if (k == 0 and half == 0) else [], acc=[] if (k == 0 and half == 0) else [P])
                        mm_ = 128
                    else:
                        for k in range(8):
                            s.mm(P[0:m, 0:n], W[:, k, off:off + m], hT[:, k, st:st + n], k == 0, k == 7,
                                 reads=[hTb[j], W], writes=[P] if k == 0 else [], acc=[] if k == 0 else [P])
                        mm_ = m
                    rows = slice(pc * 128, pc * 128 + 128)
                    if kind in ("hq", "hog", "mog", "mq", "mk", "gq", "gk"):
                        S_ = nstg()
                        if kind in ("hq", "hog"):
                            s.op("act", lambda e: e.activation(out=S_[:, 0:n], in_=P[:, 0:n], func=AF.Silu), reads=[P], writes=[S_])
                        elif kind == "mog":
                            s.op("act", lambda e: e.activation(out=S_[:, 0:n], in_=P[:, 0:n], func=AF.Sigmoid), reads=[P], writes=[S_])
                        elif kind == "mk":
                            s.op("dve", lambda e: e.tensor_scalar(out=S_[:, 0:n], in0=P[:, 0:n], scalar1=0.125, scalar2=None,
                                                                  op0=ALU.mult), reads=[P], writes=[S_])
                        else:
                            s.op("dve", lambda e: e.tensor_copy(out=S_[:, 0:n], in_=P[:, 0:n]), reads=[P], writes=[S_])
                        dst = {"hq": zq_h, "hog": zog_h, "mog": zog_m, "mq": zq_m, "mk": zk_m, "gq": zq_g}.get(kind)
                        if kind == "gk":
                            s.dma("sp", zk_g[pc, :, st:st + n], S_[:, 0:n], reads=[S_], acc=[zk_g])
                        else:
                            s.dma("sp", dst[rows, st:st + n], S_[:, 0:n], reads=[S_], acc=[dst])
                    elif kind in ("hf0", "hf1"):
                        d = 0 if kind == "hf0" else 1
                        SG = nstg()
                        FG = nstg()
                        KK = nstg()
                        s.op("act", lambda e: e.activation(out=SG[:, 0:n], in_=P[:, 0:n], func=AF.Sigmoid), reads=[P], writes=[SG])
                        s.op("dve", lambda e: e.tensor_scalar(out=FG[:, 0:n], in0=SG[:, 0:n], scalar1=oml[:, l, d, pc:pc + 1],
                                                              scalar2=lb[:, l, d, pc:pc + 1], op0=ALU.mult, op1=ALU.add),
                             reads=[SG, oml, lb], writes=[FG])
                        s.op("act", lambda e: e.activation(out=FG[:, 0:n], in_=FG[:, 0:n], func=AF.Ln), reads=[FG], writes=[FG])
                        s.op("dve", lambda e: e.tensor_scalar(out=KK[:, 0:n], in0=SG[:, 0:n], scalar1=noml[:, l, d, pc:pc + 1],
                                                              scalar2=oml[:, l, d, pc:pc + 1], op0=ALU.mult, op1=ALU.add),
                             reads=[SG, oml, noml], writes=[KK])
                        s.dma("sp", zlf_h[d, rows, st:st + n], FG[:, 0:n], reads=[FG], acc=[zlf_h])
                        s.dma("sp", zkk_h[d, rows, st:st + n], KK[:, 0:n], reads=[KK], acc=[zkk_h])
                    elif kind == "mg":
                        S1 = nstg()
                        S2 = nstg()
                        s.op("act", lambda e: e.activation(out=S1[0:16, 0:n], in_=P[0:16, 0:n], func=AF.Identity,
                                                           bias=mgbT[:, l:l + 1], scale=1.0), reads=[P, mgbT], writes=[S1])
                        s.op("act", lambda e: e.activation(out=S2[0:16, 0:n], in_=P[0:16, 0:n], func=AF.Sigmoid,
                                                           bias=mgbT[:, l:l + 1], scale=1.0), reads=[P, mgbT], writes=[S2])
                        s.op("act", lambda e: e.activation(out=S2[0:16, 0:n], in_=S2[0:16, 0:n], func=AF.Ln), reads=[S2], writes=[S2])
                        s.dma("sp", zg_m[0:8, st:st + n], S1[0:8, 0:n], reads=[S1], acc=[zg_m])
                        s.dma("sp", zg_m[8:16, st:st + n], S2[8:16, 0:n], reads=[S2], acc=[zg_m])
                    elif kind in ("nq", "nk"):
                        B_ = nstb()
                        s.op("act", lambda e: e.activation(out=B_[:, 0:n], in_=P[:, 0:n], func=AF.Copy), reads=[P], writes=[B_])
                        dst = zq_n if kind == "nq" else zk_n
                        s.dma("sp", dst[rows, st:st + n], B_[:, 0:n], reads=[B_], acc=[dst])
                    else:
                        raise ValueError(kind)
        s.release(m0)

    def head_norm(o, gate, gidx, l, chunk, blocks, sqs, pn, rss, tms):
        for j in blocks:
            st, n = BLOCKS[j]
            SQ = sqs[j % 2]
            R = rss[j % 2]
            T = tms[j % 2]
            s.op("act", lambda e: e.activation(out=SQ[:, 0:n], in_=o[:, st:st + n], func=AF.Square), reads=[o], writes=[SQ])
            s.mm(pn[:, 0:n], blkb[:, :], SQ[:, 0:n], True, True, reads=[blkb, SQ], writes=[pn])
            s.op("act", lambda e: e.activation(out=R[:, 0:n], in_=pn[:, 0:n], func=AF.Sqrt, scale=1.0 / 64, bias=EPS),
                 reads=[pn], writes=[R])
            s.op("dve", lambda e: e.reciprocal(out=R[:, 0:n], in_=R[:, 0:n]), reads=[R], writes=[R])
            s.op("dve", lambda e: e.scalar_tensor_tensor(out=T[:, 0:n], in0=o[:, st:st + n], scalar=gcol[:, l, gidx:gidx + 1],
                                                         in1=R[:, 0:n], op0=ALU.mult, op1=ALU.mult),
                 reads=[o, R, gcol], writes=[T])
            s.op("dve", lambda e: e.tensor_tensor(out=hT[:, chunk, st:st + n], in0=T[:, 0:n], in1=gate[:, st:st + n],
                                                  op=ALU.mult), reads=[T, gate], acc=[hTb[j]], selfdep=False)

    def recur(b, l, kind):
        m0 = s.mark()
        NV = 1 if kind == "h" else 2
        L = 16 if kind == "h" else 64
        NCH = NT // L
        CTXN = NCTX // L
        HB = L
        SR = 2 * L
        PG = min(128 // SR, 3)
        last = (l == nlayers - 1)
        blocks = [1, 2, 3, 4] if last else [0, 1, 2, 3, 4]
        PTR = [s.psum("ptr%d" % d, [128, 1024], BF16) for d in range(2)]
        PSC = [s.psum("psc%d" % d, [128, 512], F32) for d in range(2)]
        PSO = [s.psum("pso%d" % d, [128, 512], F32) for d in range(2)]
        PST = [s.psum("pstt%d" % d, [128, 512], F32) for d in range(2)]
        pn = PSC[0]
        lfs = [s.sbuf("r_lf%d" % d, [128, NT], F32) for d in range(2)]
        PP = s.sbuf("r_PP", [128, NT + 1], F32)
        kks = [s.sbuf("r_kk%d" % d, [128, NT], F32) for d in range(2)]
        qs = s.sbuf("r_qs", [128, NT], F32)
        igbs = [s.sbuf("r_ig%d" % d, [128, NT], F32) for d in range(2)] if kind == "m" else None
        Q = [s.sbuf("r_Q%d" % d, [128, NT], BF16) for d in range(2)]
        K = [s.sbuf("r_K%d" % d, [128, NCH, 2 * L], BF16) for d in range(2)]
        KZ = []
        KN = [s.sbuf("r_KN%d" % d, [128, NT], BF16) for d in range(2)]
        Vbd = s.sbuf("r_Vbd", [128, NCH // PG, 128], BF16)
        Vp = s.sbuf("r_Vp", [128, NCH // PG, 128], BF16)
        Rn = [s.sbuf("r_Rn%d" % d, [128, NCH], F32) for d in range(2)]
        btm = [s.sbuf("r_bt%d" % i, [128, 512], F32) for i in range(10)]
        bti = [0]

        def ntmp():
            bti[0] += 1
            return btm[bti[0] % 10]

        def blk_of(c):
            t = c * L
            return 0 if t < 256 else 1 + (t - 256) // 512
        G = [s.sbuf("r_G%d" % d, [128, NCH], F32) for d in range(2)]
        W32 = [[s.sbuf("r_W%d%d" % (d, v), [128, 64], F32) for v in range(NV)] for d in range(2)]
        Wbf = [[s.sbuf("r_Wb%d%d" % (d, v), [128, 64], BF16) for v in range(NV)] for d in range(2)]
        kts = [s.sbuf("r_kt%d" % i, [128, 128], BF16) for i in range(4)]
        pms = [s.sbuf("r_pm%d" % i, [128, L], BF16) for i in range(4)]
        for pmb in pms:
            s.op("pool", lambda e: e.memset(pmb[:, :], 0.0), writes=[pmb])
        sqs = [s.sbuf("r_sq%d" % i, [128, 512], BF16) for i in range(2)]
        rss = [s.sbuf("r_rs%d" % i, [128, 512], F32) for i in range(2)]
        tms = [s.sbuf("r_tm%d" % i, [128, 512], F32) for i in range(2)]
        Qv = [Sch.views(Q[d], 5) for d in range(2)]
        Kv = [Sch.views(K[d], 5) for d in range(2)]
        KNv = [Sch.views(KN[d], 5) for d in range(2)]
        for d in range(2):
            s.op("pool", lambda e: e.memset(K[d][:, :, :], 0.0), writes=Kv[d])
        if kind == "h":
            accs = [[lfs[0]], [lfs[1]]]
        else:
            accs = [[lfs[0], igbs[0]], [lfs[1], igbs[1]]]
        zq = zq_h if kind == "h" else zq_m
        zog = zog_h if kind == "h" else zog_m
        vbase = 0 if kind == "h" else 256
        gidx = 0 if kind == "h" else 1
        order = [list(range(NCH)), list(range(CTXN - 1, -1, -1)) + list(range(NCH - 1, CTXN - 1, -1))]
        slot = 0
        for pc in range(2):
            rows = slice(pc * 128, pc * 128 + 128)
            vcol = vbase + pc * 128
            s.dma("sp", qs[:, :], zq[rows, :], reads=[zq], writes=[qs])
            for d in range(2):
                lf, kk = lfs[d], kks[d]
                if kind == "h":
                    s.dma("sp", lf[:, :], zlf_h[d, rows, :], reads=[zlf_h], writes=[lf])
                    s.dma("sp", kk[:, :], zkk_h[d, rows, :], reads=[zkk_h], writes=[kk])
                else:
                    igb = igbs[d]
                    for hh in range(2):
                        h = 2 * pc + hh
                        s.dma("sp", lf[64 * hh:64 * hh + 64, :], zg_m[8 + 4 * d + h:9 + 4 * d + h, :].partition_broadcast(64),
                              reads=[zg_m], writes=[lf] if hh == 0 else [], acc=[] if hh == 0 else [lf])
                        s.dma("sp", igb[64 * hh:64 * hh + 64, :], zg_m[4 * d + h:4 * d + h + 1, :].partition_broadcast(64),
                              reads=[zg_m], writes=[igb] if hh == 0 else [], acc=[] if hh == 0 else [igb])
                    s.dma("sp", kk[:, :], zk_m[rows, :], reads=[zk_m], writes=[kk])
            if pc == 0:
                s.op("pool", lambda e: e.memset(Vbd[:, :, :], 0.0), writes=[Vbd])
            for g in range(PG):
                for hh in range(2):
                    s.dma("sp", Vbd[g * SR + HB * hh:g * SR + HB * hh + L, :, 64 * hh:64 * hh + 64],
                          v_tok[:, vcol + 64 * hh:vcol + 64 * hh + 64].rearrange("(cq g p) d -> g p cq d", g=PG, p=L)[g],
                          reads=[v_tok], acc=[Vbd])
                s.dma("sp", Vp[g * SR:g * SR + L, :, :],
                      v_tok[:, vcol:vcol + 128].rearrange("(cq g p) d -> g p cq d", g=PG, p=L)[g],
                      reads=[v_tok], writes=[Vp] if g == 0 else [], acc=[] if g == 0 else [Vp])
            for d in range(2):
                sg = 1.0 if d == 0 else -1.0
                lf, kk = lfs[d], kks[d]
                igb = igbs[d] if kind == "m" else None
                s.op("pool", lambda e: e.memset(PP[:, 0:1], 0.0), writes=[PP])
                s.op("dve", lambda e: e.tensor_tensor_scan(out=PP[:, 1:NT + 1], data0=lf[:, :], data1=lf[:, :], initial=0.0,
                                                           op0=ALU.add, op1=ALU.bypass), reads=[lf], acc=[PP])
                Rm = PP[:, 0:NT].rearrange("p (c l) -> p c l", l=L)[:, :, L // 2]
                if d == 0:
                    s.op("dve", lambda e: e.tensor_copy(out=Rn[d][:, 0:NCH - 1], in_=Rm[:, 1:NCH]), reads=[PP], writes=[Rn[d]])
                    s.op("dve", lambda e: e.tensor_copy(out=Rn[d][:, NCH - 1:NCH], in_=Rm[:, NCH - 1:NCH]), reads=[PP], acc=[Rn[d]])
                    s.op("dve", lambda e: e.tensor_tensor(out=G[d][:, :], in0=Rn[d][:, :], in1=Rm, op=ALU.subtract),
                         reads=[Rn[d], PP], writes=[G[d]])
                else:
                    s.op("dve", lambda e: e.tensor_copy(out=Rn[d][:, 1:NCH], in_=Rm[:, 0:NCH - 1]), reads=[PP], writes=[Rn[d]])
                    s.op("dve", lambda e: e.tensor_copy(out=Rn[d][:, CTXN:CTXN + 1], in_=Rm[:, CTXN:CTXN + 1]), reads=[PP], acc=[Rn[d]])
                    s.op("dve", lambda e: e.tensor_tensor(out=Rn[d][:, 0:1], in0=Rm[:, NCH - 1:NCH], in1=PP[:, NT:NT + 1],
                                                          op=ALU.subtract), reads=[PP], acc=[Rn[d]])
                    s.op("dve", lambda e: e.tensor_tensor(out=G[d][:, :], in0=Rm, in1=Rn[d][:, :], op=ALU.subtract),
                         reads=[Rn[d], PP], writes=[G[d]])
                s.op("act", lambda e: e.activation(out=G[d][:, :], in_=G[d][:, :], func=AF.Exp), reads=[G[d]], writes=[G[d]])
                border = [0, 1, 2, 3, 4] if d == 0 else [0, 4, 3, 2, 1]
                for j in border:
                    st, n = BLOCKS[j]
                    c0, ncb = st // L, n // L
                    off = 1 if d == 0 else 0
                    PPs = PP[:, st + off:st + off + n].rearrange("p (c l) -> p c l", l=L)
                    T1, T2, T3, T4, T5 = [ntmp() for _ in range(5)]
                    v3 = lambda T: T[:, 0:n].rearrange("p (c l) -> p c l", l=L)
                    s.op("dve", lambda e: e.tensor_tensor(out=v3(T1), in0=PPs,
                                                          in1=Rm[:, c0:c0 + ncb].unsqueeze(2).to_broadcast([128, ncb, L]),
                                                          op=ALU.subtract), reads=[PP], writes=[T1])
                    s.op("act", lambda e: e.activation(out=T2[:, 0:n], in_=T1[:, 0:n], func=AF.Exp, scale=sg), reads=[T1], writes=[T2])
                    s.op("dve", lambda e: e.scalar_tensor_tensor(out=Q[d][:, st:st + n], in0=qs[:, st:st + n],
                                                                 scalar=(0.125 if kind == "h" else 1.0), in1=T2[:, 0:n],
                                                                 op0=ALU.mult, op1=ALU.mult),
                         reads=[qs, T2], acc=[Qv[d][j]], selfdep=False)
                    if kind == "m":
                        s.op("dve", lambda e: e.scalar_tensor_tensor(out=T3[:, 0:n], in0=T1[:, 0:n], scalar=-sg, in1=igb[:, st:st + n],
                                                                     op0=ALU.mult, op1=ALU.add), reads=[T1, igb], writes=[T3])
                        s.op("act", lambda e: e.activation(out=T3[:, 0:n], in_=T3[:, 0:n], func=AF.Exp), reads=[T3], writes=[T3])
                    else:
                        s.op("act", lambda e: e.activation(out=T3[:, 0:n], in_=T1[:, 0:n], func=AF.Exp, scale=-sg), reads=[T1], writes=[T3])
                    s.op("dve", lambda e: e.tensor_tensor(out=T4[:, 0:n], in0=kk[:, st:st + n], in1=T3[:, 0:n], op=ALU.mult),
                         reads=[kk, T3], writes=[T4])
                    for hh in range(2):
                        hs = slice(64 * hh, 64 * hh + 64)
                        if hh == 0:
                            s.op("act", lambda e: e.activation(out=K[d][hs, c0:c0 + ncb, L * hh:L * hh + L],
                                                               in_=T4[hs, 0:n].rearrange("p (c l) -> p c l", l=L), func=AF.Copy),
                                 reads=[T4], acc=[Kv[d][j]], selfdep=False)
                        else:
                            s.op("pool", lambda e: e.tensor_copy(out=K[d][hs, c0:c0 + ncb, L * hh:L * hh + L],
                                                                 in_=T4[hs, 0:n].rearrange("p (c l) -> p c l", l=L)),
                                 reads=[T4], acc=[Kv[d][j]], selfdep=False)
                    s.op("pool", lambda e: e.tensor_tensor(out=KN[d][:, st:st + n].rearrange("p (c l) -> p c l", l=L), in0=v3(T4),
                                                           in1=G[d][:, c0:c0 + ncb].unsqueeze(2).to_broadcast([128, ncb, L]),
                                                           op=ALU.mult), reads=[T4, G[d]], acc=[KNv[d][j]], selfdep=False)
                for v in range(NV):
                    s.op("pool", lambda e: e.memset(W32[d][v][:, :], 0.0), writes=[W32[d][v]])
            if dbg and stop_after == "hbuild":
                for d in range(2):
                    pass
                s.dma("sp", dbgP[:, :], PP[:, :], reads=[PP], writes=[dbgPb])
                s.release(m0)
                return
            seq = [(step, d) for step in range(NCH) for d in range(2)]
            slot0 = slot

            def phaseA(i):
                step, d = seq[i]
                c = order[d][step]
                csl = slice(c * L, c * L + L)
                sl = (slot0 + i) % 4
                lastst = (step == NCH - 1)
                kt = kts[sl]
                pm = pms[sl]
                ptr, psc = PTR[d], PSC[d]
                pb = (c % PG) * SR
                if not lastst:
                    s.op("pe", lambda e: e.transpose(out=ptr[pb:pb + L, 0:128], in_=KN[d][:, csl], identity=identb[:, :]),
                         reads=[KNv[d][blk_of(c)], identb], writes=[ptr])
                    s.op("act", lambda e: e.activation(out=kt[pb:pb + L, :], in_=ptr[pb:pb + L, 0:128], func=AF.Copy),
                         reads=[ptr], writes=[kt])
                s.mm(psc[pb:pb + SR, 0:L], K[d][:, c, :], Q[d][:, csl], True, True,
                     reads=[Kv[d][blk_of(c)], Qv[d][blk_of(c)]], writes=[psc])
                s.op("dve", lambda e: e.tensor_tensor(out=pm[pb:pb + SR, :], in0=psc[pb:pb + SR, 0:L],
                                                      in1=maskFB[L][d][pb:pb + SR, :], op=ALU.mult),
                     reads=[psc, cs], writes=[pm])

            def phaseB(i):
                step, d = seq[i]
                c = order[d][step]
                csl = slice(c * L, c * L + L)
                sl = (slot0 + i) % 4
                first = (step == 0)
                lastst = (step == NCH - 1)
                kt = kts[sl]
                pm = pms[sl]
                pso, pst = PSO[d], PST[d]
                pb = (c % PG) * SR
                cq = c // PG
                for v in range(NV):
                    Vb = Vbd[pb:pb + SR, cq, :] if v == 0 else bd1[:, :]
                    vo = slice(v * 64, v * 64 + L)
                    s.mm(pso[:, vo], Vb, pm[pb:pb + SR, :], True, first, reads=[Vbd, bd1, pm],
                         writes=[pso] if v == 0 else [], acc=[] if v == 0 else [pso])
                    if not first:
                        for hh in range(2):
                            hs = slice(64 * hh, 64 * hh + 64)
                            s.mm(pso[hs, vo], Wbf[d][v][hs, :], Q[d][hs, csl], False, hh == 1,
                                 reads=[Wbf[d][v], Qv[d][blk_of(c)]], acc=[pso])
                for v in range(NV):
                    vo = slice(v * 64, v * 64 + L)
                    A_ = accs[d][v]
                    s.op("act", lambda e: e.activation(out=A_[:, csl], in_=pso[:, vo], func=AF.Copy),
                         reads=[pso], acc=[A_], selfdep=False)
                if not lastst:
                    for v in range(NV):
                        vt = slice(v * 64, v * 64 + 64)
                        for hh in range(2):
                            hs = slice(64 * hh, 64 * hh + 64)
                            rhs = Vp[pb:pb + L, cq, hs] if v == 0 else onesb[pb:pb + L, 0:64]
                            wr = (v == 0 and hh == 0)
                            s.mm(pst[hs, vt], kt[pb:pb + L, hs], rhs, True, True, reads=[kt, Vp, onesb],
                                 writes=[pst] if wr else [], acc=[] if wr else [pst])
                    for v in range(NV):
                        vt = slice(v * 64, v * 64 + 64)
                        s.op("dve", lambda e: e.scalar_tensor_tensor(out=Wbf[d][v][:, :], in0=W32[d][v][:, :],
                                                                     scalar=G[d][:, c:c + 1], in1=pst[:, vt],
                                                                     op0=ALU.mult, op1=ALU.add),
                             reads=[W32[d][v], G[d], pst], writes=[Wbf[d][v]])
                    for v in range(NV):
                        vt = slice(v * 64, v * 64 + 64)
                        s.op("dve", lambda e: e.scalar_tensor_tensor(out=W32[d][v][:, :], in0=W32[d][v][:, :],
                                                                     scalar=G[d][:, c:c + 1], in1=pst[:, vt],
                                                                     op0=ALU.mult, op1=ALU.add),
                             reads=[W32[d][v], G[d], pst], writes=[W32[d][v]])

            phaseA(0)
            for i in range(len(seq)):
                if i + 1 < len(seq):
                    phaseA(i + 1)
                phaseB(i)
            slot += len(seq)
            s.dma("sp", qs[:, :], zog[rows, :], reads=[zog] + Qv[0] + Qv[1], writes=[qs])
            chunk = (0 if kind == "h" else 2) + pc
            for j in blocks:
                st, n = BLOCKS[j]
                O = ntmp()
                if kind == "m":
                    hd = []
                    for d in range(2):
                        num, den = accs[d]
                        Ta = ntmp()
                        Th = ntmp()
                        s.op("act", lambda e: e.activation(out=Ta[:, 0:n], in_=den[:, st:st + n], func=AF.Abs), reads=[den], writes=[Ta])
                        s.op("dve", lambda e: e.tensor_scalar_max(out=Ta[:, 0:n], in0=Ta[:, 0:n], scalar1=1.0), reads=[Ta], writes=[Ta])
                        s.op("act", lambda e: e.activation(out=Ta[:, 0:n], in_=Ta[:, 0:n], func=AF.Ln), reads=[Ta], writes=[Ta])
                        s.op("act", lambda e: e.activation(out=Ta[:, 0:n], in_=Ta[:, 0:n], func=AF.Exp, scale=-1.0), reads=[Ta], writes=[Ta])
                        s.op("dve", lambda e: e.tensor_tensor(out=Th[:, 0:n], in0=num[:, st:st + n], in1=Ta[:, 0:n], op=ALU.mult),
                             reads=[num, Ta], writes=[Th])
                        hd.append(Th)
                    s.op("pool", lambda e: e.tensor_tensor(out=O[:, 0:n], in0=hd[0][:, 0:n], in1=hd[1][:, 0:n], op=ALU.add),
                         reads=hd, writes=[O])
                else:
                    s.op("pool", lambda e: e.tensor_tensor(out=O[:, 0:n], in0=accs[0][0][:, st:st + n], in1=accs[1][0][:, st:st + n],
                                                           op=ALU.add), reads=[accs[0][0], accs[1][0]], writes=[O])
                SQ = sqs[j % 2]
                R = ntmp()
                T = ntmp()
                s.op("act", lambda e: e.activation(out=SQ[:, 0:n], in_=O[:, 0:n], func=AF.Square), reads=[O], writes=[SQ])
                s.mm(pn[:, 0:n], blkb[:, :], SQ[:, 0:n], True, True, reads=[blkb, SQ], writes=[pn])
                s.op("act", lambda e: e.activation(out=R[:, 0:n], in_=pn[:, 0:n], func=AF.Ln, scale=1.0 / 64, bias=EPS),
                     reads=[pn], writes=[R])
                s.op("act", lambda e: e.activation(out=R[:, 0:n], in_=R[:, 0:n], func=AF.Exp, scale=-0.5), reads=[R], writes=[R])
                s.op("dve", lambda e: e.scalar_tensor_tensor(out=T[:, 0:n], in0=O[:, 0:n], scalar=gcol[:, l, gidx:gidx + 1],
                                                             in1=R[:, 0:n], op0=ALU.mult, op1=ALU.mult),
                     reads=[O, R, gcol], writes=[T])
                s.op("dve", lambda e: e.tensor_tensor(out=hT[:, chunk, st:st + n], in0=T[:, 0:n], in1=qs[:, st:st + n],
                                                      op=ALU.mult), reads=[T, qs], acc=[hTb[j]], selfdep=False)
        s.release(m0)

    def attn_block(qb, kb, vfn, vbuf, st, n, kcs, chunk, sc_ps, num_ps, den_ps, pTs, rd, cnt):
        j = [i for i, (a, _) in enumerate(BLOCKS) if a <= st < a + BLOCKS[i][1]][0]
        nk = len(kcs)
        its = [(ki, kc, hh) for ki, kc in enumerate(kcs) for hh in range(2)]
        scs = {}

        def issue_sc(i):
            ki, kc, hh = its[i]
            hs = slice(64 * hh, 64 * hh + 64)
            sc = sc_ps[cnt[0] % len(sc_ps)]
            pT = pTs[cnt[0] % len(pTs)]
            cnt[0] += 1
            s.mm(sc[:, 0:n], kb[hs, kc * 128:(kc + 1) * 128], qb[hs, st:st + n], True, True, reads=[kb, qb], writes=[sc])
            scs[i] = (sc, pT)

        issue_sc(0)
        if len(its) > 1:
            issue_sc(1)
        for i, (ki, kc, hh) in enumerate(its):
            hs = slice(64 * hh, 64 * hh + 64)
            if i + 2 < len(its):
                issue_sc(i + 2)
            sc, pT = scs.pop(i)
            s.op("act", lambda e: e.activation(out=pT[:, 0:n], in_=sc[:, 0:n], func=AF.Exp, scale=0.125),
                 reads=[sc], writes=[pT])
            first = (ki == 0)
            s.mm(num_ps[hs, 0:n], vfn(kc, hh), pT[:, 0:n], first, ki == nk - 1, reads=[pT, vbuf],
                 writes=[num_ps] if (first and hh == 0) else [], acc=[] if (first and hh == 0) else [num_ps])
            s.mm(den_ps[hs, 0:n], onesb[:, 0:64], pT[:, 0:n], first, ki == nk - 1, reads=[pT, onesb],
                 writes=[den_ps] if (first and hh == 0) else [], acc=[] if (first and hh == 0) else [den_ps])
        s.op("act", lambda e: e.activation(out=rd[:, 0:n], in_=den_ps[:, 0:n], func=AF.Ln), reads=[den_ps], writes=[rd])
        s.op("act", lambda e: e.activation(out=rd[:, 0:n], in_=rd[:, 0:n], func=AF.Exp, scale=-1.0), reads=[rd], writes=[rd])
        s.op("dve", lambda e: e.tensor_tensor(out=hT[:, chunk, st:st + n], in0=num_ps[:, 0:n], in1=rd[:, 0:n], op=ALU.mult),
             reads=[num_ps, rd], acc=[hTb[j]], selfdep=False)

    def gqa(b, l):
        m0 = s.mark()
        last = (l == nlayers - 1)
        rp = s.sbuf("g_rope", [128, 4096], F32)
        s.dma("sp", rp[:, :], ropec[:, :], writes=[rp])
        raw = [s.sbuf("g_raw%d" % i, [128, NT], F32) for i in range(2)]
        QK = [s.sbuf("g_qk%d" % i, [128, NT], BF16) for i in range(4)]
        sqs = [s.sbuf("g_sq%d" % i, [128, 512], BF16) for i in range(4)]
        rss = [s.sbuf("g_rs%d" % i, [128, 512], F32) for i in range(4)]
        t1s = [s.sbuf("g_t1%d" % i, [128, 512], F32) for i in range(4)]
        t2s = [s.sbuf("g_t2%d" % i, [128, 512], F32) for i in range(4)]
        t3s = [s.sbuf("g_t3%d" % i, [128, 512], F32) for i in range(4)]
        Vg = s.sbuf("g_V", [128, 18, 128], BF16)
        pTs = [s.sbuf("g_pT%d" % i, [128, 512], BF16) for i in range(4)]
        rds = [s.sbuf("g_rd%d" % i, [128, 512], F32) for i in range(2)]
        sc_ps = [s.psum("g_sc%d" % i, [128, 512], F32) for i in range(4)]
        num_ps = [s.psum("g_num%d" % i, [128, 512], F32) for i in range(2)]
        den_ps = [s.psum("g_den%d" % i, [128, 512], F32) for i in range(2)]
        s.dma("sp", Vg[:, :, :], v_tok[:, 512:640].rearrange("(c p) d -> p c d", p=128), reads=[v_tok], writes=[Vg])
        srcs = [(zq_g[0:128, :], 2), (zq_g[128:256, :], 2), (zk_g[0, :, :], 3), (zk_g[1, :, :], 3)]
        it = 0
        for idx, (src, gi) in enumerate(srcs):
            R_ = raw[idx % 2]
            s.dma("sp", R_[:, :], src, reads=[zq_g, zk_g], writes=[R_])
            for j, (st, n) in enumerate(BLOCKS):
                SQ = sqs[it % 4]
                RS = rss[it % 4]
                T1 = t1s[it % 4]
                T2 = t2s[it % 4]
                T3 = t3s[it % 4]
                P1 = sc_ps[(2 * it) % 4]
                P2 = sc_ps[(2 * it + 1) % 4]
                it += 1
                s.op("act", lambda e: e.activation(out=SQ[:, 0:n], in_=R_[:, st:st + n], func=AF.Square), reads=[R_], writes=[SQ])
                s.mm(P1[:, 0:n], blkb[:, :], SQ[:, 0:n], True, True, reads=[blkb, SQ], writes=[P1])
                s.op("act", lambda e: e.activation(out=RS[:, 0:n], in_=P1[:, 0:n], func=AF.Ln, scale=1.0 / 64, bias=EPS),
                     reads=[P1], writes=[RS])
                s.op("act", lambda e: e.activation(out=RS[:, 0:n], in_=RS[:, 0:n], func=AF.Exp, scale=-0.5), reads=[RS], writes=[RS])
                s.op("dve", lambda e: e.scalar_tensor_tensor(out=T1[:, 0:n], in0=R_[:, st:st + n], scalar=gcol[:, l, gi:gi + 1],
                                                             in1=RS[:, 0:n], op0=ALU.mult, op1=ALU.mult),
                     reads=[R_, RS, gcol], writes=[T1])
                if st >= 256:
                    s.mm(P2[:, 0:n], ropeRT, T1[:, 0:n], True, True, reads=[cs, T1], writes=[P2])
                    s.op("dve", lambda e: e.tensor_tensor(out=T2[:, 0:n], in0=T1[:, 0:n], in1=rp[:, st - 256:st - 256 + n],
                                                          op=ALU.mult), reads=[T1, rp], writes=[T2])
                    s.op("dve", lambda e: e.tensor_tensor(out=T3[:, 0:n], in0=P2[:, 0:n],
                                                          in1=rp[:, 2048 + st - 256:2048 + st - 256 + n], op=ALU.mult),
                         reads=[P2, rp], writes=[T3])
                    s.op("dve", lambda e: e.tensor_tensor(out=QK[idx][:, st:st + n], in0=T2[:, 0:n], in1=T3[:, 0:n], op=ALU.add),
                         reads=[T2, T3], acc=[QK[idx]], selfdep=False)
                else:
                    s.op("act", lambda e: e.activation(out=QK[idx][:, st:st + n], in_=T1[:, 0:n], func=AF.Copy),
                         reads=[T1], acc=[QK[idx]], selfdep=False)
        cnt = [0]
        qblocks = [1, 2, 3, 4] if last else [0, 1, 2, 3, 4]
        bi = 0
        for pc in range(2):
            for j in qblocks:
                st, n = BLOCKS[j]
                kcs = list(range(18)) if st >= 256 else [0, 1]
                attn_block(QK[pc], QK[2 + pc], lambda kc, hh: Vg[:, kc, 64 * pc:64 * pc + 64], Vg, st, n, kcs, 4 + pc,
                           sc_ps, num_ps[bi % 2], den_ps[bi % 2], pTs, rds[bi % 2], cnt)
                bi += 1
        s.release(m0)

    def na(b, l):
        m0 = s.mark()
        last = (l == nlayers - 1)
        bias8 = s.sbuf("n_bias", [128, 4 * NU, 128], BF16)
        bst = [s.sbuf("n_bst%d" % i, [128, NU, 128], F32) for i in range(2)]
        qT = [s.sbuf("n_q%d" % i, [128, NT], BF16) for i in range(2)]
        kT = [s.sbuf("n_k%d" % i, [128, NT], BF16) for i in range(2)]
        Vn = s.sbuf("n_V", [128, 18, 256], BF16)
        pTs = [s.sbuf("n_pT%d" % i, [128, 1024], BF16) for i in range(3)]
        pTd = [s.sbuf("n_pTd%d" % i, [128, 512], BF16) for i in range(2)]
        rds = [s.sbuf("n_rd%d" % i, [128, 512], F32) for i in range(2)]
        sc_ps = [s.psum("n_sc%d" % i, [128, 512], F32) for i in range(4)]
        num_ps = [s.psum("n_num%d" % i, [128, 512], F32) for i in range(2)]
        den_ps = [s.psum("n_den%d" % i, [128, 512], F32) for i in range(2)]
        for h in range(4):
            B_ = bst[h % 2]
            s.dma("sp", B_[:, :, :], nab[l, h * NU:(h + 1) * NU, :, :].rearrange("u p q -> p u q"), writes=[B_])
            s.op("pool", lambda e: e.tensor_scalar(out=bias8[:, h * NU:(h + 1) * NU, :], in0=B_[:, :, :], scalar1=8.0,
                                                   scalar2=None, op0=ALU.mult), reads=[B_], acc=[bias8])
        for pc in range(2):
            rows = slice(pc * 128, pc * 128 + 128)
            s.dma("sp", qT[pc][:, :], zq_n[rows, :], reads=[zq_n], writes=[qT[pc]])
            s.dma("sp", kT[pc][:, :], zk_n[rows, :], reads=[zk_n], writes=[kT[pc]])
        s.dma("sp", Vn[:, :, :], v_tok[:, 640:896].rearrange("(c p) d -> p c d", p=128), reads=[v_tok], writes=[Vn])
        it = 0
        for pc in range(2):
            units = [(t, hh) for t in range(16) for hh in range(2)]
            info = {}

            def na_S(ui):
                t, hh = units[ui]
                q0 = 256 + 128 * t
                loc = sorted([j for (tt, j) in _NA_TMAP if tt == t])
                allk = [(0, None), (1, None)] + [(2 + j, _NA_TMAP[(t, j)]) for j in loc]
                h = 2 * pc + hh
                hs = slice(64 * hh, 64 * hh + 64)
                g = it + ui
                banks = [sc_ps[(2 * g) % 4], sc_ps[(2 * g + 1) % 4]]
                for i, (gc, u) in enumerate(allk):
                    bk = banks[i // 4]
                    cc = slice((i % 4) * 128, (i % 4) * 128 + 128)
                    firstw = (i % 4 == 0)
                    s.mm(bk[:, cc], kT[pc][hs, gc * 128:(gc + 1) * 128], qT[pc][hs, q0:q0 + 128], True, u is None,
                         reads=[kT[pc], qT[pc]], writes=[bk] if firstw else [], acc=[] if firstw else [bk])
                    if u is not None:
                        s.mm(bk[:, cc], identb[:, :], bias8[:, h * NU + u, :], False, True, reads=[identb, bias8], acc=[bk])
                info[ui] = (allk, banks, pTs[g % 3])

            def na_EV(ui):
                t, hh = units[ui]
                q0 = 256 + 128 * t
                jb = 1 + t // 4
                h = 2 * pc + hh
                hs = slice(64 * hh, 64 * hh + 64)
                allk, banks, pT = info.pop(ui)
                nk = len(allk)
                NUM = num_ps[t % 2]
                DEN = den_ps[t % 2]
                RD = rds[t % 2]
                n0 = min(nk, 4) * 128
                s.op("act", lambda e: e.activation(out=pT[:, 0:n0], in_=banks[0][:, 0:n0], func=AF.Exp, scale=0.125),
                     reads=[banks[0]], writes=[pT])
                if nk > 4:
                    n1 = (nk - 4) * 128
                    s.op("act", lambda e: e.activation(out=pT[:, 512:512 + n1], in_=banks[1][:, 0:n1], func=AF.Exp, scale=0.125),
                         reads=[banks[1]], acc=[pT])
                for i, (gc, u) in enumerate(allk):
                    wr = (i == 0 and hh == 0)
                    s.mm(NUM[hs, 0:128], Vn[:, gc, h * 64:(h + 1) * 64], pT[:, i * 128:(i + 1) * 128], i == 0, i == nk - 1,
                         reads=[Vn, pT], writes=[NUM] if wr else [], acc=[] if wr else [NUM])
                    s.mm(DEN[hs, 0:128], onesb[:, 0:64], pT[:, i * 128:(i + 1) * 128], i == 0, i == nk - 1,
                         reads=[onesb, pT], writes=[DEN] if wr else [], acc=[] if wr else [DEN])
                if hh == 1:
                    s.op("act", lambda e: e.activation(out=RD[:, 0:128], in_=DEN[:, 0:128], func=AF.Ln), reads=[DEN], writes=[RD])
                    s.op("act", lambda e: e.activation(out=RD[:, 0:128], in_=RD[:, 0:128], func=AF.Exp, scale=-1.0), reads=[RD], writes=[RD])
                    s.op("dve", lambda e: e.tensor_tensor(out=hT[:, 6 + pc, q0:q0 + 128], in0=NUM[:, 0:128], in1=RD[:, 0:128],
                                                          op=ALU.mult), reads=[NUM, RD], acc=[hTb[jb]], selfdep=False)

            na_S(0)
            for ui in range(len(units)):
                if ui + 1 < len(units):
                    na_S(ui + 1)
                na_EV(ui)
            it += len(units)
            if not last:
                cnt = [0]
                attn_block(qT[pc], kT[pc], lambda kc, hh: Vn[:, kc, (2 * pc + hh) * 64:(2 * pc + hh) * 64 + 64], Vn, 0, 256, [0, 1],
                           6 + pc, sc_ps, num_ps[0], den_ps[0], pTd, rds[0], cnt)
        s.release(m0)

    def p4(b, l):
        last = (l == nlayers - 1)
        halves = [[0, 1, 2], [3, 4]]
        if last:
            halves[0] = [1, 2]
        for blks in halves:
            m0 = s.mark()
            base = BLOCKS[blks[0]][0]
            xh = s.sbuf("xh", [128, 8, 1280], F32)
            xhv = {j: Buf(xh.t, "xh.%d" % j) for j in blks}
            stg = [s.sbuf("p4stg%d" % i, [128, 8, 512], F32) for i in range(2)]
            wb = [s.sbuf("p4wb%d" % i, [128, 8, 512], BF16) for i in range(2)]
            w2b = [s.sbuf("p4w2b%d" % i, [128, 4, 1024], BF16) for i in range(2)]
            ub = [s.sbuf("p4u%d" % i, [128, 4, 512], BF16) for i in range(2)]
            rb = [s.sbuf("p4r%d" % i, [128, 512], F32) for i in range(3)]
            sq = s.sbuf("p4sq", [128, 8, 512], BF16)
            rs = s.sbuf("p4rs", [128, 512], F32)
            tmps = [s.sbuf("p4t%d" % i, [128, 512], F32) for i in range(3)]
            ps = [s.psum("p4ps%d" % i, [128, 512], F32) for i in range(7)]
            pss = s.psum("p4pss", [128, 512], F32)
            pi = [0]

            def nps():
                pi[0] += 1
                return ps[pi[0] % 7]

            for j in blks:
                st, n = BLOCKS[j]
                s.dma("sp", xh[:, :, st - base:st - base + n], xs[:, :, st:st + n], reads=[xs], writes=[xhv[j]])
            for cg in range(2):
                s.dma("sp", stg[cg][:, :, :], w_out[l, :, cg * 512:(cg + 1) * 512].rearrange("(k p) n -> p k n", p=128),
                      writes=[stg[cg]])
                if cg == 0:
                    s.op("act", lambda e: e.activation(out=wb[cg][:, :, :], in_=stg[cg][:, :, :], func=AF.Copy),
                         reads=[stg[cg]], writes=[wb[cg]])
                else:
                    s.op("dve", lambda e: e.tensor_copy(out=wb[cg][:, :, :], in_=stg[cg][:, :, :]), reads=[stg[cg]], writes=[wb[cg]])
            for cg in range(2):
                for j in blks:
                    st, n = BLOCKS[j]
                    col = 2 if st < 256 else b
                    for mi in range(4):
                        m = cg * 4 + mi
                        P = nps()
                        for k in range(8):
                            s.mm(P[:, 0:n], wb[cg][:, k, mi * 128:(mi + 1) * 128], hT[:, k, st:st + n], k == 0, k == 7,
                                 reads=[wb[cg], hTb[j]], writes=[P] if k == 0 else [], acc=[] if k == 0 else [P])
                        xa = xh[:, m, st - base:st - base + n]
                        s.op("dve", lambda e: e.scalar_tensor_tensor(out=xa, in0=P[:, 0:n], scalar=modT[:, l, 16 + m, col:col + 1],
                                                                     in1=xa, op0=ALU.mult, op1=ALU.add),
                             reads=[P, modT, xhv[j]], acc=[xhv[j]], selfdep=False)
            for j in blks:
                st, n = BLOCKS[j]
                col = 2 if st < 256 else b
                norm_block(xhv[j], st - base, n, st, l, 1, col, sq, pss, rs, tmps, j)
            def loadw_dma(g):
                s.dma("sp", stg[0][:, :, :], w1[l, :, g * 512:(g + 1) * 512].rearrange("(k p) n -> p k n", p=128), writes=[stg[0]])
                s.dma("sp", stg[1][:, :, :].rearrange("p k n -> p (k n)").rearrange("p (c n) -> p c n", c=4),
                      w2[l, g * 512:(g + 1) * 512, :].rearrange("(c p) n -> p c n", p=128), writes=[stg[1]])

            def loadw_cast(g):
                s.op("act", lambda e: e.activation(out=wb[g % 2][:, :, :], in_=stg[0][:, :, :], func=AF.Copy),
                     reads=[stg[0]], writes=[wb[g % 2]])
                s.op("dve", lambda e: e.tensor_copy(out=w2b[g % 2][:, :, :],
                                                    in_=stg[1][:, :, :].rearrange("p k n -> p (k n)").rearrange("p (c n) -> p c n", c=4)),
                     reads=[stg[1]], writes=[w2b[g % 2]])

            loadw_dma(0)
            loadw_cast(0)
            items = [(g, bi_, j) for g in range(8) for bi_, j in enumerate(blks)]
            Us = {}

            def mlp_u(i):
                g, bi_, j = items[i]
                if bi_ == 0 and g + 1 < 8:
                    loadw_dma(g + 1)
                W1 = wb[g % 2]
                st, n = BLOCKS[j]
                U = ub[i % 2]
                Us[i] = U
                for hc in range(4):
                    P = nps()
                    for k in range(8):
                        s.mm(P[:, 0:n], W1[:, k, hc * 128:(hc + 1) * 128], hT[:, k, st:st + n], k == 0, k == 7,
                             reads=[W1, hTb[j]], writes=[P] if k == 0 else [], acc=[] if k == 0 else [P])
                    R_ = rb[hc % 3]
                    s.op("act", lambda e: e.activation(out=R_[:, 0:n], in_=P[:, 0:n], func=AF.Relu), reads=[P], writes=[R_])
                    s.op("dve", lambda e: e.tensor_tensor(out=U[:, hc, 0:n], in0=R_[:, 0:n], in1=R_[:, 0:n], op=ALU.mult),
                         reads=[R_], writes=[U] if hc == 0 else [], acc=[] if hc == 0 else [U], selfdep=(hc == 0))

            def mlp_y(i):
                g, bi_, j = items[i]
                W2 = w2b[g % 2]
                st, n = BLOCKS[j]
                col = 2 if st < 256 else b
                U = Us.pop(i)
                for m in range(8):
                    P = nps()
                    for hc in range(4):
                        s.mm(P[:, 0:n], W2[:, hc, m * 128:(m + 1) * 128], U[:, hc, 0:n], hc == 0, hc == 3,
                             reads=[W2, U], writes=[P] if hc == 0 else [], acc=[] if hc == 0 else [P])
                    xa = xh[:, m, st - base:st - base + n]
                    s.op("dve", lambda e: e.scalar_tensor_tensor(out=xa, in0=P[:, 0:n], scalar=modT[:, l, 40 + m, col:col + 1],
                                                                 in1=xa, op0=ALU.mult, op1=ALU.add),
                         reads=[P, modT, xhv[j]], acc=[xhv[j]], selfdep=False)

            mlp_u(0)
            for i in range(len(items)):
                g, bi_, j = items[i]
                if i + 1 < len(items):
                    g2, b2, _ = items[i + 1]
                    if b2 == 0:
                        loadw_cast(g2)
                    mlp_u(i + 1)
                mlp_y(i)
            if not last:
                for j in blks:
                    st, n = BLOCKS[j]
                    s.dma("sp", xs[:, :, st:st + n], xh[:, :, st - base:st - base + n], reads=[xhv[j]], acc=[xs])
            else:
                yb = stg[0]
                youts = [Buf(stg[1].t, "yout%d" % i) for i in range(2)]
                oc = 0
                for j in blks:
                    st, n = BLOCKS[j]
                    s.op("act", lambda e: e.activation(out=sq[:, :, 0:n], in_=xh[:, :, st - base:st - base + n], func=AF.Square),
                         reads=[xhv[j]], writes=[sq])
                    for k in range(8):
                        s.mm(pss[:, 0:n], onesb[:, :], sq[:, k, 0:n], k == 0, k == 7, reads=[sq, onesb],
                             writes=[pss] if k == 0 else [], acc=[] if k == 0 else [pss])
                    s.op("act", lambda e: e.activation(out=rs[:, 0:n], in_=pss[:, 0:n], func=AF.Ln, scale=1.0 / 1024, bias=EPS),
                         reads=[pss], writes=[rs])
                    s.op("act", lambda e: e.activation(out=rs[:, 0:n], in_=rs[:, 0:n], func=AF.Exp, scale=-0.5), reads=[rs], writes=[rs])
                    for k in range(8):
                        s.op("dve", lambda e: e.scalar_tensor_tensor(out=yb[:, k, 0:n], in0=xh[:, k, st - base:st - base + n],
                                                                     scalar=gT[:, 4, k:k + 1], in1=rs[:, 0:n],
                                                                     op0=ALU.mult, op1=ALU.mult),
                             reads=[xhv[j], gT, rs], writes=[yb] if k == 0 else [], acc=[] if k == 0 else [yb], selfdep=(k == 0))
                    for ti in range(n // 128):
                        YO = youts[oc % 2]
                        yo_ap = stg[1][:, (oc % 2) * 2:(oc % 2) * 2 + 2, :].rearrange("p a n -> p (a n)")
                        oc += 1
                        for half in range(2):
                            P = nps()
                            for kk in range(4):
                                k = half * 4 + kk
                                s.op("pe", lambda e: e.transpose(out=P[:, kk * 128:(kk + 1) * 128],
                                                                 in_=yb[:, k, ti * 128:(ti + 1) * 128], identity=identf),
                                     reads=[yb, cs], writes=[P] if kk == 0 else [], acc=[] if kk == 0 else [P])
                            s.op("act", lambda e: e.activation(out=yo_ap[:, half * 512:(half + 1) * 512], in_=P[:, :], func=AF.Copy),
                                 reads=[P], writes=[YO] if half == 0 else [], acc=[] if half == 0 else [YO], selfdep=(half == 0))
                        tok = st - 256 + ti * 128
                        s.dma("sp", y[b, tok:tok + 128, :], yo_ap, reads=[YO], acc=[y])
            s.release(m0)

    prologue()
    for b in range(nseq if stop_after != "pro" else 0):
        for l in range(nlayers):
            p1(b, l)
            if stop_after == "p1":
                break
            p2(b, l)
            if stop_after == "p2":
                break
            recur(b, l, "h")
            if stop_after in ("h", "hbuild"):
                break
            recur(b, l, "m")
            if stop_after == "m":
                break
            gqa(b, l)
            if stop_after == "g":
                break
            na(b, l)
            if stop_after == "p3":
                break
            p4(b, l)
        if stop_after is not None:
            break
    if dbg:
        s.dma("sp", cat_dbg[:, :, :], hT[:, :, :], reads=hTb, writes=[cat_dbg])
    s.finish()
    build.stats = (s.nops, s.nwaits, dict(s.cnt))
    return nc


_CACHE = {}


def _host_inputs(inputs, core):
    f = lambda a: np.ascontiguousarray(np.asarray(a, dtype=np.float32))
    b0 = 2 * core
    cst, rope = _CACHE["consts"]
    m = {
        "x": f(inputs["x"][b0:b0 + 2]),
        "ctx": f(inputs["ctx"][b0:b0 + 2]),
        "cvec": f(np.concatenate([inputs["c"][b0:b0 + 2], np.asarray(inputs["c_ctx"])[None, :]], 0)),
        "w_mod": f(inputs["w_mod"]), "b_mod": f(inputs["b_mod"]),
        "norm1_g": f(inputs["norm1_g"]), "norm2_g": f(inputs["norm2_g"]),
        "w_in": f(inputs["w_in"]),
        "hgrn_lb_logits": f(np.asarray(inputs["hgrn_lb_logits"]).reshape(4, 256)),
        "hgrn_norm_g": f(inputs["hgrn_norm_g"]), "mlstm_gate_b": f(inputs["mlstm_gate_b"]),
        "mlstm_norm_g": f(inputs["mlstm_norm_g"]), "gqa_qnorm_g": f(inputs["gqa_qnorm_g"]),
        "gqa_knorm_g": f(inputs["gqa_knorm_g"]),
        "na_bias": _CACHE["na_bias"],
        "w_out": f(inputs["w_out"]), "w_mlp1": f(inputs["w_mlp1"]), "w_mlp2": f(inputs["w_mlp2"]),
        "final_norm_g": f(np.asarray(inputs["final_norm_g"]).reshape(1, 1024)),
        "consts": cst, "rope": rope,
    }
    return m


def _prep(inputs):
    _CACHE["consts"] = _consts()
    idx = _na_gather_index()
    rpb = np.asarray(inputs["na_rpb"], np.float32)
    flat = np.concatenate([rpb.reshape(2, 4, 465), np.full((2, 4, 1), NEG, np.float32)], -1)
    nb = flat[:, :, idx]
    _CACHE["na_bias"] = np.ascontiguousarray(nb.reshape(2, 4 * NU, 128, 128))


def kernel(**inputs):
    _prep(inputs)
    nc = build()
    in_maps = [_host_inputs(inputs, c) for c in range(8)]
    res = run_bass_kernel_spmd(nc, in_maps, core_ids=list(range(8)))
    out = np.concatenate([np.asarray(r["y"], np.float32) for r in res.results], axis=0)
    return out
```

```python
import numpy as np
import ml_dtypes
import concourse.bass as bass
import concourse.mybir as mybir
from concourse.bass_utils import run_bass_kernel_spmd

F32 = mybir.dt.float32
BF16 = mybir.dt.bfloat16
AF = mybir.ActivationFunctionType
ALU = mybir.AluOpType

NT = 2304
NCTX = 256
EPS = 1e-6
NEG = -1e30
BLOCKS = [(0, 256), (256, 512), (768, 512), (1280, 512), (1792, 512)]
NCH = 36


class Buf:
    __slots__ = ("t", "name", "lw", "aw", "rd")

    def __init__(self, t, name):
        self.t = t
        self.name = name
        self.lw = {}
        self.aw = {}
        self.rd = {}

    def __getitem__(self, idx):
        return self.t[idx]


class Sch:
    NDMA = 10

    def __init__(self, nc):
        self.nc = nc
        self.eng = {"pe": nc.tensor, "dve": nc.vector, "act": nc.scalar, "pool": nc.gpsimd, "sp": nc.sync}
        self.sems = {}
        self.cnt = {}
        self.key = {}
        self.nsem = 0
        for e in self.eng:
            self._newsem(e)
        for q in ("sp", "act", "pool"):
            for k in range(self.NDMA):
                key = "d%s%d" % (q, k)
                self.sems[key] = nc.alloc_semaphore("s_" + key)
                self.cnt[key] = 0
        self.seen = {e: {} for e in self.eng}
        self.dma_rr = {"sp": 0, "act": 0, "pool": 0}
        self.nwaits = 0
        self.nops = 0
        self._stack = []

    def _newsem(self, e):
        self.nsem += 1
        key = "%s@%d" % (e, self.nsem)
        self.sems[key] = self.nc.alloc_semaphore("s_%s_%d" % (e, self.nsem))
        self.cnt[key] = 0
        self.key[e] = key

    def sbuf(self, name, shape, dtype):
        self.nsem += 1
        name = "%s_%d" % (name, self.nsem)
        g = self.nc.sbuf_tensor(name, list(shape), dtype)
        t = g.__enter__()
        self._stack.append(g)
        return Buf(t, name)

    def psum(self, name, shape, dtype=F32):
        self.nsem += 1
        name = "%s_%d" % (name, self.nsem)
        g = self.nc.psum_tensor(name, list(shape), dtype)
        t = g.__enter__()
        self._stack.append(g)
        return Buf(t, name)

    def dram(self, name, shape, dtype, kind="Internal"):
        t = self.nc.dram_tensor(name, list(shape), dtype, kind=kind)
        return Buf(t, name)

    @staticmethod
    def views(buf, n):
        return [Buf(buf.t, "%s.%d" % (buf.name, i)) for i in range(n)]

    def mark(self):
        return len(self._stack)

    def release(self, mark):
        self.barrier()
        while len(self._stack) > mark:
            g = self._stack.pop()
            g.__exit__(None, None, None)

    def _need(self, e, toks, selfdep=True, wtoks=()):
        best = {}
        for k, v in toks:
            if k.startswith("pe@") and e == "pe":
                continue
            if best.get(k, 0) < v:
                best[k] = v
        for k, v in wtoks:
            if k.startswith("pe@") and e == "pe":
                continue
            if (not selfdep) and k == self.key[e]:
                continue
            if best.get(k, 0) < v:
                best[k] = v
        for k, v in best.items():
            if self.seen[e].get(k, 0) >= v:
                continue
            self.eng[e].wait_ge(self.sems[k], v)
            self.seen[e][k] = v
            self.nwaits += 1

    @staticmethod
    def _deps(reads, writes, acc):
        rt, wt = [], []
        for b in reads:
            rt.extend(b.lw.items())
            rt.extend(b.aw.items())
        for b in writes:
            wt.extend(b.lw.items())
            wt.extend(b.aw.items())
            wt.extend(b.rd.items())
        for b in acc:
            wt.extend(b.lw.items())
            wt.extend(b.rd.items())
        return rt, wt

    @staticmethod
    def _commit(tok, reads, writes, acc):
        k, v = tok
        for b in reads:
            if b.rd.get(k, 0) < v:
                b.rd[k] = v
        for b in writes:
            b.lw = {k: v}
            b.aw = {}
            b.rd = {}
        for b in acc:
            if b.aw.get(k, 0) < v:
                b.aw[k] = v

    def op(self, e, fn, reads=(), writes=(), acc=(), selfdep=True):
        rt, wt = self._deps(reads, writes, acc)
        self._need(e, rt, selfdep, wt)
        ins = fn(self.eng[e])
        self.nops += 1
        key = self.key[e]
        self.cnt[key] += 1
        ins.then_inc(self.sems[key], 1)
        self._commit((key, self.cnt[key]), reads, writes, acc)
        return ins

    def mm(self, out_ap, lhsT, rhs, start, stop, reads=(), writes=(), acc=()):
        return self.op("pe", lambda e: e.matmul(out_ap, lhsT, rhs, start=start, stop=stop,
                                                skip_group_check=True), reads, writes, acc)

    def dma(self, q, out_ap, in_ap, reads=(), writes=(), acc=(), **kw):
        key = "d%s%d" % (q, self.dma_rr[q])
        self.dma_rr[q] = (self.dma_rr[q] + 1) % self.NDMA
        rt, wt = self._deps(reads, writes, acc)
        toks = rt + wt
        if self.cnt[key] > 0:
            toks.append((key, self.cnt[key]))
        self._need(q, toks)
        self.cnt[key] += 16
        ins = self.eng[q].dma_start(out=out_ap, in_=in_ap, **kw)
        ins.then_inc(self.sems[key], 16)
        self.nops += 1
        self._commit((key, self.cnt[key]), reads, writes, acc)
        return ins

    def barrier(self):
        toks = [(k, v) for k, v in self.cnt.items() if v > 0]
        for e in self.eng:
            self._need(e, toks)
        for e in list(self.eng):
            if self.cnt[self.key[e]] > 24000:
                self._newsem(e)

    def finish(self):
        toks = [(k, v) for k, v in self.cnt.items() if v > 0]
        self._need("sp", toks)


def _na_tiles():
    uniq = {}
    tmap = {}
    for t in range(16):
        lo, hi = 10 ** 9, -1
        for b in range(2):
            r0 = min(max(2 * t + b - 4, 0), 24)
            lo = min(lo, r0 // 2)
            hi = max(hi, (r0 + 7) // 2)
        for j in range(lo, hi + 1):
            pat = []
            for a in range(2):
                for b in range(2):
                    qr = 2 * t + b
                    kr = 2 * j + a
                    r0 = min(max(qr - 4, 0), 24)
                    pat.append((kr - qr) if (r0 <= kr < r0 + 8) else None)
            pat = tuple(pat)
            if pat not in uniq:
                uniq[pat] = len(uniq)
            tmap[(t, j)] = uniq[pat]
    pats = [None] * len(uniq)
    for p, i in uniq.items():
        pats[i] = p
    return pats, tmap


_NA_PATS, _NA_TMAP = _na_tiles()
NU = len(_NA_PATS)


def _na_gather_index():
    idx = np.full((NU, 128, 128), 465, np.int64)
    qc = np.arange(64)
    cstart = np.clip(qc - 8, 0, 48)
    kc = np.arange(64)
    col_in = (kc[:, None] >= cstart[None, :]) & (kc[:, None] < cstart[None, :] + 16)
    cidx = np.clip(kc[:, None] - qc[None, :], -15, 15) + 15
    for u, pat in enumerate(_NA_PATS):
        for a in range(2):
            for b in range(2):
                dr = pat[a * 2 + b]
                if dr is None:
                    continue
                blk = np.where(col_in, (dr + 7) * 31 + cidx, 465)
                idx[u, a * 64:(a + 1) * 64, b * 64:(b + 1) * 64] = blk
    return idx


def _consts():
    c = np.zeros((128, 576), np.float32)
    c[:, 0:128] = np.eye(128, dtype=np.float32)
    blk = np.zeros((128, 128), np.float32)
    blk[0:64, 0:64] = 1.0
    blk[64:128, 64:128] = 1.0
    c[:, 128:256] = blk
    rt = np.zeros((128, 128), np.float32)
    for i in range(64):
        rt[2 * i + 1, 2 * i] = -1.0
        rt[2 * i, 2 * i + 1] = 1.0
    c[:, 256:384] = rt
    sidx = np.arange(64)[:, None]
    tidx = np.arange(64)[None, :]
    mf = (sidx <= tidx).astype(np.float32)
    mb = (sidx >= tidx).astype(np.float32)
    c[:, 384:448] = np.concatenate([mf, mf], 0)
    c[:, 448:512] = np.concatenate([mb, mb], 0)
    p = np.arange(128)[:, None] % 16
    t16 = np.arange(16)[None, :]
    c[:, 512:528] = (p <= t16).astype(np.float32)
    c[:, 528:544] = (p >= t16).astype(np.float32)
    t = np.arange(2048)
    row = (t // 64).astype(np.float32)
    col = (t % 64).astype(np.float32)
    inv = np.power(np.float32(10000.0), (-2.0 * np.arange(16, dtype=np.float32) / np.float32(32.0))).astype(np.float32)
    ang = np.concatenate([row[:, None] * inv[None, :], col[:, None] * inv[None, :]], -1).astype(np.float32)
    cos = np.cos(ang).astype(np.float32)
    sin = np.sin(ang).astype(np.float32)
    cosf = np.repeat(cos, 2, axis=1).T
    sinf = np.repeat(sin, 2, axis=1).T
    rope = np.concatenate([np.concatenate([cosf, cosf], 0), np.concatenate([sinf, sinf], 0)], 1).astype(np.float32)
    return c, np.ascontiguousarray(rope)


def build(nlayers=2, nseq=2, stop_after=None, dbg=False):
    nc = bass.Bass("TRN2", target_bir_lowering=False)
    s = Sch(nc)

    def inp(name, shape, dt=F32):
        return s.dram(name, shape, dt, kind="ExternalInput")

    x_in = inp("x", [2, 2048, 1024])
    ctx_in = inp("ctx", [2, 256, 1024])
    cvec = inp("cvec", [3, 1024])
    w_mod = inp("w_mod", [2, 1024, 6144])
    b_mod = inp("b_mod", [2, 6144])
    n1g = inp("norm1_g", [2, 1024])
    n2g = inp("norm2_g", [2, 1024])
    w_in = inp("w_in", [2, 1024, 3600])
    lbl = inp("hgrn_lb_logits", [4, 256])
    hgg = inp("hgrn_norm_g", [2, 64])
    mgb = inp("mlstm_gate_b", [2, 16])
    mgg = inp("mlstm_norm_g", [2, 64])
    qng = inp("gqa_qnorm_g", [2, 64])
    kng = inp("gqa_knorm_g", [2, 64])
    nab = inp("na_bias", [2, 4 * NU, 128, 128])
    w_out = inp("w_out", [2, 1024, 1024])
    w1 = inp("w_mlp1", [2, 1024, 4096])
    w2 = inp("w_mlp2", [2, 4096, 1024])
    fng = inp("final_norm_g", [1, 1024])
    cst = inp("consts", [128, 576])
    ropec = inp("rope", [128, 4096])
    y = s.dram("y", [2, 2048, 1024], F32, kind="ExternalOutput")

    okind = "ExternalOutput" if dbg else "Internal"
    xs = s.dram("xs", [128, 8, NT], F32, kind=okind)
    zq_h = s.dram("zq_h", [256, NT], F32, kind=okind)
    zog_h = s.dram("zog_h", [256, NT], F32, kind=okind)
    zlf_h = s.dram("zlf_h", [2, 256, NT], F32, kind=okind)
    zkk_h = s.dram("zkk_h", [2, 256, NT], F32, kind=okind)
    zq_m = s.dram("zq_m", [256, NT], F32, kind=okind)
    zk_m = s.dram("zk_m", [256, NT], F32, kind=okind)
    zog_m = s.dram("zog_m", [256, NT], F32, kind=okind)
    zg_m = s.dram("zg_m", [16, NT], F32, kind=okind)
    zq_g = s.dram("zq_g", [256, NT], F32, kind=okind)
    zk_g = s.dram("zk_g", [2, 128, NT], F32, kind=okind)
    zq_n = s.dram("zq_n", [256, NT], BF16, kind=okind)
    zk_n = s.dram("zk_n", [256, NT], BF16, kind=okind)
    v_tok = s.dram("v_tok", [NT, 896], BF16, kind=okind)
    cat_dbg = s.dram("cat_dbg", [128, 8, NT], BF16, kind=okind) if dbg else None
    h_dbg = s.dram("h_dbg", [128, 8, NT], BF16, kind=okind) if dbg else None
    mod_dbg = s.dram("mod_dbg", [128, 2 * 48 * 3], F32, kind=okind) if dbg else None

    if dbg:
        dbgQb = s.dram("dbgQ", [2, 3, 128, NT], BF16, kind=okind)
        dbgPb = s.dram("dbgP", [128, NT + 1], F32, kind=okind)
        dbgQ = dbgQb
        dbgP = dbgPb
    NSL = True

    cs = s.sbuf("cs", [128, 576], F32)
    identb = s.sbuf("identb", [128, 128], BF16)
    onesb = s.sbuf("onesb", [128, 128], BF16)
    blkb = s.sbuf("blkb", [128, 128], BF16)
    bd1 = s.sbuf("bd1", [128, 128], BF16)
    hT = s.sbuf("hT", [128, 8, NT], BF16)
    hTb = Sch.views(hT, 5)
    modT = s.sbuf("modT", [128, 2, 48, 3], F32)
    AT = s.sbuf("AT", [128, 2, 2, 8, 3], F32)
    gT = s.sbuf("gT", [128, 5, 8], F32)
    gcol = s.sbuf("gcol", [128, 2, 4], F32)
    lb = s.sbuf("lb", [128, 2, 2, 2], F32)
    oml = s.sbuf("oml", [128, 2, 2, 2], F32)
    noml = s.sbuf("noml", [128, 2, 2, 2], F32)
    mgbT = s.sbuf("mgbT", [16, 2], F32)

    identf = cs[:, 0:128]
    blkf = cs[:, 128:256]
    ropeRT = cs[:, 256:384]
    maskFB = {64: [cs[:, 384:448], cs[:, 448:512]], 16: [cs[:, 512:528], cs[:, 528:544]]}

    s.dma("sp", cs[:, :], cst[:, :], writes=[cs])
    s.op("dve", lambda e: e.tensor_copy(out=identb[:, :], in_=identf), reads=[cs], writes=[identb])
    s.op("dve", lambda e: e.tensor_copy(out=blkb[:, :], in_=blkf), reads=[cs], writes=[blkb])
    s.op("dve", lambda e: e.tensor_copy(out=bd1[:, :], in_=blkf), reads=[cs], writes=[bd1])
    s.op("pool", lambda e: e.memset(onesb[:, :], 1.0), writes=[onesb])

    def prologue():
        m0 = s.mark()
        scT = s.sbuf("scT", [128, 8, 3], F32)
        bmT = s.sbuf("bmT", [128, 2, 48], F32)
        lg = s.sbuf("lg", [128, 4, 2], F32)
        wm = [s.sbuf("wm%d" % i, [128, 8, 512], F32) for i in range(2)]
        pm = s.psum("pm_mod", [128, 512], F32)
        for r in range(3):
            s.dma("sp", scT[:, :, r], cvec[r:r + 1, :].rearrange("o (k p) -> p (o k)", p=128), acc=[scT],
                  allow_slow_non_contiguous=NSL)
        for l in range(2):
            s.dma("sp", bmT[:, l, :], b_mod[l:l + 1, :].rearrange("o (j p) -> p (o j)", p=128), acc=[bmT],
                  allow_slow_non_contiguous=NSL)
        gsrc = [n1g[0:1, :], n1g[1:2, :], n2g[0:1, :], n2g[1:2, :], fng[0:1, :]]
        for i, g in enumerate(gsrc):
            s.dma("sp", gT[:, i, :], g.rearrange("o (k p) -> p (o k)", p=128), acc=[gT], allow_slow_non_contiguous=NSL)
        for r in range(4):
            s.dma("sp", lg[:, r, :], lbl[r:r + 1, :].rearrange("o (c p) -> p (o c)", p=128), acc=[lg],
                  allow_slow_non_contiguous=NSL)
        for l in range(2):
            for i, g in enumerate([hgg, mgg, qng, kng]):
                for hh in range(2):
                    s.dma("sp", gcol[64 * hh:64 * hh + 64, l, i:i + 1], g[l:l + 1, :].rearrange("o d -> d o"),
                          acc=[gcol], allow_slow_non_contiguous=NSL)
            s.dma("sp", mgbT[:, l:l + 1], mgb[l:l + 1, :].rearrange("o g -> g o"), acc=[mgbT],
                  allow_slow_non_contiguous=NSL)
        s.op("act", lambda e: e.activation(out=scT[:, :, :], in_=scT[:, :, :], func=AF.Silu), reads=[scT], writes=[scT])
        ex = s.sbuf("ex", [128, 4, 2], F32)
        den = s.sbuf("den", [128, 2, 2], F32)
        s.op("act", lambda e: e.activation(out=ex[:, :, :], in_=lg[:, :, :], func=AF.Exp), reads=[lg], writes=[ex])
        s.op("dve", lambda e: e.tensor_tensor(out=den[:, :, :], in0=ex[:, 0:2, :], in1=ex[:, 2:4, :], op=ALU.add),
             reads=[ex], writes=[den])
        s.op("dve", lambda e: e.reciprocal(out=den[:, :, :], in_=den[:, :, :]), reads=[den], writes=[den])
        s.op("pool", lambda e: e.memset(lb[:, 0, :, :], 0.0), acc=[lb])
        s.op("dve", lambda e: e.tensor_tensor(out=lb[:, 1, :, :], in0=ex[:, 2:4, :], in1=den[:, :, :], op=ALU.mult),
             reads=[ex, den], acc=[lb])
        s.op("dve", lambda e: e.tensor_scalar(out=oml[:, :, :, :], in0=lb[:, :, :, :], scalar1=-1.0, scalar2=1.0,
                                              op0=ALU.mult, op1=ALU.add), reads=[lb], writes=[oml])
        s.op("dve", lambda e: e.tensor_scalar(out=noml[:, :, :, :], in0=lb[:, :, :, :], scalar1=1.0, scalar2=-1.0,
                                              op0=ALU.mult, op1=ALU.add), reads=[lb], writes=[noml])
        it = 0
        for l in range(2):
            for grp in range(12):
                W = wm[it % 2]
                it += 1
                s.dma("sp", W[:, :, :], w_mod[l, :, grp * 512:(grp + 1) * 512].rearrange("(k p) n -> p k n", p=128),
                      writes=[W])
                for m in range(4):
                    idx = grp * 4 + m
                    for k in range(8):
                        s.mm(pm[:, idx * 3:idx * 3 + 3], W[:, k, m * 128:(m + 1) * 128], scT[:, k, :], k == 0, k == 7,
                             reads=[W, scT], acc=[pm])
            s.op("dve", lambda e: e.tensor_tensor(
                out=modT[:, l, :, :], in0=pm[:, 0:144].rearrange("p (j r) -> p j r", r=3),
                in1=bmT[:, l, :].unsqueeze(2).to_broadcast([128, 48, 3]), op=ALU.add),
                reads=[pm, bmT], acc=[modT])
        for l in range(2):
            for w in range(2):
                for k in range(8):
                    j = (1 if w == 0 else 4) * 8 + k
                    s.op("dve", lambda e: e.tensor_scalar(out=AT[:, l, w, k, :], in0=modT[:, l, j, :], scalar1=1.0,
                                                          scalar2=gT[:, w * 2 + l, k:k + 1], op0=ALU.add, op1=ALU.mult),
                         reads=[modT, gT], acc=[AT])
        if dbg:
            s.dma("sp", mod_dbg[:, :], modT[:, :, :, :].rearrange("p l j r -> p (l j r)"), reads=[modT], writes=[mod_dbg])
        s.release(m0)

    def norm_block(X, xoff, n, st, l, w, col, sq, pss, rs, tmps, j):
        s.op("act", lambda e: e.activation(out=sq[:, :, 0:n], in_=X[:, :, xoff:xoff + n], func=AF.Square),
             reads=[X], writes=[sq])
        for k in range(8):
            s.mm(pss[:, 0:n], onesb[:, :], sq[:, k, 0:n], k == 0, k == 7, reads=[sq, onesb],
                 writes=[pss] if k == 0 else [], acc=[] if k == 0 else [pss])
        s.op("act", lambda e: e.activation(out=rs[:, 0:n], in_=pss[:, 0:n], func=AF.Ln, scale=1.0 / 1024, bias=EPS),
             reads=[pss], writes=[rs])
        s.op("act", lambda e: e.activation(out=rs[:, 0:n], in_=rs[:, 0:n], func=AF.Exp, scale=-0.5), reads=[rs], writes=[rs])
        sh = 0 if w == 0 else 3
        for k in range(8):
            T = tmps[k % len(tmps)]
            s.op("dve", lambda e: e.scalar_tensor_tensor(out=T[:, 0:n], in0=X[:, k, xoff:xoff + n],
                                                         scalar=AT[:, l, w, k, col:col + 1], in1=rs[:, 0:n],
                                                         op0=ALU.mult, op1=ALU.mult), reads=[X, rs, AT], writes=[T])
            s.op("act", lambda e: e.activation(out=hT[:, k, st:st + n], in_=T[:, 0:n], func=AF.Identity,
                                               bias=modT[:, l, sh * 8 + k, col:col + 1], scale=1.0),
                 reads=[T, modT], acc=[hTb[j]], selfdep=False)

    def p1(b, l):
        m0 = s.mark()
        xb = [s.sbuf("xb%d" % i, [128, 8, 512], F32) for i in range(2)]
        sq = [s.sbuf("sq%d" % i, [128, 8, 512], BF16) for i in range(2)]
        rs = [s.sbuf("rs%d" % i, [128, 512], F32) for i in range(2)]
        tmps = [s.sbuf("tmp%d" % i, [128, 512], F32) for i in range(4)]
        pss = [s.psum("pss%d" % i, [128, 512], F32) for i in range(2)]
        if l == 0:
            xin = [s.sbuf("xin%d" % i, [128, 1024], F32) for i in range(3)]
            pst = [s.psum("pst%d" % i, [128, 512], F32) for i in range(4)]
        cnt = 0
        for j, (st, n) in enumerate(BLOCKS):
            X = xb[j % 2]
            if l == 0:
                for ti in range(n // 128):
                    tok0 = st + ti * 128
                    src = ctx_in[b, tok0:tok0 + 128, :] if tok0 < 256 else x_in[b, tok0 - 256:tok0 - 128, :]
                    xi = xin[cnt % 3]
                    s.dma("sp", xi[:, :], src, writes=[xi])
                    for half in range(2):
                        pt = pst[(cnt * 2 + half) % 4]
                        for kk in range(4):
                            k = half * 4 + kk
                            s.op("pe", lambda e: e.transpose(out=pt[:, kk * 128:(kk + 1) * 128],
                                                             in_=xi[:, k * 128:(k + 1) * 128], identity=identf),
                                 reads=[xi, cs], writes=[pt] if kk == 0 else [], acc=[] if kk == 0 else [pt])
                        s.op("act", lambda e: e.activation(
                            out=X[:, half * 4:(half + 1) * 4, ti * 128:(ti + 1) * 128],
                            in_=pt[:, :].rearrange("p (k t) -> p k t", t=128), func=AF.Copy),
                            reads=[pt], writes=[X] if (ti == 0 and half == 0) else [],
                            acc=[] if (ti == 0 and half == 0) else [X], selfdep=False)
                    cnt += 1
                s.dma("sp", xs[:, :, st:st + n], X[:, :, 0:n], reads=[X], acc=[xs])
            else:
                s.dma("sp", X[:, :, 0:n], xs[:, :, st:st + n], reads=[xs], writes=[X])
            col = 2 if st < 256 else b
            norm_block(X, 0, n, st, l, 0, col, sq[j % 2], pss[j % 2], rs[j % 2], tmps, j)
        if dbg:
            s.dma("sp", h_dbg[:, :, :], hT[:, :, :], reads=hTb, writes=[h_dbg])
        s.release(m0)

    GROUPS = [
        (0, 512, [("f", "hq", 0, 128, 0), ("f", "hq", 128, 128, 1), ("v", 256, 256, 0)]),
        (512, 512, [("f", "hog", 0, 128, 0), ("f", "hog", 128, 128, 1), ("f", "hf0", 256, 128, 0), ("f", "hf0", 384, 128, 1)]),
        (1024, 512, [("f", "hf1", 0, 128, 0), ("f", "hf1", 128, 128, 1), ("f", "mq", 256, 128, 0), ("f", "mq", 384, 128, 1)]),
        (1536, 512, [("f", "mk", 0, 128, 0), ("f", "mk", 128, 128, 1), ("v", 256, 256, 256)]),
        (2048, 272, [("f", "mog", 0, 128, 0), ("f", "mog", 128, 128, 1), ("f", "mg", 256, 16, 0)]),
        (2320, 512, [("f", "gq", 0, 128, 0), ("f", "gq", 128, 128, 1), ("f", "gk", 256, 64, 0), ("f", "gk", 320, 64, 1),
                     ("v", 384, 128, 512)]),
        (2832, 512, [("f", "nq", 0, 128, 0), ("f", "nq", 128, 128, 1), ("f", "nk", 256, 128, 0), ("f", "nk", 384, 128, 1)]),
        (3344, 256, [("v", 0, 256, 640)]),
    ]

    def p2(b, l):
        m0 = s.mark()
        wst = [s.sbuf("wst%d" % i, [128, 8, 512], F32) for i in range(2)]
        wbf = [s.sbuf("wbf%d" % i, [128, 8, 512], BF16) for i in range(2)]
        stg = [s.sbuf("stg%d" % i, [128, 512], F32) for i in range(8)]
        stb = [s.sbuf("stb%d" % i, [128, 512], BF16) for i in range(4)]
        ps = [s.psum("p2ps%d" % i, [128, 512], F32) for i in range(6)]
        st_i = [0]
        sb_i = [0]
        ps_i = [0]

        def nstg():
            st_i[0] += 1
            return stg[st_i[0] % 8]

        def nstb():
            sb_i[0] += 1
            return stb[sb_i[0] % 4]

        def load(gi):
            c0, w, _ = GROUPS[gi]
            s.dma("sp", wst[gi % 2][:, :, 0:w], w_in[l, :, c0:c0 + w].rearrange("(k p) n -> p k n", p=128),
                  writes=[wst[gi % 2]])

        def cast(gi):
            c0, w, _ = GROUPS[gi]
            if gi % 2 == 0:
                s.op("act", lambda e: e.activation(out=wbf[gi % 2][:, :, 0:w], in_=wst[gi % 2][:, :, 0:w], func=AF.Copy),
                     reads=[wst[gi % 2]], writes=[wbf[gi % 2]])
            else:
                s.op("dve", lambda e: e.tensor_copy(out=wbf[gi % 2][:, :, 0:w], in_=wst[gi % 2][:, :, 0:w]),
                     reads=[wst[gi % 2]], writes=[wbf[gi % 2]])

        load(0)
        cast(0)
        load(1)
        for gi, (c0, w, jobs) in enumerate(GROUPS):
            if gi + 1 < len(GROUPS):
                cast(gi + 1)
            if gi + 2 < len(GROUPS):
                load(gi + 2)
            W = wbf[gi % 2]
            for job in jobs:
                if job[0] == "v":
                    _, off, ncol, vdst = job
                    for ti in range(18):
                        P = ps[ps_i[0] % 6]
                        ps_i[0] += 1
                        jb = 0 if ti < 2 else 1 + (ti - 2) // 4
                        for k in range(8):
                            s.mm(P[:, 0:ncol], hT[:, k, ti * 128:(ti + 1) * 128], W[:, k, off:off + ncol], k == 0, k == 7,
                                 reads=[hTb[jb], W], writes=[P] if k == 0 else [], acc=[] if k == 0 else [P])
                        B_ = nstb()
                        if ti % 2 == 0:
                            s.op("act", lambda e: e.activation(out=B_[:, 0:ncol], in_=P[:, 0:ncol], func=AF.Copy),
                                 reads=[P], writes=[B_])
                        else:
                            s.op("dve", lambda e: e.tensor_copy(out=B_[:, 0:ncol], in_=P[:, 0:ncol]), reads=[P], writes=[B_])
                        s.dma("sp", v_tok[ti * 128:(ti + 1) * 128, vdst:vdst + ncol], B_[:, 0:ncol], reads=[B_], acc=[v_tok])
                    continue
                _, kind, off, m, pc = job
                for j, (st, n) in enumerate(BLOCKS):
                    P = ps[ps_i[0] % 6]
                    ps_i[0] += 1
                    if kind == "gk":
                        for half in range(2):
                            for k in range(8):
                                s.mm(P[64 * half:64 * half + 64, 0:n], W[:, k, off:off + 64], hT[:, k, st:st + n],
                                     k == 0, k == 7, reads=[hTb[j], W],
                                     writes=[P] if (k == 0 and half == 0) else [], acc=[] if (k == 0 and half == 0) else [P])
                        mm_ = 128
                    else:
                        for k in range(8):
                            s.mm(P[0:m, 0:n], W[:, k, off:off + m], hT[:, k, st:st + n], k == 0, k == 7,
                                 reads=[hTb[j], W], writes=[P] if k == 0 else [], acc=[] if k == 0 else [P])
                        mm_ = m
                    rows = slice(pc * 128, pc * 128 + 128)
                    if kind in ("hq", "hog", "mog", "mq", "mk", "gq", "gk"):
                        S_ = nstg()
                        if kind in ("hq", "hog"):
                            s.op("act", lambda e: e.activation(out=S_[:, 0:n], in_=P[:, 0:n], func=AF.Silu), reads=[P], writes=[S_])
                        elif kind == "mog":
                            s.op("act", lambda e: e.activation(out=S_[:, 0:n], in_=P[:, 0:n], func=AF.Sigmoid), reads=[P], writes=[S_])
                        elif kind == "mk":
                            s.op("dve", lambda e: e.tensor_scalar(out=S_[:, 0:n], in0=P[:, 0:n], scalar1=0.125, scalar2=None,
                                                                  op0=ALU.mult), reads=[P], writes=[S_])
                        else:
                            s.op("dve", lambda e: e.tensor_copy(out=S_[:, 0:n], in_=P[:, 0:n]), reads=[P], writes=[S_])
                        dst = {"hq": zq_h, "hog": zog_h, "mog": zog_m, "mq": zq_m, "mk": zk_m, "gq": zq_g}.get(kind)
                        if kind == "gk":
                            s.dma("sp", zk_g[pc, :, st:st + n], S_[:, 0:n], reads=[S_], acc=[zk_g])
                        else:
                            s.dma("sp", dst[rows, st:st + n], S_[:, 0:n], reads=[S_], acc=[dst])
                    elif kind in ("hf0", "hf1"):
                        d = 0 if kind == "hf0" else 1
                        SG = nstg()
                        FG = nstg()
                        KK = nstg()
                        s.op("act", lambda e: e.activation(out=SG[:, 0:n], in_=P[:, 0:n], func=AF.Sigmoid), reads=[P], writes=[SG])
                        s.op("dve", lambda e: e.tensor_scalar(out=FG[:, 0:n], in0=SG[:, 0:n], scalar1=oml[:, l, d, pc:pc + 1],
                                                              scalar2=lb[:, l, d, pc:pc + 1], op0=ALU.mult, op1=ALU.add),
                             reads=[SG, oml, lb], writes=[FG])
                        s.op("act", lambda e: e.activation(out=FG[:, 0:n], in_=FG[:, 0:n], func=AF.Ln), reads=[FG], writes=[FG])
                        s.op("dve", lambda e: e.tensor_scalar(out=KK[:, 0:n], in0=SG[:, 0:n], scalar1=noml[:, l, d, pc:pc + 1],
                                                              scalar2=oml[:, l, d, pc:pc + 1], op0=ALU.mult, op1=ALU.add),
                             reads=[SG, oml, noml], writes=[KK])
                        s.dma("sp", zlf_h[d, rows, st:st + n], FG[:, 0:n], reads=[FG], acc=[zlf_h])
                        s.dma("sp", zkk_h[d, rows, st:st + n], KK[:, 0:n], reads=[KK], acc=[zkk_h])
                    elif kind == "mg":
                        S1 = nstg()
                        S2 = nstg()
                        s.op("act", lambda e: e.activation(out=S1[0:16, 0:n], in_=P[0:16, 0:n], func=AF.Identity,
                                                           bias=mgbT[:, l:l + 1], scale=1.0), reads=[P, mgbT], writes=[S1])
                        s.op("act", lambda e: e.activation(out=S2[0:16, 0:n], in_=P[0:16, 0:n], func=AF.Sigmoid,
                                                           bias=mgbT[:, l:l + 1], scale=1.0), reads=[P, mgbT], writes=[S2])
                        s.op("act", lambda e: e.activation(out=S2[0:16, 0:n], in_=S2[0:16, 0:n], func=AF.Ln), reads=[S2], writes=[S2])
                        s.dma("sp", zg_m[0:8, st:st + n], S1[0:8, 0:n], reads=[S1], acc=[zg_m])
                        s.dma("sp", zg_m[8:16, st:st + n], S2[8:16, 0:n], reads=[S2], acc=[zg_m])
                    elif kind in ("nq", "nk"):
                        B_ = nstb()
                        s.op("act", lambda e: e.activation(out=B_[:, 0:n], in_=P[:, 0:n], func=AF.Copy), reads=[P], writes=[B_])
                        dst = zq_n if kind == "nq" else zk_n
                        s.dma("sp", dst[rows, st:st + n], B_[:, 0:n], reads=[B_], acc=[dst])
                    else:
                        raise ValueError(kind)
        s.release(m0)

    def head_norm(o, gate, gidx, l, chunk, blocks, sqs, pn, rss, tms):
        for j in blocks:
            st, n = BLOCKS[j]
            SQ = sqs[j % 2]
            R = rss[j % 2]
            T = tms[j % 2]
            s.op("act", lambda e: e.activation(out=SQ[:, 0:n], in_=o[:, st:st + n], func=AF.Square), reads=[o], writes=[SQ])
            s.mm(pn[:, 0:n], blkb[:, :], SQ[:, 0:n], True, True, reads=[blkb, SQ], writes=[pn])
            s.op("act", lambda e: e.activation(out=R[:, 0:n], in_=pn[:, 0:n], func=AF.Sqrt, scale=1.0 / 64, bias=EPS),
                 reads=[pn], writes=[R])
            s.op("dve", lambda e: e.reciprocal(out=R[:, 0:n], in_=R[:, 0:n]), reads=[R], writes=[R])
            s.op("dve", lambda e: e.scalar_tensor_tensor(out=T[:, 0:n], in0=o[:, st:st + n], scalar=gcol[:, l, gidx:gidx + 1],
                                                         in1=R[:, 0:n], op0=ALU.mult, op1=ALU.mult),
                 reads=[o, R, gcol], writes=[T])
            s.op("dve", lambda e: e.tensor_tensor(out=hT[:, chunk, st:st + n], in0=T[:, 0:n], in1=gate[:, st:st + n],
                                                  op=ALU.mult), reads=[T, gate], acc=[hTb[j]], selfdep=False)

    def recur(b, l, kind):
        m0 = s.mark()
        NV = 1 if kind == "h" else 2
        L = 16 if kind == "h" else 64
        NCH = NT // L
        CTXN = NCTX // L
        HB = L
        SR = 2 * L
        PG = min(128 // SR, 3)
        last = (l == nlayers - 1)
        blocks = [1, 2, 3, 4] if last else [0, 1, 2, 3, 4]
        PTR = [s.psum("ptr%d" % d, [128, 1024], BF16) for d in range(2)]
        PSC = [s.psum("psc%d" % d, [128, 512], F32) for d in range(2)]
        PSO = [s.psum("pso%d" % d, [128, 512], F32) for d in range(2)]
        PST = [s.psum("pstt%d" % d, [128, 512], F32) for d in range(2)]
        pn = PSC[0]
        lfs = [s.sbuf("r_lf%d" % d, [128, NT], F32) for d in range(2)]
        PP = s.sbuf("r_PP", [128, NT + 1], F32)
        kks = [s.sbuf("r_kk%d" % d, [128, NT], F32) for d in range(2)]
        qs = s.sbuf("r_qs", [128, NT], F32)
        igbs = [s.sbuf("r_ig%d" % d, [128, NT], F32) for d in range(2)] if kind == "m" else None
        Q = [s.sbuf("r_Q%d" % d, [128, NT], BF16) for d in range(2)]
        K = [s.sbuf("r_K%d" % d, [128, NCH, 2 * L], BF16) for d in range(2)]
        KZ = []
        KN = [s.sbuf("r_KN%d" % d, [128, NT], BF16) for d in range(2)]
        Vbd = s.sbuf("r_Vbd", [128, NCH // PG, 128], BF16)
        Vp = s.sbuf("r_Vp", [128, NCH // PG, 128], BF16)
        Rn = [s.sbuf("r_Rn%d" % d, [128, NCH], F32) for d in range(2)]
        btm = [s.sbuf("r_bt%d" % i, [128, 512], F32) for i in range(10)]
        bti = [0]

        def ntmp():
            bti[0] += 1
            return btm[bti[0] % 10]

        def blk_of(c):
            t = c * L
            return 0 if t < 256 else 1 + (t - 256) // 512
        G = [s.sbuf("r_G%d" % d, [128, NCH], F32) for d in range(2)]
        W32 = [[s.sbuf("r_W%d%d" % (d, v), [128, 64], F32) for v in range(NV)] for d in range(2)]
        Wbf = [[s.sbuf("r_Wb%d%d" % (d, v), [128, 64], BF16) for v in range(NV)] for d in range(2)]
        kts = [s.sbuf("r_kt%d" % i, [128, 128], BF16) for i in range(4)]
        pms = [s.sbuf("r_pm%d" % i, [128, L], BF16) for i in range(4)]
        for pmb in pms:
            s.op("pool", lambda e: e.memset(pmb[:, :], 0.0), writes=[pmb])
        sqs = [s.sbuf("r_sq%d" % i, [128, 512], BF16) for i in range(2)]
        rss = [s.sbuf("r_rs%d" % i, [128, 512], F32) for i in range(2)]
        tms = [s.sbuf("r_tm%d" % i, [128, 512], F32) for i in range(2)]
        Qv = [Sch.views(Q[d], 5) for d in range(2)]
        Kv = [Sch.views(K[d], 5) for d in range(2)]
        KNv = [Sch.views(KN[d], 5) for d in range(2)]
        for d in range(2):
            s.op("pool", lambda e: e.memset(K[d][:, :, :], 0.0), writes=Kv[d])
        if kind == "h":
            accs = [[lfs[0]], [lfs[1]]]
        else:
            accs = [[lfs[0], igbs[0]], [lfs[1], igbs[1]]]
        zq = zq_h if kind == "h" else zq_m
        zog = zog_h if kind == "h" else zog_m
        vbase = 0 if kind == "h" else 256
        gidx = 0 if kind == "h" else 1
        order = [list(range(NCH)), list(range(CTXN - 1, -1, -1)) + list(range(NCH - 1, CTXN - 1, -1))]
        slot = 0
        for pc in range(2):
            rows = slice(pc * 128, pc * 128 + 128)
            vcol = vbase + pc * 128
            s.dma("sp", qs[:, :], zq[rows, :], reads=[zq], writes=[qs])
            for d in range(2):
                lf, kk = lfs[d], kks[d]
                if kind == "h":
                    s.dma("sp", lf[:, :], zlf_h[d, rows, :], reads=[zlf_h], writes=[lf])
                    s.dma("sp", kk[:, :], zkk_h[d, rows, :], reads=[zkk_h], writes=[kk])
                else:
                    igb = igbs[d]
                    for hh in range(2):
                        h = 2 * pc + hh
                        s.dma("sp", lf[64 * hh:64 * hh + 64, :], zg_m[8 + 4 * d + h:9 + 4 * d + h, :].partition_broadcast(64),
                              reads=[zg_m], writes=[lf] if hh == 0 else [], acc=[] if hh == 0 else [lf])
                        s.dma("sp", igb[64 * hh:64 * hh + 64, :], zg_m[4 * d + h:4 * d + h + 1, :].partition_broadcast(64),
                              reads=[zg_m], writes=[igb] if hh == 0 else [], acc=[] if hh == 0 else [igb])
                    s.dma("sp", kk[:, :], zk_m[rows, :], reads=[zk_m], writes=[kk])
            if pc == 0:
                s.op("pool", lambda e: e.memset(Vbd[:, :, :], 0.0), writes=[Vbd])
            for g in range(PG):
                for hh in range(2):
                    s.dma("sp", Vbd[g * SR + HB * hh:g * SR + HB * hh + L, :, 64 * hh:64 * hh + 64],
                          v_tok[:, vcol + 64 * hh:vcol + 64 * hh + 64].rearrange("(cq g p) d -> g p cq d", g=PG, p=L)[g],
                          reads=[v_tok], acc=[Vbd])
                s.dma("sp", Vp[g * SR:g * SR + L, :, :],
                      v_tok[:, vcol:vcol + 128].rearrange("(cq g p) d -> g p cq d", g=PG, p=L)[g],
                      reads=[v_tok], writes=[Vp] if g == 0 else [], acc=[] if g == 0 else [Vp])
            for d in range(2):
                sg = 1.0 if d == 0 else -1.0
                lf, kk = lfs[d], kks[d]
                igb = igbs[d] if kind == "m" else None
                s.op("pool", lambda e: e.memset(PP[:, 0:1], 0.0), writes=[PP])
                s.op("dve", lambda e: e.tensor_tensor_scan(out=PP[:, 1:NT + 1], data0=lf[:, :], data1=lf[:, :], initial=0.0,
                                                           op0=ALU.add, op1=ALU.bypass), reads=[lf], acc=[PP])
                Rm = PP[:, 0:NT].rearrange("p (c l) -> p c l", l=L)[:, :, L // 2]
                if d == 0:
                    s.op("dve", lambda e: e.tensor_copy(out=Rn[d][:, 0:NCH - 1], in_=Rm[:, 1:NCH]), reads=[PP], writes=[Rn[d]])
                    s.op("dve", lambda e: e.tensor_copy(out=Rn[d][:, NCH - 1:NCH], in_=Rm[:, NCH - 1:NCH]), reads=[PP], acc=[Rn[d]])
                    s.op("dve", lambda e: e.tensor_tensor(out=G[d][:, :], in0=Rn[d][:, :], in1=Rm, op=ALU.subtract),
                         reads=[Rn[d], PP], writes=[G[d]])
                else:
                    s.op("dve", lambda e: e.tensor_copy(out=Rn[d][:, 1:NCH], in_=Rm[:, 0:NCH - 1]), reads=[PP], writes=[Rn[d]])
                    s.op("dve", lambda e: e.tensor_copy(out=Rn[d][:, CTXN:CTXN + 1], in_=Rm[:, CTXN:CTXN + 1]), reads=[PP], acc=[Rn[d]])
                    s.op("dve", lambda e: e.tensor_tensor(out=Rn[d][:, 0:1], in0=Rm[:, NCH - 1:NCH], in1=PP[:, NT:NT + 1],
                                                          op=ALU.subtract), reads=[PP], acc=[Rn[d]])
                    s.op("dve", lambda e: e.tensor_tensor(out=G[d][:, :], in0=Rm, in1=Rn[d][:, :], op=ALU.subtract),
                         reads=[Rn[d], PP], writes=[G[d]])
                s.op("act", lambda e: e.activation(out=G[d][:, :], in_=G[d][:, :], func=AF.Exp), reads=[G[d]], writes=[G[d]])
                border = [0, 1, 2, 3, 4] if d == 0 else [0, 4, 3, 2, 1]
                for j in border:
                    st, n = BLOCKS[j]
                    c0, ncb = st // L, n // L
                    off = 1 if d == 0 else 0
                    PPs = PP[:, st + off:st + off + n].rearrange("p (c l) -> p c l", l=L)
                    T1, T2, T3, T4, T5 = [ntmp() for _ in range(5)]
                    v3 = lambda T: T[:, 0:n].rearrange("p (c l) -> p c l", l=L)
                    s.op("dve", lambda e: e.tensor_tensor(out=v3(T1), in0=PPs,
                                                          in1=Rm[:, c0:c0 + ncb].unsqueeze(2).to_broadcast([128, ncb, L]),
                                                          op=ALU.subtract), reads=[PP], writes=[T1])
                    s.op("act", lambda e: e.activation(out=T2[:, 0:n], in_=T1[:, 0:n], func=AF.Exp, scale=sg), reads=[T1], writes=[T2])
                    s.op("dve", lambda e: e.scalar_tensor_tensor(out=Q[d][:, st:st + n], in0=qs[:, st:st + n],
                                                                 scalar=(0.125 if kind == "h" else 1.0), in1=T2[:, 0:n],
                                                                 op0=ALU.mult, op1=ALU.mult),
                         reads=[qs, T2], acc=[Qv[d][j]], selfdep=False)
                    if kind == "m":
                        s.op("dve", lambda e: e.scalar_tensor_tensor(out=T3[:, 0:n], in0=T1[:, 0:n], scalar=-sg, in1=igb[:, st:st + n],
                                                                     op0=ALU.mult, op1=ALU.add), reads=[T1, igb], writes=[T3])
                        s.op("act", lambda e: e.activation(out=T3[:, 0:n], in_=T3[:, 0:n], func=AF.Exp), reads=[T3], writes=[T3])
                    else:
                        s.op("act", lambda e: e.activation(out=T3[:, 0:n], in_=T1[:, 0:n], func=AF.Exp, scale=-sg), reads=[T1], writes=[T3])
                    s.op("dve", lambda e: e.tensor_tensor(out=T4[:, 0:n], in0=kk[:, st:st + n], in1=T3[:, 0:n], op=ALU.mult),
                         reads=[kk, T3], writes=[T4])
                    for hh in range(2):
                        hs = slice(64 * hh, 64 * hh + 64)
                        if hh == 0:
                            s.op("act", lambda e: e.activation(out=K[d][hs, c0:c0 + ncb, L * hh:L * hh + L],
                                                               in_=T4[hs, 0:n].rearrange("p (c l) -> p c l", l=L), func=AF.Copy),
                                 reads=[T4], acc=[Kv[d][j]], selfdep=False)
                        else:
                            s.op("pool", lambda e: e.tensor_copy(out=K[d][hs, c0:c0 + ncb, L * hh:L * hh + L],
                                                                 in_=T4[hs, 0:n].rearrange("p (c l) -> p c l", l=L)),
                                 reads=[T4], acc=[Kv[d][j]], selfdep=False)
                    s.op("pool", lambda e: e.tensor_tensor(out=KN[d][:, st:st + n].rearrange("p (c l) -> p c l", l=L), in0=v3(T4),
                                                           in1=G[d][:, c0:c0 + ncb].unsqueeze(2).to_broadcast([128, ncb, L]),
                                                           op=ALU.mult), reads=[T4, G[d]], acc=[KNv[d][j]], selfdep=False)
                for v in range(NV):
                    s.op("pool", lambda e: e.memset(W32[d][v][:, :], 0.0), writes=[W32[d][v]])
            if dbg and stop_after == "hbuild":
                for d in range(2):
                    pass
                s.dma("sp", dbgP[:, :], PP[:, :], reads=[PP], writes=[dbgPb])
                s.release(m0)
                return
            seq = [(step, d) for step in range(NCH) for d in range(2)]
            slot0 = slot

            def phaseA(i):
                step, d = seq[i]
                c = order[d][step]
                csl = slice(c * L, c * L + L)
                sl = (slot0 + i) % 4
                lastst = (step == NCH - 1)
                kt = kts[sl]
                pm = pms[sl]
                ptr, psc = PTR[d], PSC[d]
                pb = (c % PG) * SR
                if not lastst:
                    s.op("pe", lambda e: e.transpose(out=ptr[pb:pb + L, 0:128], in_=KN[d][:, csl], identity=identb[:, :]),
                         reads=[KNv[d][blk_of(c)], identb], writes=[ptr])
                    s.op("act", lambda e: e.activation(out=kt[pb:pb + L, :], in_=ptr[pb:pb + L, 0:128], func=AF.Copy),
                         reads=[ptr], writes=[kt])
                s.mm(psc[pb:pb + SR, 0:L], K[d][:, c, :], Q[d][:, csl], True, True,
                     reads=[Kv[d][blk_of(c)], Qv[d][blk_of(c)]], writes=[psc])
                s.op("dve", lambda e: e.tensor_tensor(out=pm[pb:pb + SR, :], in0=psc[pb:pb + SR, 0:L],
                                                      in1=maskFB[L][d][pb:pb + SR, :], op=ALU.mult),
                     reads=[psc, cs], writes=[pm])

            def phaseB(i):
                step, d = seq[i]
                c = order[d][step]
                csl = slice(c * L, c * L + L)
                sl = (slot0 + i) % 4
                first = (step == 0)
                lastst = (step == NCH - 1)
                kt = kts[sl]
                pm = pms[sl]
                pso, pst = PSO[d], PST[d]
                pb = (c % PG) * SR
                cq = c // PG
                for v in range(NV):
                    Vb = Vbd[pb:pb + SR, cq, :] if v == 0 else bd1[:, :]
                    vo = slice(v * 64, v * 64 + L)
                    s.mm(pso[:, vo], Vb, pm[pb:pb + SR, :], True, first, reads=[Vbd, bd1, pm],
                         writes=[pso] if v == 0 else [], acc=[] if v == 0 else [pso])
                    if not first:
                        for hh in range(2):
                            hs = slice(64 * hh, 64 * hh + 64)
                            s.mm(pso[hs, vo], Wbf[d][v][hs, :], Q[d][hs, csl], False, hh == 1,
                                 reads=[Wbf[d][v], Qv[d][blk_of(c)]], acc=[pso])
                for v in range(NV):
                    vo = slice(v * 64, v * 64 + L)
                    A_ = accs[d][v]
                    s.op("act", lambda e: e.activation(out=A_[:, csl], in_=pso[:, vo], func=AF.Copy),
                         reads=[pso], acc=[A_], selfdep=False)
                if not lastst:
                    for v in range(NV):
                        vt = slice(v * 64, v * 64 + 64)
                        for hh in range(2):
                            hs = slice(64 * hh, 64 * hh + 64)
                            rhs = Vp[pb:pb + L, cq, hs] if v == 0 else onesb[pb:pb + L, 0:64]
                            wr = (v == 0 and hh == 0)
                            s.mm(pst[hs, vt], kt[pb:pb + L, hs], rhs, True, True, reads=[kt, Vp, onesb],
                                 writes=[pst] if wr else [], acc=[] if wr else [pst])
                    for v in range(NV):
                        vt = slice(v * 64, v * 64 + 64)
                        s.op("dve", lambda e: e.scalar_tensor_tensor(out=Wbf[d][v][:, :], in0=W32[d][v][:, :],
                                                                     scalar=G[d][:, c:c + 1], in1=pst[:, vt],
                                                                     op0=ALU.mult, op1=ALU.add),
                             reads=[W32[d][v], G[d], pst], writes=[Wbf[d][v]])
                    for v in range(NV):
                        vt = slice(v * 64, v * 64 + 64)
                        s.op("dve", lambda e: e.scalar_tensor_tensor(out=W32[d][v][:, :], in0=W32[d][v][:, :],
                                                                     scalar=G[d][:, c:c + 1], in1=pst[:, vt],
                                                                     op0=ALU.mult, op1=ALU.add),
                             reads=[W32[d][v], G[d], pst], writes=[W32[d][v]])

            phaseA(0)
            for i in range(len(seq)):
                if i + 1 < len(seq):
                    phaseA(i + 1)
                phaseB(i)
            slot += len(seq)
            s.dma("sp", qs[:, :], zog[rows, :], reads=[zog] + Qv[0] + Qv[1], writes=[qs])
            chunk = (0 if kind == "h" else 2) + pc
            for j in blocks:
                st, n = BLOCKS[j]
                O = ntmp()
                if kind == "m":
                    hd = []
                    for d in range(2):
                        num, den = accs[d]
                        Ta = ntmp()
                        Th = ntmp()
                        s.op("act", lambda e: e.activation(out=Ta[:, 0:n], in_=den[:, st:st + n], func=AF.Abs), reads=[den], writes=[Ta])
                        s.op("dve", lambda e: e.tensor_scalar_max(out=Ta[:, 0:n], in0=Ta[:, 0:n], scalar1=1.0), reads=[Ta], writes=[Ta])
                        s.op("act", lambda e: e.activation(out=Ta[:, 0:n], in_=Ta[:, 0:n], func=AF.Ln), reads=[Ta], writes=[Ta])
                        s.op("act", lambda e: e.activation(out=Ta[:, 0:n], in_=Ta[:, 0:n], func=AF.Exp, scale=-1.0), reads=[Ta], writes=[Ta])
                        s.op("dve", lambda e: e.tensor_tensor(out=Th[:, 0:n], in0=num[:, st:st + n], in1=Ta[:, 0:n], op=ALU.mult),
                             reads=[num, Ta], writes=[Th])
                        hd.append(Th)
                    s.op("pool", lambda e: e.tensor_tensor(out=O[:, 0:n], in0=hd[0][:, 0:n], in1=hd[1][:, 0:n], op=ALU.add),
                         reads=hd, writes=[O])
                else:
                    s.op("pool", lambda e: e.tensor_tensor(out=O[:, 0:n], in0=accs[0][0][:, st:st + n], in1=accs[1][0][:, st:st + n],
                                                           op=ALU.add), reads=[accs[0][0], accs[1][0]], writes=[O])
                SQ = sqs[j % 2]
                R = ntmp()
                T = ntmp()
                s.op("act", lambda e: e.activation(out=SQ[:, 0:n], in_=O[:, 0:n], func=AF.Square), reads=[O], writes=[SQ])
                s.mm(pn[:, 0:n], blkb[:, :], SQ[:, 0:n], True, True, reads=[blkb, SQ], writes=[pn])
                s.op("act", lambda e: e.activation(out=R[:, 0:n], in_=pn[:, 0:n], func=AF.Ln, scale=1.0 / 64, bias=EPS),
                     reads=[pn], writes=[R])
                s.op("act", lambda e: e.activation(out=R[:, 0:n], in_=R[:, 0:n], func=AF.Exp, scale=-0.5), reads=[R], writes=[R])
                s.op("dve", lambda e: e.scalar_tensor_tensor(out=T[:, 0:n], in0=O[:, 0:n], scalar=gcol[:, l, gidx:gidx + 1],
                                                             in1=R[:, 0:n], op0=ALU.mult, op1=ALU.mult),
                     reads=[O, R, gcol], writes=[T])
                s.op("dve", lambda e: e.tensor_tensor(out=hT[:, chunk, st:st + n], in0=T[:, 0:n], in1=qs[:, st:st + n],
                                                      op=ALU.mult), reads=[T, qs], acc=[hTb[j]], selfdep=False)
        s.release(m0)

    def attn_block(qb, kb, vfn, vbuf, st, n, kcs, chunk, sc_ps, num_ps, den_ps, pTs, rd, cnt, qz=None, ofn=None):
        j = [i for i, (a, _) in enumerate(BLOCKS) if a <= st < a + BLOCKS[i][1]][0]
        nk = len(kcs)
        its = [(ki, kc, hh) for ki, kc in enumerate(kcs) for hh in range(2)]
        scs = {}

        def issue_sc(i):
            ki, kc, hh = its[i]
            hs = slice(64 * hh, 64 * hh + 64)
            sc = sc_ps[cnt[0] % len(sc_ps)]
            pT = pTs[cnt[0] % len(pTs)]
            cnt[0] += 1
            if qz is None:
                s.mm(sc[:, 0:n], kb[hs, kc * 128:(kc + 1) * 128], qb[hs, st:st + n], True, True, reads=[kb, qb], writes=[sc])
            else:
                s.mm(sc[:, 0:n], kb[:, kc * 128:(kc + 1) * 128], qz[hh][:, st:st + n], True, True, reads=[kb, qz[hh]], writes=[sc])
            scs[i] = (sc, pT)

        issue_sc(0)
        if len(its) > 1:
            issue_sc(1)
        for i, (ki, kc, hh) in enumerate(its):
            hs = slice(64 * hh, 64 * hh + 64)
            if i + 2 < len(its):
                issue_sc(i + 2)
            sc, pT = scs.pop(i)
            s.op("act", lambda e: e.activation(out=pT[:, 0:n], in_=sc[:, 0:n], func=AF.Exp, scale=0.125),
                 reads=[sc], writes=[pT])
            first = (ki == 0)
            if qz is None:
                s.mm(num_ps[hs, 0:n], vfn(kc, hh), pT[:, 0:n], first, ki == nk - 1, reads=[pT, vbuf],
                     writes=[num_ps] if (first and hh == 0) else [], acc=[] if (first and hh == 0) else [num_ps])
                s.mm(den_ps[hs, 0:n], onesb[:, 0:64], pT[:, 0:n], first, ki == nk - 1, reads=[pT, onesb],
                     writes=[den_ps] if (first and hh == 0) else [], acc=[] if (first and hh == 0) else [den_ps])
            else:
                f0 = (i == 0)
                l0 = (i == len(its) - 1)
                s.mm(num_ps[:, 0:n], vfn(kc, hh), pT[:, 0:n], f0, l0, reads=[pT, vbuf],
                     writes=[num_ps] if f0 else [], acc=[] if f0 else [num_ps])
                s.mm(den_ps[:, 0:n], ofn(hh), pT[:, 0:n], f0, l0, reads=[pT, vbuf],
                     writes=[den_ps] if f0 else [], acc=[] if f0 else [den_ps])
        s.op("act", lambda e: e.activation(out=rd[:, 0:n], in_=den_ps[:, 0:n], func=AF.Ln), reads=[den_ps], writes=[rd])
        s.op("act", lambda e: e.activation(out=rd[:, 0:n], in_=rd[:, 0:n], func=AF.Exp, scale=-1.0), reads=[rd], writes=[rd])
        s.op("dve", lambda e: e.tensor_tensor(out=hT[:, chunk, st:st + n], in0=num_ps[:, 0:n], in1=rd[:, 0:n], op=ALU.mult),
             reads=[num_ps, rd], acc=[hTb[j]], selfdep=False)

    def gqa(b, l):
        m0 = s.mark()
        last = (l == nlayers - 1)
        rp = s.sbuf("g_rope", [128, 4096], F32)
        s.dma("sp", rp[:, :], ropec[:, :], writes=[rp])
        raw = [s.sbuf("g_raw%d" % i, [128, NT], F32) for i in range(2)]
        QK = [s.sbuf("g_qk%d" % i, [128, NT], BF16) for i in range(4)]
        sqs = [s.sbuf("g_sq%d" % i, [128, 512], BF16) for i in range(4)]
        rss = [s.sbuf("g_rs%d" % i, [128, 512], F32) for i in range(4)]
        t1s = [s.sbuf("g_t1%d" % i, [128, 512], F32) for i in range(4)]
        t2s = [s.sbuf("g_t2%d" % i, [128, 512], F32) for i in range(4)]
        t3s = [s.sbuf("g_t3%d" % i, [128, 512], F32) for i in range(4)]
        Vz = s.sbuf("g_Vz", [128, 18, 4, 128], BF16)
        Oz = s.sbuf("g_Oz", [128, 2, 128], BF16)
        Qz = [[s.sbuf("g_qz%d%d" % (p_, h_), [128, NT], BF16) for h_ in range(2)] for p_ in range(2)]
        pTs = [s.sbuf("g_pT%d" % i, [128, 512], BF16) for i in range(4)]
        rds = [s.sbuf("g_rd%d" % i, [128, 512], F32) for i in range(2)]
        sc_ps = [s.psum("g_sc%d" % i, [128, 512], F32) for i in range(4)]
        num_ps = [s.psum("g_num%d" % i, [128, 512], F32) for i in range(2)]
        den_ps = [s.psum("g_den%d" % i, [128, 512], F32) for i in range(2)]
        s.op("pool", lambda e: e.memset(Vz[:, :, :, :], 0.0), writes=[Vz])
        s.op("pool", lambda e: e.memset(Oz[:, :, :], 0.0), writes=[Oz])
        for hh in range(2):
            s.op("pool", lambda e: e.memset(Oz[:, hh, 64 * hh:64 * hh + 64], 1.0), acc=[Oz], selfdep=True)
            for p_ in range(2):
                s.op("pool", lambda e: e.memset(Qz[p_][hh][64 * (1 - hh):64 * (1 - hh) + 64, :], 0.0), writes=[Qz[p_][hh]])
                s.dma("sp", Vz[:, :, 2 * p_ + hh, 64 * hh:64 * hh + 64],
                      v_tok[:, 512 + 64 * p_:512 + 64 * p_ + 64].rearrange("(c p) d -> p c d", p=128), reads=[v_tok], acc=[Vz])
        srcs = [(zq_g[0:128, :], 2), (zq_g[128:256, :], 2), (zk_g[0, :, :], 3), (zk_g[1, :, :], 3)]
        it = 0
        for idx, (src, gi) in enumerate(srcs):
            R_ = raw[idx % 2]
            s.dma("sp", R_[:, :], src, reads=[zq_g, zk_g], writes=[R_])
            for j, (st, n) in enumerate(BLOCKS):
                SQ = sqs[it % 4]
                RS = rss[it % 4]
                T1 = t1s[it % 4]
                T2 = t2s[it % 4]
                T3 = t3s[it % 4]
                P1 = sc_ps[(2 * it) % 4]
                P2 = sc_ps[(2 * it + 1) % 4]
                it += 1
                s.op("act", lambda e: e.activation(out=SQ[:, 0:n], in_=R_[:, st:st + n], func=AF.Square), reads=[R_], writes=[SQ])
                s.mm(P1[:, 0:n], blkb[:, :], SQ[:, 0:n], True, True, reads=[blkb, SQ], writes=[P1])
                s.op("act", lambda e: e.activation(out=RS[:, 0:n], in_=P1[:, 0:n], func=AF.Ln, scale=1.0 / 64, bias=EPS),
                     reads=[P1], writes=[RS])
                s.op("act", lambda e: e.activation(out=RS[:, 0:n], in_=RS[:, 0:n], func=AF.Exp, scale=-0.5), reads=[RS], writes=[RS])
                s.op("dve", lambda e: e.scalar_tensor_tensor(out=T1[:, 0:n], in0=R_[:, st:st + n], scalar=gcol[:, l, gi:gi + 1],
                                                             in1=RS[:, 0:n], op0=ALU.mult, op1=ALU.mult),
                     reads=[R_, RS, gcol], writes=[T1])
                if st >= 256:
                    s.mm(P2[:, 0:n], ropeRT, T1[:, 0:n], True, True, reads=[cs, T1], writes=[P2])
                    s.op("dve", lambda e: e.tensor_tensor(out=T2[:, 0:n], in0=T1[:, 0:n], in1=rp[:, st - 256:st - 256 + n],
                                                          op=ALU.mult), reads=[T1, rp], writes=[T2])
                    s.op("dve", lambda e: e.tensor_tensor(out=T3[:, 0:n], in0=P2[:, 0:n],
                                                          in1=rp[:, 2048 + st - 256:2048 + st - 256 + n], op=ALU.mult),
                         reads=[P2, rp], writes=[T3])
                    s.op("dve", lambda e: e.tensor_tensor(out=QK[idx][:, st:st + n], in0=T2[:, 0:n], in1=T3[:, 0:n], op=ALU.add),
                         reads=[T2, T3], acc=[QK[idx]], selfdep=False)
                else:
                    s.op("act", lambda e: e.activation(out=QK[idx][:, st:st + n], in_=T1[:, 0:n], func=AF.Copy),
                         reads=[T1], acc=[QK[idx]], selfdep=False)
                if idx < 2:
                    for hh in range(2):
                        hs = slice(64 * hh, 64 * hh + 64)
                        s.op("pool", lambda e: e.tensor_copy(out=Qz[idx][hh][hs, st:st + n], in_=QK[idx][hs, st:st + n]),
                             reads=[QK[idx]], acc=[Qz[idx][hh]], selfdep=False)
        cnt = [0]
        qblocks = [1, 2, 3, 4] if last else [0, 1, 2, 3, 4]
        bi = 0
        for pc in range(2):
            for j in qblocks:
                st, n = BLOCKS[j]
                kcs = list(range(18)) if st >= 256 else [0, 1]
                attn_block(QK[pc], QK[2 + pc], lambda kc, hh: Vz[:, kc, 2 * pc + hh, :], Vz, st, n, kcs, 4 + pc,
                           sc_ps, num_ps[bi % 2], den_ps[bi % 2], pTs, rds[bi % 2], cnt,
                           qz=Qz[pc], ofn=lambda hh: Oz[:, hh, :])
                bi += 1
        s.release(m0)

    def na(b, l):
        m0 = s.mark()
        last = (l == nlayers - 1)
        bias8 = s.sbuf("n_bias", [128, 4 * NU, 128], BF16)
        bst = [s.sbuf("n_bst%d" % i, [128, NU, 128], F32) for i in range(2)]
        qT = [s.sbuf("n_q%d" % i, [128, NT], BF16) for i in range(2)]
        kT = [s.sbuf("n_k%d" % i, [128, NT], BF16) for i in range(2)]
        Vn = s.sbuf("n_V", [128, 18, 256], BF16)
        pTw = [s.sbuf("n_pTw%d" % i, [128, 2048], BF16) for i in range(2)]
        qz = [[s.sbuf("n_qz%d%d" % (p_, h_), [128, NT], BF16) for h_ in range(2)] for p_ in range(2)]
        Vnz = s.sbuf("n_Vz", [128, 18, 4, 128], BF16)
        Oz = s.sbuf("n_Oz", [128, 2, 128], BF16)
        negt = s.sbuf("n_neg", [128, 128], BF16)
        s.op("pool", lambda e: e.memset(negt[:, :], -8e30), writes=[negt])
        s.op("pool", lambda e: e.memset(Vnz[:, :, :, :], 0.0), writes=[Vnz])
        s.op("pool", lambda e: e.memset(Oz[:, :, :], 0.0), writes=[Oz])
        for hh in range(2):
            s.op("pool", lambda e: e.memset(Oz[:, hh, 64 * hh:64 * hh + 64], 1.0), acc=[Oz])
        for h_ in range(4):
            s.dma("sp", Vnz[:, :, h_, 64 * (h_ % 2):64 * (h_ % 2) + 64],
                  v_tok[:, 640 + 64 * h_:640 + 64 * h_ + 64].rearrange("(c p) d -> p c d", p=128), reads=[v_tok], acc=[Vnz])
        pTd = [s.sbuf("n_pTd%d" % i, [128, 512], BF16) for i in range(2)]
        rds = [s.sbuf("n_rd%d" % i, [128, 512], F32) for i in range(2)]
        sc_ps = [s.psum("n_sc%d" % i, [128, 512], F32) for i in range(4)]
        num_ps = [s.psum("n_num%d" % i, [128, 512], F32) for i in range(2)]
        den_ps = [s.psum("n_den%d" % i, [128, 512], F32) for i in range(2)]
        for h in range(4):
            B_ = bst[h % 2]
            s.dma("sp", B_[:, :, :], nab[l, h * NU:(h + 1) * NU, :, :].rearrange("u p q -> p u q"), writes=[B_])
            s.op("pool", lambda e: e.tensor_scalar(out=bias8[:, h * NU:(h + 1) * NU, :], in0=B_[:, :, :], scalar1=8.0,
                                                   scalar2=None, op0=ALU.mult), reads=[B_], acc=[bias8])
        for pc in range(2):
            rows = slice(pc * 128, pc * 128 + 128)
            s.dma("sp", qT[pc][:, :], zq_n[rows, :], reads=[zq_n], writes=[qT[pc]])
            s.dma("sp", kT[pc][:, :], zk_n[rows, :], reads=[zk_n], writes=[kT[pc]])
        s.dma("sp", Vn[:, :, :], v_tok[:, 640:896].rearrange("(c p) d -> p c d", p=128), reads=[v_tok], writes=[Vn])
        for p_ in range(2):
            for hh in range(2):
                hs = slice(64 * hh, 64 * hh + 64)
                ho = slice(64 * (1 - hh), 64 * (1 - hh) + 64)
                s.op("pool", lambda e: e.memset(qz[p_][hh][ho, :], 0.0), writes=[qz[p_][hh]])
                s.dma("sp", qz[p_][hh][hs, :], zq_n[p_ * 128 + 64 * hh:p_ * 128 + 64 * hh + 64, :], reads=[zq_n], acc=[qz[p_][hh]])
        it = 0
        for pc in range(2):
            units = [(T, hh) for T in range(8) for hh in range(2)]
            for ui, (T, hh) in enumerate(units):
                q0 = 256 + 256 * T
                jb = 1 + T // 2
                h = 2 * pc + hh
                loc = sorted(set(j for (tt, j) in _NA_TMAP if tt in (2 * T, 2 * T + 1)))
                allk = [(0, None)] + [(1, None)] + [(2 + j, j) for j in loc]
                nk = len(allk)
                g = it + ui
                NUM = num_ps[T % 2]
                DEN = den_ps[T % 2]
                RD = rds[T % 2]
                pT = pTw[g % 2]
                nbank = (nk + 1) // 2
                for bk_i in range(nbank):
                    bk = sc_ps[bk_i]
                    chunks = allk[2 * bk_i:2 * bk_i + 2]
                    for ci, (gc, j) in enumerate(chunks):
                        cc = slice(ci * 256, ci * 256 + 256)
                        s.mm(bk[:, cc], kT[pc][:, gc * 128:(gc + 1) * 128], qz[pc][hh][:, q0:q0 + 256], True, j is None,
                             reads=[kT[pc], qz[pc][hh]], writes=[bk] if ci == 0 else [], acc=[] if ci == 0 else [bk])
                        if j is not None:
                            for tt in range(2):
                                u = _NA_TMAP.get((2 * T + tt, j))
                                brhs = bias8[:, h * NU + u, :] if u is not None else negt[:, :]
                                c2 = slice(ci * 256 + 128 * tt, ci * 256 + 128 * tt + 128)
                                s.mm(bk[:, c2], identb[:, :], brhs, False, tt == 1, reads=[identb, bias8, negt], acc=[bk])
                    ncols = 256 * len(chunks)
                    s.op("act", lambda e: e.activation(out=pT[:, bk_i * 512:bk_i * 512 + ncols], in_=bk[:, 0:ncols], func=AF.Exp, scale=0.125),
                         reads=[bk], writes=[pT] if bk_i == 0 else [], acc=[] if bk_i == 0 else [pT], selfdep=(bk_i == 0))
                for i, (gc, j) in enumerate(allk):
                    f0 = (i == 0 and hh == 0)
                    l0 = (i == nk - 1 and hh == 1)
                    s.mm(NUM[:, 0:256], Vnz[:, gc, h, :], pT[:, i * 256:(i + 1) * 256], f0, l0,
                         reads=[Vnz, pT], writes=[NUM] if f0 else [], acc=[] if f0 else [NUM])
                    s.mm(DEN[:, 0:256], Oz[:, hh, :], pT[:, i * 256:(i + 1) * 256], f0, l0,
                         reads=[Oz, pT], writes=[DEN] if f0 else [], acc=[] if f0 else [DEN])
                if hh == 1:
                    s.op("act", lambda e: e.activation(out=RD[:, 0:256], in_=DEN[:, 0:256], func=AF.Ln), reads=[DEN], writes=[RD])
                    s.op("act", lambda e: e.activation(out=RD[:, 0:256], in_=RD[:, 0:256], func=AF.Exp, scale=-1.0), reads=[RD], writes=[RD])
                    s.op("dve", lambda e: e.tensor_tensor(out=hT[:, 6 + pc, q0:q0 + 256], in0=NUM[:, 0:256], in1=RD[:, 0:256],
                                                          op=ALU.mult), reads=[NUM, RD], acc=[hTb[jb]], selfdep=False)
            it += len(units)
            if not last:
                cnt = [0]
                attn_block(qT[pc], kT[pc], lambda kc, hh: Vn[:, kc, (2 * pc + hh) * 64:(2 * pc + hh) * 64 + 64], Vn, 0, 256, [0, 1],
                           6 + pc, sc_ps, num_ps[0], den_ps[0], pTd, rds[0], cnt)
        s.release(m0)

    def p4(b, l):
        last = (l == nlayers - 1)
        halves = [[0, 1, 2], [3, 4]]
        if last:
            halves[0] = [1, 2]
        for blks in halves:
            m0 = s.mark()
            base = BLOCKS[blks[0]][0]
            xh = s.sbuf("xh", [128, 8, 1280], F32)
            xhv = {j: Buf(xh.t, "xh.%d" % j) for j in blks}
            stg = [s.sbuf("p4stg%d" % i, [128, 8, 512], F32) for i in range(2)]
            wb = [s.sbuf("p4wb%d" % i, [128, 8, 512], BF16) for i in range(2)]
            w2b = [s.sbuf("p4w2b%d" % i, [128, 4, 1024], BF16) for i in range(2)]
            ub = [s.sbuf("p4u%d" % i, [128, 4, 512], BF16) for i in range(2)]
            rb = [s.sbuf("p4r%d" % i, [128, 512], F32) for i in range(3)]
            sq = s.sbuf("p4sq", [128, 8, 512], BF16)
            rs = s.sbuf("p4rs", [128, 512], F32)
            tmps = [s.sbuf("p4t%d" % i, [128, 512], F32) for i in range(3)]
            ps = [s.psum("p4ps%d" % i, [128, 512], F32) for i in range(7)]
            pss = s.psum("p4pss", [128, 512], F32)
            pi = [0]

            def nps():
                pi[0] += 1
                return ps[pi[0] % 7]

            for j in blks:
                st, n = BLOCKS[j]
                s.dma("sp", xh[:, :, st - base:st - base + n], xs[:, :, st:st + n], reads=[xs], writes=[xhv[j]])
            for cg in range(2):
                s.dma("sp", stg[cg][:, :, :], w_out[l, :, cg * 512:(cg + 1) * 512].rearrange("(k p) n -> p k n", p=128),
                      writes=[stg[cg]])
                if cg == 0:
                    s.op("act", lambda e: e.activation(out=wb[cg][:, :, :], in_=stg[cg][:, :, :], func=AF.Copy),
                         reads=[stg[cg]], writes=[wb[cg]])
                else:
                    s.op("dve", lambda e: e.tensor_copy(out=wb[cg][:, :, :], in_=stg[cg][:, :, :]), reads=[stg[cg]], writes=[wb[cg]])
            for cg in range(2):
                for j in blks:
                    st, n = BLOCKS[j]
                    col = 2 if st < 256 else b
                    for mi in range(4):
                        m = cg * 4 + mi
                        P = nps()
                        for k in range(8):
                            s.mm(P[:, 0:n], wb[cg][:, k, mi * 128:(mi + 1) * 128], hT[:, k, st:st + n], k == 0, k == 7,
                                 reads=[wb[cg], hTb[j]], writes=[P] if k == 0 else [], acc=[] if k == 0 else [P])
                        xa = xh[:, m, st - base:st - base + n]
                        s.op("dve", lambda e: e.scalar_tensor_tensor(out=xa, in0=P[:, 0:n], scalar=modT[:, l, 16 + m, col:col + 1],
                                                                     in1=xa, op0=ALU.mult, op1=ALU.add),
                             reads=[P, modT, xhv[j]], acc=[xhv[j]], selfdep=False)
            for j in blks:
                st, n = BLOCKS[j]
                col = 2 if st < 256 else b
                norm_block(xhv[j], st - base, n, st, l, 1, col, sq, pss, rs, tmps, j)
            def loadw_dma(g):
                s.dma("sp", stg[0][:, :, :], w1[l, :, g * 512:(g + 1) * 512].rearrange("(k p) n -> p k n", p=128), writes=[stg[0]])
                s.dma("sp", stg[1][:, :, :].rearrange("p k n -> p (k n)").rearrange("p (c n) -> p c n", c=4),
                      w2[l, g * 512:(g + 1) * 512, :].rearrange("(c p) n -> p c n", p=128), writes=[stg[1]])

            def loadw_cast(g):
                s.op("act", lambda e: e.activation(out=wb[g % 2][:, :, :], in_=stg[0][:, :, :], func=AF.Copy),
                     reads=[stg[0]], writes=[wb[g % 2]])
                s.op("dve", lambda e: e.tensor_copy(out=w2b[g % 2][:, :, :],
                                                    in_=stg[1][:, :, :].rearrange("p k n -> p (k n)").rearrange("p (c n) -> p c n", c=4)),
                     reads=[stg[1]], writes=[w2b[g % 2]])

            loadw_dma(0)
            loadw_cast(0)
            items = [(g, bi_, j) for g in range(8) for bi_, j in enumerate(blks)]
            Us = {}

            def mlp_u(i):
                g, bi_, j = items[i]
                if bi_ == 0 and g + 1 < 8:
                    loadw_dma(g + 1)
                W1 = wb[g % 2]
                st, n = BLOCKS[j]
                U = ub[i % 2]
                Us[i] = U
                for hc in range(4):
                    P = nps()
                    for k in range(8):
                        s.mm(P[:, 0:n], W1[:, k, hc * 128:(hc + 1) * 128], hT[:, k, st:st + n], k == 0, k == 7,
                             reads=[W1, hTb[j]], writes=[P] if k == 0 else [], acc=[] if k == 0 else [P])
                    R_ = rb[hc % 3]
                    s.op("act", lambda e: e.activation(out=R_[:, 0:n], in_=P[:, 0:n], func=AF.Relu), reads=[P], writes=[R_])
                    s.op("dve", lambda e: e.tensor_tensor(out=U[:, hc, 0:n], in0=R_[:, 0:n], in1=R_[:, 0:n], op=ALU.mult),
                         reads=[R_], writes=[U] if hc == 0 else [], acc=[] if hc == 0 else [U], selfdep=(hc == 0))

            def mlp_y(i):
                g, bi_, j = items[i]
                W2 = w2b[g % 2]
                st, n = BLOCKS[j]
                col = 2 if st < 256 else b
                U = Us.pop(i)
                for m in range(8):
                    P = nps()
                    for hc in range(4):
                        s.mm(P[:, 0:n], W2[:, hc, m * 128:(m + 1) * 128], U[:, hc, 0:n], hc == 0, hc == 3,
                             reads=[W2, U], writes=[P] if hc == 0 else [], acc=[] if hc == 0 else [P])
                    xa = xh[:, m, st - base:st - base + n]
                    s.op("dve", lambda e: e.scalar_tensor_tensor(out=xa, in0=P[:, 0:n], scalar=modT[:, l, 40 + m, col:col + 1],
                                                                 in1=xa, op0=ALU.mult, op1=ALU.add),
                         reads=[P, modT, xhv[j]], acc=[xhv[j]], selfdep=False)

            mlp_u(0)
            for i in range(len(items)):
                g, bi_, j = items[i]
                if i + 1 < len(items):
                    g2, b2, _ = items[i + 1]
                    if b2 == 0:
                        loadw_cast(g2)
                    mlp_u(i + 1)
                mlp_y(i)
            if not last:
                for j in blks:
                    st, n = BLOCKS[j]
                    s.dma("sp", xs[:, :, st:st + n], xh[:, :, st - base:st - base + n], reads=[xhv[j]], acc=[xs])
            else:
                yb = stg[0]
                youts = [Buf(stg[1].t, "yout%d" % i) for i in range(2)]
                oc = 0
                for j in blks:
                    st, n = BLOCKS[j]
                    s.op("act", lambda e: e.activation(out=sq[:, :, 0:n], in_=xh[:, :, st - base:st - base + n], func=AF.Square),
                         reads=[xhv[j]], writes=[sq])
                    for k in range(8):
                        s.mm(pss[:, 0:n], onesb[:, :], sq[:, k, 0:n], k == 0, k == 7, reads=[sq, onesb],
                             writes=[pss] if k == 0 else [], acc=[] if k == 0 else [pss])
                    s.op("act", lambda e: e.activation(out=rs[:, 0:n], in_=pss[:, 0:n], func=AF.Ln, scale=1.0 / 1024, bias=EPS),
                         reads=[pss], writes=[rs])
                    s.op("act", lambda e: e.activation(out=rs[:, 0:n], in_=rs[:, 0:n], func=AF.Exp, scale=-0.5), reads=[rs], writes=[rs])
                    for k in range(8):
                        s.op("dve", lambda e: e.scalar_tensor_tensor(out=yb[:, k, 0:n], in0=xh[:, k, st - base:st - base + n],
                                                                     scalar=gT[:, 4, k:k + 1], in1=rs[:, 0:n],
                                                                     op0=ALU.mult, op1=ALU.mult),
                             reads=[xhv[j], gT, rs], writes=[yb] if k == 0 else [], acc=[] if k == 0 else [yb], selfdep=(k == 0))
                    for ti in range(n // 128):
                        YO = youts[oc % 2]
                        yo_ap = stg[1][:, (oc % 2) * 2:(oc % 2) * 2 + 2, :].rearrange("p a n -> p (a n)")
                        oc += 1
                        for half in range(2):
                            P = nps()
                            for kk in range(4):
                                k = half * 4 + kk
                                s.op("pe", lambda e: e.transpose(out=P[:, kk * 128:(kk + 1) * 128],
                                                                 in_=yb[:, k, ti * 128:(ti + 1) * 128], identity=identf),
                                     reads=[yb, cs], writes=[P] if kk == 0 else [], acc=[] if kk == 0 else [P])
                            s.op("act", lambda e: e.activation(out=yo_ap[:, half * 512:(half + 1) * 512], in_=P[:, :], func=AF.Copy),
                                 reads=[P], writes=[YO] if half == 0 else [], acc=[] if half == 0 else [YO], selfdep=(half == 0))
                        tok = st - 256 + ti * 128
                        s.dma("sp", y[b, tok:tok + 128, :], yo_ap, reads=[YO], acc=[y])
            s.release(m0)

    prologue()
    for b in range(nseq if stop_after != "pro" else 0):
        for l in range(nlayers):
            p1(b, l)
            if stop_after == "p1":
                break
            p2(b, l)
            if stop_after == "p2":
                break
            recur(b, l, "h")
            if stop_after in ("h", "hbuild"):
                break
            recur(b, l, "m")
            if stop_after == "m":
                break
            gqa(b, l)
            if stop_after == "g":
                break
            na(b, l)
            if stop_after == "p3":
                break
            p4(b, l)
        if stop_after is not None:
            break
    if dbg:
        s.dma("sp", cat_dbg[:, :, :], hT[:, :, :], reads=hTb, writes=[cat_dbg])
    s.finish()
    build.stats = (s.nops, s.nwaits, dict(s.cnt))
    return nc


_CACHE = {}


def _host_inputs(inputs, core):
    f = lambda a: np.ascontiguousarray(np.asarray(a, dtype=np.float32))
    b0 = 2 * core
    cst, rope = _CACHE["consts"]
    m = {
        "x": f(inputs["x"][b0:b0 + 2]),
        "ctx": f(inputs["ctx"][b0:b0 + 2]),
        "cvec": f(np.concatenate([inputs["c"][b0:b0 + 2], np.asarray(inputs["c_ctx"])[None, :]], 0)),
        "w_mod": f(inputs["w_mod"]), "b_mod": f(inputs["b_mod"]),
        "norm1_g": f(inputs["norm1_g"]), "norm2_g": f(inputs["norm2_g"]),
        "w_in": f(inputs["w_in"]),
        "hgrn_lb_logits": f(np.asarray(inputs["hgrn_lb_logits"]).reshape(4, 256)),
        "hgrn_norm_g": f(inputs["hgrn_norm_g"]), "mlstm_gate_b": f(inputs["mlstm_gate_b"]),
        "mlstm_norm_g": f(inputs["mlstm_norm_g"]), "gqa_qnorm_g": f(inputs["gqa_qnorm_g"]),
        "gqa_knorm_g": f(inputs["gqa_knorm_g"]),
        "na_bias": _CACHE["na_bias"],
        "w_out": f(inputs["w_out"]), "w_mlp1": f(inputs["w_mlp1"]), "w_mlp2": f(inputs["w_mlp2"]),
        "final_norm_g": f(np.asarray(inputs["final_norm_g"]).reshape(1, 1024)),
        "consts": cst, "rope": rope,
    }
    return m


def _prep(inputs):
    _CACHE["consts"] = _consts()
    idx = _na_gather_index()
    rpb = np.asarray(inputs["na_rpb"], np.float32)
    flat = np.concatenate([rpb.reshape(2, 4, 465), np.full((2, 4, 1), NEG, np.float32)], -1)
    nb = flat[:, :, idx]
    _CACHE["na_bias"] = np.ascontiguousarray(nb.reshape(2, 4 * NU, 128, 128))


def kernel(**inputs):
    _prep(inputs)
    nc = build()
    in_maps = [_host_inputs(inputs, c) for c in range(8)]
    res = run_bass_kernel_spmd(nc, in_maps, core_ids=list(range(8)))
    out = np.concatenate([np.asarray(r["y"], np.float32) for r in res.results], axis=0)
    return out
```

```python
import numpy as np
import ml_dtypes
import concourse.bass as bass
import concourse.mybir as mybir
from concourse.bass_utils import run_bass_kernel_spmd

F32 = mybir.dt.float32
BF16 = mybir.dt.bfloat16
AF = mybir.ActivationFunctionType
ALU = mybir.AluOpType

NT = 2304
NCTX = 256
EPS = 1e-6
NEG = -1e30
BLOCKS = [(0, 256), (256, 512), (768, 512), (1280, 512), (1792, 512)]
NCH = 36


class Buf:
    __slots__ = ("t", "name", "lw", "aw", "rd")

    def __init__(self, t, name):
        self.t = t
        self.name = name
        self.lw = {}
        self.aw = {}
        self.rd = {}

    def __getitem__(self, idx):
        return self.t[idx]


class Sch:
    NDMA = 10

    def __init__(self, nc):
        self.nc = nc
        self.eng = {"pe": nc.tensor, "dve": nc.vector, "act": nc.scalar, "pool": nc.gpsimd, "sp": nc.sync}
        self.sems = {}
        self.cnt = {}
        self.key = {}
        self.nsem = 0
        for e in self.eng:
            self._newsem(e)
        for q in ("sp", "act", "pool"):
            for k in range(self.NDMA):
                key = "d%s%d" % (q, k)
                self.sems[key] = nc.alloc_semaphore("s_" + key)
                self.cnt[key] = 0
        self.seen = {e: {} for e in self.eng}
        self.dma_rr = {"sp": 0, "act": 0, "pool": 0}
        self.nwaits = 0
        self.nops = 0
        self._stack = []

    def _newsem(self, e):
        self.nsem += 1
        key = "%s@%d" % (e, self.nsem)
        self.sems[key] = self.nc.alloc_semaphore("s_%s_%d" % (e, self.nsem))
        self.cnt[key] = 0
        self.key[e] = key

    def sbuf(self, name, shape, dtype):
        self.nsem += 1
        name = "%s_%d" % (name, self.nsem)
        g = self.nc.sbuf_tensor(name, list(shape), dtype)
        t = g.__enter__()
        self._stack.append(g)
        return Buf(t, name)

    def psum(self, name, shape, dtype=F32):
        self.nsem += 1
        name = "%s_%d" % (name, self.nsem)
        g = self.nc.psum_tensor(name, list(shape), dtype)
        t = g.__enter__()
        self._stack.append(g)
        return Buf(t, name)

    def dram(self, name, shape, dtype, kind="Internal"):
        t = self.nc.dram_tensor(name, list(shape), dtype, kind=kind)
        return Buf(t, name)

    @staticmethod
    def views(buf, n):
        return [Buf(buf.t, "%s.%d" % (buf.name, i)) for i in range(n)]

    def mark(self):
        return len(self._stack)

    def release(self, mark):
        self.barrier()
        while len(self._stack) > mark:
            g = self._stack.pop()
            g.__exit__(None, None, None)

    def _need(self, e, toks, selfdep=True, wtoks=()):
        best = {}
        for k, v in toks:
            if k.startswith("pe@") and e == "pe":
                continue
            if best.get(k, 0) < v:
                best[k] = v
        for k, v in wtoks:
            if k.startswith("pe@") and e == "pe":
                continue
            if (not selfdep) and k == self.key[e]:
                continue
            if best.get(k, 0) < v:
                best[k] = v
        for k, v in best.items():
            if self.seen[e].get(k, 0) >= v:
                continue
            self.eng[e].wait_ge(self.sems[k], v)
            self.seen[e][k] = v
            self.nwaits += 1

    @staticmethod
    def _deps(reads, writes, acc):
        rt, wt = [], []
        for b in reads:
            rt.extend(b.lw.items())
            rt.extend(b.aw.items())
        for b in writes:
            wt.extend(b.lw.items())
            wt.extend(b.aw.items())
            wt.extend(b.rd.items())
        for b in acc:
            wt.extend(b.lw.items())
            wt.extend(b.rd.items())
        return rt, wt

    @staticmethod
    def _commit(tok, reads, writes, acc):
        k, v = tok
        for b in reads:
            if b.rd.get(k, 0) < v:
                b.rd[k] = v
        for b in writes:
            b.lw = {k: v}
            b.aw = {}
            b.rd = {}
        for b in acc:
            if b.aw.get(k, 0) < v:
                b.aw[k] = v

    def op(self, e, fn, reads=(), writes=(), acc=(), selfdep=True):
        rt, wt = self._deps(reads, writes, acc)
        self._need(e, rt, selfdep, wt)
        ins = fn(self.eng[e])
        self.nops += 1
        key = self.key[e]
        self.cnt[key] += 1
        ins.then_inc(self.sems[key], 1)
        self._commit((key, self.cnt[key]), reads, writes, acc)
        return ins

    def mm(self, out_ap, lhsT, rhs, start, stop, reads=(), writes=(), acc=()):
        return self.op("pe", lambda e: e.matmul(out_ap, lhsT, rhs, start=start, stop=stop,
                                                skip_group_check=True), reads, writes, acc)

    def dma(self, q, out_ap, in_ap, reads=(), writes=(), acc=(), **kw):
        key = "d%s%d" % (q, self.dma_rr[q])
        self.dma_rr[q] = (self.dma_rr[q] + 1) % self.NDMA
        rt, wt = self._deps(reads, writes, acc)
        toks = rt + wt
        if self.cnt[key] > 0:
            toks.append((key, self.cnt[key]))
        self._need(q, toks)
        self.cnt[key] += 16
        ins = self.eng[q].dma_start(out=out_ap, in_=in_ap, **kw)
        ins.then_inc(self.sems[key], 16)
        self.nops += 1
        self._commit((key, self.cnt[key]), reads, writes, acc)
        return ins

    def barrier(self):
        toks = [(k, v) for k, v in self.cnt.items() if v > 0]
        for e in self.eng:
            self._need(e, toks)
        for e in list(self.eng):
            if self.cnt[self.key[e]] > 24000:
                self._newsem(e)

    def finish(self):
        toks = [(k, v) for k, v in self.cnt.items() if v > 0]
        self._need("sp", toks)


def _na_tiles():
    uniq = {}
    tmap = {}
    for t in range(16):
        lo, hi = 10 ** 9, -1
        for b in range(2):
            r0 = min(max(2 * t + b - 4, 0), 24)
            lo = min(lo, r0 // 2)
            hi = max(hi, (r0 + 7) // 2)
        for j in range(lo, hi + 1):
            pat = []
            for a in range(2):
                for b in range(2):
                    qr = 2 * t + b
                    kr = 2 * j + a
                    r0 = min(max(qr - 4, 0), 24)
                    pat.append((kr - qr) if (r0 <= kr < r0 + 8) else None)
            pat = tuple(pat)
            if pat not in uniq:
                uniq[pat] = len(uniq)
            tmap[(t, j)] = uniq[pat]
    pats = [None] * len(uniq)
    for p, i in uniq.items():
        pats[i] = p
    return pats, tmap


_NA_PATS, _NA_TMAP = _na_tiles()
NU = len(_NA_PATS)


def _na_gather_index():
    idx = np.full((NU, 128, 128), 465, np.int64)
    qc = np.arange(64)
    cstart = np.clip(qc - 8, 0, 48)
    kc = np.arange(64)
    col_in = (kc[:, None] >= cstart[None, :]) & (kc[:, None] < cstart[None, :] + 16)
    cidx = np.clip(kc[:, None] - qc[None, :], -15, 15) + 15
    for u, pat in enumerate(_NA_PATS):
        for a in range(2):
            for b in range(2):
                dr = pat[a * 2 + b]
                if dr is None:
                    continue
                blk = np.where(col_in, (dr + 7) * 31 + cidx, 465)
                idx[u, a * 64:(a + 1) * 64, b * 64:(b + 1) * 64] = blk
    return idx


def _consts():
    c = np.zeros((128, 576), np.float32)
    c[:, 0:128] = np.eye(128, dtype=np.float32)
    blk = np.zeros((128, 128), np.float32)
    blk[0:64, 0:64] = 1.0
    blk[64:128, 64:128] = 1.0
    c[:, 128:256] = blk
    rt = np.zeros((128, 128), np.float32)
    for i in range(64):
        rt[2 * i + 1, 2 * i] = -1.0
        rt[2 * i, 2 * i + 1] = 1.0
    c[:, 256:384] = rt
    sidx = np.arange(64)[:, None]
    tidx = np.arange(64)[None, :]
    mf = (sidx <= tidx).astype(np.float32)
    mb = (sidx >= tidx).astype(np.float32)
    c[:, 384:448] = np.concatenate([mf, mf], 0)
    c[:, 448:512] = np.concatenate([mb, mb], 0)
    p = np.arange(128)[:, None] % 16
    t16 = np.arange(16)[None, :]
    c[:, 512:528] = (p <= t16).astype(np.float32)
    c[:, 528:544] = (p >= t16).astype(np.float32)
    t = np.arange(2048)
    row = (t // 64).astype(np.float32)
    col = (t % 64).astype(np.float32)
    inv = np.power(np.float32(10000.0), (-2.0 * np.arange(16, dtype=np.float32) / np.float32(32.0))).astype(np.float32)
    ang = np.concatenate([row[:, None] * inv[None, :], col[:, None] * inv[None, :]], -1).astype(np.float32)
    cos = np.cos(ang).astype(np.float32)
    sin = np.sin(ang).astype(np.float32)
    cosf = np.repeat(cos, 2, axis=1).T
    sinf = np.repeat(sin, 2, axis=1).T
    rope = np.concatenate([np.concatenate([cosf, cosf], 0), np.concatenate([sinf, sinf], 0)], 1).astype(np.float32)
    return c, np.ascontiguousarray(rope)


def build(nlayers=2, nseq=2, stop_after=None, dbg=False):
    nc = bass.Bass("TRN2", target_bir_lowering=False)
    s = Sch(nc)

    def inp(name, shape, dt=F32):
        return s.dram(name, shape, dt, kind="ExternalInput")

    x_in = inp("x", [2, 2048, 1024])
    ctx_in = inp("ctx", [2, 256, 1024])
    cvec = inp("cvec", [3, 1024])
    w_mod = inp("w_mod", [2, 1024, 6144])
    b_mod = inp("b_mod", [2, 6144])
    n1g = inp("norm1_g", [2, 1024])
    n2g = inp("norm2_g", [2, 1024])
    w_in = inp("w_in", [2, 1024, 3600])
    lbl = inp("hgrn_lb_logits", [4, 256])
    hgg = inp("hgrn_norm_g", [2, 64])
    mgb = inp("mlstm_gate_b", [2, 16])
    mgg = inp("mlstm_norm_g", [2, 64])
    qng = inp("gqa_qnorm_g", [2, 64])
    kng = inp("gqa_knorm_g", [2, 64])
    nab = inp("na_bias", [2, 4 * NU, 128, 128])
    w_out = inp("w_out", [2, 1024, 1024])
    w1 = inp("w_mlp1", [2, 1024, 4096])
    w2 = inp("w_mlp2", [2, 4096, 1024])
    fng = inp("final_norm_g", [1, 1024])
    cst = inp("consts", [128, 576])
    ropec = inp("rope", [128, 4096])
    y = s.dram("y", [2, 2048, 1024], F32, kind="ExternalOutput")

    okind = "ExternalOutput" if dbg else "Internal"
    xs = s.dram("xs", [128, 8, NT], F32, kind=okind)
    zq_h = s.dram("zq_h", [256, NT], F32, kind=okind)
    zog_h = s.dram("zog_h", [256, NT], F32, kind=okind)
    zlf_h = s.dram("zlf_h", [2, 256, NT], F32, kind=okind)
    zkk_h = s.dram("zkk_h", [2, 256, NT], F32, kind=okind)
    zq_m = s.dram("zq_m", [256, NT], F32, kind=okind)
    zk_m = s.dram("zk_m", [256, NT], F32, kind=okind)
    zog_m = s.dram("zog_m", [256, NT], F32, kind=okind)
    zg_m = s.dram("zg_m", [16, NT], F32, kind=okind)
    zq_g = s.dram("zq_g", [256, NT], F32, kind=okind)
    zk_g = s.dram("zk_g", [2, 128, NT], F32, kind=okind)
    zq_n = s.dram("zq_n", [256, NT], BF16, kind=okind)
    zk_n = s.dram("zk_n", [256, NT], BF16, kind=okind)
    v_tok = s.dram("v_tok", [NT, 896], BF16, kind=okind)
    cat_dbg = s.dram("cat_dbg", [128, 8, NT], BF16, kind=okind) if dbg else None
    h_dbg = s.dram("h_dbg", [128, 8, NT], BF16, kind=okind) if dbg else None
    mod_dbg = s.dram("mod_dbg", [128, 2 * 48 * 3], F32, kind=okind) if dbg else None

    if dbg:
        dbgQb = s.dram("dbgQ", [2, 3, 128, NT], BF16, kind=okind)
        dbgPb = s.dram("dbgP", [128, NT + 1], F32, kind=okind)
        dbgQ = dbgQb
        dbgP = dbgPb
    NSL = True

    cs = s.sbuf("cs", [128, 576], F32)
    identb = s.sbuf("identb", [128, 128], BF16)
    onesb = s.sbuf("onesb", [128, 128], BF16)
    blkb = s.sbuf("blkb", [128, 128], BF16)
    bd1 = s.sbuf("bd1", [128, 128], BF16)
    hT = s.sbuf("hT", [128, 8, NT], BF16)
    hTb = Sch.views(hT, 5)
    modT = s.sbuf("modT", [128, 2, 48, 3], F32)
    AT = s.sbuf("AT", [128, 2, 2, 8, 3], F32)
    gT = s.sbuf("gT", [128, 5, 8], F32)
    gcol = s.sbuf("gcol", [128, 2, 4], F32)
    lb = s.sbuf("lb", [128, 2, 2, 2], F32)
    oml = s.sbuf("oml", [128, 2, 2, 2], F32)
    noml = s.sbuf("noml", [128, 2, 2, 2], F32)
    mgbT = s.sbuf("mgbT", [16, 2], F32)

    identf = cs[:, 0:128]
    blkf = cs[:, 128:256]
    ropeRT = cs[:, 256:384]
    maskFB = {64: [cs[:, 384:448], cs[:, 448:512]], 16: [cs[:, 512:528], cs[:, 528:544]]}

    s.dma("sp", cs[:, :], cst[:, :], writes=[cs])
    s.op("dve", lambda e: e.tensor_copy(out=identb[:, :], in_=identf), reads=[cs], writes=[identb])
    s.op("dve", lambda e: e.tensor_copy(out=blkb[:, :], in_=blkf), reads=[cs], writes=[blkb])
    s.op("dve", lambda e: e.tensor_copy(out=bd1[:, :], in_=blkf), reads=[cs], writes=[bd1])
    s.op("pool", lambda e: e.memset(onesb[:, :], 1.0), writes=[onesb])

    def prologue():
        m0 = s.mark()
        scT = s.sbuf("scT", [128, 8, 3], F32)
        bmT = s.sbuf("bmT", [128, 2, 48], F32)
        lg = s.sbuf("lg", [128, 4, 2], F32)
        wm = [s.sbuf("wm%d" % i, [128, 8, 512], F32) for i in range(6)]
        pm = s.psum("pm_mod", [128, 512], F32)
        for r in range(3):
            s.dma("sp", scT[:, :, r], cvec[r:r + 1, :].rearrange("o (k p) -> p (o k)", p=128), acc=[scT],
                  allow_slow_non_contiguous=NSL)
        for l in range(2):
            s.dma("sp", bmT[:, l, :], b_mod[l:l + 1, :].rearrange("o (j p) -> p (o j)", p=128), acc=[bmT],
                  allow_slow_non_contiguous=NSL)
        gsrc = [n1g[0:1, :], n1g[1:2, :], n2g[0:1, :], n2g[1:2, :], fng[0:1, :]]
        for i, g in enumerate(gsrc):
            s.dma("sp", gT[:, i, :], g.rearrange("o (k p) -> p (o k)", p=128), acc=[gT], allow_slow_non_contiguous=NSL)
        for r in range(4):
            s.dma("sp", lg[:, r, :], lbl[r:r + 1, :].rearrange("o (c p) -> p (o c)", p=128), acc=[lg],
                  allow_slow_non_contiguous=NSL)
        for l in range(2):
            for i, g in enumerate([hgg, mgg, qng, kng]):
                for hh in range(2):
                    s.dma("sp", gcol[64 * hh:64 * hh + 64, l, i:i + 1], g[l:l + 1, :].rearrange("o d -> d o"),
                          acc=[gcol], allow_slow_non_contiguous=NSL)
            s.dma("sp", mgbT[:, l:l + 1], mgb[l:l + 1, :].rearrange("o g -> g o"), acc=[mgbT],
                  allow_slow_non_contiguous=NSL)
        s.op("act", lambda e: e.activation(out=scT[:, :, :], in_=scT[:, :, :], func=AF.Silu), reads=[scT], writes=[scT])
        ex = s.sbuf("ex", [128, 4, 2], F32)
        den = s.sbuf("den", [128, 2, 2], F32)
        s.op("act", lambda e: e.activation(out=ex[:, :, :], in_=lg[:, :, :], func=AF.Exp), reads=[lg], writes=[ex])
        s.op("dve", lambda e: e.tensor_tensor(out=den[:, :, :], in0=ex[:, 0:2, :], in1=ex[:, 2:4, :], op=ALU.add),
             reads=[ex], writes=[den])
        s.op("dve", lambda e: e.reciprocal(out=den[:, :, :], in_=den[:, :, :]), reads=[den], writes=[den])
        s.op("pool", lambda e: e.memset(lb[:, 0, :, :], 0.0), acc=[lb])
        s.op("dve", lambda e: e.tensor_tensor(out=lb[:, 1, :, :], in0=ex[:, 2:4, :], in1=den[:, :, :], op=ALU.mult),
             reads=[ex, den], acc=[lb])
        s.op("dve", lambda e: e.tensor_scalar(out=oml[:, :, :, :], in0=lb[:, :, :, :], scalar1=-1.0, scalar2=1.0,
                                              op0=ALU.mult, op1=ALU.add), reads=[lb], writes=[oml])
        s.op("dve", lambda e: e.tensor_scalar(out=noml[:, :, :, :], in0=lb[:, :, :, :], scalar1=1.0, scalar2=-1.0,
                                              op0=ALU.mult, op1=ALU.add), reads=[lb], writes=[noml])
        it = 0
        for l in range(2):
            for grp in range(12):
                W = wm[it % 6]
                it += 1
                s.dma("sp" if it % 2 == 0 else "act", W[:, :, :],
                      w_mod[l, :, grp * 512:(grp + 1) * 512].rearrange("(k p) n -> p k n", p=128), writes=[W])
                for m in range(4):
                    idx = grp * 4 + m
                    for k in range(8):
                        s.mm(pm[:, idx * 3:idx * 3 + 3], W[:, k, m * 128:(m + 1) * 128], scT[:, k, :], k == 0, k == 7,
                             reads=[W, scT], acc=[pm])
            s.op("dve", lambda e: e.tensor_tensor(
                out=modT[:, l, :, :], in0=pm[:, 0:144].rearrange("p (j r) -> p j r", r=3),
                in1=bmT[:, l, :].unsqueeze(2).to_broadcast([128, 48, 3]), op=ALU.add),
                reads=[pm, bmT], acc=[modT])
        for l in range(2):
            for w in range(2):
                for k in range(8):
                    j = (1 if w == 0 else 4) * 8 + k
                    s.op("dve", lambda e: e.tensor_scalar(out=AT[:, l, w, k, :], in0=modT[:, l, j, :], scalar1=1.0,
                                                          scalar2=gT[:, w * 2 + l, k:k + 1], op0=ALU.add, op1=ALU.mult),
                         reads=[modT, gT], acc=[AT])
        if dbg:
            s.dma("sp", mod_dbg[:, :], modT[:, :, :, :].rearrange("p l j r -> p (l j r)"), reads=[modT], writes=[mod_dbg])
        s.release(m0)

    def norm_block(X, xoff, n, st, l, w, col, sq, pss, rs, tmps, j):
        s.op("act", lambda e: e.activation(out=sq[:, :, 0:n], in_=X[:, :, xoff:xoff + n], func=AF.Square),
             reads=[X], writes=[sq])
        for k in range(8):
            s.mm(pss[:, 0:n], onesb[:, :], sq[:, k, 0:n], k == 0, k == 7, reads=[sq, onesb],
                 writes=[pss] if k == 0 else [], acc=[] if k == 0 else [pss])
        s.op("act", lambda e: e.activation(out=rs[:, 0:n], in_=pss[:, 0:n], func=AF.Ln, scale=1.0 / 1024, bias=EPS),
             reads=[pss], writes=[rs])
        s.op("act", lambda e: e.activation(out=rs[:, 0:n], in_=rs[:, 0:n], func=AF.Exp, scale=-0.5), reads=[rs], writes=[rs])
        sh = 0 if w == 0 else 3
        for k in range(8):
            T = tmps[k % len(tmps)]
            s.op("dve", lambda e: e.scalar_tensor_tensor(out=T[:, 0:n], in0=X[:, k, xoff:xoff + n],
                                                         scalar=AT[:, l, w, k, col:col + 1], in1=rs[:, 0:n],
                                                         op0=ALU.mult, op1=ALU.mult), reads=[X, rs, AT], writes=[T])
            s.op("act", lambda e: e.activation(out=hT[:, k, st:st + n], in_=T[:, 0:n], func=AF.Identity,
                                               bias=modT[:, l, sh * 8 + k, col:col + 1], scale=1.0),
                 reads=[T, modT], acc=[hTb[j]], selfdep=False)

    def p1(b, l):
        m0 = s.mark()
        xb = [s.sbuf("xb%d" % i, [128, 8, 512], F32) for i in range(2)]
        sq = [s.sbuf("sq%d" % i, [128, 8, 512], BF16) for i in range(2)]
        rs = [s.sbuf("rs%d" % i, [128, 512], F32) for i in range(2)]
        tmps = [s.sbuf("tmp%d" % i, [128, 512], F32) for i in range(4)]
        pss = [s.psum("pss%d" % i, [128, 512], F32) for i in range(2)]
        if l == 0:
            xin = [s.sbuf("xin%d" % i, [128, 1024], F32) for i in range(3)]
            pst = [s.psum("pst%d" % i, [128, 512], F32) for i in range(4)]
        cnt = 0
        for j, (st, n) in enumerate(BLOCKS):
            X = xb[j % 2]
            if l == 0:
                for ti in range(n // 128):
                    tok0 = st + ti * 128
                    src = ctx_in[b, tok0:tok0 + 128, :] if tok0 < 256 else x_in[b, tok0 - 256:tok0 - 128, :]
                    xi = xin[cnt % 3]
                    s.dma("sp", xi[:, :], src, writes=[xi])
                    for half in range(2):
                        pt = pst[(cnt * 2 + half) % 4]
                        for kk in range(4):
                            k = half * 4 + kk
                            s.op("pe", lambda e: e.transpose(out=pt[:, kk * 128:(kk + 1) * 128],
                                                             in_=xi[:, k * 128:(k + 1) * 128], identity=identf),
                                 reads=[xi, cs], writes=[pt] if kk == 0 else [], acc=[] if kk == 0 else [pt])
                        s.op("act", lambda e: e.activation(
                            out=X[:, half * 4:(half + 1) * 4, ti * 128:(ti + 1) * 128],
                            in_=pt[:, :].rearrange("p (k t) -> p k t", t=128), func=AF.Copy),
                            reads=[pt], writes=[X] if (ti == 0 and half == 0) else [],
                            acc=[] if (ti == 0 and half == 0) else [X], selfdep=False)
                    cnt += 1
                s.dma("sp", xs[:, :, st:st + n], X[:, :, 0:n], reads=[X], acc=[xs])
            else:
                s.dma("sp", X[:, :, 0:n], xs[:, :, st:st + n], reads=[xs], writes=[X])
            col = 2 if st < 256 else b
            norm_block(X, 0, n, st, l, 0, col, sq[j % 2], pss[j % 2], rs[j % 2], tmps, j)
        if dbg:
            s.dma("sp", h_dbg[:, :, :], hT[:, :, :], reads=hTb, writes=[h_dbg])
        s.release(m0)

    GROUPS = [
        (0, 512, [("f", "hq", 0, 128, 0), ("f", "hq", 128, 128, 1), ("v", 256, 256, 0)]),
        (512, 512, [("f", "hog", 0, 128, 0), ("f", "hog", 128, 128, 1), ("f", "hf0", 256, 128, 0), ("f", "hf0", 384, 128, 1)]),
        (1024, 512, [("f", "hf1", 0, 128, 0), ("f", "hf1", 128, 128, 1), ("f", "mq", 256, 128, 0), ("f", "mq", 384, 128, 1)]),
        (1536, 512, [("f", "mk", 0, 128, 0), ("f", "mk", 128, 128, 1), ("v", 256, 256, 256)]),
        (2048, 272, [("f", "mog", 0, 128, 0), ("f", "mog", 128, 128, 1), ("f", "mg", 256, 16, 0)]),
        (2320, 512, [("f", "gq", 0, 128, 0), ("f", "gq", 128, 128, 1), ("f", "gk", 256, 64, 0), ("f", "gk", 320, 64, 1),
                     ("v", 384, 128, 512)]),
        (2832, 512, [("f", "nq", 0, 128, 0), ("f", "nq", 128, 128, 1), ("f", "nk", 256, 128, 0), ("f", "nk", 384, 128, 1)]),
        (3344, 256, [("v", 0, 256, 640)]),
    ]

    def p2(b, l):
        m0 = s.mark()
        wst = [s.sbuf("wst%d" % i, [128, 8, 512], F32) for i in range(2)]
        wbf = [s.sbuf("wbf%d" % i, [128, 8, 512], BF16) for i in range(2)]
        stg = [s.sbuf("stg%d" % i, [128, 512], F32) for i in range(8)]
        stb = [s.sbuf("stb%d" % i, [128, 512], BF16) for i in range(4)]
        ps = [s.psum("p2ps%d" % i, [128, 512], F32) for i in range(6)]
        st_i = [0]
        sb_i = [0]
        ps_i = [0]

        def nstg():
            st_i[0] += 1
            return stg[st_i[0] % 8]

        def nstb():
            sb_i[0] += 1
            return stb[sb_i[0] % 4]

        def load(gi):
            c0, w, _ = GROUPS[gi]
            s.dma("sp", wst[gi % 2][:, :, 0:w], w_in[l, :, c0:c0 + w].rearrange("(k p) n -> p k n", p=128),
                  writes=[wst[gi % 2]])

        def cast(gi):
            c0, w, _ = GROUPS[gi]
            if gi % 2 == 0:
                s.op("act", lambda e: e.activation(out=wbf[gi % 2][:, :, 0:w], in_=wst[gi % 2][:, :, 0:w], func=AF.Copy),
                     reads=[wst[gi % 2]], writes=[wbf[gi % 2]])
            else:
                s.op("dve", lambda e: e.tensor_copy(out=wbf[gi % 2][:, :, 0:w], in_=wst[gi % 2][:, :, 0:w]),
                     reads=[wst[gi % 2]], writes=[wbf[gi % 2]])

        load(0)
        cast(0)
        load(1)
        for gi, (c0, w, jobs) in enumerate(GROUPS):
            if gi + 1 < len(GROUPS):
                cast(gi + 1)
            if gi + 2 < len(GROUPS):
                load(gi + 2)
            W = wbf[gi % 2]
            for job in jobs:
                if job[0] == "v":
                    _, off, ncol, vdst = job
                    for ti in range(18):
                        P = ps[ps_i[0] % 6]
                        ps_i[0] += 1
                        jb = 0 if ti < 2 else 1 + (ti - 2) // 4
                        for k in range(8):
                            s.mm(P[:, 0:ncol], hT[:, k, ti * 128:(ti + 1) * 128], W[:, k, off:off + ncol], k == 0, k == 7,
                                 reads=[hTb[jb], W], writes=[P] if k == 0 else [], acc=[] if k == 0 else [P])
                        B_ = nstb()
                        if ti % 2 == 0:
                            s.op("act", lambda e: e.activation(out=B_[:, 0:ncol], in_=P[:, 0:ncol], func=AF.Copy),
                                 reads=[P], writes=[B_])
                        else:
                            s.op("dve", lambda e: e.tensor_copy(out=B_[:, 0:ncol], in_=P[:, 0:ncol]), reads=[P], writes=[B_])
                        s.dma("sp", v_tok[ti * 128:(ti + 1) * 128, vdst:vdst + ncol], B_[:, 0:ncol], reads=[B_], acc=[v_tok])
                    continue
                _, kind, off, m, pc = job
                for j, (st, n) in enumerate(BLOCKS):
                    P = ps[ps_i[0] % 6]
                    ps_i[0] += 1
                    if kind == "gk":
                        for half in range(2):
                            for k in range(8):
                                s.mm(P[64 * half:64 * half + 64, 0:n], W[:, k, off:off + 64], hT[:, k, st:st + n],
                                     k == 0, k == 7, reads=[hTb[j], W],
                                     writes=[P] if (k == 0 and half == 0) else [], acc=[] if (k == 0 and half == 0) else [P])
                        mm_ = 128
                    else:
                        for k in range(8):
                            s.mm(P[0:m, 0:n], W[:, k, off:off + m], hT[:, k, st:st + n], k == 0, k == 7,
                                 reads=[hTb[j], W], writes=[P] if k == 0 else [], acc=[] if k == 0 else [P])
                        mm_ = m
                    rows = slice(pc * 128, pc * 128 + 128)
                    if kind in ("hq", "hog", "mog", "mq", "mk", "gq", "gk"):
                        S_ = nstg()
                        if kind in ("hq", "hog"):
                            s.op("act", lambda e: e.activation(out=S_[:, 0:n], in_=P[:, 0:n], func=AF.Silu), reads=[P], writes=[S_])
                        elif kind == "mog":
                            s.op("act", lambda e: e.activation(out=S_[:, 0:n], in_=P[:, 0:n], func=AF.Sigmoid), reads=[P], writes=[S_])
                        elif kind == "mk":
                            s.op("dve", lambda e: e.tensor_scalar(out=S_[:, 0:n], in0=P[:, 0:n], scalar1=0.125, scalar2=None,
                                                                  op0=ALU.mult), reads=[P], writes=[S_])
                        else:
                            s.op("dve", lambda e: e.tensor_copy(out=S_[:, 0:n], in_=P[:, 0:n]), reads=[P], writes=[S_])
                        dst = {"hq": zq_h, "hog": zog_h, "mog": zog_m, "mq": zq_m, "mk": zk_m, "gq": zq_g}.get(kind)
                        if kind == "gk":
                            s.dma("sp", zk_g[pc, :, st:st + n], S_[:, 0:n], reads=[S_], acc=[zk_g])
                        else:
                            s.dma("sp", dst[rows, st:st + n], S_[:, 0:n], reads=[S_], acc=[dst])
                    elif kind in ("hf0", "hf1"):
                        d = 0 if kind == "hf0" else 1
                        SG = nstg()
                        FG = nstg()
                        KK = nstg()
                        s.op("act", lambda e: e.activation(out=SG[:, 0:n], in_=P[:, 0:n], func=AF.Sigmoid), reads=[P], writes=[SG])
                        s.op("dve", lambda e: e.tensor_scalar(out=FG[:, 0:n], in0=SG[:, 0:n], scalar1=oml[:, l, d, pc:pc + 1],
                                                              scalar2=lb[:, l, d, pc:pc + 1], op0=ALU.mult, op1=ALU.add),
                             reads=[SG, oml, lb], writes=[FG])
                        s.op("act", lambda e: e.activation(out=FG[:, 0:n], in_=FG[:, 0:n], func=AF.Ln), reads=[FG], writes=[FG])
                        s.op("dve", lambda e: e.tensor_scalar(out=KK[:, 0:n], in0=SG[:, 0:n], scalar1=noml[:, l, d, pc:pc + 1],
                                                              scalar2=oml[:, l, d, pc:pc + 1], op0=ALU.mult, op1=ALU.add),
                             reads=[SG, oml, noml], writes=[KK])
                        s.dma("sp", zlf_h[d, rows, st:st + n], FG[:, 0:n], reads=[FG], acc=[zlf_h])
                        s.dma("sp", zkk_h[d, rows, st:st + n], KK[:, 0:n], reads=[KK], acc=[zkk_h])
                    elif kind == "mg":
                        S1 = nstg()
                        S2 = nstg()
                        s.op("act", lambda e: e.activation(out=S1[0:16, 0:n], in_=P[0:16, 0:n], func=AF.Identity,
                                                           bias=mgbT[:, l:l + 1], scale=1.0), reads=[P, mgbT], writes=[S1])
                        s.op("act", lambda e: e.activation(out=S2[0:16, 0:n], in_=P[0:16, 0:n], func=AF.Sigmoid,
                                                           bias=mgbT[:, l:l + 1], scale=1.0), reads=[P, mgbT], writes=[S2])
                        s.op("act", lambda e: e.activation(out=S2[0:16, 0:n], in_=S2[0:16, 0:n], func=AF.Ln), reads=[S2], writes=[S2])
                        s.dma("sp", zg_m[0:8, st:st + n], S1[0:8, 0:n], reads=[S1], acc=[zg_m])
                        s.dma("sp", zg_m[8:16, st:st + n], S2[8:16, 0:n], reads=[S2], acc=[zg_m])
                    elif kind in ("nq", "nk"):
                        B_ = nstb()
                        s.op("act", lambda e: e.activation(out=B_[:, 0:n], in_=P[:, 0:n], func=AF.Copy), reads=[P], writes=[B_])
                        dst = zq_n if kind == "nq" else zk_n
                        s.dma("sp", dst[rows, st:st + n], B_[:, 0:n], reads=[B_], acc=[dst])
                    else:
                        raise ValueError(kind)
        s.release(m0)

    def head_norm(o, gate, gidx, l, chunk, blocks, sqs, pn, rss, tms):
        for j in blocks:
            st, n = BLOCKS[j]
            SQ = sqs[j % 2]
            R = rss[j % 2]
            T = tms[j % 2]
            s.op("act", lambda e: e.activation(out=SQ[:, 0:n], in_=o[:, st:st + n], func=AF.Square), reads=[o], writes=[SQ])
            s.mm(pn[:, 0:n], blkb[:, :], SQ[:, 0:n], True, True, reads=[blkb, SQ], writes=[pn])
            s.op("act", lambda e: e.activation(out=R[:, 0:n], in_=pn[:, 0:n], func=AF.Sqrt, scale=1.0 / 64, bias=EPS),
                 reads=[pn], writes=[R])
            s.op("dve", lambda e: e.reciprocal(out=R[:, 0:n], in_=R[:, 0:n]), reads=[R], writes=[R])
            s.op("dve", lambda e: e.scalar_tensor_tensor(out=T[:, 0:n], in0=o[:, st:st + n], scalar=gcol[:, l, gidx:gidx + 1],
                                                         in1=R[:, 0:n], op0=ALU.mult, op1=ALU.mult),
                 reads=[o, R, gcol], writes=[T])
            s.op("dve", lambda e: e.tensor_tensor(out=hT[:, chunk, st:st + n], in0=T[:, 0:n], in1=gate[:, st:st + n],
                                                  op=ALU.mult), reads=[T, gate], acc=[hTb[j]], selfdep=False)

    def recur(b, l, kind):
        m0 = s.mark()
        NV = 1 if kind == "h" else 2
        L = 16 if kind == "h" else 64
        NCH = NT // L
        CTXN = NCTX // L
        HB = L
        SR = 2 * L
        PG = min(128 // SR, 3)
        last = (l == nlayers - 1)
        blocks = [1, 2, 3, 4] if last else [0, 1, 2, 3, 4]
        PTR = [s.psum("ptr%d" % d, [128, 1024], BF16) for d in range(2)]
        PSC = [s.psum("psc%d" % d, [128, 512], F32) for d in range(2)]
        PSO = [s.psum("pso%d" % d, [128, 512], F32) for d in range(2)]
        PST = [s.psum("pstt%d" % d, [128, 512], F32) for d in range(2)]
        pn = PSC[0]
        lfs = [s.sbuf("r_lf%d" % d, [128, NT], F32) for d in range(2)]
        PP = s.sbuf("r_PP", [128, NT + 1], F32)
        kks = [s.sbuf("r_kk%d" % d, [128, NT], F32) for d in range(2)]
        qs = s.sbuf("r_qs", [128, NT], F32)
        igbs = [s.sbuf("r_ig%d" % d, [128, NT], F32) for d in range(2)] if kind == "m" else None
        Q = [s.sbuf("r_Q%d" % d, [128, NT], BF16) for d in range(2)]
        K = [s.sbuf("r_K%d" % d, [128, NCH, 2 * L], BF16) for d in range(2)]
        KZ = []
        KN = [s.sbuf("r_KN%d" % d, [128, NT], BF16) for d in range(2)]
        Vbd = s.sbuf("r_Vbd", [128, NCH // PG, 128], BF16)
        Vp = s.sbuf("r_Vp", [128, NCH // PG, 128], BF16)
        Rn = [s.sbuf("r_Rn%d" % d, [128, NCH], F32) for d in range(2)]
        btm = [s.sbuf("r_bt%d" % i, [128, 512], F32) for i in range(10)]
        bti = [0]

        def ntmp():
            bti[0] += 1
            return btm[bti[0] % 10]

        def blk_of(c):
            t = c * L
            return 0 if t < 256 else 1 + (t - 256) // 512
        G = [s.sbuf("r_G%d" % d, [128, NCH], F32) for d in range(2)]
        W32 = [[s.sbuf("r_W%d%d" % (d, v), [128, 64], F32) for v in range(NV)] for d in range(2)]
        Wbf = [[s.sbuf("r_Wb%d%d" % (d, v), [128, 64], BF16) for v in range(NV)] for d in range(2)]
        kts = [s.sbuf("r_kt%d" % i, [128, 128], BF16) for i in range(4)]
        pms = [s.sbuf("r_pm%d" % i, [128, L], BF16) for i in range(4)]
        for pmb in pms:
            s.op("pool", lambda e: e.memset(pmb[:, :], 0.0), writes=[pmb])
        sqs = [s.sbuf("r_sq%d" % i, [128, 512], BF16) for i in range(2)]
        rss = [s.sbuf("r_rs%d" % i, [128, 512], F32) for i in range(2)]
        tms = [s.sbuf("r_tm%d" % i, [128, 512], F32) for i in range(2)]
        Qv = [Sch.views(Q[d], 5) for d in range(2)]
        Kv = [Sch.views(K[d], 5) for d in range(2)]
        KNv = [Sch.views(KN[d], 5) for d in range(2)]
        for d in range(2):
            s.op("act", lambda e: e.memzero(K[d][:, :, :]), writes=Kv[d])
        if kind == "h":
            accs = [[lfs[0]], [lfs[1]]]
        else:
            accs = [[lfs[0], igbs[0]], [lfs[1], igbs[1]]]
        zq = zq_h if kind == "h" else zq_m
        zog = zog_h if kind == "h" else zog_m
        vbase = 0 if kind == "h" else 256
        gidx = 0 if kind == "h" else 1
        order = [list(range(NCH)), list(range(CTXN - 1, -1, -1)) + list(range(NCH - 1, CTXN - 1, -1))]
        slot = 0
        for pc in range(2):
            rows = slice(pc * 128, pc * 128 + 128)
            vcol = vbase + pc * 128
            s.dma("sp", qs[:, :], zq[rows, :], reads=[zq], writes=[qs])
            for d in range(2):
                lf, kk = lfs[d], kks[d]
                if kind == "h":
                    s.dma("sp", lf[:, :], zlf_h[d, rows, :], reads=[zlf_h], writes=[lf])
                    s.dma("sp", kk[:, :], zkk_h[d, rows, :], reads=[zkk_h], writes=[kk])
                else:
                    igb = igbs[d]
                    for hh in range(2):
                        h = 2 * pc + hh
                        s.dma("sp", lf[64 * hh:64 * hh + 64, :], zg_m[8 + 4 * d + h:9 + 4 * d + h, :].partition_broadcast(64),
                              reads=[zg_m], writes=[lf] if hh == 0 else [], acc=[] if hh == 0 else [lf])
                        s.dma("sp", igb[64 * hh:64 * hh + 64, :], zg_m[4 * d + h:4 * d + h + 1, :].partition_broadcast(64),
                              reads=[zg_m], writes=[igb] if hh == 0 else [], acc=[] if hh == 0 else [igb])
                    s.dma("sp", kk[:, :], zk_m[rows, :], reads=[zk_m], writes=[kk])
            if pc == 0:
                s.op("act", lambda e: e.memzero(Vbd[:, :, :]), writes=[Vbd])
            for g in range(PG):
                for hh in range(2):
                    s.dma("sp", Vbd[g * SR + HB * hh:g * SR + HB * hh + L, :, 64 * hh:64 * hh + 64],
                          v_tok[:, vcol + 64 * hh:vcol + 64 * hh + 64].rearrange("(cq g p) d -> g p cq d", g=PG, p=L)[g],
                          reads=[v_tok], acc=[Vbd])
                s.dma("sp", Vp[g * SR:g * SR + L, :, :],
                      v_tok[:, vcol:vcol + 128].rearrange("(cq g p) d -> g p cq d", g=PG, p=L)[g],
                      reads=[v_tok], writes=[Vp] if g == 0 else [], acc=[] if g == 0 else [Vp])
            for d in range(2):
                sg = 1.0 if d == 0 else -1.0
                lf, kk = lfs[d], kks[d]
                igb = igbs[d] if kind == "m" else None
                s.op("pool", lambda e: e.memset(PP[:, 0:1], 0.0), writes=[PP])
                s.op("dve", lambda e: e.tensor_tensor_scan(out=PP[:, 1:NT + 1], data0=lf[:, :], data1=lf[:, :], initial=0.0,
                                                           op0=ALU.add, op1=ALU.bypass), reads=[lf], acc=[PP])
                Rm = PP[:, 0:NT].rearrange("p (c l) -> p c l", l=L)[:, :, L // 2]
                if d == 0:
                    s.op("dve", lambda e: e.tensor_copy(out=Rn[d][:, 0:NCH - 1], in_=Rm[:, 1:NCH]), reads=[PP], writes=[Rn[d]])
                    s.op("dve", lambda e: e.tensor_copy(out=Rn[d][:, NCH - 1:NCH], in_=Rm[:, NCH - 1:NCH]), reads=[PP], acc=[Rn[d]])
                    s.op("dve", lambda e: e.tensor_tensor(out=G[d][:, :], in0=Rn[d][:, :], in1=Rm, op=ALU.subtract),
                         reads=[Rn[d], PP], writes=[G[d]])
                else:
                    s.op("dve", lambda e: e.tensor_copy(out=Rn[d][:, 1:NCH], in_=Rm[:, 0:NCH - 1]), reads=[PP], writes=[Rn[d]])
                    s.op("dve", lambda e: e.tensor_copy(out=Rn[d][:, CTXN:CTXN + 1], in_=Rm[:, CTXN:CTXN + 1]), reads=[PP], acc=[Rn[d]])
                    s.op("dve", lambda e: e.tensor_tensor(out=Rn[d][:, 0:1], in0=Rm[:, NCH - 1:NCH], in1=PP[:, NT:NT + 1],
                                                          op=ALU.subtract), reads=[PP], acc=[Rn[d]])
                    s.op("dve", lambda e: e.tensor_tensor(out=G[d][:, :], in0=Rm, in1=Rn[d][:, :], op=ALU.subtract),
                         reads=[Rn[d], PP], writes=[G[d]])
                s.op("act", lambda e: e.activation(out=G[d][:, :], in_=G[d][:, :], func=AF.Exp), reads=[G[d]], writes=[G[d]])
                border = [0, 1, 2, 3, 4] if d == 0 else [0, 4, 3, 2, 1]
                for j in border:
                    st, n = BLOCKS[j]
                    c0, ncb = st // L, n // L
                    off = 1 if d == 0 else 0
                    PPs = PP[:, st + off:st + off + n].rearrange("p (c l) -> p c l", l=L)
                    T1, T2, T3, T4, T5 = [ntmp() for _ in range(5)]
                    v3 = lambda T: T[:, 0:n].rearrange("p (c l) -> p c l", l=L)
                    s.op("dve", lambda e: e.tensor_tensor(out=v3(T1), in0=PPs,
                                                          in1=Rm[:, c0:c0 + ncb].unsqueeze(2).to_broadcast([128, ncb, L]),
                                                          op=ALU.subtract), reads=[PP], writes=[T1])
                    s.op("act", lambda e: e.activation(out=T2[:, 0:n], in_=T1[:, 0:n], func=AF.Exp, scale=sg), reads=[T1], writes=[T2])
                    s.op("dve", lambda e: e.scalar_tensor_tensor(out=Q[d][:, st:st + n], in0=qs[:, st:st + n],
                                                                 scalar=(0.125 if kind == "h" else 1.0), in1=T2[:, 0:n],
                                                                 op0=ALU.mult, op1=ALU.mult),
                         reads=[qs, T2], acc=[Qv[d][j]], selfdep=False)
                    if kind == "m":
                        s.op("dve", lambda e: e.scalar_tensor_tensor(out=T3[:, 0:n], in0=T1[:, 0:n], scalar=-sg, in1=igb[:, st:st + n],
                                                                     op0=ALU.mult, op1=ALU.add), reads=[T1, igb], writes=[T3])
                        s.op("act", lambda e: e.activation(out=T3[:, 0:n], in_=T3[:, 0:n], func=AF.Exp), reads=[T3], writes=[T3])
                    else:
                        s.op("act", lambda e: e.activation(out=T3[:, 0:n], in_=T1[:, 0:n], func=AF.Exp, scale=-sg), reads=[T1], writes=[T3])
                    s.op("dve", lambda e: e.tensor_tensor(out=T4[:, 0:n], in0=kk[:, st:st + n], in1=T3[:, 0:n], op=ALU.mult),
                         reads=[kk, T3], writes=[T4])
                    for hh in range(2):
                        hs = slice(64 * hh, 64 * hh + 64)
                        if hh == 0:
                            s.op("act", lambda e: e.activation(out=K[d][hs, c0:c0 + ncb, L * hh:L * hh + L],
                                                               in_=T4[hs, 0:n].rearrange("p (c l) -> p c l", l=L), func=AF.Copy),
                                 reads=[T4], acc=[Kv[d][j]], selfdep=False)
                        else:
                            s.op("pool", lambda e: e.tensor_copy(out=K[d][hs, c0:c0 + ncb, L * hh:L * hh + L],
                                                                 in_=T4[hs, 0:n].rearrange("p (c l) -> p c l", l=L)),
                                 reads=[T4], acc=[Kv[d][j]], selfdep=False)
                    s.op("pool", lambda e: e.tensor_tensor(out=KN[d][:, st:st + n].rearrange("p (c l) -> p c l", l=L), in0=v3(T4),
                                                           in1=G[d][:, c0:c0 + ncb].unsqueeze(2).to_broadcast([128, ncb, L]),
                                                           op=ALU.mult), reads=[T4, G[d]], acc=[KNv[d][j]], selfdep=False)
                for v in range(NV):
                    s.op("pool", lambda e: e.memset(W32[d][v][:, :], 0.0), writes=[W32[d][v]])
            if dbg and stop_after == "hbuild":
                for d in range(2):
                    pass
                s.dma("sp", dbgP[:, :], PP[:, :], reads=[PP], writes=[dbgPb])
                s.release(m0)
                return
            seq = [(step, d) for step in range(NCH) for d in range(2)]
            slot0 = slot

            def phaseA(i):
                step, d = seq[i]
                c = order[d][step]
                csl = slice(c * L, c * L + L)
                sl = (slot0 + i) % 4
                lastst = (step == NCH - 1)
                kt = kts[sl]
                pm = pms[sl]
                ptr, psc = PTR[d], PSC[d]
                pb = (c % PG) * SR
                if not lastst:
                    s.op("pe", lambda e: e.transpose(out=ptr[pb:pb + L, 0:128], in_=KN[d][:, csl], identity=identb[:, :]),
                         reads=[KNv[d][blk_of(c)], identb], writes=[ptr])
                    s.op("act", lambda e: e.activation(out=kt[pb:pb + L, :], in_=ptr[pb:pb + L, 0:128], func=AF.Copy),
                         reads=[ptr], writes=[kt])
                s.mm(psc[pb:pb + SR, 0:L], K[d][:, c, :], Q[d][:, csl], True, True,
                     reads=[Kv[d][blk_of(c)], Qv[d][blk_of(c)]], writes=[psc])
                s.op("dve", lambda e: e.tensor_tensor(out=pm[pb:pb + SR, :], in0=psc[pb:pb + SR, 0:L],
                                                      in1=maskFB[L][d][pb:pb + SR, :], op=ALU.mult),
                     reads=[psc, cs], writes=[pm])

            def phaseB(i):
                step, d = seq[i]
                c = order[d][step]
                csl = slice(c * L, c * L + L)
                sl = (slot0 + i) % 4
                first = (step == 0)
                lastst = (step == NCH - 1)
                kt = kts[sl]
                pm = pms[sl]
                pso, pst = PSO[d], PST[d]
                pb = (c % PG) * SR
                cq = c // PG
                for v in range(NV):
                    Vb = Vbd[pb:pb + SR, cq, :] if v == 0 else bd1[:, :]
                    vo = slice(v * 64, v * 64 + L)
                    s.mm(pso[:, vo], Vb, pm[pb:pb + SR, :], True, first, reads=[Vbd, bd1, pm],
                         writes=[pso] if v == 0 else [], acc=[] if v == 0 else [pso])
                    if not first:
                        for hh in range(2):
                            hs = slice(64 * hh, 64 * hh + 64)
                            s.mm(pso[hs, vo], Wbf[d][v][hs, :], Q[d][hs, csl], False, hh == 1,
                                 reads=[Wbf[d][v], Qv[d][blk_of(c)]], acc=[pso])
                for v in range(NV):
                    vo = slice(v * 64, v * 64 + L)
                    A_ = accs[d][v]
                    s.op("act", lambda e: e.activation(out=A_[:, csl], in_=pso[:, vo], func=AF.Copy),
                         reads=[pso], acc=[A_], selfdep=False)
                if not lastst:
                    for v in range(NV):
                        vt = slice(v * 64, v * 64 + 64)
                        for hh in range(2):
                            hs = slice(64 * hh, 64 * hh + 64)
                            rhs = Vp[pb:pb + L, cq, hs] if v == 0 else onesb[pb:pb + L, 0:64]
                            wr = (v == 0 and hh == 0)
                            s.mm(pst[hs, vt], kt[pb:pb + L, hs], rhs, True, True, reads=[kt, Vp, onesb],
                                 writes=[pst] if wr else [], acc=[] if wr else [pst])
                    for v in range(NV):
                        vt = slice(v * 64, v * 64 + 64)
                        s.op("dve", lambda e: e.scalar_tensor_tensor(out=Wbf[d][v][:, :], in0=W32[d][v][:, :],
                                                                     scalar=G[d][:, c:c + 1], in1=pst[:, vt],
                                                                     op0=ALU.mult, op1=ALU.add),
                             reads=[W32[d][v], G[d], pst], writes=[Wbf[d][v]])
                    for v in range(NV):
                        vt = slice(v * 64, v * 64 + 64)
                        s.op("dve", lambda e: e.scalar_tensor_tensor(out=W32[d][v][:, :], in0=W32[d][v][:, :],
                                                                     scalar=G[d][:, c:c + 1], in1=pst[:, vt],
                                                                     op0=ALU.mult, op1=ALU.add),
                             reads=[W32[d][v], G[d], pst], writes=[W32[d][v]])

            phaseA(0)
            for i in range(len(seq)):
                if i + 1 < len(seq):
                    phaseA(i + 1)
                phaseB(i)
            slot += len(seq)
            s.dma("sp", qs[:, :], zog[rows, :], reads=[zog] + Qv[0] + Qv[1], writes=[qs])
            chunk = (0 if kind == "h" else 2) + pc
            for j in blocks:
                st, n = BLOCKS[j]
                O = ntmp()
                if kind == "m":
                    hd = []
                    for d in range(2):
                        num, den = accs[d]
                        Ta = ntmp()
                        Th = ntmp()
                        s.op("act", lambda e: e.activation(out=Ta[:, 0:n], in_=den[:, st:st + n], func=AF.Abs), reads=[den], writes=[Ta])
                        s.op("dve", lambda e: e.tensor_scalar_max(out=Ta[:, 0:n], in0=Ta[:, 0:n], scalar1=1.0), reads=[Ta], writes=[Ta])
                        s.op("act", lambda e: e.activation(out=Ta[:, 0:n], in_=Ta[:, 0:n], func=AF.Ln), reads=[Ta], writes=[Ta])
                        s.op("act", lambda e: e.activation(out=Ta[:, 0:n], in_=Ta[:, 0:n], func=AF.Exp, scale=-1.0), reads=[Ta], writes=[Ta])
                        s.op("dve", lambda e: e.tensor_tensor(out=Th[:, 0:n], in0=num[:, st:st + n], in1=Ta[:, 0:n], op=ALU.mult),
                             reads=[num, Ta], writes=[Th])
                        hd.append(Th)
                    s.op("pool", lambda e: e.tensor_tensor(out=O[:, 0:n], in0=hd[0][:, 0:n], in1=hd[1][:, 0:n], op=ALU.add),
                         reads=hd, writes=[O])
                else:
                    s.op("pool", lambda e: e.tensor_tensor(out=O[:, 0:n], in0=accs[0][0][:, st:st + n], in1=accs[1][0][:, st:st + n],
                                                           op=ALU.add), reads=[accs[0][0], accs[1][0]], writes=[O])
                SQ = sqs[j % 2]
                R = ntmp()
                T = ntmp()
                s.op("act", lambda e: e.activation(out=SQ[:, 0:n], in_=O[:, 0:n], func=AF.Square), reads=[O], writes=[SQ])
                s.mm(pn[:, 0:n], blkb[:, :], SQ[:, 0:n], True, True, reads=[blkb, SQ], writes=[pn])
                s.op("act", lambda e: e.activation(out=R[:, 0:n], in_=pn[:, 0:n], func=AF.Ln, scale=1.0 / 64, bias=EPS),
                     reads=[pn], writes=[R])
                s.op("act", lambda e: e.activation(out=R[:, 0:n], in_=R[:, 0:n], func=AF.Exp, scale=-0.5), reads=[R], writes=[R])
                s.op("dve", lambda e: e.scalar_tensor_tensor(out=T[:, 0:n], in0=O[:, 0:n], scalar=gcol[:, l, gidx:gidx + 1],
                                                             in1=R[:, 0:n], op0=ALU.mult, op1=ALU.mult),
                     reads=[O, R, gcol], writes=[T])
                s.op("dve", lambda e: e.tensor_tensor(out=hT[:, chunk, st:st + n], in0=T[:, 0:n], in1=qs[:, st:st + n],
                                                      op=ALU.mult), reads=[T, qs], acc=[hTb[j]], selfdep=False)
        s.release(m0)

    def attn_block(qb, kb, vfn, vbuf, st, n, kcs, chunk, sc_ps, num_ps, den_ps, pTs, rd, cnt, qz=None, ofn=None):
        j = [i for i, (a, _) in enumerate(BLOCKS) if a <= st < a + BLOCKS[i][1]][0]
        nk = len(kcs)
        its = [(ki, kc, hh) for ki, kc in enumerate(kcs) for hh in range(2)]
        scs = {}

        def issue_sc(i):
            ki, kc, hh = its[i]
            hs = slice(64 * hh, 64 * hh + 64)
            sc = sc_ps[cnt[0] % len(sc_ps)]
            pT = pTs[cnt[0] % len(pTs)]
            cnt[0] += 1
            if qz is None:
                s.mm(sc[:, 0:n], kb[hs, kc * 128:(kc + 1) * 128], qb[hs, st:st + n], True, True, reads=[kb, qb], writes=[sc])
            else:
                s.mm(sc[:, 0:n], kb[:, kc * 128:(kc + 1) * 128], qz[hh][:, st:st + n], True, True, reads=[kb, qz[hh]], writes=[sc])
            scs[i] = (sc, pT)

        issue_sc(0)
        if len(its) > 1:
            issue_sc(1)
        for i, (ki, kc, hh) in enumerate(its):
            hs = slice(64 * hh, 64 * hh + 64)
            if i + 2 < len(its):
                issue_sc(i + 2)
            sc, pT = scs.pop(i)
            s.op("act", lambda e: e.activation(out=pT[:, 0:n], in_=sc[:, 0:n], func=AF.Exp, scale=0.125),
                 reads=[sc], writes=[pT])
            first = (ki == 0)
            if qz is None:
                s.mm(num_ps[hs, 0:n], vfn(kc, hh), pT[:, 0:n], first, ki == nk - 1, reads=[pT, vbuf],
                     writes=[num_ps] if (first and hh == 0) else [], acc=[] if (first and hh == 0) else [num_ps])
                s.mm(den_ps[hs, 0:n], onesb[:, 0:64], pT[:, 0:n], first, ki == nk - 1, reads=[pT, onesb],
                     writes=[den_ps] if (first and hh == 0) else [], acc=[] if (first and hh == 0) else [den_ps])
            else:
                f0 = (i == 0)
                l0 = (i == len(its) - 1)
                s.mm(num_ps[:, 0:n], vfn(kc, hh), pT[:, 0:n], f0, l0, reads=[pT, vbuf],
                     writes=[num_ps] if f0 else [], acc=[] if f0 else [num_ps])
                s.mm(den_ps[:, 0:n], ofn(hh), pT[:, 0:n], f0, l0, reads=[pT, vbuf],
                     writes=[den_ps] if f0 else [], acc=[] if f0 else [den_ps])
        s.op("act", lambda e: e.activation(out=rd[:, 0:n], in_=den_ps[:, 0:n], func=AF.Ln), reads=[den_ps], writes=[rd])
        s.op("act", lambda e: e.activation(out=rd[:, 0:n], in_=rd[:, 0:n], func=AF.Exp, scale=-1.0), reads=[rd], writes=[rd])
        s.op("dve", lambda e: e.tensor_tensor(out=hT[:, chunk, st:st + n], in0=num_ps[:, 0:n], in1=rd[:, 0:n], op=ALU.mult),
             reads=[num_ps, rd], acc=[hTb[j]], selfdep=False)

    def gqa(b, l):
        m0 = s.mark()
        last = (l == nlayers - 1)
        rp = s.sbuf("g_rope", [128, 4096], F32)
        s.dma("sp", rp[:, :], ropec[:, :], writes=[rp])
        raw = [s.sbuf("g_raw%d" % i, [128, NT], F32) for i in range(2)]
        QK = [s.sbuf("g_qk%d" % i, [128, NT], BF16) for i in range(4)]
        sqs = [s.sbuf("g_sq%d" % i, [128, 512], BF16) for i in range(4)]
        rss = [s.sbuf("g_rs%d" % i, [128, 512], F32) for i in range(4)]
        t1s = [s.sbuf("g_t1%d" % i, [128, 512], F32) for i in range(4)]
        t2s = [s.sbuf("g_t2%d" % i, [128, 512], F32) for i in range(4)]
        t3s = [s.sbuf("g_t3%d" % i, [128, 512], F32) for i in range(4)]
        Vz = s.sbuf("g_Vz", [128, 18, 4, 128], BF16)
        Oz = s.sbuf("g_Oz", [128, 2, 128], BF16)
        Qz = [[s.sbuf("g_qz%d%d" % (p_, h_), [128, NT], BF16) for h_ in range(2)] for p_ in range(2)]
        pTs = [s.sbuf("g_pT%d" % i, [128, 512], BF16) for i in range(4)]
        rds = [s.sbuf("g_rd%d" % i, [128, 512], F32) for i in range(2)]
        sc_ps = [s.psum("g_sc%d" % i, [128, 512], F32) for i in range(4)]
        num_ps = [s.psum("g_num%d" % i, [128, 512], F32) for i in range(2)]
        den_ps = [s.psum("g_den%d" % i, [128, 512], F32) for i in range(2)]
        s.op("dve", lambda e: e.memset(Vz[:, :, :, :], 0.0), writes=[Vz])
        s.op("pool", lambda e: e.memset(Oz[:, :, :], 0.0), writes=[Oz])
        for hh in range(2):
            s.op("pool", lambda e: e.memset(Oz[:, hh, 64 * hh:64 * hh + 64], 1.0), acc=[Oz], selfdep=True)
            for p_ in range(2):
                s.op("dve", lambda e: e.memset(Qz[p_][hh][64 * (1 - hh):64 * (1 - hh) + 64, :], 0.0), writes=[Qz[p_][hh]])
                s.dma("sp", Vz[:, :, 2 * p_ + hh, 64 * hh:64 * hh + 64],
                      v_tok[:, 512 + 64 * p_:512 + 64 * p_ + 64].rearrange("(c p) d -> p c d", p=128), reads=[v_tok], acc=[Vz])
        srcs = [(zq_g[0:128, :], 2), (zq_g[128:256, :], 2), (zk_g[0, :, :], 3), (zk_g[1, :, :], 3)]
        it = 0
        for idx, (src, gi) in enumerate(srcs):
            R_ = raw[idx % 2]
            s.dma("sp", R_[:, :], src, reads=[zq_g, zk_g], writes=[R_])
            for j, (st, n) in enumerate(BLOCKS):
                SQ = sqs[it % 4]
                RS = rss[it % 4]
                T1 = t1s[it % 4]
                T2 = t2s[it % 4]
                T3 = t3s[it % 4]
                P1 = sc_ps[(2 * it) % 4]
                P2 = sc_ps[(2 * it + 1) % 4]
                it += 1
                s.op("act", lambda e: e.activation(out=SQ[:, 0:n], in_=R_[:, st:st + n], func=AF.Square), reads=[R_], writes=[SQ])
                s.mm(P1[:, 0:n], blkb[:, :], SQ[:, 0:n], True, True, reads=[blkb, SQ], writes=[P1])
                s.op("act", lambda e: e.activation(out=RS[:, 0:n], in_=P1[:, 0:n], func=AF.Ln, scale=1.0 / 64, bias=EPS),
                     reads=[P1], writes=[RS])
                s.op("act", lambda e: e.activation(out=RS[:, 0:n], in_=RS[:, 0:n], func=AF.Exp, scale=-0.5), reads=[RS], writes=[RS])
                s.op("dve", lambda e: e.scalar_tensor_tensor(out=T1[:, 0:n], in0=R_[:, st:st + n], scalar=gcol[:, l, gi:gi + 1],
                                                             in1=RS[:, 0:n], op0=ALU.mult, op1=ALU.mult),
                     reads=[R_, RS, gcol], writes=[T1])
                if st >= 256:
                    s.mm(P2[:, 0:n], ropeRT, T1[:, 0:n], True, True, reads=[cs, T1], writes=[P2])
                    s.op("dve", lambda e: e.tensor_tensor(out=T2[:, 0:n], in0=T1[:, 0:n], in1=rp[:, st - 256:st - 256 + n],
                                                          op=ALU.mult), reads=[T1, rp], writes=[T2])
                    s.op("dve", lambda e: e.tensor_tensor(out=T3[:, 0:n], in0=P2[:, 0:n],
                                                          in1=rp[:, 2048 + st - 256:2048 + st - 256 + n], op=ALU.mult),
                         reads=[P2, rp], writes=[T3])
                    s.op("dve", lambda e: e.tensor_tensor(out=QK[idx][:, st:st + n], in0=T2[:, 0:n], in1=T3[:, 0:n], op=ALU.add),
                         reads=[T2, T3], acc=[QK[idx]], selfdep=False)
                else:
                    s.op("act", lambda e: e.activation(out=QK[idx][:, st:st + n], in_=T1[:, 0:n], func=AF.Copy),
                         reads=[T1], acc=[QK[idx]], selfdep=False)
                if idx < 2:
                    for hh in range(2):
                        hs = slice(64 * hh, 64 * hh + 64)
                        s.op("pool", lambda e: e.tensor_copy(out=Qz[idx][hh][hs, st:st + n], in_=QK[idx][hs, st:st + n]),
                             reads=[QK[idx]], acc=[Qz[idx][hh]], selfdep=False)
        cnt = [0]
        qblocks = [1, 2, 3, 4] if last else [0, 1, 2, 3, 4]
        bi = 0
        for pc in range(2):
            for j in qblocks:
                st, n = BLOCKS[j]
                kcs = list(range(18)) if st >= 256 else [0, 1]
                attn_block(QK[pc], QK[2 + pc], lambda kc, hh: Vz[:, kc, 2 * pc + hh, :], Vz, st, n, kcs, 4 + pc,
                           sc_ps, num_ps[bi % 2], den_ps[bi % 2], pTs, rds[bi % 2], cnt,
                           qz=Qz[pc], ofn=lambda hh: Oz[:, hh, :])
                bi += 1
        s.release(m0)

    def na(b, l):
        m0 = s.mark()
        last = (l == nlayers - 1)
        bias8 = s.sbuf("n_bias", [128, 4 * NU, 128], BF16)
        bst = [s.sbuf("n_bst%d" % i, [128, NU, 128], F32) for i in range(2)]
        qT = [s.sbuf("n_q%d" % i, [128, NT], BF16) for i in range(2)]
        kT = [s.sbuf("n_k%d" % i, [128, NT], BF16) for i in range(2)]
        Vn = s.sbuf("n_V", [128, 18, 256], BF16)
        pTw = [s.sbuf("n_pTw%d" % i, [128, 2048], BF16) for i in range(2)]
        qz = [[s.sbuf("n_qz%d%d" % (p_, h_), [128, NT], BF16) for h_ in range(2)] for p_ in range(2)]
        Vnz = s.sbuf("n_Vz", [128, 18, 4, 128], BF16)
        Oz = s.sbuf("n_Oz", [128, 2, 128], BF16)
        negt = s.sbuf("n_neg", [128, 128], BF16)
        s.op("pool", lambda e: e.memset(negt[:, :], -8e30), writes=[negt])
        s.op("dve", lambda e: e.memset(Vnz[:, :, :, :], 0.0), writes=[Vnz])
        s.op("pool", lambda e: e.memset(Oz[:, :, :], 0.0), writes=[Oz])
        for hh in range(2):
            s.op("pool", lambda e: e.memset(Oz[:, hh, 64 * hh:64 * hh + 64], 1.0), acc=[Oz])
        for h_ in range(4):
            s.dma("sp", Vnz[:, :, h_, 64 * (h_ % 2):64 * (h_ % 2) + 64],
                  v_tok[:, 640 + 64 * h_:640 + 64 * h_ + 64].rearrange("(c p) d -> p c d", p=128), reads=[v_tok], acc=[Vnz])
        pTd = [s.sbuf("n_pTd%d" % i, [128, 512], BF16) for i in range(2)]
        rds = [s.sbuf("n_rd%d" % i, [128, 512], F32) for i in range(2)]
        sc_ps = [s.psum("n_sc%d" % i, [128, 512], F32) for i in range(4)]
        num_ps = [s.psum("n_num%d" % i, [128, 512], F32) for i in range(2)]
        den_ps = [s.psum("n_den%d" % i, [128, 512], F32) for i in range(2)]
        for h in range(4):
            B_ = bst[h % 2]
            s.dma("sp", B_[:, :, :], nab[l, h * NU:(h + 1) * NU, :, :].rearrange("u p q -> p u q"), writes=[B_])
            s.op("act", lambda e: e.activation(out=bias8[:, h * NU:(h + 1) * NU, :], in_=B_[:, :, :], func=AF.Copy, scale=8.0),
                 reads=[B_], acc=[bias8], selfdep=False)
        for pc in range(2):
            rows = slice(pc * 128, pc * 128 + 128)
            s.dma("sp", qT[pc][:, :], zq_n[rows, :], reads=[zq_n], writes=[qT[pc]])
            s.dma("sp", kT[pc][:, :], zk_n[rows, :], reads=[zk_n], writes=[kT[pc]])
        s.dma("sp", Vn[:, :, :], v_tok[:, 640:896].rearrange("(c p) d -> p c d", p=128), reads=[v_tok], writes=[Vn])
        for p_ in range(2):
            for hh in range(2):
                hs = slice(64 * hh, 64 * hh + 64)
                ho = slice(64 * (1 - hh), 64 * (1 - hh) + 64)
                s.op("dve", lambda e: e.memset(qz[p_][hh][ho, :], 0.0), writes=[qz[p_][hh]])
                s.dma("sp", qz[p_][hh][hs, :], zq_n[p_ * 128 + 64 * hh:p_ * 128 + 64 * hh + 64, :], reads=[zq_n], acc=[qz[p_][hh]])
        it = 0
        for pc in range(2):
            units = [(T, hh) for T in range(8) for hh in range(2)]
            for ui, (T, hh) in enumerate(units):
                q0 = 256 + 256 * T
                jb = 1 + T // 2
                h = 2 * pc + hh
                loc = sorted(set(j for (tt, j) in _NA_TMAP if tt in (2 * T, 2 * T + 1)))
                allk = [(0, None)] + [(1, None)] + [(2 + j, j) for j in loc]
                nk = len(allk)
                g = it + ui
                NUM = num_ps[T % 2]
                DEN = den_ps[T % 2]
                RD = rds[T % 2]
                pT = pTw[g % 2]
                nbank = (nk + 1) // 2
                for bk_i in range(nbank):
                    bk = sc_ps[bk_i]
                    chunks = allk[2 * bk_i:2 * bk_i + 2]
                    for ci, (gc, j) in enumerate(chunks):
                        cc = slice(ci * 256, ci * 256 + 256)
                        s.mm(bk[:, cc], kT[pc][:, gc * 128:(gc + 1) * 128], qz[pc][hh][:, q0:q0 + 256], True, j is None,
                             reads=[kT[pc], qz[pc][hh]], writes=[bk] if ci == 0 else [], acc=[] if ci == 0 else [bk])
                        if j is not None:
                            for tt in range(2):
                                u = _NA_TMAP.get((2 * T + tt, j))
                                brhs = bias8[:, h * NU + u, :] if u is not None else negt[:, :]
                                c2 = slice(ci * 256 + 128 * tt, ci * 256 + 128 * tt + 128)
                                s.mm(bk[:, c2], identb[:, :], brhs, False, tt == 1, reads=[identb, bias8, negt], acc=[bk])
                    ncols = 256 * len(chunks)
                    s.op("act", lambda e: e.activation(out=pT[:, bk_i * 512:bk_i * 512 + ncols], in_=bk[:, 0:ncols], func=AF.Exp, scale=0.125),
                         reads=[bk], writes=[pT] if bk_i == 0 else [], acc=[] if bk_i == 0 else [pT], selfdep=(bk_i == 0))
                for i, (gc, j) in enumerate(allk):
                    f0 = (i == 0 and hh == 0)
                    l0 = (i == nk - 1 and hh == 1)
                    s.mm(NUM[:, 0:256], Vnz[:, gc, h, :], pT[:, i * 256:(i + 1) * 256], f0, l0,
                         reads=[Vnz, pT], writes=[NUM] if f0 else [], acc=[] if f0 else [NUM])
                    s.mm(DEN[:, 0:256], Oz[:, hh, :], pT[:, i * 256:(i + 1) * 256], f0, l0,
                         reads=[Oz, pT], writes=[DEN] if f0 else [], acc=[] if f0 else [DEN])
                if hh == 1:
                    s.op("act", lambda e: e.activation(out=RD[:, 0:256], in_=DEN[:, 0:256], func=AF.Ln), reads=[DEN], writes=[RD])
                    s.op("act", lambda e: e.activation(out=RD[:, 0:256], in_=RD[:, 0:256], func=AF.Exp, scale=-1.0), reads=[RD], writes=[RD])
                    s.op("dve", lambda e: e.tensor_tensor(out=hT[:, 6 + pc, q0:q0 + 256], in0=NUM[:, 0:256], in1=RD[:, 0:256],
                                                          op=ALU.mult), reads=[NUM, RD], acc=[hTb[jb]], selfdep=False)
            it += len(units)
            if not last:
                cnt = [0]
                attn_block(qT[pc], kT[pc], lambda kc, hh: Vn[:, kc, (2 * pc + hh) * 64:(2 * pc + hh) * 64 + 64], Vn, 0, 256, [0, 1],
                           6 + pc, sc_ps, num_ps[0], den_ps[0], pTd, rds[0], cnt)
        s.release(m0)

    def p4(b, l):
        last = (l == nlayers - 1)
        halves = [[0, 1, 2], [3, 4]]
        if last:
            halves[0] = [1, 2]
        for blks in halves:
            m0 = s.mark()
            base = BLOCKS[blks[0]][0]
            xh = s.sbuf("xh", [128, 8, 1280], F32)
            xhv = {j: Buf(xh.t, "xh.%d" % j) for j in blks}
            stg = [s.sbuf("p4stg%d" % i, [128, 8, 512], F32) for i in range(2)]
            wb = [s.sbuf("p4wb%d" % i, [128, 8, 512], BF16) for i in range(2)]
            w2b = [s.sbuf("p4w2b%d" % i, [128, 4, 1024], BF16) for i in range(2)]
            ub = [s.sbuf("p4u%d" % i, [128, 4, 512], BF16) for i in range(2)]
            rb = [s.sbuf("p4r%d" % i, [128, 512], F32) for i in range(3)]
            sq = s.sbuf("p4sq", [128, 8, 512], BF16)
            rs = s.sbuf("p4rs", [128, 512], F32)
            tmps = [s.sbuf("p4t%d" % i, [128, 512], F32) for i in range(3)]
            ps = [s.psum("p4ps%d" % i, [128, 512], F32) for i in range(7)]
            pss = s.psum("p4pss", [128, 512], F32)
            pi = [0]

            def nps():
                pi[0] += 1
                return ps[pi[0] % 7]

            for cg in range(2):
                s.dma("sp", stg[cg][:, :, :], w_out[l, :, cg * 512:(cg + 1) * 512].rearrange("(k p) n -> p k n", p=128),
                      writes=[stg[cg]])
            for j in blks:
                st, n = BLOCKS[j]
                s.dma("sp", xh[:, :, st - base:st - base + n], xs[:, :, st:st + n], reads=[xs], writes=[xhv[j]])
            for cg in range(0):
                pass
            for cg in range(2):
                if cg == 0:
                    s.op("act", lambda e: e.activation(out=wb[cg][:, :, :], in_=stg[cg][:, :, :], func=AF.Copy),
                         reads=[stg[cg]], writes=[wb[cg]])
                else:
                    s.op("dve", lambda e: e.tensor_copy(out=wb[cg][:, :, :], in_=stg[cg][:, :, :]), reads=[stg[cg]], writes=[wb[cg]])
            for cg in range(2):
                for j in blks:
                    st, n = BLOCKS[j]
                    col = 2 if st < 256 else b
                    for mi in range(4):
                        m = cg * 4 + mi
                        P = nps()
                        for k in range(8):
                            s.mm(P[:, 0:n], wb[cg][:, k, mi * 128:(mi + 1) * 128], hT[:, k, st:st + n], k == 0, k == 7,
                                 reads=[wb[cg], hTb[j]], writes=[P] if k == 0 else [], acc=[] if k == 0 else [P])
                        xa = xh[:, m, st - base:st - base + n]
                        s.op("dve", lambda e: e.scalar_tensor_tensor(out=xa, in0=P[:, 0:n], scalar=modT[:, l, 16 + m, col:col + 1],
                                                                     in1=xa, op0=ALU.mult, op1=ALU.add),
                             reads=[P, modT, xhv[j]], acc=[xhv[j]], selfdep=False)
            for j in blks:
                st, n = BLOCKS[j]
                col = 2 if st < 256 else b
                norm_block(xhv[j], st - base, n, st, l, 1, col, sq, pss, rs, tmps, j)
            def loadw_dma(g):
                s.dma("sp", stg[0][:, :, :], w1[l, :, g * 512:(g + 1) * 512].rearrange("(k p) n -> p k n", p=128), writes=[stg[0]])
                s.dma("sp", stg[1][:, :, :].rearrange("p k n -> p (k n)").rearrange("p (c n) -> p c n", c=4),
                      w2[l, g * 512:(g + 1) * 512, :].rearrange("(c p) n -> p c n", p=128), writes=[stg[1]])

            def loadw_cast(g):
                s.op("act", lambda e: e.activation(out=wb[g % 2][:, :, :], in_=stg[0][:, :, :], func=AF.Copy),
                     reads=[stg[0]], writes=[wb[g % 2]])
                s.op("dve", lambda e: e.tensor_copy(out=w2b[g % 2][:, :, :],
                                                    in_=stg[1][:, :, :].rearrange("p k n -> p (k n)").rearrange("p (c n) -> p c n", c=4)),
                     reads=[stg[1]], writes=[w2b[g % 2]])

            loadw_dma(0)
            loadw_cast(0)
            items = [(g, bi_, j) for g in range(8) for bi_, j in enumerate(blks)]
            Us = {}

            def mlp_u(i):
                g, bi_, j = items[i]
                if bi_ == 0 and g + 1 < 8:
                    loadw_dma(g + 1)
                W1 = wb[g % 2]
                st, n = BLOCKS[j]
                U = ub[i % 2]
                Us[i] = U
                for hc in range(4):
                    P = nps()
                    for k in range(8):
                        s.mm(P[:, 0:n], W1[:, k, hc * 128:(hc + 1) * 128], hT[:, k, st:st + n], k == 0, k == 7,
                             reads=[W1, hTb[j]], writes=[P] if k == 0 else [], acc=[] if k == 0 else [P])
                    R_ = rb[hc % 3]
                    s.op("act", lambda e: e.activation(out=R_[:, 0:n], in_=P[:, 0:n], func=AF.Relu), reads=[P], writes=[R_])
                    s.op("dve", lambda e: e.tensor_tensor(out=U[:, hc, 0:n], in0=R_[:, 0:n], in1=R_[:, 0:n], op=ALU.mult),
                         reads=[R_], writes=[U] if hc == 0 else [], acc=[] if hc == 0 else [U], selfdep=(hc == 0))

            def mlp_y(i):
                g, bi_, j = items[i]
                W2 = w2b[g % 2]
                st, n = BLOCKS[j]
                col = 2 if st < 256 else b
                U = Us.pop(i)
                for m in range(8):
                    P = nps()
                    for hc in range(4):
                        s.mm(P[:, 0:n], W2[:, hc, m * 128:(m + 1) * 128], U[:, hc, 0:n], hc == 0, hc == 3,
                             reads=[W2, U], writes=[P] if hc == 0 else [], acc=[] if hc == 0 else [P])
                    xa = xh[:, m, st - base:st - base + n]
                    s.op("dve", lambda e: e.scalar_tensor_tensor(out=xa, in0=P[:, 0:n], scalar=modT[:, l, 40 + m, col:col + 1],
                                                                 in1=xa, op0=ALU.mult, op1=ALU.add),
                         reads=[P, modT, xhv[j]], acc=[xhv[j]], selfdep=False)

            mlp_u(0)
            for i in range(len(items)):
                g, bi_, j = items[i]
                if i + 1 < len(items):
                    g2, b2, _ = items[i + 1]
                    if b2 == 0:
                        loadw_cast(g2)
                    mlp_u(i + 1)
                mlp_y(i)
            if not last:
                for j in blks:
                    st, n = BLOCKS[j]
                    s.dma("sp", xs[:, :, st:st + n], xh[:, :, st - base:st - base + n], reads=[xhv[j]], acc=[xs])
            else:
                yb = stg[0]
                youts = [Buf(stg[1].t, "yout%d" % i) for i in range(2)]
                oc = 0
                for j in blks:
                    st, n = BLOCKS[j]
                    s.op("act", lambda e: e.activation(out=sq[:, :, 0:n], in_=xh[:, :, st - base:st - base + n], func=AF.Square),
                         reads=[xhv[j]], writes=[sq])
                    for k in range(8):
                        s.mm(pss[:, 0:n], onesb[:, :], sq[:, k, 0:n], k == 0, k == 7, reads=[sq, onesb],
                             writes=[pss] if k == 0 else [], acc=[] if k == 0 else [pss])
                    s.op("act", lambda e: e.activation(out=rs[:, 0:n], in_=pss[:, 0:n], func=AF.Ln, scale=1.0 / 1024, bias=EPS),
                         reads=[pss], writes=[rs])
                    s.op("act", lambda e: e.activation(out=rs[:, 0:n], in_=rs[:, 0:n], func=AF.Exp, scale=-0.5), reads=[rs], writes=[rs])
                    for k in range(8):
                        s.op("dve", lambda e: e.scalar_tensor_tensor(out=yb[:, k, 0:n], in0=xh[:, k, st - base:st - base + n],
                                                                     scalar=gT[:, 4, k:k + 1], in1=rs[:, 0:n],
                                                                     op0=ALU.mult, op1=ALU.mult),
                             reads=[xhv[j], gT, rs], writes=[yb] if k == 0 else [], acc=[] if k == 0 else [yb], selfdep=(k == 0))
                    for ti in range(n // 128):
                        YO = youts[oc % 2]
                        yo_ap = stg[1][:, (oc % 2) * 2:(oc % 2) * 2 + 2, :].rearrange("p a n -> p (a n)")
                        oc += 1
                        for half in range(2):
                            P = nps()
                            for kk in range(4):
                                k = half * 4 + kk
                                s.op("pe", lambda e: e.transpose(out=P[:, kk * 128:(kk + 1) * 128],
                                                                 in_=yb[:, k, ti * 128:(ti + 1) * 128], identity=identf),
                                     reads=[yb, cs], writes=[P] if kk == 0 else [], acc=[] if kk == 0 else [P])
                            s.op("act", lambda e: e.activation(out=yo_ap[:, half * 512:(half + 1) * 512], in_=P[:, :], func=AF.Copy),
                                 reads=[P], writes=[YO] if half == 0 else [], acc=[] if half == 0 else [YO], selfdep=(half == 0))
                        tok = st - 256 + ti * 128
                        s.dma("sp", y[b, tok:tok + 128, :], yo_ap, reads=[YO], acc=[y])
            s.release(m0)

    prologue()
    for b in range(nseq if stop_after != "pro" else 0):
        for l in range(nlayers):
            p1(b, l)
            if stop_after == "p1":
                break
            p2(b, l)
            if stop_after == "p2":
                break
            recur(b, l, "h")
            if stop_after in ("h", "hbuild"):
                break
            recur(b, l, "m")
            if stop_after == "m":
                break
            gqa(b, l)
            if stop_after == "g":
                break
            na(b, l)
            if stop_after == "p3":
                break
            p4(b, l)
        if stop_after is not None:
            break
    if dbg:
        s.dma("sp", cat_dbg[:, :, :], hT[:, :, :], reads=hTb, writes=[cat_dbg])
    s.finish()
    build.stats = (s.nops, s.nwaits, dict(s.cnt))
    return nc


_CACHE = {}


def _host_inputs(inputs, core):
    f = lambda a: np.ascontiguousarray(np.asarray(a, dtype=np.float32))
    b0 = 2 * core
    cst, rope = _CACHE["consts"]
    m = {
        "x": f(inputs["x"][b0:b0 + 2]),
        "ctx": f(inputs["ctx"][b0:b0 + 2]),
        "cvec": f(np.concatenate([inputs["c"][b0:b0 + 2], np.asarray(inputs["c_ctx"])[None, :]], 0)),
        "w_mod": f(inputs["w_mod"]), "b_mod": f(inputs["b_mod"]),
        "norm1_g": f(inputs["norm1_g"]), "norm2_g": f(inputs["norm2_g"]),
        "w_in": f(inputs["w_in"]),
        "hgrn_lb_logits": f(np.asarray(inputs["hgrn_lb_logits"]).reshape(4, 256)),
        "hgrn_norm_g": f(inputs["hgrn_norm_g"]), "mlstm_gate_b": f(inputs["mlstm_gate_b"]),
        "mlstm_norm_g": f(inputs["mlstm_norm_g"]), "gqa_qnorm_g": f(inputs["gqa_qnorm_g"]),
        "gqa_knorm_g": f(inputs["gqa_knorm_g"]),
        "na_bias": _CACHE["na_bias"],
        "w_out": f(inputs["w_out"]), "w_mlp1": f(inputs["w_mlp1"]), "w_mlp2": f(inputs["w_mlp2"]),
        "final_norm_g": f(np.asarray(inputs["final_norm_g"]).reshape(1, 1024)),
        "consts": cst, "rope": rope,
    }
    return m


def _prep(inputs):
    _CACHE["consts"] = _consts()
    idx = _na_gather_index()
    rpb = np.asarray(inputs["na_rpb"], np.float32)
    flat = np.concatenate([rpb.reshape(2, 4, 465), np.full((2, 4, 1), NEG, np.float32)], -1)
    nb = flat[:, :, idx]
    _CACHE["na_bias"] = np.ascontiguousarray(nb.reshape(2, 4 * NU, 128, 128))


def kernel(**inputs):
    _prep(inputs)
    nc = build()
    in_maps = [_host_inputs(inputs, c) for c in range(8)]
    res = run_bass_kernel_spmd(nc, in_maps, core_ids=list(range(8)))
    out = np.concatenate([np.asarray(r["y"], np.float32) for r in res.results], axis=0)
    return out
```

```python
import numpy as np
import ml_dtypes
import concourse.bass as bass
import concourse.mybir as mybir
from concourse.bass_utils import run_bass_kernel_spmd

F32 = mybir.dt.float32
BF16 = mybir.dt.bfloat16
AF = mybir.ActivationFunctionType
ALU = mybir.AluOpType

NT = 2304
NCTX = 256
EPS = 1e-6
NEG = -1e30
BLOCKS = [(0, 256), (256, 512), (768, 512), (1280, 512), (1792, 512)]
NCH = 36


class Buf:
    __slots__ = ("t", "name", "lw", "aw", "rd")

    def __init__(self, t, name):
        self.t = t
        self.name = name
        self.lw = {}
        self.aw = {}
        self.rd = {}

    def __getitem__(self, idx):
        return self.t[idx]


class Sch:
    NDMA = 10

    def __init__(self, nc):
        self.nc = nc
        self.eng = {"pe": nc.tensor, "dve": nc.vector, "act": nc.scalar, "pool": nc.gpsimd, "sp": nc.sync}
        self.sems = {}
        self.cnt = {}
        self.key = {}
        self.nsem = 0
        for e in self.eng:
            self._newsem(e)
        for q in ("sp", "act", "pool"):
            for k in range(self.NDMA):
                key = "d%s%d" % (q, k)
                self.sems[key] = nc.alloc_semaphore("s_" + key)
                self.cnt[key] = 0
        self.seen = {e: {} for e in self.eng}
        self.dma_rr = {"sp": 0, "act": 0, "pool": 0}
        self.nwaits = 0
        self.nops = 0
        self._stack = []

    def _newsem(self, e):
        self.nsem += 1
        key = "%s@%d" % (e, self.nsem)
        self.sems[key] = self.nc.alloc_semaphore("s_%s_%d" % (e, self.nsem))
        self.cnt[key] = 0
        self.key[e] = key

    def sbuf(self, name, shape, dtype):
        self.nsem += 1
        name = "%s_%d" % (name, self.nsem)
        g = self.nc.sbuf_tensor(name, list(shape), dtype)
        t = g.__enter__()
        self._stack.append(g)
        return Buf(t, name)

    def psum(self, name, shape, dtype=F32):
        self.nsem += 1
        name = "%s_%d" % (name, self.nsem)
        g = self.nc.psum_tensor(name, list(shape), dtype)
        t = g.__enter__()
        self._stack.append(g)
        return Buf(t, name)

    def dram(self, name, shape, dtype, kind="Internal"):
        t = self.nc.dram_tensor(name, list(shape), dtype, kind=kind)
        return Buf(t, name)

    @staticmethod
    def views(buf, n):
        return [Buf(buf.t, "%s.%d" % (buf.name, i)) for i in range(n)]

    def mark(self):
        return len(self._stack)

    def release(self, mark):
        self.barrier()
        while len(self._stack) > mark:
            g = self._stack.pop()
            g.__exit__(None, None, None)

    def _need(self, e, toks, selfdep=True, wtoks=()):
        best = {}
        for k, v in toks:
            if k.startswith("pe@") and e == "pe":
                continue
            if best.get(k, 0) < v:
                best[k] = v
        for k, v in wtoks:
            if k.startswith("pe@") and e == "pe":
                continue
            if (not selfdep) and k == self.key[e]:
                continue
            if best.get(k, 0) < v:
                best[k] = v
        for k, v in best.items():
            if self.seen[e].get(k, 0) >= v:
                continue
            self.eng[e].wait_ge(self.sems[k], v)
            self.seen[e][k] = v
            self.nwaits += 1

    @staticmethod
    def _deps(reads, writes, acc):
        rt, wt = [], []
        for b in reads:
            rt.extend(b.lw.items())
            rt.extend(b.aw.items())
        for b in writes:
            wt.extend(b.lw.items())
            wt.extend(b.aw.items())
            wt.extend(b.rd.items())
        for b in acc:
            wt.extend(b.lw.items())
            wt.extend(b.rd.items())
        return rt, wt

    @staticmethod
    def _commit(tok, reads, writes, acc):
        k, v = tok
        for b in reads:
            if b.rd.get(k, 0) < v:
                b.rd[k] = v
        for b in writes:
            b.lw = {k: v}
            b.aw = {}
            b.rd = {}
        for b in acc:
            if b.aw.get(k, 0) < v:
                b.aw[k] = v

    def op(self, e, fn, reads=(), writes=(), acc=(), selfdep=True):
        rt, wt = self._deps(reads, writes, acc)
        self._need(e, rt, selfdep, wt)
        ins = fn(self.eng[e])
        self.nops += 1
        key = self.key[e]
        self.cnt[key] += 1
        ins.then_inc(self.sems[key], 1)
        self._commit((key, self.cnt[key]), reads, writes, acc)
        return ins

    def mm(self, out_ap, lhsT, rhs, start, stop, reads=(), writes=(), acc=()):
        return self.op("pe", lambda e: e.matmul(out_ap, lhsT, rhs, start=start, stop=stop,
                                                skip_group_check=True), reads, writes, acc)

    def dma(self, q, out_ap, in_ap, reads=(), writes=(), acc=(), **kw):
        key = "d%s%d" % (q, self.dma_rr[q])
        self.dma_rr[q] = (self.dma_rr[q] + 1) % self.NDMA
        rt, wt = self._deps(reads, writes, acc)
        toks = rt + wt
        if self.cnt[key] > 0:
            toks.append((key, self.cnt[key]))
        self._need(q, toks)
        self.cnt[key] += 16
        ins = self.eng[q].dma_start(out=out_ap, in_=in_ap, **kw)
        ins.then_inc(self.sems[key], 16)
        self.nops += 1
        self._commit((key, self.cnt[key]), reads, writes, acc)
        return ins

    def barrier(self):
        toks = [(k, v) for k, v in self.cnt.items() if v > 0]
        for e in self.eng:
            self._need(e, toks)
        for e in list(self.eng):
            if self.cnt[self.key[e]] > 24000:
                self._newsem(e)

    def finish(self):
        toks = [(k, v) for k, v in self.cnt.items() if v > 0]
        self._need("sp", toks)


def _na_tiles():
    uniq = {}
    tmap = {}
    for t in range(16):
        lo, hi = 10 ** 9, -1
        for b in range(2):
            r0 = min(max(2 * t + b - 4, 0), 24)
            lo = min(lo, r0 // 2)
            hi = max(hi, (r0 + 7) // 2)
        for j in range(lo, hi + 1):
            pat = []
            for a in range(2):
                for b in range(2):
                    qr = 2 * t + b
                    kr = 2 * j + a
                    r0 = min(max(qr - 4, 0), 24)
                    pat.append((kr - qr) if (r0 <= kr < r0 + 8) else None)
            pat = tuple(pat)
            if pat not in uniq:
                uniq[pat] = len(uniq)
            tmap[(t, j)] = uniq[pat]
    pats = [None] * len(uniq)
    for p, i in uniq.items():
        pats[i] = p
    return pats, tmap


_NA_PATS, _NA_TMAP = _na_tiles()
NU = len(_NA_PATS)


def _na_gather_index():
    idx = np.full((NU, 128, 128), 465, np.int64)
    qc = np.arange(64)
    cstart = np.clip(qc - 8, 0, 48)
    kc = np.arange(64)
    col_in = (kc[:, None] >= cstart[None, :]) & (kc[:, None] < cstart[None, :] + 16)
    cidx = np.clip(kc[:, None] - qc[None, :], -15, 15) + 15
    for u, pat in enumerate(_NA_PATS):
        for a in range(2):
            for b in range(2):
                dr = pat[a * 2 + b]
                if dr is None:
                    continue
                blk = np.where(col_in, (dr + 7) * 31 + cidx, 465)
                idx[u, a * 64:(a + 1) * 64, b * 64:(b + 1) * 64] = blk
    return idx


def _consts():
    c = np.zeros((128, 576), np.float32)
    c[:, 0:128] = np.eye(128, dtype=np.float32)
    blk = np.zeros((128, 128), np.float32)
    blk[0:64, 0:64] = 1.0
    blk[64:128, 64:128] = 1.0
    c[:, 128:256] = blk
    rt = np.zeros((128, 128), np.float32)
    for i in range(64):
        rt[2 * i + 1, 2 * i] = -1.0
        rt[2 * i, 2 * i + 1] = 1.0
    c[:, 256:384] = rt
    sidx = np.arange(64)[:, None]
    tidx = np.arange(64)[None, :]
    mf = (sidx <= tidx).astype(np.float32)
    mb = (sidx >= tidx).astype(np.float32)
    c[:, 384:448] = np.concatenate([mf, mf], 0)
    c[:, 448:512] = np.concatenate([mb, mb], 0)
    p = np.arange(128)[:, None] % 16
    t16 = np.arange(16)[None, :]
    c[:, 512:528] = (p <= t16).astype(np.float32)
    c[:, 528:544] = (p >= t16).astype(np.float32)
    t = np.arange(2048)
    row = (t // 64).astype(np.float32)
    col = (t % 64).astype(np.float32)
    inv = np.power(np.float32(10000.0), (-2.0 * np.arange(16, dtype=np.float32) / np.float32(32.0))).astype(np.float32)
    ang = np.concatenate([row[:, None] * inv[None, :], col[:, None] * inv[None, :]], -1).astype(np.float32)
    cos = np.cos(ang).astype(np.float32)
    sin = np.sin(ang).astype(np.float32)
    cosf = np.repeat(cos, 2, axis=1).T
    sinf = np.repeat(sin, 2, axis=1).T
    rope = np.concatenate([np.concatenate([cosf, cosf], 0), np.concatenate([sinf, sinf], 0)], 1).astype(np.float32)
    return c, np.ascontiguousarray(rope)


def build(nlayers=2, nseq=2, stop_after=None, dbg=False):
    nc = bass.Bass("TRN2", target_bir_lowering=False)
    s = Sch(nc)

    def inp(name, shape, dt=F32):
        return s.dram(name, shape, dt, kind="ExternalInput")

    x_in = inp("x", [2, 2048, 1024])
    ctx_in = inp("ctx", [2, 256, 1024])
    cvec = inp("cvec", [3, 1024])
    w_mod = inp("w_mod", [2, 1024, 6144])
    b_mod = inp("b_mod", [2, 6144])
    n1g = inp("norm1_g", [2, 1024])
    n2g = inp("norm2_g", [2, 1024])
    w_in = inp("w_in", [2, 1024, 3600])
    lbl = inp("hgrn_lb_logits", [4, 256])
    hgg = inp("hgrn_norm_g", [2, 64])
    mgb = inp("mlstm_gate_b", [2, 16])
    mgg = inp("mlstm_norm_g", [2, 64])
    qng = inp("gqa_qnorm_g", [2, 64])
    kng = inp("gqa_knorm_g", [2, 64])
    nab = inp("na_bias", [2, 4 * NU, 128, 128])
    w_out = inp("w_out", [2, 1024, 1024])
    w1 = inp("w_mlp1", [2, 1024, 4096])
    w2 = inp("w_mlp2", [2, 4096, 1024])
    fng = inp("final_norm_g", [1, 1024])
    cst = inp("consts", [128, 576])
    ropec = inp("rope", [128, 4096])
    y = s.dram("y", [2, 2048, 1024], F32, kind="ExternalOutput")

    okind = "ExternalOutput" if dbg else "Internal"
    xs = s.dram("xs", [128, 8, NT], F32, kind=okind)
    zq_h = s.dram("zq_h", [256, NT], F32, kind=okind)
    zog_h = s.dram("zog_h", [256, NT], F32, kind=okind)
    zlf_h = s.dram("zlf_h", [2, 256, NT], F32, kind=okind)
    zkk_h = s.dram("zkk_h", [2, 256, NT], F32, kind=okind)
    zq_m = s.dram("zq_m", [256, NT], F32, kind=okind)
    zk_m = s.dram("zk_m", [256, NT], F32, kind=okind)
    zog_m = s.dram("zog_m", [256, NT], F32, kind=okind)
    zg_m = s.dram("zg_m", [16, NT], F32, kind=okind)
    zq_g = s.dram("zq_g", [256, NT], F32, kind=okind)
    zk_g = s.dram("zk_g", [2, 128, NT], F32, kind=okind)
    zq_n = s.dram("zq_n", [256, NT], BF16, kind=okind)
    zk_n = s.dram("zk_n", [256, NT], BF16, kind=okind)
    v_tok = s.dram("v_tok", [NT, 896], BF16, kind=okind)
    cat_dbg = s.dram("cat_dbg", [128, 8, NT], BF16, kind=okind) if dbg else None
    h_dbg = s.dram("h_dbg", [128, 8, NT], BF16, kind=okind) if dbg else None
    mod_dbg = s.dram("mod_dbg", [128, 2 * 48 * 3], F32, kind=okind) if dbg else None

    if dbg:
        dbgQb = s.dram("dbgQ", [2, 3, 128, NT], BF16, kind=okind)
        dbgPb = s.dram("dbgP", [128, NT + 1], F32, kind=okind)
        dbgQ = dbgQb
        dbgP = dbgPb
    NSL = True

    cs = s.sbuf("cs", [128, 576], F32)
    identb = s.sbuf("identb", [128, 128], BF16)
    onesb = s.sbuf("onesb", [128, 128], BF16)
    blkb = s.sbuf("blkb", [128, 128], BF16)
    bd1 = s.sbuf("bd1", [128, 128], BF16)
    hT = s.sbuf("hT", [128, 8, NT], BF16)
    hTb = Sch.views(hT, 5)
    modT = s.sbuf("modT", [128, 2, 48, 3], F32)
    AT = s.sbuf("AT", [128, 2, 2, 8, 3], F32)
    gT = s.sbuf("gT", [128, 5, 8], F32)
    gcol = s.sbuf("gcol", [128, 2, 4], F32)
    lb = s.sbuf("lb", [128, 2, 2, 2], F32)
    oml = s.sbuf("oml", [128, 2, 2, 2], F32)
    noml = s.sbuf("noml", [128, 2, 2, 2], F32)
    mgbT = s.sbuf("mgbT", [16, 2], F32)

    identf = cs[:, 0:128]
    blkf = cs[:, 128:256]
    ropeRT = cs[:, 256:384]
    maskFB = {64: [cs[:, 384:448], cs[:, 448:512]], 16: [cs[:, 512:528], cs[:, 528:544]]}

    s.dma("sp", cs[:, :], cst[:, :], writes=[cs])
    s.op("dve", lambda e: e.tensor_copy(out=identb[:, :], in_=identf), reads=[cs], writes=[identb])
    s.op("dve", lambda e: e.tensor_copy(out=blkb[:, :], in_=blkf), reads=[cs], writes=[blkb])
    s.op("dve", lambda e: e.tensor_copy(out=bd1[:, :], in_=blkf), reads=[cs], writes=[bd1])
    s.op("pool", lambda e: e.memset(onesb[:, :], 1.0), writes=[onesb])

    def prologue():
        m0 = s.mark()
        scT = s.sbuf("scT", [128, 8, 3], F32)
        bmT = s.sbuf("bmT", [128, 2, 48], F32)
        lg = s.sbuf("lg", [128, 4, 2], F32)
        wm = [s.sbuf("wm%d" % i, [128, 8, 512], F32) for i in range(6)]
        pm = s.psum("pm_mod", [128, 512], F32)
        for r in range(3):
            s.dma("sp", scT[:, :, r], cvec[r:r + 1, :].rearrange("o (k p) -> p (o k)", p=128), acc=[scT],
                  allow_slow_non_contiguous=NSL)
        for l in range(2):
            s.dma("sp", bmT[:, l, :], b_mod[l:l + 1, :].rearrange("o (j p) -> p (o j)", p=128), acc=[bmT],
                  allow_slow_non_contiguous=NSL)
        gsrc = [n1g[0:1, :], n1g[1:2, :], n2g[0:1, :], n2g[1:2, :], fng[0:1, :]]
        for i, g in enumerate(gsrc):
            s.dma("sp", gT[:, i, :], g.rearrange("o (k p) -> p (o k)", p=128), acc=[gT], allow_slow_non_contiguous=NSL)
        for r in range(4):
            s.dma("sp", lg[:, r, :], lbl[r:r + 1, :].rearrange("o (c p) -> p (o c)", p=128), acc=[lg],
                  allow_slow_non_contiguous=NSL)
        for l in range(2):
            for i, g in enumerate([hgg, mgg, qng, kng]):
                for hh in range(2):
                    s.dma("sp", gcol[64 * hh:64 * hh + 64, l, i:i + 1], g[l:l + 1, :].rearrange("o d -> d o"),
                          acc=[gcol], allow_slow_non_contiguous=NSL)
            s.dma("sp", mgbT[:, l:l + 1], mgb[l:l + 1, :].rearrange("o g -> g o"), acc=[mgbT],
                  allow_slow_non_contiguous=NSL)
        s.op("act", lambda e: e.activation(out=scT[:, :, :], in_=scT[:, :, :], func=AF.Silu), reads=[scT], writes=[scT])
        ex = s.sbuf("ex", [128, 4, 2], F32)
        den = s.sbuf("den", [128, 2, 2], F32)
        s.op("act", lambda e: e.activation(out=ex[:, :, :], in_=lg[:, :, :], func=AF.Exp), reads=[lg], writes=[ex])
        s.op("dve", lambda e: e.tensor_tensor(out=den[:, :, :], in0=ex[:, 0:2, :], in1=ex[:, 2:4, :], op=ALU.add),
             reads=[ex], writes=[den])
        s.op("dve", lambda e: e.reciprocal(out=den[:, :, :], in_=den[:, :, :]), reads=[den], writes=[den])
        s.op("pool", lambda e: e.memset(lb[:, 0, :, :], 0.0), acc=[lb])
        s.op("dve", lambda e: e.tensor_tensor(out=lb[:, 1, :, :], in0=ex[:, 2:4, :], in1=den[:, :, :], op=ALU.mult),
             reads=[ex, den], acc=[lb])
        s.op("dve", lambda e: e.tensor_scalar(out=oml[:, :, :, :], in0=lb[:, :, :, :], scalar1=-1.0, scalar2=1.0,
                                              op0=ALU.mult, op1=ALU.add), reads=[lb], writes=[oml])
        s.op("dve", lambda e: e.tensor_scalar(out=noml[:, :, :, :], in0=lb[:, :, :, :], scalar1=1.0, scalar2=-1.0,
                                              op0=ALU.mult, op1=ALU.add), reads=[lb], writes=[noml])
        it = 0
        for l in range(2):
            for grp in range(12):
                W = wm[it % 6]
                it += 1
                s.dma("sp" if it % 2 == 0 else "act", W[:, :, :],
                      w_mod[l, :, grp * 512:(grp + 1) * 512].rearrange("(k p) n -> p k n", p=128), writes=[W])
                for m in range(4):
                    idx = grp * 4 + m
                    for k in range(8):
                        s.mm(pm[:, idx * 3:idx * 3 + 3], W[:, k, m * 128:(m + 1) * 128], scT[:, k, :], k == 0, k == 7,
                             reads=[W, scT], acc=[pm])
            s.op("dve", lambda e: e.tensor_tensor(
                out=modT[:, l, :, :], in0=pm[:, 0:144].rearrange("p (j r) -> p j r", r=3),
                in1=bmT[:, l, :].unsqueeze(2).to_broadcast([128, 48, 3]), op=ALU.add),
                reads=[pm, bmT], acc=[modT])
        for l in range(2):
            for w in range(2):
                for k in range(8):
                    j = (1 if w == 0 else 4) * 8 + k
                    s.op("dve", lambda e: e.tensor_scalar(out=AT[:, l, w, k, :], in0=modT[:, l, j, :], scalar1=1.0,
                                                          scalar2=gT[:, w * 2 + l, k:k + 1], op0=ALU.add, op1=ALU.mult),
                         reads=[modT, gT], acc=[AT])
        if dbg:
            s.dma("sp", mod_dbg[:, :], modT[:, :, :, :].rearrange("p l j r -> p (l j r)"), reads=[modT], writes=[mod_dbg])
        s.release(m0)

    def norm_block(X, xoff, n, st, l, w, col, sq, pss, rs, tmps, j):
        s.op("act", lambda e: e.activation(out=sq[:, :, 0:n], in_=X[:, :, xoff:xoff + n], func=AF.Square),
             reads=[X], writes=[sq])
        for k in range(8):
            s.mm(pss[:, 0:n], onesb[:, :], sq[:, k, 0:n], k == 0, k == 7, reads=[sq, onesb],
                 writes=[pss] if k == 0 else [], acc=[] if k == 0 else [pss])
        s.op("act", lambda e: e.activation(out=rs[:, 0:n], in_=pss[:, 0:n], func=AF.Ln, scale=1.0 / 1024, bias=EPS),
             reads=[pss], writes=[rs])
        s.op("act", lambda e: e.activation(out=rs[:, 0:n], in_=rs[:, 0:n], func=AF.Exp, scale=-0.5), reads=[rs], writes=[rs])
        sh = 0 if w == 0 else 3
        for k in range(8):
            T = tmps[k % len(tmps)]
            s.op("dve", lambda e: e.scalar_tensor_tensor(out=T[:, 0:n], in0=X[:, k, xoff:xoff + n],
                                                         scalar=AT[:, l, w, k, col:col + 1], in1=rs[:, 0:n],
                                                         op0=ALU.mult, op1=ALU.mult), reads=[X, rs, AT], writes=[T])
            s.op("act", lambda e: e.activation(out=hT[:, k, st:st + n], in_=T[:, 0:n], func=AF.Identity,
                                               bias=modT[:, l, sh * 8 + k, col:col + 1], scale=1.0),
                 reads=[T, modT], acc=[hTb[j]], selfdep=False)

    def p1(b, l):
        m0 = s.mark()
        xb = [s.sbuf("xb%d" % i, [128, 8, 512], F32) for i in range(2)]
        sq = [s.sbuf("sq%d" % i, [128, 8, 512], BF16) for i in range(2)]
        rs = [s.sbuf("rs%d" % i, [128, 512], F32) for i in range(2)]
        tmps = [s.sbuf("tmp%d" % i, [128, 512], F32) for i in range(4)]
        pss = [s.psum("pss%d" % i, [128, 512], F32) for i in range(2)]
        if l == 0:
            xin = [s.sbuf("xin%d" % i, [128, 1024], F32) for i in range(3)]
            pst = [s.psum("pst%d" % i, [128, 512], F32) for i in range(4)]
        cnt = 0
        for j, (st, n) in enumerate(BLOCKS):
            X = xb[j % 2]
            if l == 0:
                for ti in range(n // 128):
                    tok0 = st + ti * 128
                    src = ctx_in[b, tok0:tok0 + 128, :] if tok0 < 256 else x_in[b, tok0 - 256:tok0 - 128, :]
                    xi = xin[cnt % 3]
                    s.dma("sp", xi[:, :], src, writes=[xi])
                    for half in range(2):
                        pt = pst[(cnt * 2 + half) % 4]
                        for kk in range(4):
                            k = half * 4 + kk
                            s.op("pe", lambda e: e.transpose(out=pt[:, kk * 128:(kk + 1) * 128],
                                                             in_=xi[:, k * 128:(k + 1) * 128], identity=identf),
                                 reads=[xi, cs], writes=[pt] if kk == 0 else [], acc=[] if kk == 0 else [pt])
                        s.op("act", lambda e: e.activation(
                            out=X[:, half * 4:(half + 1) * 4, ti * 128:(ti + 1) * 128],
                            in_=pt[:, :].rearrange("p (k t) -> p k t", t=128), func=AF.Copy),
                            reads=[pt], writes=[X] if (ti == 0 and half == 0) else [],
                            acc=[] if (ti == 0 and half == 0) else [X], selfdep=False)
                    cnt += 1
                s.dma("sp", xs[:, :, st:st + n], X[:, :, 0:n], reads=[X], acc=[xs])
            else:
                s.dma("sp", X[:, :, 0:n], xs[:, :, st:st + n], reads=[xs], writes=[X])
            col = 2 if st < 256 else b
            norm_block(X, 0, n, st, l, 0, col, sq[j % 2], pss[j % 2], rs[j % 2], tmps, j)
        if dbg:
            s.dma("sp", h_dbg[:, :, :], hT[:, :, :], reads=hTb, writes=[h_dbg])
        s.release(m0)

    GROUPS = [
        (0, 512, [("f", "hq", 0, 128, 0), ("f", "hq", 128, 128, 1), ("v", 256, 256, 0)]),
        (512, 512, [("f", "hog", 0, 128, 0), ("f", "hog", 128, 128, 1), ("f", "hf0", 256, 128, 0), ("f", "hf0", 384, 128, 1)]),
        (1024, 512, [("f", "hf1", 0, 128, 0), ("f", "hf1", 128, 128, 1), ("f", "mq", 256, 128, 0), ("f", "mq", 384, 128, 1)]),
        (1536, 512, [("f", "mk", 0, 128, 0), ("f", "mk", 128, 128, 1), ("v", 256, 256, 256)]),
        (2048, 272, [("f", "mog", 0, 128, 0), ("f", "mog", 128, 128, 1), ("f", "mg", 256, 16, 0)]),
        (2320, 512, [("f", "gq", 0, 128, 0), ("f", "gq", 128, 128, 1), ("f", "gk", 256, 128, 0),
                     ("v", 384, 128, 512)]),
        (2832, 512, [("f", "nq", 0, 128, 0), ("f", "nq", 128, 128, 1), ("f", "nk", 256, 128, 0), ("f", "nk", 384, 128, 1)]),
        (3344, 256, [("v", 0, 256, 640)]),
    ]

    def p2(b, l):
        m0 = s.mark()
        wst = [s.sbuf("wst%d" % i, [128, 8, 512], F32) for i in range(2)]
        wbf = [s.sbuf("wbf%d" % i, [128, 8, 512], BF16) for i in range(2)]
        stg = [s.sbuf("stg%d" % i, [128, 512], F32) for i in range(8)]
        stb = [s.sbuf("stb%d" % i, [128, 512], BF16) for i in range(4)]
        ps = [s.psum("p2ps%d" % i, [128, 512], F32) for i in range(6)]
        st_i = [0]
        sb_i = [0]
        ps_i = [0]

        def nstg():
            st_i[0] += 1
            return stg[st_i[0] % 8]

        def nstb():
            sb_i[0] += 1
            return stb[sb_i[0] % 4]

        def load(gi):
            c0, w, _ = GROUPS[gi]
            s.dma("sp", wst[gi % 2][:, :, 0:w], w_in[l, :, c0:c0 + w].rearrange("(k p) n -> p k n", p=128),
                  writes=[wst[gi % 2]])

        def cast(gi):
            c0, w, _ = GROUPS[gi]
            if gi % 2 == 0:
                s.op("act", lambda e: e.activation(out=wbf[gi % 2][:, :, 0:w], in_=wst[gi % 2][:, :, 0:w], func=AF.Copy),
                     reads=[wst[gi % 2]], writes=[wbf[gi % 2]])
            else:
                s.op("dve", lambda e: e.tensor_copy(out=wbf[gi % 2][:, :, 0:w], in_=wst[gi % 2][:, :, 0:w]),
                     reads=[wst[gi % 2]], writes=[wbf[gi % 2]])

        load(0)
        cast(0)
        load(1)
        for gi, (c0, w, jobs) in enumerate(GROUPS):
            if gi + 1 < len(GROUPS):
                cast(gi + 1)
            if gi + 2 < len(GROUPS):
                load(gi + 2)
            W = wbf[gi % 2]
            for job in jobs:
                if job[0] == "v":
                    _, off, ncol, vdst = job
                    for ti in range(18):
                        P = ps[ps_i[0] % 6]
                        ps_i[0] += 1
                        jb = 0 if ti < 2 else 1 + (ti - 2) // 4
                        for k in range(8):
                            s.mm(P[:, 0:ncol], hT[:, k, ti * 128:(ti + 1) * 128], W[:, k, off:off + ncol], k == 0, k == 7,
                                 reads=[hTb[jb], W], writes=[P] if k == 0 else [], acc=[] if k == 0 else [P])
                        B_ = nstb()
                        if ti % 2 == 0:
                            s.op("act", lambda e: e.activation(out=B_[:, 0:ncol], in_=P[:, 0:ncol], func=AF.Copy),
                                 reads=[P], writes=[B_])
                        else:
                            s.op("dve", lambda e: e.tensor_copy(out=B_[:, 0:ncol], in_=P[:, 0:ncol]), reads=[P], writes=[B_])
                        s.dma("sp", v_tok[ti * 128:(ti + 1) * 128, vdst:vdst + ncol], B_[:, 0:ncol], reads=[B_], acc=[v_tok])
                    continue
                _, kind, off, m, pc = job
                for j, (st, n) in enumerate(BLOCKS):
                    P = ps[ps_i[0] % 6]
                    ps_i[0] += 1
                    if True:
                        for k in range(8):
                            s.mm(P[0:m, 0:n], W[:, k, off:off + m], hT[:, k, st:st + n], k == 0, k == 7,
                                 reads=[hTb[j], W], writes=[P] if k == 0 else [], acc=[] if k == 0 else [P])
                        mm_ = m
                    rows = slice(pc * 128, pc * 128 + 128)
                    if kind in ("hq", "hog", "mog", "mq", "mk", "gq", "gk"):
                        S_ = nstg()
                        if kind in ("hq", "hog"):
                            s.op("act", lambda e: e.activation(out=S_[:, 0:n], in_=P[:, 0:n], func=AF.Silu), reads=[P], writes=[S_])
                        elif kind == "mog":
                            s.op("act", lambda e: e.activation(out=S_[:, 0:n], in_=P[:, 0:n], func=AF.Sigmoid), reads=[P], writes=[S_])
                        elif kind == "mk":
                            s.op("dve", lambda e: e.tensor_scalar(out=S_[:, 0:n], in0=P[:, 0:n], scalar1=0.125, scalar2=None,
                                                                  op0=ALU.mult), reads=[P], writes=[S_])
                        else:
                            s.op("dve", lambda e: e.tensor_copy(out=S_[:, 0:n], in_=P[:, 0:n]), reads=[P], writes=[S_])
                        dst = {"hq": zq_h, "hog": zog_h, "mog": zog_m, "mq": zq_m, "mk": zk_m, "gq": zq_g}.get(kind)
                        if kind == "gk":
                            s.dma("sp", zk_g[pc, :, st:st + n], S_[:, 0:n], reads=[S_], acc=[zk_g])
                        else:
                            s.dma("sp", dst[rows, st:st + n], S_[:, 0:n], reads=[S_], acc=[dst])
                    elif kind in ("hf0", "hf1"):
                        d = 0 if kind == "hf0" else 1
                        SG = nstg()
                        FG = nstg()
                        KK = nstg()
                        s.op("act", lambda e: e.activation(out=SG[:, 0:n], in_=P[:, 0:n], func=AF.Sigmoid), reads=[P], writes=[SG])
                        s.op("dve", lambda e: e.tensor_scalar(out=FG[:, 0:n], in0=SG[:, 0:n], scalar1=oml[:, l, d, pc:pc + 1],
                                                              scalar2=lb[:, l, d, pc:pc + 1], op0=ALU.mult, op1=ALU.add),
                             reads=[SG, oml, lb], writes=[FG])
                        s.op("act", lambda e: e.activation(out=FG[:, 0:n], in_=FG[:, 0:n], func=AF.Ln), reads=[FG], writes=[FG])
                        s.op("dve", lambda e: e.tensor_scalar(out=KK[:, 0:n], in0=SG[:, 0:n], scalar1=noml[:, l, d, pc:pc + 1],
                                                              scalar2=oml[:, l, d, pc:pc + 1], op0=ALU.mult, op1=ALU.add),
                             reads=[SG, oml, noml], writes=[KK])
                        s.dma("sp", zlf_h[d, rows, st:st + n], FG[:, 0:n], reads=[FG], acc=[zlf_h])
                        s.dma("sp", zkk_h[d, rows, st:st + n], KK[:, 0:n], reads=[KK], acc=[zkk_h])
                    elif kind == "mg":
                        S1 = nstg()
                        S2 = nstg()
                        s.op("act", lambda e: e.activation(out=S1[0:16, 0:n], in_=P[0:16, 0:n], func=AF.Identity,
                                                           bias=mgbT[:, l:l + 1], scale=1.0), reads=[P, mgbT], writes=[S1])
                        s.op("act", lambda e: e.activation(out=S2[0:16, 0:n], in_=P[0:16, 0:n], func=AF.Sigmoid,
                                                           bias=mgbT[:, l:l + 1], scale=1.0), reads=[P, mgbT], writes=[S2])
                        s.op("act", lambda e: e.activation(out=S2[0:16, 0:n], in_=S2[0:16, 0:n], func=AF.Ln), reads=[S2], writes=[S2])
                        s.dma("sp", zg_m[0:8, st:st + n], S1[0:8, 0:n], reads=[S1], acc=[zg_m])
                        s.dma("sp", zg_m[8:16, st:st + n], S2[8:16, 0:n], reads=[S2], acc=[zg_m])
                    elif kind in ("nq", "nk"):
                        B_ = nstb()
                        s.op("act", lambda e: e.activation(out=B_[:, 0:n], in_=P[:, 0:n], func=AF.Copy), reads=[P], writes=[B_])
                        dst = zq_n if kind == "nq" else zk_n
                        s.dma("sp", dst[rows, st:st + n], B_[:, 0:n], reads=[B_], acc=[dst])
                    else:
                        raise ValueError(kind)
        s.release(m0)

    def head_norm(o, gate, gidx, l, chunk, blocks, sqs, pn, rss, tms):
        for j in blocks:
            st, n = BLOCKS[j]
            SQ = sqs[j % 2]
            R = rss[j % 2]
            T = tms[j % 2]
            s.op("act", lambda e: e.activation(out=SQ[:, 0:n], in_=o[:, st:st + n], func=AF.Square), reads=[o], writes=[SQ])
            s.mm(pn[:, 0:n], blkb[:, :], SQ[:, 0:n], True, True, reads=[blkb, SQ], writes=[pn])
            s.op("act", lambda e: e.activation(out=R[:, 0:n], in_=pn[:, 0:n], func=AF.Sqrt, scale=1.0 / 64, bias=EPS),
                 reads=[pn], writes=[R])
            s.op("dve", lambda e: e.reciprocal(out=R[:, 0:n], in_=R[:, 0:n]), reads=[R], writes=[R])
            s.op("dve", lambda e: e.scalar_tensor_tensor(out=T[:, 0:n], in0=o[:, st:st + n], scalar=gcol[:, l, gidx:gidx + 1],
                                                         in1=R[:, 0:n], op0=ALU.mult, op1=ALU.mult),
                 reads=[o, R, gcol], writes=[T])
            s.op("dve", lambda e: e.tensor_tensor(out=hT[:, chunk, st:st + n], in0=T[:, 0:n], in1=gate[:, st:st + n],
                                                  op=ALU.mult), reads=[T, gate], acc=[hTb[j]], selfdep=False)

    def recur(b, l, kind):
        m0 = s.mark()
        NV = 1 if kind == "h" else 2
        L = 16 if kind == "h" else 64
        NCH = NT // L
        CTXN = NCTX // L
        HB = L
        SR = 2 * L
        PG = min(128 // SR, 3)
        last = (l == nlayers - 1)
        blocks = [1, 2, 3, 4] if last else [0, 1, 2, 3, 4]
        PTR = [s.psum("ptr%d" % d, [128, 1024], BF16) for d in range(2)]
        PSC = [s.psum("psc%d" % d, [128, 512], F32) for d in range(2)]
        PSO = [s.psum("pso%d" % d, [128, 512], F32) for d in range(2)]
        PST = [s.psum("pstt%d" % d, [128, 512], F32) for d in range(2)]
        pn = PSC[0]
        lfs = [s.sbuf("r_lf%d" % d, [128, NT], F32) for d in range(2)]
        PP = s.sbuf("r_PP", [128, NT + 1], F32)
        kks = [s.sbuf("r_kk%d" % d, [128, NT], F32) for d in range(2)]
        qs = s.sbuf("r_qs", [128, NT], F32)
        igbs = [s.sbuf("r_ig%d" % d, [128, NT], F32) for d in range(2)] if kind == "m" else None
        Q = [s.sbuf("r_Q%d" % d, [128, NT], BF16) for d in range(2)]
        K = [s.sbuf("r_K%d" % d, [128, NCH, 2 * L], BF16) for d in range(2)]
        KZ = []
        KN = [s.sbuf("r_KN%d" % d, [128, NT], BF16) for d in range(2)]
        Vbd = s.sbuf("r_Vbd", [128, NCH // PG, 128], BF16)
        Vp = s.sbuf("r_Vp", [128, NCH // PG, 128], BF16)
        Rn = [s.sbuf("r_Rn%d" % d, [128, NCH], F32) for d in range(2)]
        btm = [s.sbuf("r_bt%d" % i, [128, 512], F32) for i in range(10)]
        bti = [0]

        def ntmp():
            bti[0] += 1
            return btm[bti[0] % 10]

        def blk_of(c):
            t = c * L
            return 0 if t < 256 else 1 + (t - 256) // 512
        G = [s.sbuf("r_G%d" % d, [128, NCH], F32) for d in range(2)]
        W32 = [[s.sbuf("r_W%d%d" % (d, v), [128, 64], F32) for v in range(NV)] for d in range(2)]
        Wbf = [[s.sbuf("r_Wb%d%d" % (d, v), [128, 64], BF16) for v in range(NV)] for d in range(2)]
        kts = [s.sbuf("r_kt%d" % i, [128, 128], BF16) for i in range(4)]
        pms = [s.sbuf("r_pm%d" % i, [128, L], BF16) for i in range(4)]
        for pmb in pms:
            s.op("pool", lambda e: e.memset(pmb[:, :], 0.0), writes=[pmb])
        sqs = [s.sbuf("r_sq%d" % i, [128, 512], BF16) for i in range(2)]
        rss = [s.sbuf("r_rs%d" % i, [128, 512], F32) for i in range(2)]
        tms = [s.sbuf("r_tm%d" % i, [128, 512], F32) for i in range(2)]
        Qv = [Sch.views(Q[d], 5) for d in range(2)]
        Kv = [Sch.views(K[d], 5) for d in range(2)]
        KNv = [Sch.views(KN[d], 5) for d in range(2)]
        for d in range(2):
            s.op("act", lambda e: e.memzero(K[d][:, :, :]), writes=Kv[d])
        if kind == "h":
            accs = [[lfs[0]], [lfs[1]]]
        else:
            accs = [[lfs[0], igbs[0]], [lfs[1], igbs[1]]]
        zq = zq_h if kind == "h" else zq_m
        zog = zog_h if kind == "h" else zog_m
        vbase = 0 if kind == "h" else 256
        gidx = 0 if kind == "h" else 1
        order = [list(range(NCH)), list(range(CTXN - 1, -1, -1)) + list(range(NCH - 1, CTXN - 1, -1))]
        slot = 0
        for pc in range(2):
            rows = slice(pc * 128, pc * 128 + 128)
            vcol = vbase + pc * 128
            s.dma("sp", qs[:, :], zq[rows, :], reads=[zq], writes=[qs])
            for d in range(2):
                lf, kk = lfs[d], kks[d]
                if kind == "h":
                    s.dma("sp", lf[:, :], zlf_h[d, rows, :], reads=[zlf_h], writes=[lf])
                    s.dma("sp", kk[:, :], zkk_h[d, rows, :], reads=[zkk_h], writes=[kk])
                else:
                    igb = igbs[d]
                    for hh in range(2):
                        h = 2 * pc + hh
                        s.dma("sp", lf[64 * hh:64 * hh + 64, :], zg_m[8 + 4 * d + h:9 + 4 * d + h, :].partition_broadcast(64),
                              reads=[zg_m], writes=[lf] if hh == 0 else [], acc=[] if hh == 0 else [lf])
                        s.dma("sp", igb[64 * hh:64 * hh + 64, :], zg_m[4 * d + h:4 * d + h + 1, :].partition_broadcast(64),
                              reads=[zg_m], writes=[igb] if hh == 0 else [], acc=[] if hh == 0 else [igb])
                    s.dma("sp", kk[:, :], zk_m[rows, :], reads=[zk_m], writes=[kk])
            if pc == 0:
                s.op("act", lambda e: e.memzero(Vbd[:, :, :]), writes=[Vbd])
            for g in range(PG):
                for hh in range(2):
                    s.dma("sp", Vbd[g * SR + HB * hh:g * SR + HB * hh + L, :, 64 * hh:64 * hh + 64],
                          v_tok[:, vcol + 64 * hh:vcol + 64 * hh + 64].rearrange("(cq g p) d -> g p cq d", g=PG, p=L)[g],
                          reads=[v_tok], acc=[Vbd])
                s.dma("sp", Vp[g * SR:g * SR + L, :, :],
                      v_tok[:, vcol:vcol + 128].rearrange("(cq g p) d -> g p cq d", g=PG, p=L)[g],
                      reads=[v_tok], writes=[Vp] if g == 0 else [], acc=[] if g == 0 else [Vp])
            for d in range(2):
                sg = 1.0 if d == 0 else -1.0
                lf, kk = lfs[d], kks[d]
                igb = igbs[d] if kind == "m" else None
                s.op("pool", lambda e: e.memset(PP[:, 0:1], 0.0), writes=[PP])
                s.op("dve", lambda e: e.tensor_tensor_scan(out=PP[:, 1:NT + 1], data0=lf[:, :], data1=lf[:, :], initial=0.0,
                                                           op0=ALU.add, op1=ALU.bypass), reads=[lf], acc=[PP])
                Rm = PP[:, 0:NT].rearrange("p (c l) -> p c l", l=L)[:, :, L // 2]
                if d == 0:
                    s.op("dve", lambda e: e.tensor_copy(out=Rn[d][:, 0:NCH - 1], in_=Rm[:, 1:NCH]), reads=[PP], writes=[Rn[d]])
                    s.op("dve", lambda e: e.tensor_copy(out=Rn[d][:, NCH - 1:NCH], in_=Rm[:, NCH - 1:NCH]), reads=[PP], acc=[Rn[d]])
                    s.op("dve", lambda e: e.tensor_tensor(out=G[d][:, :], in0=Rn[d][:, :], in1=Rm, op=ALU.subtract),
                         reads=[Rn[d], PP], writes=[G[d]])
                else:
                    s.op("dve", lambda e: e.tensor_copy(out=Rn[d][:, 1:NCH], in_=Rm[:, 0:NCH - 1]), reads=[PP], writes=[Rn[d]])
                    s.op("dve", lambda e: e.tensor_copy(out=Rn[d][:, CTXN:CTXN + 1], in_=Rm[:, CTXN:CTXN + 1]), reads=[PP], acc=[Rn[d]])
                    s.op("dve", lambda e: e.tensor_tensor(out=Rn[d][:, 0:1], in0=Rm[:, NCH - 1:NCH], in1=PP[:, NT:NT + 1],
                                                          op=ALU.subtract), reads=[PP], acc=[Rn[d]])
                    s.op("dve", lambda e: e.tensor_tensor(out=G[d][:, :], in0=Rm, in1=Rn[d][:, :], op=ALU.subtract),
                         reads=[Rn[d], PP], writes=[G[d]])
                s.op("act", lambda e: e.activation(out=G[d][:, :], in_=G[d][:, :], func=AF.Exp), reads=[G[d]], writes=[G[d]])
                border = [0, 1, 2, 3, 4] if d == 0 else [0, 4, 3, 2, 1]
                for j in border:
                    st, n = BLOCKS[j]
                    c0, ncb = st // L, n // L
                    off = 1 if d == 0 else 0
                    PPs = PP[:, st + off:st + off + n].rearrange("p (c l) -> p c l", l=L)
                    T1, T2, T3, T4, T5 = [ntmp() for _ in range(5)]
                    v3 = lambda T: T[:, 0:n].rearrange("p (c l) -> p c l", l=L)
                    s.op("dve", lambda e: e.tensor_tensor(out=v3(T1), in0=PPs,
                                                          in1=Rm[:, c0:c0 + ncb].unsqueeze(2).to_broadcast([128, ncb, L]),
                                                          op=ALU.subtract), reads=[PP], writes=[T1])
                    s.op("act", lambda e: e.activation(out=T2[:, 0:n], in_=T1[:, 0:n], func=AF.Exp, scale=sg), reads=[T1], writes=[T2])
                    s.op("dve", lambda e: e.scalar_tensor_tensor(out=Q[d][:, st:st + n], in0=qs[:, st:st + n],
                                                                 scalar=(0.125 if kind == "h" else 1.0), in1=T2[:, 0:n],
                                                                 op0=ALU.mult, op1=ALU.mult),
                         reads=[qs, T2], acc=[Qv[d][j]], selfdep=False)
                    if kind == "m":
                        s.op("dve", lambda e: e.scalar_tensor_tensor(out=T3[:, 0:n], in0=T1[:, 0:n], scalar=-sg, in1=igb[:, st:st + n],
                                                                     op0=ALU.mult, op1=ALU.add), reads=[T1, igb], writes=[T3])
                        s.op("act", lambda e: e.activation(out=T3[:, 0:n], in_=T3[:, 0:n], func=AF.Exp), reads=[T3], writes=[T3])
                    else:
                        s.op("act", lambda e: e.activation(out=T3[:, 0:n], in_=T1[:, 0:n], func=AF.Exp, scale=-sg), reads=[T1], writes=[T3])
                    s.op("dve", lambda e: e.tensor_tensor(out=T4[:, 0:n], in0=kk[:, st:st + n], in1=T3[:, 0:n], op=ALU.mult),
                         reads=[kk, T3], writes=[T4])
                    for hh in range(2):
                        hs = slice(64 * hh, 64 * hh + 64)
                        if hh == 0:
                            s.op("act", lambda e: e.activation(out=K[d][hs, c0:c0 + ncb, L * hh:L * hh + L],
                                                               in_=T4[hs, 0:n].rearrange("p (c l) -> p c l", l=L), func=AF.Copy),
                                 reads=[T4], acc=[Kv[d][j]], selfdep=False)
                        else:
                            s.op("pool", lambda e: e.tensor_copy(out=K[d][hs, c0:c0 + ncb, L * hh:L * hh + L],
                                                                 in_=T4[hs, 0:n].rearrange("p (c l) -> p c l", l=L)),
                                 reads=[T4], acc=[Kv[d][j]], selfdep=False)
                    s.op("pool", lambda e: e.tensor_tensor(out=KN[d][:, st:st + n].rearrange("p (c l) -> p c l", l=L), in0=v3(T4),
                                                           in1=G[d][:, c0:c0 + ncb].unsqueeze(2).to_broadcast([128, ncb, L]),
                                                           op=ALU.mult), reads=[T4, G[d]], acc=[KNv[d][j]], selfdep=False)
                for v in range(NV):
                    s.op("pool", lambda e: e.memset(W32[d][v][:, :], 0.0), writes=[W32[d][v]])
            if dbg and stop_after == "hbuild":
                for d in range(2):
                    pass
                s.dma("sp", dbgP[:, :], PP[:, :], reads=[PP], writes=[dbgPb])
                s.release(m0)
                return
            seq = [(step, d) for step in range(NCH) for d in range(2)]
            slot0 = slot

            def phaseA(i):
                step, d = seq[i]
                c = order[d][step]
                csl = slice(c * L, c * L + L)
                sl = (slot0 + i) % 4
                lastst = (step == NCH - 1)
                kt = kts[sl]
                pm = pms[sl]
                ptr, psc = PTR[d], PSC[d]
                pb = (c % PG) * SR
                if not lastst:
                    s.op("pe", lambda e: e.transpose(out=ptr[pb:pb + L, 0:128], in_=KN[d][:, csl], identity=identb[:, :]),
                         reads=[KNv[d][blk_of(c)], identb], writes=[ptr])
                    s.op("act", lambda e: e.activation(out=kt[pb:pb + L, :], in_=ptr[pb:pb + L, 0:128], func=AF.Copy),
                         reads=[ptr], writes=[kt])
                s.mm(psc[pb:pb + SR, 0:L], K[d][:, c, :], Q[d][:, csl], True, True,
                     reads=[Kv[d][blk_of(c)], Qv[d][blk_of(c)]], writes=[psc])
                s.op("dve", lambda e: e.tensor_tensor(out=pm[pb:pb + SR, :], in0=psc[pb:pb + SR, 0:L],
                                                      in1=maskFB[L][d][pb:pb + SR, :], op=ALU.mult),
                     reads=[psc, cs], writes=[pm])

            def phaseB(i):
                step, d = seq[i]
                c = order[d][step]
                csl = slice(c * L, c * L + L)
                sl = (slot0 + i) % 4
                first = (step == 0)
                lastst = (step == NCH - 1)
                kt = kts[sl]
                pm = pms[sl]
                pso, pst = PSO[d], PST[d]
                pb = (c % PG) * SR
                cq = c // PG
                for v in range(NV):
                    Vb = Vbd[pb:pb + SR, cq, :] if v == 0 else bd1[:, :]
                    vo = slice(v * 64, v * 64 + L)
                    s.mm(pso[:, vo], Vb, pm[pb:pb + SR, :], True, first, reads=[Vbd, bd1, pm],
                         writes=[pso] if v == 0 else [], acc=[] if v == 0 else [pso])
                    if not first:
                        for hh in range(2):
                            hs = slice(64 * hh, 64 * hh + 64)
                            s.mm(pso[hs, vo], Wbf[d][v][hs, :], Q[d][hs, csl], False, hh == 1,
                                 reads=[Wbf[d][v], Qv[d][blk_of(c)]], acc=[pso])
                for v in range(NV):
                    vo = slice(v * 64, v * 64 + L)
                    A_ = accs[d][v]
                    s.op("act", lambda e: e.activation(out=A_[:, csl], in_=pso[:, vo], func=AF.Copy),
                         reads=[pso], acc=[A_], selfdep=False)
                if not lastst:
                    for v in range(NV):
                        vt = slice(v * 64, v * 64 + 64)
                        for hh in range(2):
                            hs = slice(64 * hh, 64 * hh + 64)
                            rhs = Vp[pb:pb + L, cq, hs] if v == 0 else onesb[pb:pb + L, 0:64]
                            wr = (v == 0 and hh == 0)
                            s.mm(pst[hs, vt], kt[pb:pb + L, hs], rhs, True, True, reads=[kt, Vp, onesb],
                                 writes=[pst] if wr else [], acc=[] if wr else [pst])
                    for v in range(NV):
                        vt = slice(v * 64, v * 64 + 64)
                        s.op("dve", lambda e: e.scalar_tensor_tensor(out=Wbf[d][v][:, :], in0=W32[d][v][:, :],
                                                                     scalar=G[d][:, c:c + 1], in1=pst[:, vt],
                                                                     op0=ALU.mult, op1=ALU.add),
                             reads=[W32[d][v], G[d], pst], writes=[Wbf[d][v]])
                    for v in range(NV):
                        vt = slice(v * 64, v * 64 + 64)
                        s.op("dve", lambda e: e.scalar_tensor_tensor(out=W32[d][v][:, :], in0=W32[d][v][:, :],
                                                                     scalar=G[d][:, c:c + 1], in1=pst[:, vt],
                                                                     op0=ALU.mult, op1=ALU.add),
                             reads=[W32[d][v], G[d], pst], writes=[W32[d][v]])

            phaseA(0)
            for i in range(len(seq)):
                if i + 1 < len(seq):
                    phaseA(i + 1)
                phaseB(i)
            slot += len(seq)
            s.dma("sp", qs[:, :], zog[rows, :], reads=[zog] + Qv[0] + Qv[1], writes=[qs])
            chunk = (0 if kind == "h" else 2) + pc
            for j in blocks:
                st, n = BLOCKS[j]
                O = ntmp()
                if kind == "m":
                    hd = []
                    for d in range(2):
                        num, den = accs[d]
                        Ta = ntmp()
                        Th = ntmp()
                        s.op("act", lambda e: e.activation(out=Ta[:, 0:n], in_=den[:, st:st + n], func=AF.Abs), reads=[den], writes=[Ta])
                        s.op("dve", lambda e: e.tensor_scalar_max(out=Ta[:, 0:n], in0=Ta[:, 0:n], scalar1=1.0), reads=[Ta], writes=[Ta])
                        s.op("act", lambda e: e.activation(out=Ta[:, 0:n], in_=Ta[:, 0:n], func=AF.Ln), reads=[Ta], writes=[Ta])
                        s.op("act", lambda e: e.activation(out=Ta[:, 0:n], in_=Ta[:, 0:n], func=AF.Exp, scale=-1.0), reads=[Ta], writes=[Ta])
                        s.op("dve", lambda e: e.tensor_tensor(out=Th[:, 0:n], in0=num[:, st:st + n], in1=Ta[:, 0:n], op=ALU.mult),
                             reads=[num, Ta], writes=[Th])
                        hd.append(Th)
                    s.op("pool", lambda e: e.tensor_tensor(out=O[:, 0:n], in0=hd[0][:, 0:n], in1=hd[1][:, 0:n], op=ALU.add),
                         reads=hd, writes=[O])
                else:
                    s.op("pool", lambda e: e.tensor_tensor(out=O[:, 0:n], in0=accs[0][0][:, st:st + n], in1=accs[1][0][:, st:st + n],
                                                           op=ALU.add), reads=[accs[0][0], accs[1][0]], writes=[O])
                SQ = sqs[j % 2]
                R = ntmp()
                T = ntmp()
                s.op("act", lambda e: e.activation(out=SQ[:, 0:n], in_=O[:, 0:n], func=AF.Square), reads=[O], writes=[SQ])
                s.mm(pn[:, 0:n], blkb[:, :], SQ[:, 0:n], True, True, reads=[blkb, SQ], writes=[pn])
                s.op("act", lambda e: e.activation(out=R[:, 0:n], in_=pn[:, 0:n], func=AF.Ln, scale=1.0 / 64, bias=EPS),
                     reads=[pn], writes=[R])
                s.op("act", lambda e: e.activation(out=R[:, 0:n], in_=R[:, 0:n], func=AF.Exp, scale=-0.5), reads=[R], writes=[R])
                s.op("dve", lambda e: e.scalar_tensor_tensor(out=T[:, 0:n], in0=O[:, 0:n], scalar=gcol[:, l, gidx:gidx + 1],
                                                             in1=R[:, 0:n], op0=ALU.mult, op1=ALU.mult),
                     reads=[O, R, gcol], writes=[T])
                s.op("dve", lambda e: e.tensor_tensor(out=hT[:, chunk, st:st + n], in0=T[:, 0:n], in1=qs[:, st:st + n],
                                                      op=ALU.mult), reads=[T, qs], acc=[hTb[j]], selfdep=False)
        s.release(m0)

    def attn_block(qb, kb, vfn, vbuf, st, n, kcs, chunk, sc_ps, num_ps, den_ps, pTs, rd, cnt, qz=None, ofn=None):
        j = [i for i, (a, _) in enumerate(BLOCKS) if a <= st < a + BLOCKS[i][1]][0]
        nk = len(kcs)
        its = [(ki, kc, hh) for ki, kc in enumerate(kcs) for hh in range(2)]
        scs = {}

        def issue_sc(i):
            ki, kc, hh = its[i]
            hs = slice(64 * hh, 64 * hh + 64)
            sc = sc_ps[cnt[0] % len(sc_ps)]
            pT = pTs[cnt[0] % len(pTs)]
            cnt[0] += 1
            if qz is None:
                s.mm(sc[:, 0:n], kb[hs, kc * 128:(kc + 1) * 128], qb[hs, st:st + n], True, True, reads=[kb, qb], writes=[sc])
            else:
                s.mm(sc[:, 0:n], kb[:, kc * 128:(kc + 1) * 128], qz[hh][:, st:st + n], True, True, reads=[kb, qz[hh]], writes=[sc])
            scs[i] = (sc, pT)

        issue_sc(0)
        if len(its) > 1:
            issue_sc(1)
        for i, (ki, kc, hh) in enumerate(its):
            hs = slice(64 * hh, 64 * hh + 64)
            if i + 2 < len(its):
                issue_sc(i + 2)
            sc, pT = scs.pop(i)
            s.op("act", lambda e: e.activation(out=pT[:, 0:n], in_=sc[:, 0:n], func=AF.Exp, scale=0.125),
                 reads=[sc], writes=[pT])
            first = (ki == 0)
            if qz is None:
                s.mm(num_ps[hs, 0:n], vfn(kc, hh), pT[:, 0:n], first, ki == nk - 1, reads=[pT, vbuf],
                     writes=[num_ps] if (first and hh == 0) else [], acc=[] if (first and hh == 0) else [num_ps])
                s.mm(den_ps[hs, 0:n], onesb[:, 0:64], pT[:, 0:n], first, ki == nk - 1, reads=[pT, onesb],
                     writes=[den_ps] if (first and hh == 0) else [], acc=[] if (first and hh == 0) else [den_ps])
            else:
                f0 = (i == 0)
                l0 = (i == len(its) - 1)
                s.mm(num_ps[:, 0:n], vfn(kc, hh), pT[:, 0:n], f0, l0, reads=[pT, vbuf],
                     writes=[num_ps] if f0 else [], acc=[] if f0 else [num_ps])
                s.mm(den_ps[:, 0:n], ofn(hh), pT[:, 0:n], f0, l0, reads=[pT, vbuf],
                     writes=[den_ps] if f0 else [], acc=[] if f0 else [den_ps])
        s.op("act", lambda e: e.activation(out=rd[:, 0:n], in_=den_ps[:, 0:n], func=AF.Ln), reads=[den_ps], writes=[rd])
        s.op("act", lambda e: e.activation(out=rd[:, 0:n], in_=rd[:, 0:n], func=AF.Exp, scale=-1.0), reads=[rd], writes=[rd])
        s.op("dve", lambda e: e.tensor_tensor(out=hT[:, chunk, st:st + n], in0=num_ps[:, 0:n], in1=rd[:, 0:n], op=ALU.mult),
             reads=[num_ps, rd], acc=[hTb[j]], selfdep=False)

    def gqa(b, l):
        m0 = s.mark()
        last = (l == nlayers - 1)
        rp = s.sbuf("g_rope", [128, 4096], F32)
        s.dma("sp", rp[:, :], ropec[:, :], writes=[rp])
        raw = [s.sbuf("g_raw%d" % i, [128, NT], F32) for i in range(2)]
        QK = [s.sbuf("g_qk%d" % i, [128, NT], BF16) for i in range(4)]
        KDx = s.sbuf("g_kd1", [128, NT], BF16)
        sqs = [s.sbuf("g_sq%d" % i, [128, 512], BF16) for i in range(4)]
        rss = [s.sbuf("g_rs%d" % i, [128, 512], F32) for i in range(4)]
        t1s = [s.sbuf("g_t1%d" % i, [128, 512], F32) for i in range(4)]
        t2s = [s.sbuf("g_t2%d" % i, [128, 512], F32) for i in range(4)]
        t3s = [s.sbuf("g_t3%d" % i, [128, 512], F32) for i in range(4)]
        Vz = s.sbuf("g_Vz", [128, 18, 4, 128], BF16)
        Oz = s.sbuf("g_Oz", [128, 2, 128], BF16)
        Qz = [[s.sbuf("g_qz%d%d" % (p_, h_), [128, NT], BF16) for h_ in range(2)] for p_ in range(2)]
        pTs = [s.sbuf("g_pT%d" % i, [128, 512], BF16) for i in range(4)]
        rds = [s.sbuf("g_rd%d" % i, [128, 512], F32) for i in range(2)]
        sc_ps = [s.psum("g_sc%d" % i, [128, 512], F32) for i in range(4)]
        num_ps = [s.psum("g_num%d" % i, [128, 512], F32) for i in range(2)]
        den_ps = [s.psum("g_den%d" % i, [128, 512], F32) for i in range(2)]
        s.op("dve", lambda e: e.memset(Vz[:, :, :, :], 0.0), writes=[Vz])
        s.op("pool", lambda e: e.memset(Oz[:, :, :], 0.0), writes=[Oz])
        for hh in range(2):
            s.op("pool", lambda e: e.memset(Oz[:, hh, 64 * hh:64 * hh + 64], 1.0), acc=[Oz], selfdep=True)
            for p_ in range(2):
                s.op("dve", lambda e: e.memset(Qz[p_][hh][64 * (1 - hh):64 * (1 - hh) + 64, :], 0.0), writes=[Qz[p_][hh]])
                s.dma("sp", Vz[:, :, 2 * p_ + hh, 64 * hh:64 * hh + 64],
                      v_tok[:, 512 + 64 * p_:512 + 64 * p_ + 64].rearrange("(c p) d -> p c d", p=128), reads=[v_tok], acc=[Vz])
        srcs = [(zq_g[0:128, :], 2), (zq_g[128:256, :], 2), (zk_g[0, :, :], 3)]
        it = 0
        for idx, (src, gi) in enumerate(srcs):
            R_ = raw[idx % 2]
            s.dma("sp", R_[:, :], src, reads=[zq_g, zk_g], writes=[R_])
            for j, (st, n) in enumerate(BLOCKS):
                SQ = sqs[it % 4]
                RS = rss[it % 4]
                T1 = t1s[it % 4]
                T2 = t2s[it % 4]
                T3 = t3s[it % 4]
                P1 = sc_ps[(2 * it) % 4]
                P2 = sc_ps[(2 * it + 1) % 4]
                it += 1
                s.op("act", lambda e: e.activation(out=SQ[:, 0:n], in_=R_[:, st:st + n], func=AF.Square), reads=[R_], writes=[SQ])
                s.mm(P1[:, 0:n], blkb[:, :], SQ[:, 0:n], True, True, reads=[blkb, SQ], writes=[P1])
                s.op("act", lambda e: e.activation(out=RS[:, 0:n], in_=P1[:, 0:n], func=AF.Ln, scale=1.0 / 64, bias=EPS),
                     reads=[P1], writes=[RS])
                s.op("act", lambda e: e.activation(out=RS[:, 0:n], in_=RS[:, 0:n], func=AF.Exp, scale=-0.5), reads=[RS], writes=[RS])
                s.op("dve", lambda e: e.scalar_tensor_tensor(out=T1[:, 0:n], in0=R_[:, st:st + n], scalar=gcol[:, l, gi:gi + 1],
                                                             in1=RS[:, 0:n], op0=ALU.mult, op1=ALU.mult),
                     reads=[R_, RS, gcol], writes=[T1])
                if st >= 256:
                    s.mm(P2[:, 0:n], ropeRT, T1[:, 0:n], True, True, reads=[cs, T1], writes=[P2])
                    s.op("dve", lambda e: e.tensor_tensor(out=T2[:, 0:n], in0=T1[:, 0:n], in1=rp[:, st - 256:st - 256 + n],
                                                          op=ALU.mult), reads=[T1, rp], writes=[T2])
                    s.op("dve", lambda e: e.tensor_tensor(out=T3[:, 0:n], in0=P2[:, 0:n],
                                                          in1=rp[:, 2048 + st - 256:2048 + st - 256 + n], op=ALU.mult),
                         reads=[P2, rp], writes=[T3])
                    s.op("dve", lambda e: e.tensor_tensor(out=QK[idx][:, st:st + n], in0=T2[:, 0:n], in1=T3[:, 0:n], op=ALU.add),
                         reads=[T2, T3], acc=[QK[idx]], selfdep=False)
                else:
                    s.op("act", lambda e: e.activation(out=QK[idx][:, st:st + n], in_=T1[:, 0:n], func=AF.Copy),
                         reads=[T1], acc=[QK[idx]], selfdep=False)
                if idx < 2:
                    for hh in range(2):
                        hs = slice(64 * hh, 64 * hh + 64)
                        s.op("pool", lambda e: e.tensor_copy(out=Qz[idx][hh][hs, st:st + n], in_=QK[idx][hs, st:st + n]),
                             reads=[QK[idx]], acc=[Qz[idx][hh]], selfdep=False)
        KD = [QK[3], KDx]
        for g in range(2):
            for half in range(2):
                s.dma("sp", KD[g][64 * half:64 * half + 64, :], QK[2][64 * g:64 * g + 64, :], reads=[QK[2]],
                      writes=[KD[g]] if half == 0 else [], acc=[] if half == 0 else [KD[g]])
        cnt = [0]
        qblocks = [1, 2, 3, 4] if last else [0, 1, 2, 3, 4]
        bi = 0
        for pc in range(2):
            for j in qblocks:
                st, n = BLOCKS[j]
                kcs = list(range(18)) if st >= 256 else [0, 1]
                attn_block(QK[pc], KD[pc], lambda kc, hh: Vz[:, kc, 2 * pc + hh, :], Vz, st, n, kcs, 4 + pc,
                           sc_ps, num_ps[bi % 2], den_ps[bi % 2], pTs, rds[bi % 2], cnt,
                           qz=Qz[pc], ofn=lambda hh: Oz[:, hh, :])
                bi += 1
        s.release(m0)

    def na(b, l):
        m0 = s.mark()
        last = (l == nlayers - 1)
        bias8 = s.sbuf("n_bias", [128, 4 * NU, 128], BF16)
        bst = [s.sbuf("n_bst%d" % i, [128, NU, 128], F32) for i in range(2)]
        qT = [s.sbuf("n_q%d" % i, [128, NT], BF16) for i in range(2)]
        kT = [s.sbuf("n_k%d" % i, [128, NT], BF16) for i in range(2)]
        Vn = s.sbuf("n_V", [128, 18, 256], BF16)
        pTw = [s.sbuf("n_pTw%d" % i, [128, 2048], BF16) for i in range(2)]
        qz = [[s.sbuf("n_qz%d%d" % (p_, h_), [128, NT], BF16) for h_ in range(2)] for p_ in range(2)]
        Vnz = s.sbuf("n_Vz", [128, 18, 4, 128], BF16)
        Oz = s.sbuf("n_Oz", [128, 2, 128], BF16)
        negt = s.sbuf("n_neg", [128, 128], BF16)
        s.op("pool", lambda e: e.memset(negt[:, :], -8e30), writes=[negt])
        s.op("dve", lambda e: e.memset(Vnz[:, :, :, :], 0.0), writes=[Vnz])
        s.op("pool", lambda e: e.memset(Oz[:, :, :], 0.0), writes=[Oz])
        for hh in range(2):
            s.op("pool", lambda e: e.memset(Oz[:, hh, 64 * hh:64 * hh + 64], 1.0), acc=[Oz])
        for h_ in range(4):
            s.dma("sp", Vnz[:, :, h_, 64 * (h_ % 2):64 * (h_ % 2) + 64],
                  v_tok[:, 640 + 64 * h_:640 + 64 * h_ + 64].rearrange("(c p) d -> p c d", p=128), reads=[v_tok], acc=[Vnz])
        pTd = [s.sbuf("n_pTd%d" % i, [128, 512], BF16) for i in range(2)]
        rds = [s.sbuf("n_rd%d" % i, [128, 512], F32) for i in range(2)]
        sc_ps = [s.psum("n_sc%d" % i, [128, 512], F32) for i in range(4)]
        num_ps = [s.psum("n_num%d" % i, [128, 512], F32) for i in range(2)]
        den_ps = [s.psum("n_den%d" % i, [128, 512], F32) for i in range(2)]
        for h in range(4):
            B_ = bst[h % 2]
            s.dma("sp", B_[:, :, :], nab[l, h * NU:(h + 1) * NU, :, :].rearrange("u p q -> p u q"), writes=[B_])
            s.op("act", lambda e: e.activation(out=bias8[:, h * NU:(h + 1) * NU, :], in_=B_[:, :, :], func=AF.Copy, scale=8.0),
                 reads=[B_], acc=[bias8], selfdep=False)
        for pc in range(2):
            rows = slice(pc * 128, pc * 128 + 128)
            s.dma("sp", qT[pc][:, :], zq_n[rows, :], reads=[zq_n], writes=[qT[pc]])
            s.dma("sp", kT[pc][:, :], zk_n[rows, :], reads=[zk_n], writes=[kT[pc]])
        s.dma("sp", Vn[:, :, :], v_tok[:, 640:896].rearrange("(c p) d -> p c d", p=128), reads=[v_tok], writes=[Vn])
        for p_ in range(2):
            for hh in range(2):
                hs = slice(64 * hh, 64 * hh + 64)
                ho = slice(64 * (1 - hh), 64 * (1 - hh) + 64)
                s.op("dve", lambda e: e.memset(qz[p_][hh][ho, :], 0.0), writes=[qz[p_][hh]])
                s.dma("sp", qz[p_][hh][hs, :], zq_n[p_ * 128 + 64 * hh:p_ * 128 + 64 * hh + 64, :], reads=[zq_n], acc=[qz[p_][hh]])
        it = 0
        for pc in range(2):
            units = [(T, hh) for T in range(8) for hh in range(2)]
            for ui, (T, hh) in enumerate(units):
                q0 = 256 + 256 * T
                jb = 1 + T // 2
                h = 2 * pc + hh
                loc = sorted(set(j for (tt, j) in _NA_TMAP if tt in (2 * T, 2 * T + 1)))
                allk = [(0, None)] + [(1, None)] + [(2 + j, j) for j in loc]
                nk = len(allk)
                g = it + ui
                NUM = num_ps[T % 2]
                DEN = den_ps[T % 2]
                RD = rds[T % 2]
                pT = pTw[g % 2]
                nbank = (nk + 1) // 2
                for bk_i in range(nbank):
                    bk = sc_ps[bk_i]
                    chunks = allk[2 * bk_i:2 * bk_i + 2]
                    for ci, (gc, j) in enumerate(chunks):
                        cc = slice(ci * 256, ci * 256 + 256)
                        s.mm(bk[:, cc], kT[pc][:, gc * 128:(gc + 1) * 128], qz[pc][hh][:, q0:q0 + 256], True, j is None,
                             reads=[kT[pc], qz[pc][hh]], writes=[bk] if ci == 0 else [], acc=[] if ci == 0 else [bk])
                        if j is not None:
                            for tt in range(2):
                                u = _NA_TMAP.get((2 * T + tt, j))
                                brhs = bias8[:, h * NU + u, :] if u is not None else negt[:, :]
                                c2 = slice(ci * 256 + 128 * tt, ci * 256 + 128 * tt + 128)
                                s.mm(bk[:, c2], identb[:, :], brhs, False, tt == 1, reads=[identb, bias8, negt], acc=[bk])
                    ncols = 256 * len(chunks)
                    s.op("act", lambda e: e.activation(out=pT[:, bk_i * 512:bk_i * 512 + ncols], in_=bk[:, 0:ncols], func=AF.Exp, scale=0.125),
                         reads=[bk], writes=[pT] if bk_i == 0 else [], acc=[] if bk_i == 0 else [pT], selfdep=(bk_i == 0))
                for i, (gc, j) in enumerate(allk):
                    f0 = (i == 0 and hh == 0)
                    l0 = (i == nk - 1 and hh == 1)
                    s.mm(NUM[:, 0:256], Vnz[:, gc, h, :], pT[:, i * 256:(i + 1) * 256], f0, l0,
                         reads=[Vnz, pT], writes=[NUM] if f0 else [], acc=[] if f0 else [NUM])
                    s.mm(DEN[:, 0:256], Oz[:, hh, :], pT[:, i * 256:(i + 1) * 256], f0, l0,
                         reads=[Oz, pT], writes=[DEN] if f0 else [], acc=[] if f0 else [DEN])
                if hh == 1:
                    s.op("act", lambda e: e.activation(out=RD[:, 0:256], in_=DEN[:, 0:256], func=AF.Ln), reads=[DEN], writes=[RD])
                    s.op("act", lambda e: e.activation(out=RD[:, 0:256], in_=RD[:, 0:256], func=AF.Exp, scale=-1.0), reads=[RD], writes=[RD])
                    s.op("dve", lambda e: e.tensor_tensor(out=hT[:, 6 + pc, q0:q0 + 256], in0=NUM[:, 0:256], in1=RD[:, 0:256],
                                                          op=ALU.mult), reads=[NUM, RD], acc=[hTb[jb]], selfdep=False)
            it += len(units)
            if not last:
                cnt = [0]
                attn_block(qT[pc], kT[pc], lambda kc, hh: Vn[:, kc, (2 * pc + hh) * 64:(2 * pc + hh) * 64 + 64], Vn, 0, 256, [0, 1],
                           6 + pc, sc_ps, num_ps[0], den_ps[0], pTd, rds[0], cnt)
        s.release(m0)

    def p4(b, l):
        last = (l == nlayers - 1)
        halves = [[0, 1, 2], [3, 4]]
        if last:
            halves[0] = [1, 2]
        for blks in halves:
            m0 = s.mark()
            base = BLOCKS[blks[0]][0]
            xh = s.sbuf("xh", [128, 8, 1280], F32)
            xhv = {j: Buf(xh.t, "xh.%d" % j) for j in blks}
            stg = [s.sbuf("p4stg%d" % i, [128, 8, 512], F32) for i in range(2)]
            wb = [s.sbuf("p4wb%d" % i, [128, 8, 512], BF16) for i in range(2)]
            w2b = [s.sbuf("p4w2b%d" % i, [128, 4, 1024], BF16) for i in range(2)]
            ub = [s.sbuf("p4u%d" % i, [128, 4, 512], BF16) for i in range(2)]
            rb = [s.sbuf("p4r%d" % i, [128, 512], F32) for i in range(3)]
            sq = s.sbuf("p4sq", [128, 8, 512], BF16)
            rs = s.sbuf("p4rs", [128, 512], F32)
            tmps = [s.sbuf("p4t%d" % i, [128, 512], F32) for i in range(3)]
            ps = [s.psum("p4ps%d" % i, [128, 512], F32) for i in range(7)]
            pss = s.psum("p4pss", [128, 512], F32)
            pi = [0]

            def nps():
                pi[0] += 1
                return ps[pi[0] % 7]

            for cg in range(2):
                s.dma("sp", stg[cg][:, :, :], w_out[l, :, cg * 512:(cg + 1) * 512].rearrange("(k p) n -> p k n", p=128),
                      writes=[stg[cg]])
            for j in blks:
                st, n = BLOCKS[j]
                s.dma("sp", xh[:, :, st - base:st - base + n], xs[:, :, st:st + n], reads=[xs], writes=[xhv[j]])
            for cg in range(0):
                pass
            for cg in range(2):
                if cg == 0:
                    s.op("act", lambda e: e.activation(out=wb[cg][:, :, :], in_=stg[cg][:, :, :], func=AF.Copy),
                         reads=[stg[cg]], writes=[wb[cg]])
                else:
                    s.op("dve", lambda e: e.tensor_copy(out=wb[cg][:, :, :], in_=stg[cg][:, :, :]), reads=[stg[cg]], writes=[wb[cg]])
            for cg in range(2):
                for j in blks:
                    st, n = BLOCKS[j]
                    col = 2 if st < 256 else b
                    for mi in range(4):
                        m = cg * 4 + mi
                        P = nps()
                        for k in range(8):
                            s.mm(P[:, 0:n], wb[cg][:, k, mi * 128:(mi + 1) * 128], hT[:, k, st:st + n], k == 0, k == 7,
                                 reads=[wb[cg], hTb[j]], writes=[P] if k == 0 else [], acc=[] if k == 0 else [P])
                        xa = xh[:, m, st - base:st - base + n]
                        s.op("dve", lambda e: e.scalar_tensor_tensor(out=xa, in0=P[:, 0:n], scalar=modT[:, l, 16 + m, col:col + 1],
                                                                     in1=xa, op0=ALU.mult, op1=ALU.add),
                             reads=[P, modT, xhv[j]], acc=[xhv[j]], selfdep=False)
            for j in blks:
                st, n = BLOCKS[j]
                col = 2 if st < 256 else b
                norm_block(xhv[j], st - base, n, st, l, 1, col, sq, pss, rs, tmps, j)
            def loadw_dma(g):
                s.dma("sp", stg[0][:, :, :], w1[l, :, g * 512:(g + 1) * 512].rearrange("(k p) n -> p k n", p=128), writes=[stg[0]])
                s.dma("sp", stg[1][:, :, :].rearrange("p k n -> p (k n)").rearrange("p (c n) -> p c n", c=4),
                      w2[l, g * 512:(g + 1) * 512, :].rearrange("(c p) n -> p c n", p=128), writes=[stg[1]])

            def loadw_cast(g):
                s.op("act", lambda e: e.activation(out=wb[g % 2][:, :, :], in_=stg[0][:, :, :], func=AF.Copy),
                     reads=[stg[0]], writes=[wb[g % 2]])
                s.op("dve", lambda e: e.tensor_copy(out=w2b[g % 2][:, :, :],
                                                    in_=stg[1][:, :, :].rearrange("p k n -> p (k n)").rearrange("p (c n) -> p c n", c=4)),
                     reads=[stg[1]], writes=[w2b[g % 2]])

            loadw_dma(0)
            loadw_cast(0)
            items = [(g, bi_, j) for g in range(8) for bi_, j in enumerate(blks)]
            Us = {}

            def mlp_u(i):
                g, bi_, j = items[i]
                if bi_ == 0 and g + 1 < 8:
                    loadw_dma(g + 1)
                W1 = wb[g % 2]
                st, n = BLOCKS[j]
                U = ub[i % 2]
                Us[i] = U
                for hc in range(4):
                    P = nps()
                    for k in range(8):
                        s.mm(P[:, 0:n], W1[:, k, hc * 128:(hc + 1) * 128], hT[:, k, st:st + n], k == 0, k == 7,
                             reads=[W1, hTb[j]], writes=[P] if k == 0 else [], acc=[] if k == 0 else [P])
                    R_ = rb[hc % 3]
                    s.op("act", lambda e: e.activation(out=R_[:, 0:n], in_=P[:, 0:n], func=AF.Relu), reads=[P], writes=[R_])
                    s.op("dve", lambda e: e.tensor_tensor(out=U[:, hc, 0:n], in0=R_[:, 0:n], in1=R_[:, 0:n], op=ALU.mult),
                         reads=[R_], writes=[U] if hc == 0 else [], acc=[] if hc == 0 else [U], selfdep=(hc == 0))

            def mlp_y(i):
                g, bi_, j = items[i]
                W2 = w2b[g % 2]
                st, n = BLOCKS[j]
                col = 2 if st < 256 else b
                U = Us.pop(i)
                for m in range(8):
                    P = nps()
                    for hc in range(4):
                        s.mm(P[:, 0:n], W2[:, hc, m * 128:(m + 1) * 128], U[:, hc, 0:n], hc == 0, hc == 3,
                             reads=[W2, U], writes=[P] if hc == 0 else [], acc=[] if hc == 0 else [P])
                    xa = xh[:, m, st - base:st - base + n]
                    s.op("dve", lambda e: e.scalar_tensor_tensor(out=xa, in0=P[:, 0:n], scalar=modT[:, l, 40 + m, col:col + 1],
                                                                 in1=xa, op0=ALU.mult, op1=ALU.add),
                         reads=[P, modT, xhv[j]], acc=[xhv[j]], selfdep=False)

            mlp_u(0)
            for i in range(len(items)):
                g, bi_, j = items[i]
                if i + 1 < len(items):
                    g2, b2, _ = items[i + 1]
                    if b2 == 0:
                        loadw_cast(g2)
                    mlp_u(i + 1)
                mlp_y(i)
            if not last:
                for j in blks:
                    st, n = BLOCKS[j]
                    s.dma("sp", xs[:, :, st:st + n], xh[:, :, st - base:st - base + n], reads=[xhv[j]], acc=[xs])
            else:
                yb = stg[0]
                youts = [Buf(stg[1].t, "yout%d" % i) for i in range(2)]
                oc = 0
                for j in blks:
                    st, n = BLOCKS[j]
                    s.op("act", lambda e: e.activation(out=sq[:, :, 0:n], in_=xh[:, :, st - base:st - base + n], func=AF.Square),
                         reads=[xhv[j]], writes=[sq])
                    for k in range(8):
                        s.mm(pss[:, 0:n], onesb[:, :], sq[:, k, 0:n], k == 0, k == 7, reads=[sq, onesb],
                             writes=[pss] if k == 0 else [], acc=[] if k == 0 else [pss])
                    s.op("act", lambda e: e.activation(out=rs[:, 0:n], in_=pss[:, 0:n], func=AF.Ln, scale=1.0 / 1024, bias=EPS),
                         reads=[pss], writes=[rs])
                    s.op("act", lambda e: e.activation(out=rs[:, 0:n], in_=rs[:, 0:n], func=AF.Exp, scale=-0.5), reads=[rs], writes=[rs])
                    for k in range(8):
                        s.op("dve", lambda e: e.scalar_tensor_tensor(out=yb[:, k, 0:n], in0=xh[:, k, st - base:st - base + n],
                                                                     scalar=gT[:, 4, k:k + 1], in1=rs[:, 0:n],
                                                                     op0=ALU.mult, op1=ALU.mult),
                             reads=[xhv[j], gT, rs], writes=[yb] if k == 0 else [], acc=[] if k == 0 else [yb], selfdep=(k == 0))
                    for ti in range(n // 128):
                        YO = youts[oc % 2]
                        yo_ap = stg[1][:, (oc % 2) * 2:(oc % 2) * 2 + 2, :].rearrange("p a n -> p (a n)")
                        oc += 1
                        for half in range(2):
                            P = nps()
                            for kk in range(4):
                                k = half * 4 + kk
                                s.op("pe", lambda e: e.transpose(out=P[:, kk * 128:(kk + 1) * 128],
                                                                 in_=yb[:, k, ti * 128:(ti + 1) * 128], identity=identf),
                                     reads=[yb, cs], writes=[P] if kk == 0 else [], acc=[] if kk == 0 else [P])
                            s.op("act", lambda e: e.activation(out=yo_ap[:, half * 512:(half + 1) * 512], in_=P[:, :], func=AF.Copy),
                                 reads=[P], writes=[YO] if half == 0 else [], acc=[] if half == 0 else [YO], selfdep=(half == 0))
                        tok = st - 256 + ti * 128
                        s.dma("sp", y[b, tok:tok + 128, :], yo_ap, reads=[YO], acc=[y])
            s.release(m0)

    prologue()
    for b in range(nseq if stop_after != "pro" else 0):
        for l in range(nlayers):
            p1(b, l)
            if stop_after == "p1":
                break
            p2(b, l)
            if stop_after == "p2":
                break
            recur(b, l, "h")
            if stop_after in ("h", "hbuild"):
                break
            recur(b, l, "m")
            if stop_after == "m":
                break
            gqa(b, l)
            if stop_after == "g":
                break
            na(b, l)
            if stop_after == "p3":
                break
            p4(b, l)
        if stop_after is not None:
            break
    if dbg:
        s.dma("sp", cat_dbg[:, :, :], hT[:, :, :], reads=hTb, writes=[cat_dbg])
    s.finish()
    build.stats = (s.nops, s.nwaits, dict(s.cnt))
    return nc


_CACHE = {}


def _host_inputs(inputs, core):
    f = lambda a: np.ascontiguousarray(np.asarray(a, dtype=np.float32))
    b0 = 2 * core
    cst, rope = _CACHE["consts"]
    m = {
        "x": f(inputs["x"][b0:b0 + 2]),
        "ctx": f(inputs["ctx"][b0:b0 + 2]),
        "cvec": f(np.concatenate([inputs["c"][b0:b0 + 2], np.asarray(inputs["c_ctx"])[None, :]], 0)),
        "w_mod": f(inputs["w_mod"]), "b_mod": f(inputs["b_mod"]),
        "norm1_g": f(inputs["norm1_g"]), "norm2_g": f(inputs["norm2_g"]),
        "w_in": f(inputs["w_in"]),
        "hgrn_lb_logits": f(np.asarray(inputs["hgrn_lb_logits"]).reshape(4, 256)),
        "hgrn_norm_g": f(inputs["hgrn_norm_g"]), "mlstm_gate_b": f(inputs["mlstm_gate_b"]),
        "mlstm_norm_g": f(inputs["mlstm_norm_g"]), "gqa_qnorm_g": f(inputs["gqa_qnorm_g"]),
        "gqa_knorm_g": f(inputs["gqa_knorm_g"]),
        "na_bias": _CACHE["na_bias"],
        "w_out": f(inputs["w_out"]), "w_mlp1": f(inputs["w_mlp1"]), "w_mlp2": f(inputs["w_mlp2"]),
        "final_norm_g": f(np.asarray(inputs["final_norm_g"]).reshape(1, 1024)),
        "consts": cst, "rope": rope,
    }
    return m


def _prep(inputs):
    _CACHE["consts"] = _consts()
    idx = _na_gather_index()
    rpb = np.asarray(inputs["na_rpb"], np.float32)
    flat = np.concatenate([rpb.reshape(2, 4, 465), np.full((2, 4, 1), NEG, np.float32)], -1)
    nb = flat[:, :, idx]
    _CACHE["na_bias"] = np.ascontiguousarray(nb.reshape(2, 4 * NU, 128, 128))


def kernel(**inputs):
    _prep(inputs)
    nc = build()
    in_maps = [_host_inputs(inputs, c) for c in range(8)]
    res = run_bass_kernel_spmd(nc, in_maps, core_ids=list(range(8)))
    out = np.concatenate([np.asarray(r["y"], np.float32) for r in res.results], axis=0)
    return out
```
